# Optimizing a Trainium2 kernel written in Bass

```python
import math
import jax, jax.numpy as jnp
from jax import lax
import numpy as np

D_MODEL = 1024
BATCH = 4
SEQ = 4096
DEPTH = 1

SB_HEADS = 8
SB_HEAD_DIM = 64
SB_WIDTH = SB_HEADS * SB_HEAD_DIM
Q_BLOCK = 128
CONV_WIDTH = D_MODEL // 2
CONV_K = 3
MEM_LEN = 256
MEM_HEADS = 4
MEM_HEAD_DIM = D_MODEL // MEM_HEADS
FFN_HIDDEN = -(-8 * D_MODEL // (3 * 256)) * 256
EPS = 1e-6

IN_SPLITS = (SB_WIDTH, SB_WIDTH, SB_WIDTH, CONV_WIDTH, CONV_WIDTH, CONV_WIDTH, D_MODEL, D_MODEL)
IN_WIDTH = sum(IN_SPLITS)

kernel_name = "hybrid_stickbreak_shortconv_memxattn_swiglu"


def rms_norm(x, g):
    xf = x.astype(jnp.float32)
    var = jnp.mean(xf * xf, axis=-1, keepdims=True)
    return (xf * lax.rsqrt(var + EPS) * g.astype(jnp.float32)).astype(x.dtype)


def stick_breaking_attention(q, k, v):
    seq = q.shape[2]
    scale = 1.0 / math.sqrt(q.shape[-1])
    outs = []
    for blk in range(seq // Q_BLOCK):
        t0 = blk * Q_BLOCK
        n_keys = t0 + Q_BLOCK
        qb = q[:, :, t0:n_keys].astype(jnp.float32)
        kb = k[:, :, :n_keys].astype(jnp.float32)
        vb = v[:, :, :n_keys].astype(jnp.float32)
        z = jnp.einsum('bhqd,bhkd->bhqk', qb, kb) * scale
        q_pos = t0 + jnp.arange(Q_BLOCK)[:, None]
        k_pos = jnp.arange(n_keys)[None, :]
        mask = k_pos < q_pos
        log_fail = jnp.where(mask, jax.nn.log_sigmoid(-z), 0.0)
        log_later = lax.cumsum(log_fail, axis=3, reverse=True) - log_fail
        weights = jnp.where(mask, jnp.exp(jax.nn.log_sigmoid(z) + log_later), 0.0)
        outs.append(jnp.einsum('bhqk,bhkd->bhqd', weights, vb))
    return jnp.concatenate(outs, axis=2).astype(v.dtype)


def causal_depthwise_conv(u, w):
    c = u.shape[-1]
    return lax.conv_general_dilated(
        u, w[:, None, :].astype(u.dtype), window_strides=(1,), padding=((CONV_K - 1, 0),),
        dimension_numbers=('NWC', 'WIO', 'NWC'), feature_group_count=c)


def memory_cross_attention(h, m, w_q, w_kv, w_o):
    b, s, _ = h.shape
    mlen = m.shape[1]
    q = (h @ w_q).reshape(b, s, MEM_HEADS, MEM_HEAD_DIM)
    kv = (m @ w_kv).reshape(b, mlen, 2, MEM_HEADS, MEM_HEAD_DIM)
    k, v = kv[:, :, 0], kv[:, :, 1]
    scores = jnp.einsum('bshd,bmhd->bhsm', q.astype(jnp.float32), k.astype(jnp.float32))
    probs = jax.nn.softmax(scores / math.sqrt(MEM_HEAD_DIM), axis=-1)
    o = jnp.einsum('bhsm,bmhd->bshd', probs, v.astype(jnp.float32)).astype(h.dtype)
    return o.reshape(b, s, D_MODEL) @ w_o


def setup_inputs(seed: int = 0) -> dict:
    key = jax.random.key(seed)
    ks = jax.random.split(key, 20)
    f32 = jnp.float32

    def w(k, shape, fan_in):
        return jax.random.normal(k, shape, f32) * (fan_in ** -0.5)

    def gain(k, shape):
        return 1.0 + 0.05 * jax.random.normal(k, shape, f32)

    return {
        "x": jax.random.normal(ks[0], (BATCH, SEQ, D_MODEL), f32),
        "mem": jax.random.normal(ks[1], (BATCH, MEM_LEN, D_MODEL), f32),
        "norm_mix": gain(ks[2], (DEPTH, D_MODEL)),
        "w_in": w(ks[3], (DEPTH, D_MODEL, IN_WIDTH), D_MODEL),
        "conv_w": w(ks[4], (DEPTH, CONV_K, CONV_WIDTH), CONV_K),
        "w_branch_a": w(ks[5], (DEPTH, SB_WIDTH, D_MODEL), SB_WIDTH),
        "w_branch_b": w(ks[6], (DEPTH, CONV_WIDTH, D_MODEL), CONV_WIDTH),
        "w_mix_out": w(ks[7], (DEPTH, D_MODEL, D_MODEL), D_MODEL),
        "norm_mem_q": gain(ks[8], (DEPTH, D_MODEL)),
        "norm_mem_kv": gain(ks[9], (DEPTH, D_MODEL)),
        "w_mem_q": w(ks[10], (DEPTH, D_MODEL, D_MODEL), D_MODEL),
        "w_mem_kv": w(ks[11], (DEPTH, D_MODEL, 2 * D_MODEL), D_MODEL),
        "w_mem_o": w(ks[12], (DEPTH, D_MODEL, D_MODEL), D_MODEL),
        "norm_ffn": gain(ks[13], (DEPTH, D_MODEL)),
        "w_ffn_in": w(ks[14], (DEPTH, D_MODEL, 2 * FFN_HIDDEN), D_MODEL),
        "w_ffn_out": w(ks[15], (DEPTH, FFN_HIDDEN, D_MODEL), FFN_HIDDEN),
        "norm_final": gain(ks[16], (D_MODEL,)),
    }


def reference(x, mem, norm_mix, w_in, conv_w, w_branch_a, w_branch_b, w_mix_out,
              norm_mem_q, norm_mem_kv, w_mem_q, w_mem_kv, w_mem_o,
              norm_ffn, w_ffn_in, w_ffn_out, norm_final):
    b, s, _ = x.shape
    split_points = list(np.cumsum(IN_SPLITS)[:-1])
    for l in range(DEPTH):
        h = rms_norm(x, norm_mix[l])
        proj = h @ w_in[l]
        q_a, k_a, v_a, u_b, gate_b, gate_c, g_a, g_b = jnp.split(proj, split_points, axis=-1)

        def heads(t):
            return t.reshape(b, s, SB_HEADS, SB_HEAD_DIM).transpose(0, 2, 1, 3)

        o_a = stick_breaking_attention(heads(q_a), heads(k_a), heads(v_a))
        o_a = o_a.transpose(0, 2, 1, 3).reshape(b, s, SB_WIDTH)
        y_b = gate_b * causal_depthwise_conv(gate_c * u_b, conv_w[l])

        branch_a = o_a @ w_branch_a[l]
        branch_b = y_b @ w_branch_b[l]
        merged = jax.nn.sigmoid(g_a) * branch_a + jax.nn.sigmoid(g_b) * branch_b
        x = x + merged @ w_mix_out[l]

        x = x + memory_cross_attention(rms_norm(x, norm_mem_q[l]), rms_norm(mem, norm_mem_kv[l]),
                                       w_mem_q[l], w_mem_kv[l], w_mem_o[l])

        hf = rms_norm(x, norm_ffn[l])
        gate, up = jnp.split(hf @ w_ffn_in[l], 2, axis=-1)
        x = x + (jax.nn.silu(gate) * up) @ w_ffn_out[l]
    return rms_norm(x, norm_final)
```

```python
import numpy as np
import concourse.bass as bass
import concourse.mybir as mybir
from concourse.bass_utils import run_bass_kernel_spmd
from contextlib import ExitStack

F32 = mybir.dt.float32
BF16 = mybir.dt.bfloat16
AF = mybir.ActivationFunctionType
ALU = mybir.AluOpType
AX = mybir.AxisListType

D = 1024
SEQ = 4096
NB = 4
TS = 512
T_OF = {0: (0, 3, 4, 7), 1: (1, 2, 5, 6)}
NCH = (2, 4, 6, 8)
NEG = -30000.0
EPS = 1e-6
FFN_H = 2816
NOWN = 2048
SAME_ENGINE_SYNC = True


class Buf:
    __slots__ = ("name", "w", "r")

    def __init__(self, name):
        self.name = name
        self.w = None
        self.r = {}


class Sched:
    ENGS = ("pe", "act", "dve", "pool", "sp")

    def __init__(self, nc, stack, n_dma_sems=8):
        self.nc = nc
        self.prog = {e: [] for e in self.ENGS}
        self.count = {e: 0 for e in self.ENGS}
        self.sem = {e: stack.enter_context(nc.semaphore("s_" + e)) for e in self.ENGS}
        self.seen = {e: {} for e in self.ENGS}
        self.dsem = {}
        self.dval = {}
        self.dring = {}
        self.dpos = {}
        idx = 0
        for q in ("sp", "pool"):
            ring = []
            for i in range(n_dma_sems):
                self.dsem[idx] = stack.enter_context(nc.semaphore("d_%s%d" % (q, i)))
                self.dval[idx] = 0
                ring.append(idx)
                idx += 1
            self.dring[q] = ring
            self.dpos[q] = 0
        self.n_inst = {e: 0 for e in self.ENGS}
        self.last_marked = {e: True for e in self.ENGS}

    def _wait(self, e, ev):
        if ev is None:
            return
        if ev[0] == "e":
            _, src, seq = ev
            if src == e and (e == "pe" or not SAME_ENGINE_SYNC):
                return
            assert self.count[src] >= seq, ("dependency on unissued mark", e, ev)
            key = ("e", src)
            val = seq
            sem = self.sem[src]
        else:
            _, sidx, val = ev
            key = ("d", sidx)
            sem = self.dsem[sidx]
        if self.seen[e].get(key, 0) >= val:
            return
        self.seen[e][key] = val
        self.prog[e].append(lambda eng, sem=sem, val=val: eng.wait_ge(sem, val))

    def _deps(self, e, reads, writes):
        for b in reads:
            self._wait(e, b.w)
        for b in writes:
            self._wait(e, b.w)
            for (k0, k1), v in list(b.r.items()):
                self._wait(e, (k0, k1, v))

    def _record(self, ev, reads, writes):
        key = (ev[0], ev[1])
        for b in reads:
            if b.r.get(key, 0) < ev[2]:
                b.r[key] = ev[2]
        for b in writes:
            b.w = ev
            b.r = {}

    def op(self, e, meth, reads, writes, *args, _mark=True, **kw):
        self._deps(e, reads, writes)
        sem = self.sem[e]
        if _mark:
            self.count[e] += 1
            self.prog[e].append(lambda eng: getattr(eng, meth)(*args, **kw).then_inc(sem, 1))
            ev = ("e", e, self.count[e])
        else:
            self.prog[e].append(lambda eng: getattr(eng, meth)(*args, **kw))
            ev = ("e", e, self.count[e] + 1)
        self.last_marked[e] = _mark
        self.n_inst[e] += 1
        self._record(ev, reads, writes)
        return ev

    def dma(self, q, xfers, reads=(), writes=()):
        self._deps(q, reads, writes)
        ring = self.dring[q]
        sidx = ring[self.dpos[q] % len(ring)]
        self.dpos[q] += 1
        if self.dval[sidx] > 0:
            self._wait(q, ("d", sidx, self.dval[sidx]))
        sem = self.dsem[sidx]
        for (o, i, kw) in xfers:
            self.dval[sidx] += 16
            self.prog[q].append(lambda eng, o=o, i=i, kw=kw: eng.dma_start(out=o, in_=i, **kw).then_inc(sem, 16))
        ev = ("d", sidx, self.dval[sidx])
        self._record(ev, reads, writes)
        return ev

    def barrier(self):
        for e in ("pe", "act", "dve"):
            assert self.last_marked[e], ("barrier with unmarked tail", e)
        for sidx in self.dring["sp"]:
            if self.dval[sidx] > 0:
                self._wait("sp", ("d", sidx, self.dval[sidx]))
        for f in ("pe", "act", "dve"):
            self._wait("sp", ("e", f, self.count[f]))
        self.count["sp"] += 1
        sem = self.sem["sp"]
        self.prog["sp"].append(lambda eng, sem=sem: eng.sem_inc(sem, 1))
        for e in ("pe", "act", "dve"):
            self._wait(e, ("e", "sp", self.count["sp"]))

    def emit(self):
        with self.nc.Block() as block:
            def mk(name):
                def body(engine):
                    for c in self.prog[name]:
                        c(engine)
                return body
            block.sync(mk("sp"))
            block.gpsimd(mk("pool"))
            block.scalar(mk("act"))
            block.vector(mk("dve"))
            block.tensor(mk("pe"))


def build_nc():
    nc = bass.Bass("TRN2", target_bir_lowering=False)

    def din(name, shape):
        return nc.dram_tensor(name, list(shape), F32, kind="ExternalInput").ap()

    xall = din("xall", [SEQ, D])
    xown = din("xown", [NOWN + 128, D])
    memb = din("memb", [256, D])
    seld_d = din("seld", [128, 4, 2, 128])
    seln_d = din("seln", [1, 4, 2, 128])
    diag_d = din("diag", [128, 4, 512])
    norm_mix = din("norm_mix", [1, D])
    w_in = din("w_in", [D, 5120])
    conv_w = din("conv_w", [3, 512])
    w_ba = din("w_branch_a", [512, D])
    w_bb = din("w_branch_b", [512, D])
    w_mix_out = din("w_mix_out", [D, D])
    norm_mem_q = din("norm_mem_q", [1, D])
    norm_mem_kv = din("norm_mem_kv", [1, D])
    w_mem_q = din("w_mem_q", [D, D])
    w_mem_kv = din("w_mem_kv", [D, 2 * D])
    w_mem_o = din("w_mem_o", [D, D])
    norm_ffn = din("norm_ffn", [1, D])
    w_ffn_in = din("w_ffn_in", [D, 2 * FFN_H])
    w_ffn_out = din("w_ffn_out", [FFN_H, D])
    norm_final = din("norm_final", [1, D])
    out_d = nc.dram_tensor("out", [NOWN, D], F32, kind="ExternalOutput").ap()

    with ExitStack() as st:
        S = Sched(nc, st)
        ARENA_F32 = 52900
        arena = st.enter_context(nc.sbuf_tensor("arena", [128, ARENA_F32], F32))
        psum = st.enter_context(nc.psum_tensor("psum", [128, 4096], F32))

        def R(off, shape, dt):
            esz = 4 if dt == F32 else 2
            n = int(np.prod(shape[1:]))
            nbytes = n * esz
            assert off % 4 == 0 and nbytes % 4 == 0, (off, shape)
            assert off + nbytes <= ARENA_F32 * 4, (off, shape)
            ap = arena[:, off // 4:(off + nbytes) // 4]
            if dt != F32:
                ap = ap.bitcast(dt)
            if len(shape) == 3:
                ap = ap.rearrange("p (a b) -> p a b", a=shape[1])
            elif len(shape) == 4:
                ap = ap.rearrange("p (a b c) -> p a b c", a=shape[1], b=shape[2])
            return ap

        def bank(i, n=1):
            return psum[:, i * 512:(i + n) * 512]

        def bank_bf(i, n=1):
            return psum[:, i * 512:(i + n) * 512].bitcast(BF16)

        b_ps = [Buf("ps%d" % i) for i in range(8)]

        ident = R(0, [128, 128], BF16); b_ident = Buf("ident")
        identf = R(256, [128, 128], F32); b_identf = Buf("identf")
        convw = R(768, [128, 4, 3], F32); b_convw = Buf("convw")
        stats = R(1024, [128, 256], F32)
        seld = R(2048, [128, 4, 2, 128], BF16); b_seld = Buf("seld")
        seln = R(4096, [128, 4, 2, 128], BF16); b_seln = Buf("seln")
        negrow = R(6144, [128, 512], BF16); b_negrow = Buf("negrow")
        zeros = R(7168, [128, 512], F32); b_zeros = Buf("zeros")
        diag = R(9216, [128, 4, 512], BF16); b_diag = Buf("diag")
        GAIN0 = 13312
        g_rep = [R(GAIN0 + i * 4096, [128, D], F32) for i in range(2)]
        b_g = [Buf("g%d" % i) for i in range(2)]
        RING0 = 21504
        NSLOT = 6
        DYN = RING0 + NSLOT * 8192

        ssq = stats[:, 0:8]; rstd = stats[:, 8:16]
        b_st = [Buf("st%d" % i) for i in range(8)]
        st_pos = [0]
        mx4 = stats[:, 16:20]; nb4 = stats[:, 20:24]; sm4 = stats[:, 24:28]; rs4 = stats[:, 28:32]
        b_mx = Buf("mx"); b_nb = Buf("nb"); b_sm = Buf("sm"); b_rs = Buf("rs")
        cuh = stats[:, 32:40]; b_cuh = Buf("cuh")

        class WStream:
            def __init__(self):
                self.blocks = []
                self.issued = 0
                self.released = set()
                self.bufs = [Buf("ring%d" % i) for i in range(NSLOT)]
                self.next_get = 0

            def add(self, pieces):
                self.blocks.append(pieces)
                return len(self.blocks) - 1

            def slot_ap(self, i):
                return R(RING0 + (i % NSLOT) * 8192, [128, 8, 512], BF16)

            def _issue(self):
                while self.issued < len(self.blocks):
                    k = self.issued
                    if k >= NSLOT and (k - NSLOT) not in self.released:
                        break
                    if k > self.next_get + NSLOT - 1:
                        break
                    sl = self.slot_ap(k)
                    xf = []
                    for (src, kc0, kcn, ncols) in self.blocks[k]:
                        xf.append((sl[:, kc0:kc0 + kcn, 0:ncols], src, {}))
                    S.dma("pool", xf, writes=[self.bufs[k % NSLOT]])
                    self.issued += 1

            def get(self, idx):
                assert idx == self.next_get, (idx, self.next_get)
                self.next_get += 1
                self._issue()
                assert self.issued > idx, ("weight block not issued", idx)
                return self.slot_ap(idx), self.bufs[idx % NSLOT]

            def release(self, idx):
                self.released.add(idx)
                self._issue()

        WS = WStream()

        def wpiece(w, r0, nrows, c0, ncols, kc0=0):
            src = w[r0:r0 + nrows, c0:c0 + ncols].rearrange("(kc p) n -> p kc n", p=128)
            return (src, kc0, nrows // 128, ncols)

        i_wk = WS.add([wpiece(w_in, 0, D, 512, 512)])
        i_wv = WS.add([wpiece(w_in, 0, D, 1024, 512)])
        i_wq = WS.add([wpiece(w_in, 0, D, 0, 512)])
        i_wu = WS.add([wpiece(w_in, 0, D, 1536, 512)])
        i_wgb = WS.add([wpiece(w_in, 0, D, 2048, 512)])
        i_wgc = WS.add([wpiece(w_in, 0, D, 2560, 512)])
        i_d = []
        for c4 in range(2):
            a = WS.add([wpiece(w_ba, 0, 512, c4 * 512, 512, kc0=0), wpiece(w_bb, 0, 512, c4 * 512, 512, kc0=4)])
            b_ = WS.add([wpiece(w_in, 0, D, 3072 + c4 * 512, 512)])
            c_ = WS.add([wpiece(w_in, 0, D, 4096 + c4 * 512, 512)])
            i_d.append((a, b_, c_))
        i_wmo = [WS.add([wpiece(w_mix_out, 0, D, hf * 512, 512)]) for hf in range(2)]
        i_wkvK = [WS.add([wpiece(w_mem_kv, 0, D, k * 512, 512)]) for k in range(2)]
        i_wkvV = [WS.add([wpiece(w_mem_kv, 0, D, D + k * 512, 512)]) for k in range(2)]
        i_wmq = [WS.add([wpiece(w_mem_q, 0, D, k * 512, 512)]) for k in range(2)]
        i_wmo2 = [WS.add([wpiece(w_mem_o, 0, D, k * 512, 512)]) for k in range(2)]
        i_ffn = []
        for tb in range(2):
            gu = []
            for blk in range(6):
                ncol = 512 if blk < 5 else 256
                ig = WS.add([wpiece(w_ffn_in, 0, D, blk * 512, ncol)])
                iu = WS.add([wpiece(w_ffn_in, 0, D, FFN_H + blk * 512, ncol)])
                gu.append((ig, iu, ncol))
            fo = []
            for hf in range(2):
                ks = []
                for k3 in range(3):
                    nr = 1024 if k3 < 2 else FFN_H - 2048
                    ks.append(WS.add([wpiece(w_ffn_out, k3 * 1024, nr, hf * 512, 512)]))
                fo.append(ks)
            i_ffn.append((gu, fo))

        evac_rr = [0]

        def evac(out_ap, in_ap, reads, writes, eng=None):
            if eng is None:
                eng = ("act", "dve")[evac_rr[0] % 2]
                evac_rr[0] += 1
            if eng == "act":
                return S.op("act", "activation", reads, writes, out=out_ap, in_=in_ap, func=AF.Copy)
            return S.op("dve", "tensor_copy", reads, writes, out=out_ap, in_=in_ap)

        def mm_group(out_ap, pairs, b_out, reads):
            n = len(pairs)
            for k, (l, r) in enumerate(pairs):
                S.op("pe", "matmul", reads, [b_out], out_ap, lhsT=l, rhs=r, start=(k == 0), stop=(k == n - 1),
                     _mark=(k == n - 1))

        def load_gain(i, src):
            S.dma("sp", [(g_rep[i], src.broadcast_to([128, D]), {})], writes=[b_g[i]])

        def norm_tile(x_ap, b_x, gi, out_ap, b_out, junk_ap, b_junk_):
            k = st_pos[0] % 8
            st_pos[0] += 1
            bs = b_st[k]
            S.op("act", "activation", [b_x], [b_junk_, bs], out=junk_ap, in_=x_ap, func=AF.Square,
                 accum_out=ssq[:, k:k + 1])
            S.op("act", "activation", [bs], [bs], out=rstd[:, k:k + 1], in_=ssq[:, k:k + 1], func=AF.Ln,
                 scale=1.0 / D, bias=EPS)
            S.op("act", "activation", [bs], [bs], out=rstd[:, k:k + 1], in_=rstd[:, k:k + 1], func=AF.Exp, scale=-0.5)
            S.op("dve", "scalar_tensor_tensor", [b_x, bs, b_g[gi]], [b_out], out=out_ap, in0=x_ap,
                 scalar=rstd[:, k:k + 1], in1=g_rep[gi], op0=ALU.mult, op1=ALU.mult)

        tr_rr = [0]

        def transpose_tile(xn_ap, b_xn_, dst3, b_dst, tbanks):
            bk = tbanks[tr_rr[0] % len(tbanks)]
            tr_rr[0] += 1
            pst = bank_bf(bk)
            for c in range(8):
                S.op("pe", "transpose", [b_xn_, b_ident], [b_ps[bk]], out=pst[:, c * 128:(c + 1) * 128],
                     in_=xn_ap[:, c * 128:(c + 1) * 128], identity=ident, _mark=(c == 7))
            S.op("act", "activation", [b_ps[bk]], [b_dst], out=dst3, in_=pst.rearrange("p (a b) -> p a b", a=8),
                 func=AF.Copy)

        S.op("dve", "memset", [], [b_identf], identf, 1.0)
        S.op("pool", "affine_select", [b_identf], [b_identf], out=identf, in_=identf, pattern=[[-1, 128]],
             compare_op=ALU.is_equal, fill=0.0, base=0, channel_multiplier=1)
        S.op("dve", "tensor_copy", [b_identf], [b_ident], out=ident, in_=identf)
        S.op("dve", "memset", [], [b_negrow], negrow, NEG)
        S.op("dve", "memset", [], [b_zeros], zeros, 0.0)
        S.dma("pool", [(seld, seld_d, {})], writes=[b_seld])
        S.dma("pool", [(seln[0:1], seln_d, {})], writes=[b_seln])
        S.dma("pool", [(diag, diag_d, {})], writes=[b_diag])
        S.dma("sp", [(convw[:, j, :], conv_w[:, j * 128:(j + 1) * 128].rearrange("i p -> p i"),
                      {"allow_slow_non_contiguous": True}) for j in range(4)], writes=[b_convw])
        load_gain(0, norm_mix)

        KT = R(DYN + 0, [128, 4, SEQ], BF16)
        V = R(DYN + 32768, [128, 32, 512], BF16)
        QT = R(DYN + 65536, [128, 4, NOWN], BF16)
        OAT = R(DYN + 81920, [128, 4, NOWN], BF16)
        b_oat = [[Buf("oat%d_%d" % (p, s)) for s in range(4)] for p in range(4)]
        TMP = DYN + 98304
        xring = [R(TMP + i * 4096, [128, D], F32) for i in range(4)]; b_xr = [Buf("xr%d" % i) for i in range(4)]
        xn = [R(TMP + 16384 + i * 2048, [128, D], BF16) for i in range(2)]; b_xn = [Buf("xn%d" % i) for i in range(2)]
        junk = R(TMP + 20480, [128, D], BF16); b_junk = Buf("junk")
        hTa = [R(TMP + 22528 + i * 8192, [128, 8, 512], BF16) for i in range(2)]
        b_hTa = [[Buf("hTa%d_%d" % (i, t)) for t in range(4)] for i in range(2)]
        b_kt = [[Buf("kt%d_%d" % (p, ch)) for ch in range(8)] for p in range(4)]
        b_v = [Buf("v%d" % ch) for ch in range(8)]
        b_qt = [[Buf("qt%d_%d" % (p, s)) for s in range(4)] for p in range(4)]

        wk, b_wk = WS.get(i_wk)
        wv, b_wv = WS.get(i_wv)
        wq, b_wq = WS.get(i_wq)

        xl = [0]
        mmb = [0]

        def stream_tile(src_rows, gi, dst3, b_dst, tbanks):
            i = xl[0]
            xl[0] += 1
            xt = xring[i % 4]; bx = b_xr[i % 4]
            S.dma("sp", [(xt, src_rows, {})], writes=[bx])
            xo = xn[i % 2]; bxo = b_xn[i % 2]
            norm_tile(xt, bx, gi, xo, bxo, junk, b_junk)
            transpose_tile(xo, bxo, dst3, b_dst, tbanks)

        for ch in range(8):
            hb = hTa[ch % 2]; bh = b_hTa[ch % 2]
            for t in range(4):
                r0 = ch * 512 + t * 128
                stream_tile(xall[r0:r0 + 128, :], 0, hb[:, :, t * 128:(t + 1) * 128], bh[t], (6, 7))
            for p in range(4):
                bk = mmb[0] % 6; mmb[0] += 1
                mm_group(bank(bk), [(wk[:, kc, p * 128:(p + 1) * 128], hb[:, kc, :]) for kc in range(8)],
                         b_ps[bk], [b_wk] + bh)
                evac(KT[:, p, ch * 512:(ch + 1) * 512], bank(bk), [b_ps[bk]], [b_kt[p][ch]])
            for t in range(4):
                bk = mmb[0] % 6; mmb[0] += 1
                mm_group(bank(bk), [(hb[:, kc, t * 128:(t + 1) * 128], wv[:, kc, :]) for kc in range(8)],
                         b_ps[bk], [b_wv, bh[t]])
                evac(V[:, ch * 4 + t, :], bank(bk), [b_ps[bk]], [b_v[ch]])
        WS.release(i_wk); WS.release(i_wv)

        for s in range(4):
            hb = hTa[s % 2]; bh = b_hTa[s % 2]
            for t in range(4):
                r0 = s * 512 + t * 128
                stream_tile(xown[r0:r0 + 128, :], 0, hb[:, :, t * 128:(t + 1) * 128], bh[t], (6, 7))
            for p in range(4):
                bk = mmb[0] % 6; mmb[0] += 1
                mm_group(bank(bk), [(wq[:, kc, p * 128:(p + 1) * 128], hb[:, kc, :]) for kc in range(8)],
                         b_ps[bk], [b_wq] + bh)
                evac(QT[:, p, s * 512:(s + 1) * 512], bank(bk), [b_ps[bk]], [b_qt[p][s]])
        WS.release(i_wq)
        S.barrier()

        Fb = [R(TMP + i * 8192, [128, 4, 512], F32) for i in range(2)]; b_F = [Buf("F%d" % i) for i in range(2)]
        Pb = R(TMP + 16384, [128, 4, 516], F32); b_P = Buf("P")
        Wb = [R(TMP + 24640 + i * 4096, [128, 4, 512], BF16) for i in range(2)]; b_W = [Buf("W%d" % i) for i in range(2)]
        WTb = [R(TMP + 32832 + i * 4096, [128, 4, 512], BF16) for i in range(2)]; b_WT = [Buf("WT%d" % i) for i in range(2)]
        b_Z = Buf("Z")
        b_WTps = Buf("WTps")
        WTps = bank_bf(4, 2).rearrange("p (a b) -> p a b", a=4)
        g = 0
        for s in range(4):
            n = NCH[s]
            for h in range(8):
                p = h // 2
                rows = slice(0, 64) if h % 2 == 0 else slice(64, 128)
                ob = 6 + (h % 2)
                psO = psum[rows, ob * 512:(ob + 1) * 512]
                for c in range(n):
                    cr = 8 - n + c
                    k0 = cr * 512
                    F_ = Fb[g % 2]; bF = b_F[g % 2]
                    W_ = Wb[g % 2]; bW = b_W[g % 2]
                    WT_ = WTb[g % 2]; bWT = b_WT[g % 2]
                    for i in range(4):
                        q0 = s * 512 + i * 128
                        zb = bank(i)
                        last = (c >= 2)
                        S.op("pe", "matmul", [b_qt[p][s], b_kt[p][cr]], [b_Z], zb, lhsT=QT[rows, p, q0:q0 + 128],
                             rhs=KT[rows, p, k0:k0 + 512], start=True, stop=last, _mark=(last and i == 3))
                        if c < 2:
                            S.op("pe", "matmul", [b_seld, b_diag], [b_Z], zb, lhsT=seld[:, s, c, :], rhs=diag[:, i, :],
                                 start=False, stop=False, _mark=False)
                            S.op("pe", "matmul", [b_seln, b_negrow], [b_Z], zb, lhsT=seln[0:1, s, c, :],
                                 rhs=negrow[0:1, :], start=False, stop=True, _mark=(i == 3))
                    S.op("act", "activation", [b_Z], [bF], out=F_.rearrange("p a b -> p (a b)"), in_=bank(0, 4),
                         func=AF.Sigmoid, scale=-0.125)
                    if c == 0:
                        S.op("dve", "memset", [], [b_P], Pb[:, :, 0:1], 1.0)
                    else:
                        S.op("dve", "tensor_copy", [b_P], [b_P], out=Pb[:, :, 0:1], in_=Pb[:, :, 512:513])
                    for i in range(4):
                        S.op("dve", "tensor_tensor_scan", [bF, b_P, b_zeros], [b_P], out=Pb[:, i, 1:513],
                             data0=F_[:, i, :], data1=zeros, initial=Pb[:, i, 0:1], op0=ALU.mult, op1=ALU.add)
                    S.op("dve", "tensor_tensor", [b_P], [bW], out=W_, in0=Pb[:, :, 0:512], in1=Pb[:, :, 1:513],
                         op=ALU.subtract)
                    for m in range(4):
                        for i in range(4):
                            S.op("pe", "transpose", [bW, b_ident], [b_WTps], out=WTps[:, m, i * 128:(i + 1) * 128],
                                 in_=W_[:, i, m * 128:(m + 1) * 128], identity=ident, _mark=(m == 3 and i == 3))
                    S.op("act", "activation", [b_WTps], [bWT], out=WT_.rearrange("p a b -> p (a b)"),
                         in_=bank_bf(4, 2), func=AF.Copy)
                    for m in range(4):
                        first = (c == 0 and m == 0)
                        lastpv = (c == n - 1 and m == 3)
                        S.op("pe", "matmul", [bWT, b_v[cr]], [b_ps[ob]], psO, lhsT=V[:, cr * 4 + m, h * 64:(h + 1) * 64],
                             rhs=WT_[:, m, :], start=first, stop=lastpv, _mark=(m == 3))
                    g += 1
                S.op("act", "activation", [b_ps[ob]], [b_oat[p][s]], out=OAT[rows, p, s * 512:(s + 1) * 512],
                     in_=psO, func=AF.Copy)
        S.barrier()

        hTo = R(DYN + 0, [128, 8, NOWN + 128], BF16); b_hTo = [Buf("hTo%d" % t) for t in range(17)]
        YBT = R(DYN + 34816, [128, 4, NOWN], BF16); b_ybt = [Buf("ybt%d" % j) for j in range(4)]
        xring2 = [R(DYN + 51200 + i * 4096, [128, D], F32) for i in range(2)]
        xn2 = [R(DYN + 59392 + i * 2048, [128, D], BF16) for i in range(2)]
        junk2 = R(DYN + 63488, [128, D], BF16)
        u_sb = R(DYN + 65536, [128, 512], F32); b_usb = Buf("usb")
        acc = R(DYN + 67584, [128, 512], F32); b_acc = Buf("acc")
        cub = R(DYN + 69632, [128, 2, 516], F32); b_cub = [Buf("cub0"), Buf("cub1")]
        MRG = R(DYN + 98304, [128, 8, NOWN], BF16)
        b_mrg = [[Buf("mrg%d_%d" % (c, s)) for s in range(4)] for c in range(8)]
        sgt = [R(DYN + 131072 + i * 2048, [128, 512], F32) for i in range(4)]; b_sgt = [Buf("sg%d" % i) for i in range(4)]
        b_x2 = [Buf("x2r%d" % i) for i in range(2)]; b_xn2 = [Buf("xn2_%d" % i) for i in range(2)]; b_junk2 = Buf("junk2")

        for tt in range(17):
            i = tt % 2
            S.dma("sp", [(xring2[i], xown[tt * 128:(tt + 1) * 128, :], {})], writes=[b_x2[i]])
            norm_tile(xring2[i], b_x2[i], 0, xn2[i], b_xn2[i], junk2, b_junk2)
            transpose_tile(xn2[i], b_xn2[i], hTo[:, :, tt * 128:(tt + 1) * 128], b_hTo[tt], (6, 7))

        wu, b_wu = WS.get(i_wu)
        wgb, b_wgb = WS.get(i_wgb)
        wgc, b_wgc = WS.get(i_wgc)
        cb = 0
        for j in range(4):
            jc = slice(j * 128, (j + 1) * 128)
            mm_group(bank(0)[:, 0:128], [(wu[:, kc, jc], hTo[:, kc, NOWN:NOWN + 128]) for kc in range(8)],
                     b_ps[0], [b_wu, b_hTo[16]])
            mm_group(bank(1)[:, 0:128], [(wgc[:, kc, jc], hTo[:, kc, NOWN:NOWN + 128]) for kc in range(8)],
                     b_ps[1], [b_wgc, b_hTo[16]])
            S.op("act", "activation", [b_ps[0]], [b_usb], out=u_sb[:, 0:8], in_=bank(0)[:, 0:8], func=AF.Copy)
            S.op("dve", "tensor_tensor", [b_ps[1], b_usb], [b_cuh], out=cuh, in0=bank(1)[:, 0:8], in1=u_sb[:, 0:8],
                 op=ALU.mult)
            for s in range(4):
                hs = slice(s * 512, (s + 1) * 512)
                rd = [b_hTo[s * 4 + t] for t in range(4)]
                pb = 2 + (cb % 2) * 3
                cu = cub[:, cb % 2, :]; bcu = b_cub[cb % 2]
                cb += 1
                mm_group(bank(pb), [(wu[:, kc, jc], hTo[:, kc, hs]) for kc in range(8)], b_ps[pb], [b_wu] + rd)
                mm_group(bank(pb + 1), [(wgc[:, kc, jc], hTo[:, kc, hs]) for kc in range(8)], b_ps[pb + 1], [b_wgc] + rd)
                mm_group(bank(pb + 2), [(wgb[:, kc, jc], hTo[:, kc, hs]) for kc in range(8)], b_ps[pb + 2], [b_wgb] + rd)
                S.op("act", "activation", [b_ps[pb]], [b_usb], out=u_sb, in_=bank(pb), func=AF.Copy)
                S.op("dve", "tensor_copy", [b_cuh], [bcu], out=cu[:, 0:2], in_=cuh[:, 2 * s:2 * s + 2])
                S.op("dve", "tensor_tensor", [b_ps[pb + 1], b_usb], [bcu], out=cu[:, 2:514], in0=bank(pb + 1), in1=u_sb,
                     op=ALU.mult)
                S.op("dve", "tensor_scalar", [bcu, b_convw], [b_acc], out=acc, in0=cu[:, 2:514],
                     scalar1=convw[:, j, 2:3], scalar2=None, op0=ALU.mult)
                S.op("dve", "scalar_tensor_tensor", [bcu, b_convw, b_acc], [b_acc], out=acc, in0=cu[:, 1:513],
                     scalar=convw[:, j, 1:2], in1=acc, op0=ALU.mult, op1=ALU.add)
                S.op("dve", "scalar_tensor_tensor", [bcu, b_convw, b_acc], [b_acc], out=acc, in0=cu[:, 0:512],
                     scalar=convw[:, j, 0:1], in1=acc, op0=ALU.mult, op1=ALU.add)
                S.op("dve", "tensor_tensor", [b_ps[pb + 2], b_acc], [b_ybt[j]], out=YBT[:, j, hs], in0=bank(pb + 2),
                     in1=acc, op=ALU.mult)
        WS.release(i_wu); WS.release(i_wgb); WS.release(i_wgc)

        it = 0
        for c4 in range(2):
            ia, iga, igb = i_d[c4]
            wab, b_wab = WS.get(ia)
            wga, b_wga = WS.get(iga)
            wgB, b_wgB = WS.get(igb)
            for cc in range(4):
                c = c4 * 4 + cc
                ccs = slice(cc * 128, (cc + 1) * 128)
                for s in range(4):
                    hs = slice(s * 512, (s + 1) * 512)
                    rd = [b_hTo[s * 4 + t] for t in range(4)]
                    pb = (it % 2) * 4
                    sa = sgt[(it % 2) * 2]; bsa = b_sgt[(it % 2) * 2]
                    sb = sgt[(it % 2) * 2 + 1]; bsb = b_sgt[(it % 2) * 2 + 1]
                    it += 1
                    mm_group(bank(pb), [(wab[:, kc, ccs], OAT[:, kc, hs]) for kc in range(4)], b_ps[pb],
                             [b_wab] + [b_oat[kc][s] for kc in range(4)])
                    mm_group(bank(pb + 1), [(wab[:, 4 + kc, ccs], YBT[:, kc, hs]) for kc in range(4)], b_ps[pb + 1],
                             [b_wab] + b_ybt)
                    mm_group(bank(pb + 2), [(wga[:, kc, ccs], hTo[:, kc, hs]) for kc in range(8)], b_ps[pb + 2], [b_wga] + rd)
                    mm_group(bank(pb + 3), [(wgB[:, kc, ccs], hTo[:, kc, hs]) for kc in range(8)], b_ps[pb + 3], [b_wgB] + rd)
                    S.op("act", "activation", [b_ps[pb + 2]], [bsa], out=sa, in_=bank(pb + 2), func=AF.Sigmoid)
                    S.op("act", "activation", [b_ps[pb + 3]], [bsb], out=sb, in_=bank(pb + 3), func=AF.Sigmoid)
                    S.op("dve", "tensor_tensor", [b_ps[pb], bsa], [bsa], out=sa, in0=bank(pb), in1=sa, op=ALU.mult)
                    S.op("dve", "tensor_tensor", [b_ps[pb + 1], bsb], [bsb], out=sb, in0=bank(pb + 1), in1=sb, op=ALU.mult)
                    S.op("dve", "tensor_tensor", [bsa, bsb], [b_mrg[c][s]], out=MRG[:, c, hs], in0=sa, in1=sb, op=ALU.add)
            WS.release(ia); WS.release(iga); WS.release(igb)
        S.barrier()

        X = R(DYN + 0, [128, 16, D], F32); b_X = [Buf("X%d" % t) for t in range(16)]
        for t in range(16):
            S.dma("sp", [(X[:, t, :], xown[t * 128:(t + 1) * 128, :], {})], writes=[b_X[t]])
        rb = [0]
        for hf in range(2):
            wm, b_wm = WS.get(i_wmo[hf])
            for t in range(16):
                bk = rb[0] % 8; rb[0] += 1
                mm_group(bank(bk), [(MRG[:, kc, t * 128:(t + 1) * 128], wm[:, kc, :]) for kc in range(8)], b_ps[bk],
                         [b_wm] + [b_mrg[kc][t // 4] for kc in range(8)])
                xs = X[:, t, hf * 512:(hf + 1) * 512]
                S.op("dve", "tensor_tensor", [b_ps[bk], b_X[t]], [b_X[t]], out=xs, in0=bank(bk), in1=xs, op=ALU.add)
            WS.release(i_wmo[hf])
        S.barrier()

        hTs = [R(DYN + 65536 + i * 8192, [128, 8, 512], BF16) for i in range(2)]
        b_hTs = [[Buf("hTs%d_%d" % (i, t)) for t in range(4)] for i in range(2)]
        qmT = R(DYN + 81920, [128, 8, 512], BF16); b_qmT = [Buf("qmT%d" % c) for c in range(8)]
        omT = R(DYN + 90112, [128, 8, 512], BF16); b_omT = [Buf("omT%d" % c) for c in range(8)]
        mT = R(DYN + 98304, [128, 8, 256], BF16); b_mT = [Buf("mT0"), Buf("mT1")]
        KmT = R(DYN + 102400, [128, 8, 256], BF16); b_KmT = Buf("KmT")
        Vm = R(DYN + 106496, [128, 2, D], BF16); b_Vm = Buf("Vm")
        Esm = R(DYN + 110592, [128, 4, 256], F32); b_Esm = Buf("Esm")
        probs = [R(DYN + 114688 + i * 2048, [128, 4, 256], BF16) for i in range(2)]; b_probs = [Buf("pr0"), Buf("pr1")]
        pT = R(DYN + 118784, [128, 8, 512], BF16); b_pT = Buf("pT")
        xn3 = [R(DYN + 126976 + i * 2048, [128, D], BF16) for i in range(2)]; b_xn3 = [Buf("xn3_0"), Buf("xn3_1")]
        junk3 = R(DYN + 131072, [128, D], BF16); b_junk3 = Buf("junk3")
        memring = [R(DYN + 118784 + i * 4096, [128, D], F32) for i in range(2)]; b_mr = [Buf("mr0"), Buf("mr1")]

        load_gain(1, norm_mem_kv)
        load_gain(0, norm_mem_q)
        for mt in range(2):
            S.dma("sp", [(memring[mt], memb[mt * 128:(mt + 1) * 128, :], {})], writes=[b_mr[mt]])
            norm_tile(memring[mt], b_mr[mt], 1, xn3[mt], b_xn3[mt], junk3, b_junk3)
            transpose_tile(xn3[mt], b_xn3[mt], mT[:, :, mt * 128:(mt + 1) * 128], b_mT[mt], (4, 5))
        wK = [WS.get(i_wkvK[k]) for k in range(2)]
        for c in range(8):
            bk = 6 + c % 2
            w_, bw_ = wK[c // 4]
            mm_group(bank(bk)[:, 0:256], [(w_[:, kc, (c % 4) * 128:(c % 4 + 1) * 128], mT[:, kc, :]) for kc in range(8)],
                     b_ps[bk], [bw_] + b_mT)
            evac(KmT[:, c, :], bank(bk)[:, 0:256], [b_ps[bk]], [b_KmT])
        WS.release(i_wkvK[0]); WS.release(i_wkvK[1])
        wVv = [WS.get(i_wkvV[k]) for k in range(2)]
        for mt in range(2):
            for hf in range(2):
                bk = 6 + hf
                w_, bw_ = wVv[hf]
                mm_group(bank(bk), [(mT[:, kc, mt * 128:(mt + 1) * 128], w_[:, kc, :]) for kc in range(8)],
                         b_ps[bk], [bw_, b_mT[mt]])
                evac(Vm[:, mt, hf * 512:(hf + 1) * 512], bank(bk), [b_ps[bk]], [b_Vm])
        WS.release(i_wkvV[0]); WS.release(i_wkvV[1])
        S.barrier()

        wMQ = [WS.get(i_wmq[k]) for k in range(2)]
        wMO = [WS.get(i_wmo2[k]) for k in range(2)]
        sc_it = 0
        gb = [0]
        for s in range(4):
            hb = hTs[s % 2]; bh = b_hTs[s % 2]
            for t in range(4):
                tt = s * 4 + t
                i = tt % 2
                norm_tile(X[:, tt, :], b_X[tt], 0, xn3[i], b_xn3[i], junk3, b_junk3)
                transpose_tile(xn3[i], b_xn3[i], hb[:, :, t * 128:(t + 1) * 128], bh[t], (4, 5))
            for c in range(8):
                bk = 6 + gb[0] % 2; gb[0] += 1
                w_, bw_ = wMQ[c // 4]
                mm_group(bank(bk), [(w_[:, kc, (c % 4) * 128:(c % 4 + 1) * 128], hb[:, kc, :]) for kc in range(8)],
                         b_ps[bk], [bw_] + bh)
                evac(qmT[:, c, :], bank(bk), [b_ps[bk]], [b_qmT[c]])
            for t in range(4):
                sb0 = (sc_it % 2) * 2
                pr = probs[sc_it % 2]; bpr = b_probs[sc_it % 2]
                tb_ = 4 + (sc_it % 2)
                sc_it += 1
                psS = bank(sb0, 2).rearrange("p (a b) -> p a b", a=4)
                b_S = b_ps[sb0]
                for h in range(4):
                    for cc in range(2):
                        c = 2 * h + cc
                        S.op("pe", "matmul", [b_qmT[c], b_KmT], [b_S], psS[:, h, :], lhsT=qmT[:, c, t * 128:(t + 1) * 128],
                             rhs=KmT[:, c, :], start=(cc == 0), stop=(cc == 1), _mark=(h == 3 and cc == 1))
                S.op("dve", "tensor_reduce", [b_S], [b_mx], out=mx4, in_=psS, axis=AX.X, op=ALU.max)
                S.op("dve", "tensor_scalar", [b_mx], [b_nb], out=nb4, in0=mx4, scalar1=-1.0 / 16, scalar2=None, op0=ALU.mult)
                for h in range(4):
                    S.op("act", "activation", [b_S, b_nb], [b_Esm, b_sm], out=Esm[:, h, :], in_=psS[:, h, :], func=AF.Exp,
                         scale=1.0 / 16, bias=nb4[:, h:h + 1], accum_out=sm4[:, h:h + 1])
                S.op("dve", "reciprocal", [b_sm], [b_rs], out=rs4, in_=sm4)
                for h in range(4):
                    S.op("dve", "tensor_scalar", [b_Esm, b_rs], [bpr], out=pr[:, h, :], in0=Esm[:, h, :],
                         scalar1=rs4[:, h:h + 1], scalar2=None, op0=ALU.mult)
                pst = bank_bf(tb_)
                for h in range(4):
                    for mt in range(2):
                        k8 = h * 2 + mt
                        S.op("pe", "transpose", [bpr, b_ident], [b_ps[tb_]], out=pst[:, k8 * 128:(k8 + 1) * 128],
                             in_=pr[:, h, mt * 128:(mt + 1) * 128], identity=ident, _mark=(k8 == 7))
                S.op("act", "activation", [b_ps[tb_]], [b_pT], out=pT[:, :, t * 128:(t + 1) * 128],
                     in_=pst.rearrange("p (a b) -> p a b", a=8), func=AF.Copy)
            for h in range(4):
                for dc in range(2):
                    c = h * 2 + dc
                    bk = 6 + gb[0] % 2; gb[0] += 1
                    mm_group(bank(bk), [(Vm[:, mt, c * 128:(c + 1) * 128], pT[:, h * 2 + mt, :]) for mt in range(2)],
                             b_ps[bk], [b_Vm, b_pT])
                    evac(omT[:, c, :], bank(bk), [b_ps[bk]], [b_omT[c]])
            for t in range(4):
                tt = s * 4 + t
                for hf in range(2):
                    bk = 6 + gb[0] % 2; gb[0] += 1
                    w_, bw_ = wMO[hf]
                    mm_group(bank(bk), [(omT[:, kc, t * 128:(t + 1) * 128], w_[:, kc, :]) for kc in range(8)],
                             b_ps[bk], [bw_] + b_omT)
                    xs = X[:, tt, hf * 512:(hf + 1) * 512]
                    S.op("dve", "tensor_tensor", [b_ps[bk], b_X[tt]], [b_X[tt]], out=xs, in0=bank(bk), in1=xs, op=ALU.add)
        for k in range(2):
            WS.release(i_wmq[k]); WS.release(i_wmo2[k])
        S.barrier()

        hTb = R(DYN + 65536, [128, 8, 1024], BF16); b_hTb = [Buf("hTb%d" % t) for t in range(8)]
        aT = R(DYN + 81920, [128, 22, 1024], BF16); b_aT = [Buf("aT%d" % j) for j in range(22)]
        sgl = [R(DYN + 126976 + i * 2048, [128, 512], F32) for i in range(2)]; b_sgl = [Buf("sgl0"), Buf("sgl1")]
        xn4 = [R(DYN + 131072 + i * 2048, [128, D], BF16) for i in range(2)]; b_xn4 = [Buf("xn4_0"), Buf("xn4_1")]
        junk4 = R(DYN + 135168, [128, D], BF16); b_junk4 = Buf("junk4")
        load_gain(1, norm_ffn)
        fit = 0
        for tb in range(2):
            gu, fo = i_ffn[tb]
            for t in range(8):
                tt = tb * 8 + t
                i = tt % 2
                norm_tile(X[:, tt, :], b_X[tt], 1, xn4[i], b_xn4[i], junk4, b_junk4)
                transpose_tile(xn4[i], b_xn4[i], hTb[:, :, t * 128:(t + 1) * 128], b_hTb[t], (6, 7))
            for blk in range(6):
                ig, iu, ncol = gu[blk]
                wg_, b_wg = WS.get(ig)
                wu_, b_wu_ = WS.get(iu)
                for cc in range(ncol // 128):
                    j = blk * 4 + cc
                    ccs = slice(cc * 128, (cc + 1) * 128)
                    for s2 in range(2):
                        hs = slice(s2 * 512, (s2 + 1) * 512)
                        rd = b_hTb[s2 * 4:(s2 + 1) * 4]
                        pb = (fit % 3) * 2
                        sg_ = sgl[fit % 2]; bsg = b_sgl[fit % 2]
                        fit += 1
                        mm_group(bank(pb), [(wg_[:, kc, ccs], hTb[:, kc, hs]) for kc in range(8)], b_ps[pb], [b_wg] + rd)
                        mm_group(bank(pb + 1), [(wu_[:, kc, ccs], hTb[:, kc, hs]) for kc in range(8)], b_ps[pb + 1],
                                 [b_wu_] + rd)
                        S.op("act", "activation", [b_ps[pb]], [bsg], out=sg_, in_=bank(pb), func=AF.Silu)
                        S.op("dve", "tensor_tensor", [b_ps[pb + 1], bsg], [b_aT[j]], out=aT[:, j, hs], in0=bank(pb + 1),
                             in1=sg_, op=ALU.mult)
                WS.release(ig); WS.release(iu)
            for hf in range(2):
                wfo = [WS.get(fo[hf][k3]) for k3 in range(3)]
                for t in range(8):
                    tt = tb * 8 + t
                    bk = 6 + (fit % 2); fit += 1
                    mm_group(bank(bk), [(aT[:, j, t * 128:(t + 1) * 128], wfo[j // 8][0][:, j % 8, :]) for j in range(22)],
                             b_ps[bk], [w[1] for w in wfo] + b_aT)
                    xs = X[:, tt, hf * 512:(hf + 1) * 512]
                    S.op("dve", "tensor_tensor", [b_ps[bk], b_X[tt]], [b_X[tt]], out=xs, in0=bank(bk), in1=xs, op=ALU.add)
                for k3 in range(3):
                    WS.release(fo[hf][k3])
        S.barrier()

        otmp = [R(DYN + 65536 + i * 4096, [128, D], F32) for i in range(2)]; b_ot = [Buf("ot0"), Buf("ot1")]
        junk5 = R(DYN + 73728, [128, D], BF16); b_junk5 = Buf("junk5")
        load_gain(0, norm_final)
        out_evs = []
        for tt in range(16):
            i = tt % 2
            norm_tile(X[:, tt, :], b_X[tt], 0, otmp[i], b_ot[i], junk5, b_junk5)
            out_evs.append(S.dma("sp", [(out_d[tt * 128:(tt + 1) * 128, :], otmp[i], {})], reads=[b_ot[i]]))
        for ev in out_evs:
            S._wait("sp", ev)
        print("inst counts", S.n_inst, {e: len(S.prog[e]) for e in S.ENGS})
        S.emit()
    return nc


def _host_inputs(inputs):
    x = np.asarray(inputs["x"], dtype=np.float32)
    mem = np.asarray(inputs["mem"], dtype=np.float32)
    diag = np.zeros((128, 4, 512), np.float32)
    pp = np.arange(128)[:, None]
    kr = np.arange(512)[None, :]
    for i in range(4):
        ql = i * 128 + pp
        diag[:, i, :] = np.where(kr > 511 - ql, 0.0, NEG)
    eye = np.eye(128, dtype=np.float32)
    shared = {
        "diag": diag,
        "norm_mix": np.ascontiguousarray(inputs["norm_mix"][0:1]),
        "w_in": np.ascontiguousarray(inputs["w_in"][0]),
        "conv_w": np.ascontiguousarray(inputs["conv_w"][0]),
        "w_branch_a": np.ascontiguousarray(inputs["w_branch_a"][0]),
        "w_branch_b": np.ascontiguousarray(inputs["w_branch_b"][0]),
        "w_mix_out": np.ascontiguousarray(inputs["w_mix_out"][0]),
        "norm_mem_q": np.ascontiguousarray(inputs["norm_mem_q"][0:1]),
        "norm_mem_kv": np.ascontiguousarray(inputs["norm_mem_kv"][0:1]),
        "w_mem_q": np.ascontiguousarray(inputs["w_mem_q"][0]),
        "w_mem_kv": np.ascontiguousarray(inputs["w_mem_kv"][0]),
        "w_mem_o": np.ascontiguousarray(inputs["w_mem_o"][0]),
        "norm_ffn": np.ascontiguousarray(inputs["norm_ffn"][0:1]),
        "w_ffn_in": np.ascontiguousarray(inputs["w_ffn_in"][0]),
        "w_ffn_out": np.ascontiguousarray(inputs["w_ffn_out"][0]),
        "norm_final": np.ascontiguousarray(np.asarray(inputs["norm_final"]).reshape(1, D)),
    }
    shared = {k: np.asarray(v, dtype=np.float32) for k, v in shared.items()}
    in_maps = []
    for core in range(8):
        b, par = core // 2, core % 2
        tiles = T_OF[par]
        xown = np.zeros((NOWN + 128, D), np.float32)
        seld = np.zeros((128, 4, 2, 128), np.float32)
        seln = np.zeros((1, 4, 2, 128), np.float32)
        for j, T in enumerate(tiles):
            xown[j * 512:(j + 1) * 512] = x[b, T * 512:(T + 1) * 512]
            if T > 0:
                xown[NOWN + 2 * j:NOWN + 2 * j + 2] = x[b, T * 512 - 2:T * 512]
            tmax = NCH[j] - 1
            if T == tmax:
                seld[:, j, 0, :] = eye
            else:
                seln[0, j, 0, :] = 1.0
                seld[:, j, 1, :] = eye
        m = dict(shared)
        m["xall"] = np.ascontiguousarray(x[b, ::-1, :])
        m["xown"] = xown
        m["memb"] = np.ascontiguousarray(mem[b])
        m["seld"] = seld
        m["seln"] = seln
        in_maps.append(m)
    return in_maps


_NC_CACHE = {}


def kernel(**inputs):
    in_maps = _host_inputs(inputs)
    if "nc" not in _NC_CACHE:
        _NC_CACHE["nc"] = build_nc()
    nc = _NC_CACHE["nc"]
    res = run_bass_kernel_spmd(nc, in_maps, core_ids=list(range(8)))
    out = np.zeros((NB, SEQ, D), np.float32)
    for core in range(8):
        b, par = core // 2, core % 2
        o = np.asarray(res.results[core]["out"], dtype=np.float32)
        for j, T in enumerate(T_OF[par]):
            out[b, T * 512:(T + 1) * 512] = o[j * 512:(j + 1) * 512]
    return out
```

```python
import numpy as np
import concourse.bass as bass
import concourse.mybir as mybir
from concourse.bass_utils import run_bass_kernel_spmd
from contextlib import ExitStack

F32 = mybir.dt.float32
BF16 = mybir.dt.bfloat16
AF = mybir.ActivationFunctionType
ALU = mybir.AluOpType
AX = mybir.AxisListType

D = 1024
SEQ = 4096
NB = 4
TS = 512
T_OF = {0: (0, 3, 4, 7), 1: (1, 2, 5, 6)}
NCH = (2, 4, 6, 8)
NEG = -30000.0
EPS = 1e-6
FFN_H = 2816
NOWN = 2048
SAME_ENGINE_SYNC = True


class Buf:
    __slots__ = ("name", "w", "r")

    def __init__(self, name):
        self.name = name
        self.w = None
        self.r = {}


class Sched:
    ENGS = ("pe", "act", "dve", "pool", "sp")

    def __init__(self, nc, stack, n_dma_sems=8):
        self.nc = nc
        self.prog = {e: [] for e in self.ENGS}
        self.count = {e: 0 for e in self.ENGS}
        self.sem = {e: stack.enter_context(nc.semaphore("s_" + e)) for e in self.ENGS}
        self.seen = {e: {} for e in self.ENGS}
        self.dsem = {}
        self.dval = {}
        self.dring = {}
        self.dpos = {}
        idx = 0
        for q in ("sp", "pool"):
            ring = []
            for i in range(n_dma_sems):
                self.dsem[idx] = stack.enter_context(nc.semaphore("d_%s%d" % (q, i)))
                self.dval[idx] = 0
                ring.append(idx)
                idx += 1
            self.dring[q] = ring
            self.dpos[q] = 0
        self.n_inst = {e: 0 for e in self.ENGS}
        self.last_marked = {e: True for e in self.ENGS}

    def _wait(self, e, ev):
        if ev is None:
            return
        if ev[0] == "e":
            _, src, seq = ev
            if src == e and (e == "pe" or not SAME_ENGINE_SYNC):
                return
            assert self.count[src] >= seq, ("dependency on unissued mark", e, ev)
            key = ("e", src)
            val = seq
            sem = self.sem[src]
        else:
            _, sidx, val = ev
            key = ("d", sidx)
            sem = self.dsem[sidx]
        if self.seen[e].get(key, 0) >= val:
            return
        self.seen[e][key] = val
        self.prog[e].append(lambda eng, sem=sem, val=val: eng.wait_ge(sem, val))

    def _deps(self, e, reads, writes):
        for b in reads:
            self._wait(e, b.w)
        for b in writes:
            self._wait(e, b.w)
            for (k0, k1), v in list(b.r.items()):
                self._wait(e, (k0, k1, v))

    def _record(self, ev, reads, writes):
        key = (ev[0], ev[1])
        for b in reads:
            if b.r.get(key, 0) < ev[2]:
                b.r[key] = ev[2]
        for b in writes:
            b.w = ev
            b.r = {}

    def op(self, e, meth, reads, writes, *args, _mark=True, **kw):
        self._deps(e, reads, writes)
        sem = self.sem[e]
        if _mark:
            self.count[e] += 1
            self.prog[e].append(lambda eng: getattr(eng, meth)(*args, **kw).then_inc(sem, 1))
            ev = ("e", e, self.count[e])
        else:
            self.prog[e].append(lambda eng: getattr(eng, meth)(*args, **kw))
            ev = ("e", e, self.count[e] + 1)
        self.last_marked[e] = _mark
        self.n_inst[e] += 1
        self._record(ev, reads, writes)
        return ev

    def dma(self, q, xfers, reads=(), writes=()):
        self._deps(q, reads, writes)
        ring = self.dring[q]
        sidx = ring[self.dpos[q] % len(ring)]
        self.dpos[q] += 1
        if self.dval[sidx] > 0:
            self._wait(q, ("d", sidx, self.dval[sidx]))
        sem = self.dsem[sidx]
        for (o, i, kw) in xfers:
            self.dval[sidx] += 16
            self.prog[q].append(lambda eng, o=o, i=i, kw=kw: eng.dma_start(out=o, in_=i, **kw).then_inc(sem, 16))
        ev = ("d", sidx, self.dval[sidx])
        self._record(ev, reads, writes)
        return ev

    def barrier(self):
        for e in ("pe", "act", "dve"):
            assert self.last_marked[e], ("barrier with unmarked tail", e)
        for sidx in self.dring["sp"]:
            if self.dval[sidx] > 0:
                self._wait("sp", ("d", sidx, self.dval[sidx]))
        for f in ("pe", "act", "dve"):
            self._wait("sp", ("e", f, self.count[f]))
        self.count["sp"] += 1
        sem = self.sem["sp"]
        self.prog["sp"].append(lambda eng, sem=sem: eng.sem_inc(sem, 1))
        for e in ("pe", "act", "dve"):
            self._wait(e, ("e", "sp", self.count["sp"]))

    def emit(self):
        with self.nc.Block() as block:
            def mk(name):
                def body(engine):
                    for c in self.prog[name]:
                        c(engine)
                return body
            block.sync(mk("sp"))
            block.gpsimd(mk("pool"))
            block.scalar(mk("act"))
            block.vector(mk("dve"))
            block.tensor(mk("pe"))


def build_nc():
    nc = bass.Bass("TRN2", target_bir_lowering=False)

    def din(name, shape):
        return nc.dram_tensor(name, list(shape), F32, kind="ExternalInput").ap()

    xall = din("xall", [SEQ, D])
    xown = din("xown", [NOWN + 128, D])
    memb = din("memb", [256, D])
    seld_d = din("seld", [128, 4, 2, 128])
    seln_d = din("seln", [1, 4, 2, 128])
    diag_d = din("diag", [128, 4, 512])
    norm_mix = din("norm_mix", [1, D])
    w_in = din("w_in", [D, 5120])
    conv_w = din("conv_w", [3, 512])
    w_ba = din("w_branch_a", [512, D])
    w_bb = din("w_branch_b", [512, D])
    w_mix_out = din("w_mix_out", [D, D])
    norm_mem_q = din("norm_mem_q", [1, D])
    norm_mem_kv = din("norm_mem_kv", [1, D])
    w_mem_q = din("w_mem_q", [D, D])
    w_mem_kv = din("w_mem_kv", [D, 2 * D])
    w_mem_o = din("w_mem_o", [D, D])
    norm_ffn = din("norm_ffn", [1, D])
    w_ffn_in = din("w_ffn_in", [D, 2 * FFN_H])
    w_ffn_out = din("w_ffn_out", [FFN_H, D])
    norm_final = din("norm_final", [1, D])
    out_d = nc.dram_tensor("out", [NOWN, D], F32, kind="ExternalOutput").ap()

    with ExitStack() as st:
        S = Sched(nc, st)
        ARENA_F32 = 52900
        arena = st.enter_context(nc.sbuf_tensor("arena", [128, ARENA_F32], F32))
        psum = st.enter_context(nc.psum_tensor("psum", [128, 4096], F32))

        def R(off, shape, dt):
            esz = 4 if dt == F32 else 2
            n = int(np.prod(shape[1:]))
            nbytes = n * esz
            assert off % 4 == 0 and nbytes % 4 == 0, (off, shape)
            assert off + nbytes <= ARENA_F32 * 4, (off, shape)
            ap = arena[:, off // 4:(off + nbytes) // 4]
            if dt != F32:
                ap = ap.bitcast(dt)
            if len(shape) == 3:
                ap = ap.rearrange("p (a b) -> p a b", a=shape[1])
            elif len(shape) == 4:
                ap = ap.rearrange("p (a b c) -> p a b c", a=shape[1], b=shape[2])
            return ap

        def bank(i, n=1):
            return psum[:, i * 512:(i + n) * 512]

        def bank_bf(i, n=1):
            return psum[:, i * 512:(i + n) * 512].bitcast(BF16)

        b_ps = [Buf("ps%d" % i) for i in range(8)]

        ident = R(0, [128, 128], BF16); b_ident = Buf("ident")
        identf = R(256, [128, 128], F32); b_identf = Buf("identf")
        convw = R(768, [128, 4, 3], F32); b_convw = Buf("convw")
        stats = R(1024, [128, 256], F32)
        seld = R(2048, [128, 4, 2, 128], BF16); b_seld = Buf("seld")
        seln = R(4096, [128, 4, 2, 128], BF16); b_seln = Buf("seln")
        negrow = R(6144, [128, 512], BF16); b_negrow = Buf("negrow")
        zeros = R(7168, [128, 512], F32); b_zeros = Buf("zeros")
        diag = R(9216, [128, 4, 512], BF16); b_diag = Buf("diag")
        GAIN0 = 13312
        g_rep = [R(GAIN0 + i * 4096, [128, D], F32) for i in range(2)]
        b_g = [Buf("g%d" % i) for i in range(2)]
        RING0 = 21504
        NSLOT = 5
        DYN = RING0 + NSLOT * 8192

        ssq = stats[:, 0:8]; rstd = stats[:, 8:16]
        b_st = [Buf("st%d" % i) for i in range(8)]
        st_pos = [0]
        mx4 = stats[:, 16:20]; nb4 = stats[:, 20:24]; sm4 = stats[:, 24:28]; rs4 = stats[:, 28:32]
        b_mx = Buf("mx"); b_nb = Buf("nb"); b_sm = Buf("sm"); b_rs = Buf("rs")
        cuh = stats[:, 32:40]; b_cuh = Buf("cuh")

        class WStream:
            def __init__(self):
                self.blocks = []
                self.issued = 0
                self.released = set()
                self.bufs = [Buf("ring%d" % i) for i in range(NSLOT)]
                self.next_get = 0

            def add(self, pieces):
                self.blocks.append(pieces)
                return len(self.blocks) - 1

            def slot_ap(self, i):
                return R(RING0 + (i % NSLOT) * 8192, [128, 8, 512], BF16)

            def _issue(self):
                while self.issued < len(self.blocks):
                    k = self.issued
                    if k >= NSLOT and (k - NSLOT) not in self.released:
                        break
                    if k > self.next_get + NSLOT - 1:
                        break
                    sl = self.slot_ap(k)
                    xf = []
                    for (src, kc0, kcn, ncols) in self.blocks[k]:
                        xf.append((sl[:, kc0:kc0 + kcn, 0:ncols], src, {}))
                    S.dma("pool", xf, writes=[self.bufs[k % NSLOT]])
                    self.issued += 1

            def get(self, idx):
                assert idx == self.next_get, (idx, self.next_get)
                self.next_get += 1
                self._issue()
                assert self.issued > idx, ("weight block not issued", idx)
                return self.slot_ap(idx), self.bufs[idx % NSLOT]

            def release(self, idx):
                self.released.add(idx)
                self._issue()

        WS = WStream()

        def wpiece(w, r0, nrows, c0, ncols, kc0=0):
            src = w[r0:r0 + nrows, c0:c0 + ncols].rearrange("(kc p) n -> p kc n", p=128)
            return (src, kc0, nrows // 128, ncols)

        i_wk = WS.add([wpiece(w_in, 0, D, 512, 512)])
        i_wv = WS.add([wpiece(w_in, 0, D, 1024, 512)])
        i_wq = WS.add([wpiece(w_in, 0, D, 0, 512)])
        i_wu = WS.add([wpiece(w_in, 0, D, 1536, 512)])
        i_wgb = WS.add([wpiece(w_in, 0, D, 2048, 512)])
        i_wgc = WS.add([wpiece(w_in, 0, D, 2560, 512)])
        i_d = []
        for c4 in range(2):
            a = WS.add([wpiece(w_ba, 0, 512, c4 * 512, 512, kc0=0), wpiece(w_bb, 0, 512, c4 * 512, 512, kc0=4)])
            b_ = WS.add([wpiece(w_in, 0, D, 3072 + c4 * 512, 512)])
            c_ = WS.add([wpiece(w_in, 0, D, 4096 + c4 * 512, 512)])
            i_d.append((a, b_, c_))
        i_wmo = [WS.add([wpiece(w_mix_out, 0, D, hf * 512, 512)]) for hf in range(2)]
        i_wkvK = [WS.add([wpiece(w_mem_kv, 0, D, k * 512, 512)]) for k in range(2)]
        i_wkvV = [WS.add([wpiece(w_mem_kv, 0, D, D + k * 512, 512)]) for k in range(2)]
        i_wmq = [WS.add([wpiece(w_mem_q, 0, D, k * 512, 512)]) for k in range(2)]
        i_wmo2 = [WS.add([wpiece(w_mem_o, 0, D, k * 512, 512)]) for k in range(2)]
        i_ffn = []
        for tb in range(2):
            gu = []
            for blk in range(6):
                ncol = 512 if blk < 5 else 256
                ig = WS.add([wpiece(w_ffn_in, 0, D, blk * 512, ncol)])
                iu = WS.add([wpiece(w_ffn_in, 0, D, FFN_H + blk * 512, ncol)])
                gu.append((ig, iu, ncol))
            fo = []
            for hf in range(2):
                ks = []
                for k3 in range(3):
                    nr = 1024 if k3 < 2 else FFN_H - 2048
                    ks.append(WS.add([wpiece(w_ffn_out, k3 * 1024, nr, hf * 512, 512)]))
                fo.append(ks)
            i_ffn.append((gu, fo))

        evac_rr = [0]

        def evac(out_ap, in_ap, reads, writes, eng=None):
            if eng is None:
                eng = ("act", "dve")[evac_rr[0] % 2]
                evac_rr[0] += 1
            if eng == "act":
                return S.op("act", "activation", reads, writes, out=out_ap, in_=in_ap, func=AF.Copy)
            return S.op("dve", "tensor_copy", reads, writes, out=out_ap, in_=in_ap)

        def mm_group(out_ap, pairs, b_out, reads):
            n = len(pairs)
            for k, (l, r) in enumerate(pairs):
                S.op("pe", "matmul", reads, [b_out], out_ap, lhsT=l, rhs=r, start=(k == 0), stop=(k == n - 1),
                     _mark=(k == n - 1))

        def load_gain(i, src):
            S.dma("sp", [(g_rep[i], src.broadcast_to([128, D]), {})], writes=[b_g[i]])

        def norm_tile(x_ap, b_x, gi, out_ap, b_out, junk_ap, b_junk_):
            k = st_pos[0] % 8
            st_pos[0] += 1
            bs = b_st[k]
            S.op("act", "activation", [b_x], [b_junk_, bs], out=junk_ap, in_=x_ap, func=AF.Square,
                 accum_out=ssq[:, k:k + 1])
            S.op("act", "activation", [bs], [bs], out=rstd[:, k:k + 1], in_=ssq[:, k:k + 1], func=AF.Ln,
                 scale=1.0 / D, bias=EPS)
            S.op("act", "activation", [bs], [bs], out=rstd[:, k:k + 1], in_=rstd[:, k:k + 1], func=AF.Exp, scale=-0.5)
            S.op("dve", "scalar_tensor_tensor", [b_x, bs, b_g[gi]], [b_out], out=out_ap, in0=x_ap,
                 scalar=rstd[:, k:k + 1], in1=g_rep[gi], op0=ALU.mult, op1=ALU.mult)

        tr_rr = [0]

        def transpose_tile(xn_ap, b_xn_, dst3, b_dst, tbanks):
            bk = tbanks[tr_rr[0] % len(tbanks)]
            tr_rr[0] += 1
            pst = bank_bf(bk)
            for c in range(8):
                S.op("pe", "transpose", [b_xn_, b_ident], [b_ps[bk]], out=pst[:, c * 128:(c + 1) * 128],
                     in_=xn_ap[:, c * 128:(c + 1) * 128], identity=ident, _mark=(c == 7))
            S.op("act", "activation", [b_ps[bk]], [b_dst], out=dst3, in_=pst.rearrange("p (a b) -> p a b", a=8),
                 func=AF.Copy)

        S.op("dve", "memset", [], [b_identf], identf, 1.0)
        S.op("pool", "affine_select", [b_identf], [b_identf], out=identf, in_=identf, pattern=[[-1, 128]],
             compare_op=ALU.is_equal, fill=0.0, base=0, channel_multiplier=1)
        S.op("dve", "tensor_copy", [b_identf], [b_ident], out=ident, in_=identf)
        S.op("dve", "memset", [], [b_negrow], negrow, NEG)
        S.op("dve", "memset", [], [b_zeros], zeros, 0.0)
        S.dma("pool", [(seld, seld_d, {})], writes=[b_seld])
        S.dma("pool", [(seln[0:1], seln_d, {})], writes=[b_seln])
        S.dma("pool", [(diag, diag_d, {})], writes=[b_diag])
        S.dma("sp", [(convw[:, j, :], conv_w[:, j * 128:(j + 1) * 128].rearrange("i p -> p i"),
                      {"allow_slow_non_contiguous": True}) for j in range(4)], writes=[b_convw])
        load_gain(0, norm_mix)

        KT = R(DYN + 0, [128, 4, SEQ], BF16)
        V = R(DYN + 32768, [128, 32, 512], BF16)
        QT = R(DYN + 65536, [128, 4, NOWN], BF16)
        OAT = R(DYN + 81920, [128, 4, NOWN], BF16)
        b_oat = [[Buf("oat%d_%d" % (p, s)) for s in range(4)] for p in range(4)]
        TMP = DYN + 98304
        xring = [R(TMP + i * 4096, [128, D], F32) for i in range(4)]; b_xr = [Buf("xr%d" % i) for i in range(4)]
        xn = [R(TMP + 16384 + i * 2048, [128, D], BF16) for i in range(2)]; b_xn = [Buf("xn%d" % i) for i in range(2)]
        junk = R(TMP + 20480, [128, D], BF16); b_junk = Buf("junk")
        hTa = [R(TMP + 22528 + i * 8192, [128, 8, 512], BF16) for i in range(2)]
        b_hTa = [[Buf("hTa%d_%d" % (i, t)) for t in range(4)] for i in range(2)]
        b_kt = [[Buf("kt%d_%d" % (p, ch)) for ch in range(8)] for p in range(4)]
        b_v = [Buf("v%d" % ch) for ch in range(8)]
        b_qt = [[Buf("qt%d_%d" % (p, s)) for s in range(4)] for p in range(4)]

        wk, b_wk = WS.get(i_wk)
        wv, b_wv = WS.get(i_wv)
        wq, b_wq = WS.get(i_wq)

        xl = [0]
        mmb = [0]

        def stream_tile(src_rows, gi, dst3, b_dst, tbanks):
            i = xl[0]
            xl[0] += 1
            xt = xring[i % 4]; bx = b_xr[i % 4]
            S.dma("sp", [(xt, src_rows, {})], writes=[bx])
            xo = xn[i % 2]; bxo = b_xn[i % 2]
            norm_tile(xt, bx, gi, xo, bxo, junk, b_junk)
            transpose_tile(xo, bxo, dst3, b_dst, tbanks)

        for ch in range(8):
            hb = hTa[ch % 2]; bh = b_hTa[ch % 2]
            for t in range(4):
                r0 = ch * 512 + t * 128
                stream_tile(xall[r0:r0 + 128, :], 0, hb[:, :, t * 128:(t + 1) * 128], bh[t], (6, 7))
            for p in range(4):
                bk = mmb[0] % 6; mmb[0] += 1
                mm_group(bank(bk), [(wk[:, kc, p * 128:(p + 1) * 128], hb[:, kc, :]) for kc in range(8)],
                         b_ps[bk], [b_wk] + bh)
                evac(KT[:, p, ch * 512:(ch + 1) * 512], bank(bk), [b_ps[bk]], [b_kt[p][ch]])
            for t in range(4):
                bk = mmb[0] % 6; mmb[0] += 1
                mm_group(bank(bk), [(hb[:, kc, t * 128:(t + 1) * 128], wv[:, kc, :]) for kc in range(8)],
                         b_ps[bk], [b_wv, bh[t]])
                evac(V[:, ch * 4 + t, :], bank(bk), [b_ps[bk]], [b_v[ch]])
        WS.release(i_wk); WS.release(i_wv)

        for s in range(4):
            hb = hTa[s % 2]; bh = b_hTa[s % 2]
            for t in range(4):
                r0 = s * 512 + t * 128
                stream_tile(xown[r0:r0 + 128, :], 0, hb[:, :, t * 128:(t + 1) * 128], bh[t], (6, 7))
            for p in range(4):
                bk = mmb[0] % 6; mmb[0] += 1
                mm_group(bank(bk), [(wq[:, kc, p * 128:(p + 1) * 128], hb[:, kc, :]) for kc in range(8)],
                         b_ps[bk], [b_wq] + bh)
                evac(QT[:, p, s * 512:(s + 1) * 512], bank(bk), [b_ps[bk]], [b_qt[p][s]])
        WS.release(i_wq)
        S.barrier()

        Fb = [R(TMP + i * 8208, [128, 4, 513], F32) for i in range(2)]; b_F = [Buf("F%d" % i) for i in range(2)]
        D1 = R(TMP + 16416, [128, 4, 513], F32); b_D1 = Buf("D1")
        Pb = R(TMP + 24624, [128, 4, 513], F32); b_P = Buf("P")
        Wb = [R(TMP + 32832 + i * 4096, [128, 4, 512], BF16) for i in range(2)]; b_W = [Buf("W%d" % i) for i in range(2)]
        WTb = [R(TMP + 41024 + i * 4096, [128, 4, 512], BF16) for i in range(2)]; b_WT = [Buf("WT%d" % i) for i in range(2)]
        b_Z = Buf("Z")
        b_WTps = Buf("WTps")
        WTps = bank_bf(4, 2).rearrange("p (a b) -> p a b", a=4)
        for i in range(2):
            S.op("dve", "memset", [], [b_F[i]], Fb[i].rearrange("p a b -> p (a b)"), 0.0)
        S.op("dve", "memset", [], [b_D1], D1.rearrange("p a b -> p (a b)"), 0.0)
        groups = [(s, h, c) for s in range(4) for h in range(8) for c in range(NCH[s])]
        G = len(groups)

        def geom(gi):
            s, h, c = groups[gi]
            n = NCH[s]
            p = h // 2
            rows = slice(0, 64) if h % 2 == 0 else slice(64, 128)
            ob = 6 + (h % 2)
            cr = 8 - n + c
            return s, h, c, n, p, rows, ob, cr

        def stage1(gi):
            s, h, c, n, p, rows, ob, cr = geom(gi)
            k0 = cr * 512
            F_ = Fb[gi % 2]; bF = b_F[gi % 2]
            for i in range(4):
                q0 = s * 512 + i * 128
                zb = bank(i)
                last = (c >= 2)
                S.op("pe", "matmul", [b_qt[p][s], b_kt[p][cr]], [b_Z], zb, lhsT=QT[rows, p, q0:q0 + 128],
                     rhs=KT[rows, p, k0:k0 + 512], start=True, stop=last, _mark=(last and i == 3))
                if c < 2:
                    S.op("pe", "matmul", [b_seld, b_diag], [b_Z], zb, lhsT=seld[:, s, c, :], rhs=diag[:, i, :],
                         start=False, stop=False, _mark=False)
                    S.op("pe", "matmul", [b_seln, b_negrow], [b_Z], zb, lhsT=seln[0:1, s, c, :],
                         rhs=negrow[0:1, :], start=False, stop=True, _mark=(i == 3))
            S.op("act", "activation", [b_Z], [bF], out=F_[:, :, 1:513],
                 in_=bank(0, 4).rearrange("p (a b) -> p a b", a=4), func=AF.Sigmoid, scale=-0.125)

        def stage2(gi):
            s, h, c, n, p, rows, ob, cr = geom(gi)
            F_ = Fb[gi % 2]; bF = b_F[gi % 2]
            W_ = Wb[gi % 2]; bW = b_W[gi % 2]
            if c == 0:
                S.op("dve", "memset", [], [b_D1], D1[:, :, 0:1], 1.0)
            else:
                S.op("dve", "tensor_copy", [b_P], [b_D1], out=D1[:, :, 0:1], in_=Pb[:, :, 512:513])
            S.op("dve", "tensor_tensor_scan", [bF, b_D1], [b_P], out=Pb.rearrange("p a b -> p (a b)"),
                 data0=F_.rearrange("p a b -> p (a b)"), data1=D1.rearrange("p a b -> p (a b)"), initial=0.0,
                 op0=ALU.mult, op1=ALU.add)
            S.op("dve", "tensor_tensor", [b_P], [bW], out=W_, in0=Pb[:, :, 0:512], in1=Pb[:, :, 1:513],
                 op=ALU.subtract)

        def stage3(gi):
            s, h, c, n, p, rows, ob, cr = geom(gi)
            W_ = Wb[gi % 2]; bW = b_W[gi % 2]
            WT_ = WTb[gi % 2]; bWT = b_WT[gi % 2]
            psO = psum[rows, ob * 512:(ob + 1) * 512]
            for m in range(4):
                for i in range(4):
                    S.op("pe", "transpose", [bW, b_ident], [b_WTps], out=WTps[:, m, i * 128:(i + 1) * 128],
                         in_=W_[:, i, m * 128:(m + 1) * 128], identity=ident, _mark=(m == 3 and i == 3))
            S.op("act", "activation", [b_WTps], [bWT], out=WT_.rearrange("p a b -> p (a b)"),
                 in_=bank_bf(4, 2), func=AF.Copy)
            for m in range(4):
                first = (c == 0 and m == 0)
                lastpv = (c == n - 1 and m == 3)
                S.op("pe", "matmul", [bWT, b_v[cr]], [b_ps[ob]], psO, lhsT=V[:, cr * 4 + m, h * 64:(h + 1) * 64],
                     rhs=WT_[:, m, :], start=first, stop=lastpv, _mark=(m == 3))
            if c == n - 1:
                S.op("act", "activation", [b_ps[ob]], [b_oat[p][s]], out=OAT[rows, p, s * 512:(s + 1) * 512],
                     in_=psO, func=AF.Copy)

        for step in range(G + 2):
            if step < G:
                stage1(step)
            if 0 <= step - 1 < G:
                stage2(step - 1)
            if 0 <= step - 2 < G:
                stage3(step - 2)
        S.barrier()

        hTo = R(DYN + 0, [128, 8, NOWN + 128], BF16); b_hTo = [Buf("hTo%d" % t) for t in range(17)]
        YBT = R(DYN + 34816, [128, 4, NOWN], BF16); b_ybt = [Buf("ybt%d" % j) for j in range(4)]
        xring2 = [R(DYN + 51200 + i * 4096, [128, D], F32) for i in range(2)]
        xn2 = [R(DYN + 59392 + i * 2048, [128, D], BF16) for i in range(2)]
        junk2 = R(DYN + 63488, [128, D], BF16)
        u_sb = R(DYN + 65536, [128, 512], F32); b_usb = Buf("usb")
        acc = R(DYN + 67584, [128, 512], F32); b_acc = Buf("acc")
        cub = R(DYN + 69632, [128, 2, 516], F32); b_cub = [Buf("cub0"), Buf("cub1")]
        MRG = R(DYN + 98304, [128, 8, NOWN], BF16)
        b_mrg = [[Buf("mrg%d_%d" % (c, s)) for s in range(4)] for c in range(8)]
        sgt = [R(DYN + 131072 + i * 2048, [128, 512], F32) for i in range(4)]; b_sgt = [Buf("sg%d" % i) for i in range(4)]
        b_x2 = [Buf("x2r%d" % i) for i in range(2)]; b_xn2 = [Buf("xn2_%d" % i) for i in range(2)]; b_junk2 = Buf("junk2")

        for tt in range(17):
            i = tt % 2
            S.dma("sp", [(xring2[i], xown[tt * 128:(tt + 1) * 128, :], {})], writes=[b_x2[i]])
            norm_tile(xring2[i], b_x2[i], 0, xn2[i], b_xn2[i], junk2, b_junk2)
            transpose_tile(xn2[i], b_xn2[i], hTo[:, :, tt * 128:(tt + 1) * 128], b_hTo[tt], (6, 7))

        wu, b_wu = WS.get(i_wu)
        wgb, b_wgb = WS.get(i_wgb)
        wgc, b_wgc = WS.get(i_wgc)
        cb = 0
        for j in range(4):
            jc = slice(j * 128, (j + 1) * 128)
            mm_group(bank(0)[:, 0:128], [(wu[:, kc, jc], hTo[:, kc, NOWN:NOWN + 128]) for kc in range(8)],
                     b_ps[0], [b_wu, b_hTo[16]])
            mm_group(bank(1)[:, 0:128], [(wgc[:, kc, jc], hTo[:, kc, NOWN:NOWN + 128]) for kc in range(8)],
                     b_ps[1], [b_wgc, b_hTo[16]])
            S.op("act", "activation", [b_ps[0]], [b_usb], out=u_sb[:, 0:8], in_=bank(0)[:, 0:8], func=AF.Copy)
            S.op("dve", "tensor_tensor", [b_ps[1], b_usb], [b_cuh], out=cuh, in0=bank(1)[:, 0:8], in1=u_sb[:, 0:8],
                 op=ALU.mult)
            for s in range(4):
                hs = slice(s * 512, (s + 1) * 512)
                rd = [b_hTo[s * 4 + t] for t in range(4)]
                pb = 2 + (cb % 2) * 3
                cu = cub[:, cb % 2, :]; bcu = b_cub[cb % 2]
                cb += 1
                mm_group(bank(pb), [(wu[:, kc, jc], hTo[:, kc, hs]) for kc in range(8)], b_ps[pb], [b_wu] + rd)
                mm_group(bank(pb + 1), [(wgc[:, kc, jc], hTo[:, kc, hs]) for kc in range(8)], b_ps[pb + 1], [b_wgc] + rd)
                mm_group(bank(pb + 2), [(wgb[:, kc, jc], hTo[:, kc, hs]) for kc in range(8)], b_ps[pb + 2], [b_wgb] + rd)
                S.op("act", "activation", [b_ps[pb]], [b_usb], out=u_sb, in_=bank(pb), func=AF.Copy)
                S.op("dve", "tensor_copy", [b_cuh], [bcu], out=cu[:, 0:2], in_=cuh[:, 2 * s:2 * s + 2])
                S.op("dve", "tensor_tensor", [b_ps[pb + 1], b_usb], [bcu], out=cu[:, 2:514], in0=bank(pb + 1), in1=u_sb,
                     op=ALU.mult)
                S.op("dve", "tensor_scalar", [bcu, b_convw], [b_acc], out=acc, in0=cu[:, 2:514],
                     scalar1=convw[:, j, 2:3], scalar2=None, op0=ALU.mult)
                S.op("dve", "scalar_tensor_tensor", [bcu, b_convw, b_acc], [b_acc], out=acc, in0=cu[:, 1:513],
                     scalar=convw[:, j, 1:2], in1=acc, op0=ALU.mult, op1=ALU.add)
                S.op("dve", "scalar_tensor_tensor", [bcu, b_convw, b_acc], [b_acc], out=acc, in0=cu[:, 0:512],
                     scalar=convw[:, j, 0:1], in1=acc, op0=ALU.mult, op1=ALU.add)
                S.op("dve", "tensor_tensor", [b_ps[pb + 2], b_acc], [b_ybt[j]], out=YBT[:, j, hs], in0=bank(pb + 2),
                     in1=acc, op=ALU.mult)
        WS.release(i_wu); WS.release(i_wgb); WS.release(i_wgc)

        it = 0
        for c4 in range(2):
            ia, iga, igb = i_d[c4]
            wab, b_wab = WS.get(ia)
            wga, b_wga = WS.get(iga)
            wgB, b_wgB = WS.get(igb)
            for cc in range(4):
                c = c4 * 4 + cc
                ccs = slice(cc * 128, (cc + 1) * 128)
                for s in range(4):
                    hs = slice(s * 512, (s + 1) * 512)
                    rd = [b_hTo[s * 4 + t] for t in range(4)]
                    pb = (it % 2) * 4
                    sa = sgt[(it % 2) * 2]; bsa = b_sgt[(it % 2) * 2]
                    sb = sgt[(it % 2) * 2 + 1]; bsb = b_sgt[(it % 2) * 2 + 1]
                    it += 1
                    mm_group(bank(pb), [(wab[:, kc, ccs], OAT[:, kc, hs]) for kc in range(4)], b_ps[pb],
                             [b_wab] + [b_oat[kc][s] for kc in range(4)])
                    mm_group(bank(pb + 1), [(wab[:, 4 + kc, ccs], YBT[:, kc, hs]) for kc in range(4)], b_ps[pb + 1],
                             [b_wab] + b_ybt)
                    mm_group(bank(pb + 2), [(wga[:, kc, ccs], hTo[:, kc, hs]) for kc in range(8)], b_ps[pb + 2], [b_wga] + rd)
                    mm_group(bank(pb + 3), [(wgB[:, kc, ccs], hTo[:, kc, hs]) for kc in range(8)], b_ps[pb + 3], [b_wgB] + rd)
                    S.op("act", "activation", [b_ps[pb + 2]], [bsa], out=sa, in_=bank(pb + 2), func=AF.Sigmoid)
                    S.op("act", "activation", [b_ps[pb + 3]], [bsb], out=sb, in_=bank(pb + 3), func=AF.Sigmoid)
                    S.op("dve", "tensor_tensor", [b_ps[pb], bsa], [bsa], out=sa, in0=bank(pb), in1=sa, op=ALU.mult)
                    S.op("dve", "tensor_tensor", [b_ps[pb + 1], bsb], [bsb], out=sb, in0=bank(pb + 1), in1=sb, op=ALU.mult)
                    S.op("dve", "tensor_tensor", [bsa, bsb], [b_mrg[c][s]], out=MRG[:, c, hs], in0=sa, in1=sb, op=ALU.add)
            WS.release(ia); WS.release(iga); WS.release(igb)
        S.barrier()

        X = R(DYN + 0, [128, 16, D], F32); b_X = [Buf("X%d" % t) for t in range(16)]
        for t in range(16):
            S.dma("sp", [(X[:, t, :], xown[t * 128:(t + 1) * 128, :], {})], writes=[b_X[t]])
        rb = [0]
        for hf in range(2):
            wm, b_wm = WS.get(i_wmo[hf])
            for t in range(16):
                bk = rb[0] % 8; rb[0] += 1
                mm_group(bank(bk), [(MRG[:, kc, t * 128:(t + 1) * 128], wm[:, kc, :]) for kc in range(8)], b_ps[bk],
                         [b_wm] + [b_mrg[kc][t // 4] for kc in range(8)])
                xs = X[:, t, hf * 512:(hf + 1) * 512]
                S.op("dve", "tensor_tensor", [b_ps[bk], b_X[t]], [b_X[t]], out=xs, in0=bank(bk), in1=xs, op=ALU.add)
            WS.release(i_wmo[hf])
        S.barrier()

        hTs = [R(DYN + 65536 + i * 8192, [128, 8, 512], BF16) for i in range(2)]
        b_hTs = [[Buf("hTs%d_%d" % (i, t)) for t in range(4)] for i in range(2)]
        qmT = R(DYN + 81920, [128, 8, 512], BF16); b_qmT = [Buf("qmT%d" % c) for c in range(8)]
        omT = R(DYN + 90112, [128, 8, 512], BF16); b_omT = [Buf("omT%d" % c) for c in range(8)]
        mT = R(DYN + 98304, [128, 8, 256], BF16); b_mT = [Buf("mT0"), Buf("mT1")]
        KmT = R(DYN + 102400, [128, 8, 256], BF16); b_KmT = Buf("KmT")
        Vm = R(DYN + 106496, [128, 2, D], BF16); b_Vm = Buf("Vm")
        Esm = R(DYN + 110592, [128, 4, 256], F32); b_Esm = Buf("Esm")
        probs = [R(DYN + 114688 + i * 2048, [128, 4, 256], BF16) for i in range(2)]; b_probs = [Buf("pr0"), Buf("pr1")]
        pT = R(DYN + 118784, [128, 8, 512], BF16); b_pT = Buf("pT")
        xn3 = [R(DYN + 126976 + i * 2048, [128, D], BF16) for i in range(2)]; b_xn3 = [Buf("xn3_0"), Buf("xn3_1")]
        junk3 = R(DYN + 131072, [128, D], BF16); b_junk3 = Buf("junk3")
        memring = [R(DYN + 118784 + i * 4096, [128, D], F32) for i in range(2)]; b_mr = [Buf("mr0"), Buf("mr1")]

        load_gain(1, norm_mem_kv)
        load_gain(0, norm_mem_q)
        for mt in range(2):
            S.dma("sp", [(memring[mt], memb[mt * 128:(mt + 1) * 128, :], {})], writes=[b_mr[mt]])
            norm_tile(memring[mt], b_mr[mt], 1, xn3[mt], b_xn3[mt], junk3, b_junk3)
            transpose_tile(xn3[mt], b_xn3[mt], mT[:, :, mt * 128:(mt + 1) * 128], b_mT[mt], (4, 5))
        wK = [WS.get(i_wkvK[k]) for k in range(2)]
        for c in range(8):
            bk = 6 + c % 2
            w_, bw_ = wK[c // 4]
            mm_group(bank(bk)[:, 0:256], [(w_[:, kc, (c % 4) * 128:(c % 4 + 1) * 128], mT[:, kc, :]) for kc in range(8)],
                     b_ps[bk], [bw_] + b_mT)
            evac(KmT[:, c, :], bank(bk)[:, 0:256], [b_ps[bk]], [b_KmT])
        WS.release(i_wkvK[0]); WS.release(i_wkvK[1])
        wVv = [WS.get(i_wkvV[k]) for k in range(2)]
        for mt in range(2):
            for hf in range(2):
                bk = 6 + hf
                w_, bw_ = wVv[hf]
                mm_group(bank(bk), [(mT[:, kc, mt * 128:(mt + 1) * 128], w_[:, kc, :]) for kc in range(8)],
                         b_ps[bk], [bw_, b_mT[mt]])
                evac(Vm[:, mt, hf * 512:(hf + 1) * 512], bank(bk), [b_ps[bk]], [b_Vm])
        WS.release(i_wkvV[0]); WS.release(i_wkvV[1])
        S.barrier()

        wMQ = [WS.get(i_wmq[k]) for k in range(2)]
        wMO = [WS.get(i_wmo2[k]) for k in range(2)]
        sc_it = 0
        gb = [0]
        for s in range(4):
            hb = hTs[s % 2]; bh = b_hTs[s % 2]
            for t in range(4):
                tt = s * 4 + t
                i = tt % 2
                norm_tile(X[:, tt, :], b_X[tt], 0, xn3[i], b_xn3[i], junk3, b_junk3)
                transpose_tile(xn3[i], b_xn3[i], hb[:, :, t * 128:(t + 1) * 128], bh[t], (4, 5))
            for c in range(8):
                bk = 6 + gb[0] % 2; gb[0] += 1
                w_, bw_ = wMQ[c // 4]
                mm_group(bank(bk), [(w_[:, kc, (c % 4) * 128:(c % 4 + 1) * 128], hb[:, kc, :]) for kc in range(8)],
                         b_ps[bk], [bw_] + bh)
                evac(qmT[:, c, :], bank(bk), [b_ps[bk]], [b_qmT[c]])
            for t in range(4):
                sb0 = (sc_it % 2) * 2
                pr = probs[sc_it % 2]; bpr = b_probs[sc_it % 2]
                tb_ = 4 + (sc_it % 2)
                sc_it += 1
                psS = bank(sb0, 2).rearrange("p (a b) -> p a b", a=4)
                b_S = b_ps[sb0]
                for h in range(4):
                    for cc in range(2):
                        c = 2 * h + cc
                        S.op("pe", "matmul", [b_qmT[c], b_KmT], [b_S], psS[:, h, :], lhsT=qmT[:, c, t * 128:(t + 1) * 128],
                             rhs=KmT[:, c, :], start=(cc == 0), stop=(cc == 1), _mark=(h == 3 and cc == 1))
                S.op("dve", "tensor_reduce", [b_S], [b_mx], out=mx4, in_=psS, axis=AX.X, op=ALU.max)
                S.op("dve", "tensor_scalar", [b_mx], [b_nb], out=nb4, in0=mx4, scalar1=-1.0 / 16, scalar2=None, op0=ALU.mult)
                for h in range(4):
                    S.op("act", "activation", [b_S, b_nb], [b_Esm, b_sm], out=Esm[:, h, :], in_=psS[:, h, :], func=AF.Exp,
                         scale=1.0 / 16, bias=nb4[:, h:h + 1], accum_out=sm4[:, h:h + 1])
                S.op("dve", "reciprocal", [b_sm], [b_rs], out=rs4, in_=sm4)
                for h in range(4):
                    S.op("dve", "tensor_scalar", [b_Esm, b_rs], [bpr], out=pr[:, h, :], in0=Esm[:, h, :],
                         scalar1=rs4[:, h:h + 1], scalar2=None, op0=ALU.mult)
                pst = bank_bf(tb_)
                for h in range(4):
                    for mt in range(2):
                        k8 = h * 2 + mt
                        S.op("pe", "transpose", [bpr, b_ident], [b_ps[tb_]], out=pst[:, k8 * 128:(k8 + 1) * 128],
                             in_=pr[:, h, mt * 128:(mt + 1) * 128], identity=ident, _mark=(k8 == 7))
                S.op("act", "activation", [b_ps[tb_]], [b_pT], out=pT[:, :, t * 128:(t + 1) * 128],
                     in_=pst.rearrange("p (a b) -> p a b", a=8), func=AF.Copy)
            for h in range(4):
                for dc in range(2):
                    c = h * 2 + dc
                    bk = 6 + gb[0] % 2; gb[0] += 1
                    mm_group(bank(bk), [(Vm[:, mt, c * 128:(c + 1) * 128], pT[:, h * 2 + mt, :]) for mt in range(2)],
                             b_ps[bk], [b_Vm, b_pT])
                    evac(omT[:, c, :], bank(bk), [b_ps[bk]], [b_omT[c]])
            for t in range(4):
                tt = s * 4 + t
                for hf in range(2):
                    bk = 6 + gb[0] % 2; gb[0] += 1
                    w_, bw_ = wMO[hf]
                    mm_group(bank(bk), [(omT[:, kc, t * 128:(t + 1) * 128], w_[:, kc, :]) for kc in range(8)],
                             b_ps[bk], [bw_] + b_omT)
                    xs = X[:, tt, hf * 512:(hf + 1) * 512]
                    S.op("dve", "tensor_tensor", [b_ps[bk], b_X[tt]], [b_X[tt]], out=xs, in0=bank(bk), in1=xs, op=ALU.add)
        for k in range(2):
            WS.release(i_wmq[k]); WS.release(i_wmo2[k])
        S.barrier()

        hTb = R(DYN + 65536, [128, 8, 1024], BF16); b_hTb = [Buf("hTb%d" % t) for t in range(8)]
        aT = R(DYN + 81920, [128, 22, 1024], BF16); b_aT = [Buf("aT%d" % j) for j in range(22)]
        sgl = [R(DYN + 126976 + i * 2048, [128, 512], F32) for i in range(2)]; b_sgl = [Buf("sgl0"), Buf("sgl1")]
        xn4 = [R(DYN + 131072 + i * 2048, [128, D], BF16) for i in range(2)]; b_xn4 = [Buf("xn4_0"), Buf("xn4_1")]
        junk4 = R(DYN + 135168, [128, D], BF16); b_junk4 = Buf("junk4")
        load_gain(1, norm_ffn)
        fit = 0
        for tb in range(2):
            gu, fo = i_ffn[tb]
            for t in range(8):
                tt = tb * 8 + t
                i = tt % 2
                norm_tile(X[:, tt, :], b_X[tt], 1, xn4[i], b_xn4[i], junk4, b_junk4)
                transpose_tile(xn4[i], b_xn4[i], hTb[:, :, t * 128:(t + 1) * 128], b_hTb[t], (6, 7))
            for blk in range(6):
                ig, iu, ncol = gu[blk]
                wg_, b_wg = WS.get(ig)
                wu_, b_wu_ = WS.get(iu)
                for cc in range(ncol // 128):
                    j = blk * 4 + cc
                    ccs = slice(cc * 128, (cc + 1) * 128)
                    for s2 in range(2):
                        hs = slice(s2 * 512, (s2 + 1) * 512)
                        rd = b_hTb[s2 * 4:(s2 + 1) * 4]
                        pb = (fit % 3) * 2
                        sg_ = sgl[fit % 2]; bsg = b_sgl[fit % 2]
                        fit += 1
                        mm_group(bank(pb), [(wg_[:, kc, ccs], hTb[:, kc, hs]) for kc in range(8)], b_ps[pb], [b_wg] + rd)
                        mm_group(bank(pb + 1), [(wu_[:, kc, ccs], hTb[:, kc, hs]) for kc in range(8)], b_ps[pb + 1],
                                 [b_wu_] + rd)
                        S.op("act", "activation", [b_ps[pb]], [bsg], out=sg_, in_=bank(pb), func=AF.Silu)
                        S.op("dve", "tensor_tensor", [b_ps[pb + 1], bsg], [b_aT[j]], out=aT[:, j, hs], in0=bank(pb + 1),
                             in1=sg_, op=ALU.mult)
                WS.release(ig); WS.release(iu)
            for hf in range(2):
                wfo = [WS.get(fo[hf][k3]) for k3 in range(3)]
                for t in range(8):
                    tt = tb * 8 + t
                    bk = 6 + (fit % 2); fit += 1
                    mm_group(bank(bk), [(aT[:, j, t * 128:(t + 1) * 128], wfo[j // 8][0][:, j % 8, :]) for j in range(22)],
                             b_ps[bk], [w[1] for w in wfo] + b_aT)
                    xs = X[:, tt, hf * 512:(hf + 1) * 512]
                    S.op("dve", "tensor_tensor", [b_ps[bk], b_X[tt]], [b_X[tt]], out=xs, in0=bank(bk), in1=xs, op=ALU.add)
                for k3 in range(3):
                    WS.release(fo[hf][k3])
        S.barrier()

        otmp = [R(DYN + 65536 + i * 4096, [128, D], F32) for i in range(2)]; b_ot = [Buf("ot0"), Buf("ot1")]
        junk5 = R(DYN + 73728, [128, D], BF16); b_junk5 = Buf("junk5")
        load_gain(0, norm_final)
        out_evs = []
        for tt in range(16):
            i = tt % 2
            norm_tile(X[:, tt, :], b_X[tt], 0, otmp[i], b_ot[i], junk5, b_junk5)
            out_evs.append(S.dma("sp", [(out_d[tt * 128:(tt + 1) * 128, :], otmp[i], {})], reads=[b_ot[i]]))
        for ev in out_evs:
            S._wait("sp", ev)
        print("inst counts", S.n_inst, {e: len(S.prog[e]) for e in S.ENGS})
        S.emit()
    return nc


def _host_inputs(inputs):
    x = np.asarray(inputs["x"], dtype=np.float32)
    mem = np.asarray(inputs["mem"], dtype=np.float32)
    diag = np.zeros((128, 4, 512), np.float32)
    pp = np.arange(128)[:, None]
    kr = np.arange(512)[None, :]
    for i in range(4):
        ql = i * 128 + pp
        diag[:, i, :] = np.where(kr > 511 - ql, 0.0, NEG)
    eye = np.eye(128, dtype=np.float32)
    shared = {
        "diag": diag,
        "norm_mix": np.ascontiguousarray(inputs["norm_mix"][0:1]),
        "w_in": np.ascontiguousarray(inputs["w_in"][0]),
        "conv_w": np.ascontiguousarray(inputs["conv_w"][0]),
        "w_branch_a": np.ascontiguousarray(inputs["w_branch_a"][0]),
        "w_branch_b": np.ascontiguousarray(inputs["w_branch_b"][0]),
        "w_mix_out": np.ascontiguousarray(inputs["w_mix_out"][0]),
        "norm_mem_q": np.ascontiguousarray(inputs["norm_mem_q"][0:1]),
        "norm_mem_kv": np.ascontiguousarray(inputs["norm_mem_kv"][0:1]),
        "w_mem_q": np.ascontiguousarray(inputs["w_mem_q"][0]),
        "w_mem_kv": np.ascontiguousarray(inputs["w_mem_kv"][0]),
        "w_mem_o": np.ascontiguousarray(inputs["w_mem_o"][0]),
        "norm_ffn": np.ascontiguousarray(inputs["norm_ffn"][0:1]),
        "w_ffn_in": np.ascontiguousarray(inputs["w_ffn_in"][0]),
        "w_ffn_out": np.ascontiguousarray(inputs["w_ffn_out"][0]),
        "norm_final": np.ascontiguousarray(np.asarray(inputs["norm_final"]).reshape(1, D)),
    }
    shared = {k: np.asarray(v, dtype=np.float32) for k, v in shared.items()}
    in_maps = []
    for core in range(8):
        b, par = core // 2, core % 2
        tiles = T_OF[par]
        xown = np.zeros((NOWN + 128, D), np.float32)
        seld = np.zeros((128, 4, 2, 128), np.float32)
        seln = np.zeros((1, 4, 2, 128), np.float32)
        for j, T in enumerate(tiles):
            xown[j * 512:(j + 1) * 512] = x[b, T * 512:(T + 1) * 512]
            if T > 0:
                xown[NOWN + 2 * j:NOWN + 2 * j + 2] = x[b, T * 512 - 2:T * 512]
            tmax = NCH[j] - 1
            if T == tmax:
                seld[:, j, 0, :] = eye
            else:
                seln[0, j, 0, :] = 1.0
                seld[:, j, 1, :] = eye
        m = dict(shared)
        m["xall"] = np.ascontiguousarray(x[b, ::-1, :])
        m["xown"] = xown
        m["memb"] = np.ascontiguousarray(mem[b])
        m["seld"] = seld
        m["seln"] = seln
        in_maps.append(m)
    return in_maps


_NC_CACHE = {}


def kernel(**inputs):
    in_maps = _host_inputs(inputs)
    if "nc" not in _NC_CACHE:
        _NC_CACHE["nc"] = build_nc()
    nc = _NC_CACHE["nc"]
    res = run_bass_kernel_spmd(nc, in_maps, core_ids=list(range(8)))
    out = np.zeros((NB, SEQ, D), np.float32)
    for core in range(8):
        b, par = core // 2, core % 2
        o = np.asarray(res.results[core]["out"], dtype=np.float32)
        for j, T in enumerate(T_OF[par]):
            out[b, T * 512:(T + 1) * 512] = o[j * 512:(j + 1) * 512]
    return out
```

```python
import numpy as np
import concourse.bass as bass
import concourse.mybir as mybir
from concourse.bass_utils import run_bass_kernel_spmd
from contextlib import ExitStack

F32 = mybir.dt.float32
BF16 = mybir.dt.bfloat16
AF = mybir.ActivationFunctionType
ALU = mybir.AluOpType
AX = mybir.AxisListType

D = 1024
SEQ = 4096
NB = 4
TS = 512
T_OF = {0: (0, 3, 4, 7), 1: (1, 2, 5, 6)}
NCH = (2, 4, 6, 8)
NEG = -30000.0
EPS = 1e-6
FFN_H = 2816
NOWN = 2048
SAME_ENGINE_SYNC = True


class Buf:
    __slots__ = ("name", "w", "r")

    def __init__(self, name):
        self.name = name
        self.w = None
        self.r = {}


class Sched:
    ENGS = ("pe", "act", "dve", "pool", "sp")

    def __init__(self, nc, stack, n_dma_sems=8):
        self.nc = nc
        self.prog = {e: [] for e in self.ENGS}
        self.count = {e: 0 for e in self.ENGS}
        self.sem = {e: stack.enter_context(nc.semaphore("s_" + e)) for e in self.ENGS}
        self.seen = {e: {} for e in self.ENGS}
        self.dsem = {}
        self.dval = {}
        self.dring = {}
        self.dpos = {}
        idx = 0
        for q in ("sp", "pool"):
            ring = []
            for i in range(n_dma_sems):
                self.dsem[idx] = stack.enter_context(nc.semaphore("d_%s%d" % (q, i)))
                self.dval[idx] = 0
                ring.append(idx)
                idx += 1
            self.dring[q] = ring
            self.dpos[q] = 0
        self.n_inst = {e: 0 for e in self.ENGS}
        self.last_marked = {e: True for e in self.ENGS}

    def _wait(self, e, ev):
        if ev is None:
            return
        if ev[0] == "e":
            _, src, seq = ev
            if src == e and (e == "pe" or not SAME_ENGINE_SYNC):
                return
            assert self.count[src] >= seq, ("dependency on unissued mark", e, ev)
            key = ("e", src)
            val = seq
            sem = self.sem[src]
        else:
            _, sidx, val = ev
            key = ("d", sidx)
            sem = self.dsem[sidx]
        if self.seen[e].get(key, 0) >= val:
            return
        self.seen[e][key] = val
        self.prog[e].append(lambda eng, sem=sem, val=val: eng.wait_ge(sem, val))

    def _deps(self, e, reads, writes):
        for b in reads:
            self._wait(e, b.w)
        for b in writes:
            self._wait(e, b.w)
            for (k0, k1), v in list(b.r.items()):
                self._wait(e, (k0, k1, v))

    def _record(self, ev, reads, writes):
        key = (ev[0], ev[1])
        for b in reads:
            if b.r.get(key, 0) < ev[2]:
                b.r[key] = ev[2]
        for b in writes:
            b.w = ev
            b.r = {}

    def op(self, e, meth, reads, writes, *args, _mark=True, **kw):
        self._deps(e, reads, writes)
        sem = self.sem[e]
        if _mark:
            self.count[e] += 1
            self.prog[e].append(lambda eng: getattr(eng, meth)(*args, **kw).then_inc(sem, 1))
            ev = ("e", e, self.count[e])
        else:
            self.prog[e].append(lambda eng: getattr(eng, meth)(*args, **kw))
            ev = ("e", e, self.count[e] + 1)
        self.last_marked[e] = _mark
        self.n_inst[e] += 1
        self._record(ev, reads, writes)
        return ev

    def dma(self, q, xfers, reads=(), writes=()):
        self._deps(q, reads, writes)
        ring = self.dring[q]
        sidx = ring[self.dpos[q] % len(ring)]
        self.dpos[q] += 1
        if self.dval[sidx] > 0:
            self._wait(q, ("d", sidx, self.dval[sidx]))
        sem = self.dsem[sidx]
        for (o, i, kw) in xfers:
            self.dval[sidx] += 16
            self.prog[q].append(lambda eng, o=o, i=i, kw=kw: eng.dma_start(out=o, in_=i, **kw).then_inc(sem, 16))
        ev = ("d", sidx, self.dval[sidx])
        self._record(ev, reads, writes)
        return ev

    def barrier(self):
        for e in ("pe", "act", "dve"):
            assert self.last_marked[e], ("barrier with unmarked tail", e)
        for sidx in self.dring["sp"]:
            if self.dval[sidx] > 0:
                self._wait("sp", ("d", sidx, self.dval[sidx]))
        for f in ("pe", "act", "dve"):
            self._wait("sp", ("e", f, self.count[f]))
        self.count["sp"] += 1
        sem = self.sem["sp"]
        self.prog["sp"].append(lambda eng, sem=sem: eng.sem_inc(sem, 1))
        for e in ("pe", "act", "dve"):
            self._wait(e, ("e", "sp", self.count["sp"]))

    def emit(self):
        with self.nc.Block() as block:
            def mk(name):
                def body(engine):
                    for c in self.prog[name]:
                        c(engine)
                return body
            block.sync(mk("sp"))
            block.gpsimd(mk("pool"))
            block.scalar(mk("act"))
            block.vector(mk("dve"))
            block.tensor(mk("pe"))


def build_nc():
    nc = bass.Bass("TRN2", target_bir_lowering=False)

    def din(name, shape):
        return nc.dram_tensor(name, list(shape), F32, kind="ExternalInput").ap()

    xall = din("xall", [SEQ, D])
    xown = din("xown", [NOWN + 128, D])
    memb = din("memb", [256, D])
    seld_d = din("seld", [128, 4, 2, 128])
    seln_d = din("seln", [1, 4, 2, 128])
    diag_d = din("diag", [128, 4, 512])
    norm_mix = din("norm_mix", [1, D])
    w_in = din("w_in", [D, 5120])
    conv_w = din("conv_w", [3, 512])
    w_ba = din("w_branch_a", [512, D])
    w_bb = din("w_branch_b", [512, D])
    w_mix_out = din("w_mix_out", [D, D])
    norm_mem_q = din("norm_mem_q", [1, D])
    norm_mem_kv = din("norm_mem_kv", [1, D])
    w_mem_q = din("w_mem_q", [D, D])
    w_mem_kv = din("w_mem_kv", [D, 2 * D])
    w_mem_o = din("w_mem_o", [D, D])
    norm_ffn = din("norm_ffn", [1, D])
    w_ffn_in = din("w_ffn_in", [D, 2 * FFN_H])
    w_ffn_out = din("w_ffn_out", [FFN_H, D])
    norm_final = din("norm_final", [1, D])
    out_d = nc.dram_tensor("out", [NOWN, D], F32, kind="ExternalOutput").ap()

    with ExitStack() as st:
        S = Sched(nc, st)
        ARENA_F32 = 52900
        arena = st.enter_context(nc.sbuf_tensor("arena", [128, ARENA_F32], F32))
        psum = st.enter_context(nc.psum_tensor("psum", [128, 4096], F32))

        def R(off, shape, dt):
            esz = 4 if dt == F32 else 2
            n = int(np.prod(shape[1:]))
            nbytes = n * esz
            assert off % 4 == 0 and nbytes % 4 == 0, (off, shape)
            assert off + nbytes <= ARENA_F32 * 4, (off, shape)
            ap = arena[:, off // 4:(off + nbytes) // 4]
            if dt != F32:
                ap = ap.bitcast(dt)
            if len(shape) == 3:
                ap = ap.rearrange("p (a b) -> p a b", a=shape[1])
            elif len(shape) == 4:
                ap = ap.rearrange("p (a b c) -> p a b c", a=shape[1], b=shape[2])
            return ap

        def bank(i, n=1):
            return psum[:, i * 512:(i + n) * 512]

        def bank_bf(i, n=1):
            return psum[:, i * 512:(i + n) * 512].bitcast(BF16)

        b_ps = [Buf("ps%d" % i) for i in range(8)]

        ident = R(0, [128, 128], BF16); b_ident = Buf("ident")
        identf = R(256, [128, 128], F32); b_identf = Buf("identf")
        convw = R(768, [128, 4, 3], F32); b_convw = Buf("convw")
        stats = R(1024, [128, 256], F32)
        seld = R(2048, [128, 4, 2, 128], BF16); b_seld = Buf("seld")
        seln = R(4096, [128, 4, 2, 128], BF16); b_seln = Buf("seln")
        negrow = R(6144, [128, 512], BF16); b_negrow = Buf("negrow")
        zeros = R(7168, [128, 512], F32); b_zeros = Buf("zeros")
        diag = R(9216, [128, 4, 512], BF16); b_diag = Buf("diag")
        GAIN0 = 13312
        g_rep = [R(GAIN0 + i * 4096, [128, D], F32) for i in range(2)]
        b_g = [Buf("g%d" % i) for i in range(2)]
        RING0 = 21504
        NSLOT = 5
        DYN = RING0 + NSLOT * 8192

        ssq = stats[:, 0:8]; rstd = stats[:, 8:16]
        b_st = [Buf("st%d" % i) for i in range(8)]
        st_pos = [0]
        mx4 = stats[:, 16:20]; nb4 = stats[:, 20:24]; sm4 = stats[:, 24:28]; rs4 = stats[:, 28:32]
        b_mx = Buf("mx"); b_nb = Buf("nb"); b_sm = Buf("sm"); b_rs = Buf("rs")
        cuh = stats[:, 32:40]; b_cuh = Buf("cuh")

        class WStream:
            def __init__(self):
                self.blocks = []
                self.issued = 0
                self.released = set()
                self.bufs = [Buf("ring%d" % i) for i in range(NSLOT)]
                self.next_get = 0

            def add(self, pieces):
                self.blocks.append(pieces)
                return len(self.blocks) - 1

            def slot_ap(self, i):
                return R(RING0 + (i % NSLOT) * 8192, [128, 8, 512], BF16)

            def _issue(self):
                while self.issued < len(self.blocks):
                    k = self.issued
                    if k >= NSLOT and (k - NSLOT) not in self.released:
                        break
                    if k > self.next_get + NSLOT - 1:
                        break
                    sl = self.slot_ap(k)
                    xf = []
                    for (src, kc0, kcn, ncols) in self.blocks[k]:
                        xf.append((sl[:, kc0:kc0 + kcn, 0:ncols], src, {}))
                    S.dma("pool", xf, writes=[self.bufs[k % NSLOT]])
                    self.issued += 1

            def get(self, idx):
                assert idx == self.next_get, (idx, self.next_get)
                self.next_get += 1
                self._issue()
                assert self.issued > idx, ("weight block not issued", idx)
                return self.slot_ap(idx), self.bufs[idx % NSLOT]

            def release(self, idx):
                self.released.add(idx)
                self._issue()

        WS = WStream()

        def wpiece(w, r0, nrows, c0, ncols, kc0=0):
            src = w[r0:r0 + nrows, c0:c0 + ncols].rearrange("(kc p) n -> p kc n", p=128)
            return (src, kc0, nrows // 128, ncols)

        i_wk = WS.add([wpiece(w_in, 0, D, 512, 512)])
        i_wv = WS.add([wpiece(w_in, 0, D, 1024, 512)])
        i_wq = WS.add([wpiece(w_in, 0, D, 0, 512)])
        i_wu = WS.add([wpiece(w_in, 0, D, 1536, 512)])
        i_wgb = WS.add([wpiece(w_in, 0, D, 2048, 512)])
        i_wgc = WS.add([wpiece(w_in, 0, D, 2560, 512)])
        i_d = []
        for c4 in range(2):
            a = WS.add([wpiece(w_ba, 0, 512, c4 * 512, 512, kc0=0), wpiece(w_bb, 0, 512, c4 * 512, 512, kc0=4)])
            b_ = WS.add([wpiece(w_in, 0, D, 3072 + c4 * 512, 512)])
            c_ = WS.add([wpiece(w_in, 0, D, 4096 + c4 * 512, 512)])
            i_d.append((a, b_, c_))
        i_wmo = [WS.add([wpiece(w_mix_out, 0, D, hf * 512, 512)]) for hf in range(2)]
        i_wkvK = [WS.add([wpiece(w_mem_kv, 0, D, k * 512, 512)]) for k in range(2)]
        i_wkvV = [WS.add([wpiece(w_mem_kv, 0, D, D + k * 512, 512)]) for k in range(2)]
        i_wmq = [WS.add([wpiece(w_mem_q, 0, D, k * 512, 512)]) for k in range(2)]
        i_wmo2 = [WS.add([wpiece(w_mem_o, 0, D, k * 512, 512)]) for k in range(2)]
        i_ffn = []
        for tb in range(2):
            gu = []
            for blk in range(6):
                ncol = 512 if blk < 5 else 256
                ig = WS.add([wpiece(w_ffn_in, 0, D, blk * 512, ncol)])
                iu = WS.add([wpiece(w_ffn_in, 0, D, FFN_H + blk * 512, ncol)])
                gu.append((ig, iu, ncol))
            fo = []
            for hf in range(2):
                ks = []
                for k3 in range(3):
                    nr = 1024 if k3 < 2 else FFN_H - 2048
                    ks.append(WS.add([wpiece(w_ffn_out, k3 * 1024, nr, hf * 512, 512)]))
                fo.append(ks)
            i_ffn.append((gu, fo))

        evac_rr = [0]

        def evac(out_ap, in_ap, reads, writes, eng=None):
            if eng is None:
                eng = ("act", "dve")[evac_rr[0] % 2]
                evac_rr[0] += 1
            if eng == "act":
                return S.op("act", "activation", reads, writes, out=out_ap, in_=in_ap, func=AF.Copy)
            return S.op("dve", "tensor_copy", reads, writes, out=out_ap, in_=in_ap)

        def mm_group(out_ap, pairs, b_out, reads):
            n = len(pairs)
            for k, (l, r) in enumerate(pairs):
                S.op("pe", "matmul", reads, [b_out], out_ap, lhsT=l, rhs=r, start=(k == 0), stop=(k == n - 1),
                     _mark=(k == n - 1))

        def load_gain(i, src):
            S.dma("sp", [(g_rep[i], src.broadcast_to([128, D]), {})], writes=[b_g[i]])

        def norm_tile(x_ap, b_x, gi, out_ap, b_out, junk_ap, b_junk_):
            k = st_pos[0] % 8
            st_pos[0] += 1
            bs = b_st[k]
            S.op("act", "activation", [b_x], [b_junk_, bs], out=junk_ap, in_=x_ap, func=AF.Square,
                 accum_out=ssq[:, k:k + 1])
            S.op("act", "activation", [bs], [bs], out=rstd[:, k:k + 1], in_=ssq[:, k:k + 1], func=AF.Ln,
                 scale=1.0 / D, bias=EPS)
            S.op("act", "activation", [bs], [bs], out=rstd[:, k:k + 1], in_=rstd[:, k:k + 1], func=AF.Exp, scale=-0.5)
            S.op("dve", "scalar_tensor_tensor", [b_x, bs, b_g[gi]], [b_out], out=out_ap, in0=x_ap,
                 scalar=rstd[:, k:k + 1], in1=g_rep[gi], op0=ALU.mult, op1=ALU.mult)

        tr_rr = [0]

        def transpose_tile(xn_ap, b_xn_, dst3, b_dst, tbanks):
            bk = tbanks[tr_rr[0] % len(tbanks)]
            tr_rr[0] += 1
            pst = bank_bf(bk)
            for c in range(8):
                S.op("pe", "transpose", [b_xn_, b_ident], [b_ps[bk]], out=pst[:, c * 128:(c + 1) * 128],
                     in_=xn_ap[:, c * 128:(c + 1) * 128], identity=ident, _mark=(c == 7))
            S.op("act", "activation", [b_ps[bk]], [b_dst], out=dst3, in_=pst.rearrange("p (a b) -> p a b", a=8),
                 func=AF.Copy)

        S.op("dve", "memset", [], [b_identf], identf, 1.0)
        S.op("pool", "affine_select", [b_identf], [b_identf], out=identf, in_=identf, pattern=[[-1, 128]],
             compare_op=ALU.is_equal, fill=0.0, base=0, channel_multiplier=1)
        S.op("dve", "tensor_copy", [b_identf], [b_ident], out=ident, in_=identf)
        S.op("dve", "memset", [], [b_negrow], negrow, NEG)
        S.op("dve", "memset", [], [b_zeros], zeros, 0.0)
        S.dma("pool", [(seld, seld_d, {})], writes=[b_seld])
        S.dma("pool", [(seln[0:1], seln_d, {})], writes=[b_seln])
        S.dma("pool", [(diag, diag_d, {})], writes=[b_diag])
        S.dma("sp", [(convw[:, j, :], conv_w[:, j * 128:(j + 1) * 128].rearrange("i p -> p i"),
                      {"allow_slow_non_contiguous": True}) for j in range(4)], writes=[b_convw])
        load_gain(0, norm_mix)

        KT = R(DYN + 0, [128, 4, SEQ], BF16)
        V = R(DYN + 32768, [128, 32, 512], BF16)
        QT = R(DYN + 65536, [128, 4, NOWN], BF16)
        OAT = R(DYN + 81920, [128, 4, NOWN], BF16)
        b_oat = [[Buf("oat%d_%d" % (p, s)) for s in range(4)] for p in range(4)]
        TMP = DYN + 98304
        xring = [R(TMP + i * 4096, [128, D], F32) for i in range(4)]; b_xr = [Buf("xr%d" % i) for i in range(4)]
        xn = [R(TMP + 16384 + i * 2048, [128, D], BF16) for i in range(2)]; b_xn = [Buf("xn%d" % i) for i in range(2)]
        junk = R(TMP + 20480, [128, D], BF16); b_junk = Buf("junk")
        hTa = [R(TMP + 22528 + i * 8192, [128, 8, 512], BF16) for i in range(2)]
        b_hTa = [[Buf("hTa%d_%d" % (i, t)) for t in range(4)] for i in range(2)]
        b_kt = [[Buf("kt%d_%d" % (p, ch)) for ch in range(8)] for p in range(4)]
        b_v = [Buf("v%d" % ch) for ch in range(8)]
        b_qt = [[Buf("qt%d_%d" % (p, s)) for s in range(4)] for p in range(4)]

        wk, b_wk = WS.get(i_wk)
        wv, b_wv = WS.get(i_wv)
        wq, b_wq = WS.get(i_wq)

        xl = [0]
        mmb = [0]

        def stream_tile(src_rows, gi, dst3, b_dst, tbanks):
            i = xl[0]
            xl[0] += 1
            xt = xring[i % 4]; bx = b_xr[i % 4]
            S.dma("sp", [(xt, src_rows, {})], writes=[bx])
            xo = xn[i % 2]; bxo = b_xn[i % 2]
            norm_tile(xt, bx, gi, xo, bxo, junk, b_junk)
            transpose_tile(xo, bxo, dst3, b_dst, tbanks)

        for ch in range(8):
            hb = hTa[ch % 2]; bh = b_hTa[ch % 2]
            for t in range(4):
                r0 = ch * 512 + t * 128
                stream_tile(xall[r0:r0 + 128, :], 0, hb[:, :, t * 128:(t + 1) * 128], bh[t], (6, 7))
            for p in range(4):
                bk = mmb[0] % 6; mmb[0] += 1
                mm_group(bank(bk), [(wk[:, kc, p * 128:(p + 1) * 128], hb[:, kc, :]) for kc in range(8)],
                         b_ps[bk], [b_wk] + bh)
                evac(KT[:, p, ch * 512:(ch + 1) * 512], bank(bk), [b_ps[bk]], [b_kt[p][ch]])
            for t in range(4):
                bk = mmb[0] % 6; mmb[0] += 1
                mm_group(bank(bk), [(hb[:, kc, t * 128:(t + 1) * 128], wv[:, kc, :]) for kc in range(8)],
                         b_ps[bk], [b_wv, bh[t]])
                evac(V[:, ch * 4 + t, :], bank(bk), [b_ps[bk]], [b_v[ch]])
        WS.release(i_wk); WS.release(i_wv)

        for s in range(4):
            hb = hTa[s % 2]; bh = b_hTa[s % 2]
            for t in range(4):
                r0 = s * 512 + t * 128
                stream_tile(xown[r0:r0 + 128, :], 0, hb[:, :, t * 128:(t + 1) * 128], bh[t], (6, 7))
            for p in range(4):
                bk = mmb[0] % 6; mmb[0] += 1
                mm_group(bank(bk), [(wq[:, kc, p * 128:(p + 1) * 128], hb[:, kc, :]) for kc in range(8)],
                         b_ps[bk], [b_wq] + bh)
                evac(QT[:, p, s * 512:(s + 1) * 512], bank(bk), [b_ps[bk]], [b_qt[p][s]])
        WS.release(i_wq)
        S.barrier()

        Fb = [R(TMP + i * 8208, [128, 4, 513], F32) for i in range(2)]; b_F = [Buf("F%d" % i) for i in range(2)]
        D1 = R(TMP + 16416, [128, 4, 513], F32); b_D1 = Buf("D1")
        Pb = R(TMP + 24624, [128, 4, 513], F32); b_P = Buf("P")
        Wb = [R(TMP + 32832 + i * 4096, [128, 4, 512], BF16) for i in range(2)]; b_W = [Buf("W%d" % i) for i in range(2)]
        WTb = [R(TMP + 41024 + i * 4096, [128, 4, 512], BF16) for i in range(2)]; b_WT = [Buf("WT%d" % i) for i in range(2)]
        b_Z = Buf("Z")
        b_WTps = Buf("WTps")
        WTps = bank_bf(4, 2).rearrange("p (a b) -> p a b", a=4)
        for i in range(2):
            S.op("dve", "memset", [], [b_F[i]], Fb[i].rearrange("p a b -> p (a b)"), 0.0)
        S.op("dve", "memset", [], [b_D1], D1.rearrange("p a b -> p (a b)"), 0.0)
        groups = [(s, h, c) for s in range(4) for h in range(8) for c in range(NCH[s])]
        G = len(groups)

        def geom(gi):
            s, h, c = groups[gi]
            n = NCH[s]
            p = h // 2
            rows = slice(0, 64) if h % 2 == 0 else slice(64, 128)
            ob = 6 + (h % 2)
            cr = 8 - n + c
            return s, h, c, n, p, rows, ob, cr

        def stage1(gi):
            s, h, c, n, p, rows, ob, cr = geom(gi)
            k0 = cr * 512
            F_ = Fb[gi % 2]; bF = b_F[gi % 2]
            for i in range(4):
                q0 = s * 512 + i * 128
                zb = bank(i)
                last = (c >= 2)
                S.op("pe", "matmul", [b_qt[p][s], b_kt[p][cr]], [b_Z], zb, lhsT=QT[rows, p, q0:q0 + 128],
                     rhs=KT[rows, p, k0:k0 + 512], start=True, stop=last, _mark=(last and i == 3))
                if c < 2:
                    S.op("pe", "matmul", [b_seld, b_diag], [b_Z], zb, lhsT=seld[:, s, c, :], rhs=diag[:, i, :],
                         start=False, stop=False, _mark=False)
                    S.op("pe", "matmul", [b_seln, b_negrow], [b_Z], zb, lhsT=seln[0:1, s, c, :],
                         rhs=negrow[0:1, :], start=False, stop=True, _mark=(i == 3))
            S.op("act", "activation", [b_Z], [bF], out=F_[:, :, 1:513],
                 in_=bank(0, 4).rearrange("p (a b) -> p a b", a=4), func=AF.Sigmoid, scale=-0.125)

        def stage2(gi):
            s, h, c, n, p, rows, ob, cr = geom(gi)
            F_ = Fb[gi % 2]; bF = b_F[gi % 2]
            W_ = Wb[gi % 2]; bW = b_W[gi % 2]
            if c == 0:
                S.op("dve", "memset", [], [b_D1], D1[:, :, 0:1], 1.0)
            else:
                S.op("dve", "tensor_copy", [b_P], [b_D1], out=D1[:, :, 0:1], in_=Pb[:, :, 512:513])
            S.op("dve", "tensor_tensor_scan", [bF, b_D1], [b_P], out=Pb.rearrange("p a b -> p (a b)"),
                 data0=F_.rearrange("p a b -> p (a b)"), data1=D1.rearrange("p a b -> p (a b)"), initial=0.0,
                 op0=ALU.mult, op1=ALU.add)
            S.op("dve", "tensor_tensor", [b_P], [bW], out=W_, in0=Pb[:, :, 0:512], in1=Pb[:, :, 1:513],
                 op=ALU.subtract)

        def stage3(gi):
            s, h, c, n, p, rows, ob, cr = geom(gi)
            W_ = Wb[gi % 2]; bW = b_W[gi % 2]
            WT_ = WTb[gi % 2]; bWT = b_WT[gi % 2]
            for m in range(4):
                for i in range(4):
                    S.op("pe", "transpose", [bW, b_ident], [b_WTps], out=WTps[:, m, i * 128:(i + 1) * 128],
                         in_=W_[:, i, m * 128:(m + 1) * 128], identity=ident, _mark=(m == 3 and i == 3))
            S.op("act", "activation", [b_WTps], [bWT], out=WT_.rearrange("p a b -> p (a b)"),
                 in_=bank_bf(4, 2), func=AF.Copy)

        def stage4(gi):
            s, h, c, n, p, rows, ob, cr = geom(gi)
            WT_ = WTb[gi % 2]; bWT = b_WT[gi % 2]
            psO = psum[rows, ob * 512:(ob + 1) * 512]
            for m in range(4):
                first = (c == 0 and m == 0)
                lastpv = (c == n - 1 and m == 3)
                S.op("pe", "matmul", [bWT, b_v[cr]], [b_ps[ob]], psO, lhsT=V[:, cr * 4 + m, h * 64:(h + 1) * 64],
                     rhs=WT_[:, m, :], start=first, stop=lastpv, _mark=(m == 3))
            if c == n - 1:
                S.op("act", "activation", [b_ps[ob]], [b_oat[p][s]], out=OAT[rows, p, s * 512:(s + 1) * 512],
                     in_=psO, func=AF.Copy)

        for step in range(G + 3):
            if step < G:
                stage1(step)
            if 0 <= step - 1 < G:
                stage2(step - 1)
            if 0 <= step - 2 < G:
                stage3(step - 2)
            if 0 <= step - 3 < G:
                stage4(step - 3)
        S.barrier()

        hTo = R(DYN + 0, [128, 8, NOWN + 128], BF16); b_hTo = [Buf("hTo%d" % t) for t in range(17)]
        YBT = R(DYN + 34816, [128, 4, NOWN], BF16); b_ybt = [Buf("ybt%d" % j) for j in range(4)]
        xring2 = [R(DYN + 51200 + i * 4096, [128, D], F32) for i in range(2)]
        xn2 = [R(DYN + 59392 + i * 2048, [128, D], BF16) for i in range(2)]
        junk2 = R(DYN + 63488, [128, D], BF16)
        u_sb = R(DYN + 65536, [128, 512], F32); b_usb = Buf("usb")
        acc = R(DYN + 67584, [128, 512], F32); b_acc = Buf("acc")
        cub = R(DYN + 69632, [128, 2, 516], F32); b_cub = [Buf("cub0"), Buf("cub1")]
        MRG = R(DYN + 98304, [128, 8, NOWN], BF16)
        b_mrg = [[Buf("mrg%d_%d" % (c, s)) for s in range(4)] for c in range(8)]
        sgt = [R(DYN + 131072 + i * 2048, [128, 512], F32) for i in range(4)]; b_sgt = [Buf("sg%d" % i) for i in range(4)]
        b_x2 = [Buf("x2r%d" % i) for i in range(2)]; b_xn2 = [Buf("xn2_%d" % i) for i in range(2)]; b_junk2 = Buf("junk2")

        for tt in range(17):
            i = tt % 2
            S.dma("sp", [(xring2[i], xown[tt * 128:(tt + 1) * 128, :], {})], writes=[b_x2[i]])
            norm_tile(xring2[i], b_x2[i], 0, xn2[i], b_xn2[i], junk2, b_junk2)
            transpose_tile(xn2[i], b_xn2[i], hTo[:, :, tt * 128:(tt + 1) * 128], b_hTo[tt], (6, 7))

        wu, b_wu = WS.get(i_wu)
        wgb, b_wgb = WS.get(i_wgb)
        wgc, b_wgc = WS.get(i_wgc)
        cb = 0
        for j in range(4):
            jc = slice(j * 128, (j + 1) * 128)
            mm_group(bank(0)[:, 0:128], [(wu[:, kc, jc], hTo[:, kc, NOWN:NOWN + 128]) for kc in range(8)],
                     b_ps[0], [b_wu, b_hTo[16]])
            mm_group(bank(1)[:, 0:128], [(wgc[:, kc, jc], hTo[:, kc, NOWN:NOWN + 128]) for kc in range(8)],
                     b_ps[1], [b_wgc, b_hTo[16]])
            S.op("act", "activation", [b_ps[0]], [b_usb], out=u_sb[:, 0:8], in_=bank(0)[:, 0:8], func=AF.Copy)
            S.op("dve", "tensor_tensor", [b_ps[1], b_usb], [b_cuh], out=cuh, in0=bank(1)[:, 0:8], in1=u_sb[:, 0:8],
                 op=ALU.mult)
            for s in range(4):
                hs = slice(s * 512, (s + 1) * 512)
                rd = [b_hTo[s * 4 + t] for t in range(4)]
                pb = 2 + (cb % 2) * 3
                cu = cub[:, cb % 2, :]; bcu = b_cub[cb % 2]
                cb += 1
                mm_group(bank(pb), [(wu[:, kc, jc], hTo[:, kc, hs]) for kc in range(8)], b_ps[pb], [b_wu] + rd)
                mm_group(bank(pb + 1), [(wgc[:, kc, jc], hTo[:, kc, hs]) for kc in range(8)], b_ps[pb + 1], [b_wgc] + rd)
                mm_group(bank(pb + 2), [(wgb[:, kc, jc], hTo[:, kc, hs]) for kc in range(8)], b_ps[pb + 2], [b_wgb] + rd)
                S.op("act", "activation", [b_ps[pb]], [b_usb], out=u_sb, in_=bank(pb), func=AF.Copy)
                S.op("dve", "tensor_copy", [b_cuh], [bcu], out=cu[:, 0:2], in_=cuh[:, 2 * s:2 * s + 2])
                S.op("dve", "tensor_tensor", [b_ps[pb + 1], b_usb], [bcu], out=cu[:, 2:514], in0=bank(pb + 1), in1=u_sb,
                     op=ALU.mult)
                S.op("dve", "tensor_scalar", [bcu, b_convw], [b_acc], out=acc, in0=cu[:, 2:514],
                     scalar1=convw[:, j, 2:3], scalar2=None, op0=ALU.mult)
                S.op("dve", "scalar_tensor_tensor", [bcu, b_convw, b_acc], [b_acc], out=acc, in0=cu[:, 1:513],
                     scalar=convw[:, j, 1:2], in1=acc, op0=ALU.mult, op1=ALU.add)
                S.op("dve", "scalar_tensor_tensor", [bcu, b_convw, b_acc], [b_acc], out=acc, in0=cu[:, 0:512],
                     scalar=convw[:, j, 0:1], in1=acc, op0=ALU.mult, op1=ALU.add)
                S.op("dve", "tensor_tensor", [b_ps[pb + 2], b_acc], [b_ybt[j]], out=YBT[:, j, hs], in0=bank(pb + 2),
                     in1=acc, op=ALU.mult)
        WS.release(i_wu); WS.release(i_wgb); WS.release(i_wgc)

        it = 0
        for c4 in range(2):
            ia, iga, igb = i_d[c4]
            wab, b_wab = WS.get(ia)
            wga, b_wga = WS.get(iga)
            wgB, b_wgB = WS.get(igb)
            for cc in range(4):
                c = c4 * 4 + cc
                ccs = slice(cc * 128, (cc + 1) * 128)
                for s in range(4):
                    hs = slice(s * 512, (s + 1) * 512)
                    rd = [b_hTo[s * 4 + t] for t in range(4)]
                    pb = (it % 2) * 4
                    sa = sgt[(it % 2) * 2]; bsa = b_sgt[(it % 2) * 2]
                    sb = sgt[(it % 2) * 2 + 1]; bsb = b_sgt[(it % 2) * 2 + 1]
                    it += 1
                    mm_group(bank(pb), [(wab[:, kc, ccs], OAT[:, kc, hs]) for kc in range(4)], b_ps[pb],
                             [b_wab] + [b_oat[kc][s] for kc in range(4)])
                    mm_group(bank(pb + 1), [(wab[:, 4 + kc, ccs], YBT[:, kc, hs]) for kc in range(4)], b_ps[pb + 1],
                             [b_wab] + b_ybt)
                    mm_group(bank(pb + 2), [(wga[:, kc, ccs], hTo[:, kc, hs]) for kc in range(8)], b_ps[pb + 2], [b_wga] + rd)
                    mm_group(bank(pb + 3), [(wgB[:, kc, ccs], hTo[:, kc, hs]) for kc in range(8)], b_ps[pb + 3], [b_wgB] + rd)
                    S.op("act", "activation", [b_ps[pb + 2]], [bsa], out=sa, in_=bank(pb + 2), func=AF.Sigmoid)
                    S.op("act", "activation", [b_ps[pb + 3]], [bsb], out=sb, in_=bank(pb + 3), func=AF.Sigmoid)
                    S.op("dve", "tensor_tensor", [b_ps[pb], bsa], [bsa], out=sa, in0=bank(pb), in1=sa, op=ALU.mult)
                    S.op("dve", "tensor_tensor", [b_ps[pb + 1], bsb], [bsb], out=sb, in0=bank(pb + 1), in1=sb, op=ALU.mult)
                    S.op("dve", "tensor_tensor", [bsa, bsb], [b_mrg[c][s]], out=MRG[:, c, hs], in0=sa, in1=sb, op=ALU.add)
            WS.release(ia); WS.release(iga); WS.release(igb)
        S.barrier()

        X = R(DYN + 0, [128, 16, D], F32); b_X = [Buf("X%d" % t) for t in range(16)]
        for t in range(16):
            S.dma("sp", [(X[:, t, :], xown[t * 128:(t + 1) * 128, :], {})], writes=[b_X[t]])
        rb = [0]
        for hf in range(2):
            wm, b_wm = WS.get(i_wmo[hf])
            for t in range(16):
                bk = rb[0] % 8; rb[0] += 1
                mm_group(bank(bk), [(MRG[:, kc, t * 128:(t + 1) * 128], wm[:, kc, :]) for kc in range(8)], b_ps[bk],
                         [b_wm] + [b_mrg[kc][t // 4] for kc in range(8)])
                xs = X[:, t, hf * 512:(hf + 1) * 512]
                S.op("dve", "tensor_tensor", [b_ps[bk], b_X[t]], [b_X[t]], out=xs, in0=bank(bk), in1=xs, op=ALU.add)
            WS.release(i_wmo[hf])
        S.barrier()

        hTs = [R(DYN + 65536 + i * 8192, [128, 8, 512], BF16) for i in range(2)]
        b_hTs = [[Buf("hTs%d_%d" % (i, t)) for t in range(4)] for i in range(2)]
        qmT = R(DYN + 81920, [128, 8, 512], BF16); b_qmT = [Buf("qmT%d" % c) for c in range(8)]
        omT = R(DYN + 90112, [128, 8, 512], BF16); b_omT = [Buf("omT%d" % c) for c in range(8)]
        mT = R(DYN + 98304, [128, 8, 256], BF16); b_mT = [Buf("mT0"), Buf("mT1")]
        KmT = R(DYN + 102400, [128, 8, 256], BF16); b_KmT = Buf("KmT")
        Vm = R(DYN + 106496, [128, 2, D], BF16); b_Vm = Buf("Vm")
        Esm = R(DYN + 110592, [128, 4, 256], F32); b_Esm = Buf("Esm")
        probs = [R(DYN + 114688 + i * 2048, [128, 4, 256], BF16) for i in range(2)]; b_probs = [Buf("pr0"), Buf("pr1")]
        pT = R(DYN + 118784, [128, 8, 512], BF16); b_pT = Buf("pT")
        xn3 = [R(DYN + 126976 + i * 2048, [128, D], BF16) for i in range(2)]; b_xn3 = [Buf("xn3_0"), Buf("xn3_1")]
        junk3 = R(DYN + 131072, [128, D], BF16); b_junk3 = Buf("junk3")
        memring = [R(DYN + 118784 + i * 4096, [128, D], F32) for i in range(2)]; b_mr = [Buf("mr0"), Buf("mr1")]

        load_gain(1, norm_mem_kv)
        load_gain(0, norm_mem_q)
        for mt in range(2):
            S.dma("sp", [(memring[mt], memb[mt * 128:(mt + 1) * 128, :], {})], writes=[b_mr[mt]])
            norm_tile(memring[mt], b_mr[mt], 1, xn3[mt], b_xn3[mt], junk3, b_junk3)
            transpose_tile(xn3[mt], b_xn3[mt], mT[:, :, mt * 128:(mt + 1) * 128], b_mT[mt], (4, 5))
        wK = [WS.get(i_wkvK[k]) for k in range(2)]
        for c in range(8):
            bk = 6 + c % 2
            w_, bw_ = wK[c // 4]
            mm_group(bank(bk)[:, 0:256], [(w_[:, kc, (c % 4) * 128:(c % 4 + 1) * 128], mT[:, kc, :]) for kc in range(8)],
                     b_ps[bk], [bw_] + b_mT)
            evac(KmT[:, c, :], bank(bk)[:, 0:256], [b_ps[bk]], [b_KmT])
        WS.release(i_wkvK[0]); WS.release(i_wkvK[1])
        wVv = [WS.get(i_wkvV[k]) for k in range(2)]
        for mt in range(2):
            for hf in range(2):
                bk = 6 + hf
                w_, bw_ = wVv[hf]
                mm_group(bank(bk), [(mT[:, kc, mt * 128:(mt + 1) * 128], w_[:, kc, :]) for kc in range(8)],
                         b_ps[bk], [bw_, b_mT[mt]])
                evac(Vm[:, mt, hf * 512:(hf + 1) * 512], bank(bk), [b_ps[bk]], [b_Vm])
        WS.release(i_wkvV[0]); WS.release(i_wkvV[1])
        S.barrier()

        wMQ = [WS.get(i_wmq[k]) for k in range(2)]
        wMO = [WS.get(i_wmo2[k]) for k in range(2)]
        sc_it = 0
        gb = [0]
        for s in range(4):
            hb = hTs[s % 2]; bh = b_hTs[s % 2]
            for t in range(4):
                tt = s * 4 + t
                i = tt % 2
                norm_tile(X[:, tt, :], b_X[tt], 0, xn3[i], b_xn3[i], junk3, b_junk3)
                transpose_tile(xn3[i], b_xn3[i], hb[:, :, t * 128:(t + 1) * 128], bh[t], (4, 5))
            for c in range(8):
                bk = 6 + gb[0] % 2; gb[0] += 1
                w_, bw_ = wMQ[c // 4]
                mm_group(bank(bk), [(w_[:, kc, (c % 4) * 128:(c % 4 + 1) * 128], hb[:, kc, :]) for kc in range(8)],
                         b_ps[bk], [bw_] + bh)
                evac(qmT[:, c, :], bank(bk), [b_ps[bk]], [b_qmT[c]])
            for t in range(4):
                sb0 = (sc_it % 2) * 2
                pr = probs[sc_it % 2]; bpr = b_probs[sc_it % 2]
                tb_ = 4 + (sc_it % 2)
                sc_it += 1
                psS = bank(sb0, 2).rearrange("p (a b) -> p a b", a=4)
                b_S = b_ps[sb0]
                for h in range(4):
                    for cc in range(2):
                        c = 2 * h + cc
                        S.op("pe", "matmul", [b_qmT[c], b_KmT], [b_S], psS[:, h, :], lhsT=qmT[:, c, t * 128:(t + 1) * 128],
                             rhs=KmT[:, c, :], start=(cc == 0), stop=(cc == 1), _mark=(h == 3 and cc == 1))
                S.op("dve", "tensor_reduce", [b_S], [b_mx], out=mx4, in_=psS, axis=AX.X, op=ALU.max)
                S.op("dve", "tensor_scalar", [b_mx], [b_nb], out=nb4, in0=mx4, scalar1=-1.0 / 16, scalar2=None, op0=ALU.mult)
                for h in range(4):
                    S.op("act", "activation", [b_S, b_nb], [b_Esm, b_sm], out=Esm[:, h, :], in_=psS[:, h, :], func=AF.Exp,
                         scale=1.0 / 16, bias=nb4[:, h:h + 1], accum_out=sm4[:, h:h + 1])
                S.op("dve", "reciprocal", [b_sm], [b_rs], out=rs4, in_=sm4)
                for h in range(4):
                    S.op("dve", "tensor_scalar", [b_Esm, b_rs], [bpr], out=pr[:, h, :], in0=Esm[:, h, :],
                         scalar1=rs4[:, h:h + 1], scalar2=None, op0=ALU.mult)
                pst = bank_bf(tb_)
                for h in range(4):
                    for mt in range(2):
                        k8 = h * 2 + mt
                        S.op("pe", "transpose", [bpr, b_ident], [b_ps[tb_]], out=pst[:, k8 * 128:(k8 + 1) * 128],
                             in_=pr[:, h, mt * 128:(mt + 1) * 128], identity=ident, _mark=(k8 == 7))
                S.op("act", "activation", [b_ps[tb_]], [b_pT], out=pT[:, :, t * 128:(t + 1) * 128],
                     in_=pst.rearrange("p (a b) -> p a b", a=8), func=AF.Copy)
            for h in range(4):
                for dc in range(2):
                    c = h * 2 + dc
                    bk = 6 + gb[0] % 2; gb[0] += 1
                    mm_group(bank(bk), [(Vm[:, mt, c * 128:(c + 1) * 128], pT[:, h * 2 + mt, :]) for mt in range(2)],
                             b_ps[bk], [b_Vm, b_pT])
                    evac(omT[:, c, :], bank(bk), [b_ps[bk]], [b_omT[c]])
            for t in range(4):
                tt = s * 4 + t
                for hf in range(2):
                    bk = 6 + gb[0] % 2; gb[0] += 1
                    w_, bw_ = wMO[hf]
                    mm_group(bank(bk), [(omT[:, kc, t * 128:(t + 1) * 128], w_[:, kc, :]) for kc in range(8)],
                             b_ps[bk], [bw_] + b_omT)
                    xs = X[:, tt, hf * 512:(hf + 1) * 512]
                    S.op("dve", "tensor_tensor", [b_ps[bk], b_X[tt]], [b_X[tt]], out=xs, in0=bank(bk), in1=xs, op=ALU.add)
        for k in range(2):
            WS.release(i_wmq[k]); WS.release(i_wmo2[k])
        S.barrier()

        hTb = R(DYN + 65536, [128, 8, 1024], BF16); b_hTb = [Buf("hTb%d" % t) for t in range(8)]
        aT = R(DYN + 81920, [128, 22, 1024], BF16); b_aT = [Buf("aT%d" % j) for j in range(22)]
        sgl = [R(DYN + 126976 + i * 2048, [128, 512], F32) for i in range(2)]; b_sgl = [Buf("sgl0"), Buf("sgl1")]
        xn4 = [R(DYN + 131072 + i * 2048, [128, D], BF16) for i in range(2)]; b_xn4 = [Buf("xn4_0"), Buf("xn4_1")]
        junk4 = R(DYN + 135168, [128, D], BF16); b_junk4 = Buf("junk4")
        load_gain(1, norm_ffn)
        fit = 0
        for tb in range(2):
            gu, fo = i_ffn[tb]
            for t in range(8):
                tt = tb * 8 + t
                i = tt % 2
                norm_tile(X[:, tt, :], b_X[tt], 1, xn4[i], b_xn4[i], junk4, b_junk4)
                transpose_tile(xn4[i], b_xn4[i], hTb[:, :, t * 128:(t + 1) * 128], b_hTb[t], (6, 7))
            for blk in range(6):
                ig, iu, ncol = gu[blk]
                wg_, b_wg = WS.get(ig)
                wu_, b_wu_ = WS.get(iu)
                for cc in range(ncol // 128):
                    j = blk * 4 + cc
                    ccs = slice(cc * 128, (cc + 1) * 128)
                    for s2 in range(2):
                        hs = slice(s2 * 512, (s2 + 1) * 512)
                        rd = b_hTb[s2 * 4:(s2 + 1) * 4]
                        pb = (fit % 3) * 2
                        sg_ = sgl[fit % 2]; bsg = b_sgl[fit % 2]
                        fit += 1
                        mm_group(bank(pb), [(wg_[:, kc, ccs], hTb[:, kc, hs]) for kc in range(8)], b_ps[pb], [b_wg] + rd)
                        mm_group(bank(pb + 1), [(wu_[:, kc, ccs], hTb[:, kc, hs]) for kc in range(8)], b_ps[pb + 1],
                                 [b_wu_] + rd)
                        S.op("act", "activation", [b_ps[pb]], [bsg], out=sg_, in_=bank(pb), func=AF.Silu)
                        S.op("dve", "tensor_tensor", [b_ps[pb + 1], bsg], [b_aT[j]], out=aT[:, j, hs], in0=bank(pb + 1),
                             in1=sg_, op=ALU.mult)
                WS.release(ig); WS.release(iu)
            for hf in range(2):
                wfo = [WS.get(fo[hf][k3]) for k3 in range(3)]
                for t in range(8):
                    tt = tb * 8 + t
                    bk = 6 + (fit % 2); fit += 1
                    mm_group(bank(bk), [(aT[:, j, t * 128:(t + 1) * 128], wfo[j // 8][0][:, j % 8, :]) for j in range(22)],
                             b_ps[bk], [w[1] for w in wfo] + b_aT)
                    xs = X[:, tt, hf * 512:(hf + 1) * 512]
                    S.op("dve", "tensor_tensor", [b_ps[bk], b_X[tt]], [b_X[tt]], out=xs, in0=bank(bk), in1=xs, op=ALU.add)
                for k3 in range(3):
                    WS.release(fo[hf][k3])
        S.barrier()

        otmp = [R(DYN + 65536 + i * 4096, [128, D], F32) for i in range(2)]; b_ot = [Buf("ot0"), Buf("ot1")]
        junk5 = R(DYN + 73728, [128, D], BF16); b_junk5 = Buf("junk5")
        load_gain(0, norm_final)
        out_evs = []
        for tt in range(16):
            i = tt % 2
            norm_tile(X[:, tt, :], b_X[tt], 0, otmp[i], b_ot[i], junk5, b_junk5)
            out_evs.append(S.dma("sp", [(out_d[tt * 128:(tt + 1) * 128, :], otmp[i], {})], reads=[b_ot[i]]))
        for ev in out_evs:
            S._wait("sp", ev)
        print("inst counts", S.n_inst, {e: len(S.prog[e]) for e in S.ENGS})
        S.emit()
    return nc


def _host_inputs(inputs):
    x = np.asarray(inputs["x"], dtype=np.float32)
    mem = np.asarray(inputs["mem"], dtype=np.float32)
    diag = np.zeros((128, 4, 512), np.float32)
    pp = np.arange(128)[:, None]
    kr = np.arange(512)[None, :]
    for i in range(4):
        ql = i * 128 + pp
        diag[:, i, :] = np.where(kr > 511 - ql, 0.0, NEG)
    eye = np.eye(128, dtype=np.float32)
    shared = {
        "diag": diag,
        "norm_mix": np.ascontiguousarray(inputs["norm_mix"][0:1]),
        "w_in": np.ascontiguousarray(inputs["w_in"][0]),
        "conv_w": np.ascontiguousarray(inputs["conv_w"][0]),
        "w_branch_a": np.ascontiguousarray(inputs["w_branch_a"][0]),
        "w_branch_b": np.ascontiguousarray(inputs["w_branch_b"][0]),
        "w_mix_out": np.ascontiguousarray(inputs["w_mix_out"][0]),
        "norm_mem_q": np.ascontiguousarray(inputs["norm_mem_q"][0:1]),
        "norm_mem_kv": np.ascontiguousarray(inputs["norm_mem_kv"][0:1]),
        "w_mem_q": np.ascontiguousarray(inputs["w_mem_q"][0]),
        "w_mem_kv": np.ascontiguousarray(inputs["w_mem_kv"][0]),
        "w_mem_o": np.ascontiguousarray(inputs["w_mem_o"][0]),
        "norm_ffn": np.ascontiguousarray(inputs["norm_ffn"][0:1]),
        "w_ffn_in": np.ascontiguousarray(inputs["w_ffn_in"][0]),
        "w_ffn_out": np.ascontiguousarray(inputs["w_ffn_out"][0]),
        "norm_final": np.ascontiguousarray(np.asarray(inputs["norm_final"]).reshape(1, D)),
    }
    shared = {k: np.asarray(v, dtype=np.float32) for k, v in shared.items()}
    in_maps = []
    for core in range(8):
        b, par = core // 2, core % 2
        tiles = T_OF[par]
        xown = np.zeros((NOWN + 128, D), np.float32)
        seld = np.zeros((128, 4, 2, 128), np.float32)
        seln = np.zeros((1, 4, 2, 128), np.float32)
        for j, T in enumerate(tiles):
            xown[j * 512:(j + 1) * 512] = x[b, T * 512:(T + 1) * 512]
            if T > 0:
                xown[NOWN + 2 * j:NOWN + 2 * j + 2] = x[b, T * 512 - 2:T * 512]
            tmax = NCH[j] - 1
            if T == tmax:
                seld[:, j, 0, :] = eye
            else:
                seln[0, j, 0, :] = 1.0
                seld[:, j, 1, :] = eye
        m = dict(shared)
        m["xall"] = np.ascontiguousarray(x[b, ::-1, :])
        m["xown"] = xown
        m["memb"] = np.ascontiguousarray(mem[b])
        m["seld"] = seld
        m["seln"] = seln
        in_maps.append(m)
    return in_maps


_NC_CACHE = {}


def kernel(**inputs):
    in_maps = _host_inputs(inputs)
    if "nc" not in _NC_CACHE:
        _NC_CACHE["nc"] = build_nc()
    nc = _NC_CACHE["nc"]
    res = run_bass_kernel_spmd(nc, in_maps, core_ids=list(range(8)))
    out = np.zeros((NB, SEQ, D), np.float32)
    for core in range(8):
        b, par = core // 2, core % 2
        o = np.asarray(res.results[core]["out"], dtype=np.float32)
        for j, T in enumerate(T_OF[par]):
            out[b, T * 512:(T + 1) * 512] = o[j * 512:(j + 1) * 512]
    return out
```

```python
import numpy as np
import concourse.bass as bass
import concourse.mybir as mybir
from concourse.bass_utils import run_bass_kernel_spmd
from contextlib import ExitStack

F32 = mybir.dt.float32
BF16 = mybir.dt.bfloat16
AF = mybir.ActivationFunctionType
ALU = mybir.AluOpType
AX = mybir.AxisListType

D = 1024
SEQ = 4096
NB = 4
TS = 512
T_OF = {0: (0, 3, 4, 7), 1: (1, 2, 5, 6)}
NCH = (2, 4, 6, 8)
NEG = -30000.0
EPS = 1e-6
FFN_H = 2816
NOWN = 2048
SAME_ENGINE_SYNC = True


class Buf:
    __slots__ = ("name", "w", "r")

    def __init__(self, name):
        self.name = name
        self.w = None
        self.r = {}


class Sched:
    ENGS = ("pe", "act", "dve", "pool", "sp")

    def __init__(self, nc, stack, n_dma_sems=8):
        self.nc = nc
        self.prog = {e: [] for e in self.ENGS}
        self.count = {e: 0 for e in self.ENGS}
        self.sem = {e: stack.enter_context(nc.semaphore("s_" + e)) for e in self.ENGS}
        self.seen = {e: {} for e in self.ENGS}
        self.dsem = {}
        self.dval = {}
        self.dring = {}
        self.dpos = {}
        idx = 0
        for q in ("sp", "pool"):
            ring = []
            for i in range(n_dma_sems):
                self.dsem[idx] = stack.enter_context(nc.semaphore("d_%s%d" % (q, i)))
                self.dval[idx] = 0
                ring.append(idx)
                idx += 1
            self.dring[q] = ring
            self.dpos[q] = 0
        self.n_inst = {e: 0 for e in self.ENGS}
        self.last_marked = {e: True for e in self.ENGS}

    def _wait(self, e, ev):
        if ev is None:
            return
        if ev[0] == "e":
            _, src, seq = ev
            if src == e and (e == "pe" or not SAME_ENGINE_SYNC):
                return
            assert self.count[src] >= seq, ("dependency on unissued mark", e, ev)
            key = ("e", src)
            val = seq
            sem = self.sem[src]
        else:
            _, sidx, val = ev
            key = ("d", sidx)
            sem = self.dsem[sidx]
        if self.seen[e].get(key, 0) >= val:
            return
        self.seen[e][key] = val
        self.prog[e].append(lambda eng, sem=sem, val=val: eng.wait_ge(sem, val))

    def _deps(self, e, reads, writes):
        for b in reads:
            self._wait(e, b.w)
        for b in writes:
            self._wait(e, b.w)
            for (k0, k1), v in list(b.r.items()):
                self._wait(e, (k0, k1, v))

    def _record(self, ev, reads, writes):
        key = (ev[0], ev[1])
        for b in reads:
            if b.r.get(key, 0) < ev[2]:
                b.r[key] = ev[2]
        for b in writes:
            b.w = ev
            b.r = {}

    def op(self, e, meth, reads, writes, *args, _mark=True, **kw):
        self._deps(e, reads, writes)
        sem = self.sem[e]
        if _mark:
            self.count[e] += 1
            self.prog[e].append(lambda eng: getattr(eng, meth)(*args, **kw).then_inc(sem, 1))
            ev = ("e", e, self.count[e])
        else:
            self.prog[e].append(lambda eng: getattr(eng, meth)(*args, **kw))
            ev = ("e", e, self.count[e] + 1)
        self.last_marked[e] = _mark
        self.n_inst[e] += 1
        self._record(ev, reads, writes)
        return ev

    def dma(self, q, xfers, reads=(), writes=()):
        self._deps(q, reads, writes)
        ring = self.dring[q]
        sidx = ring[self.dpos[q] % len(ring)]
        self.dpos[q] += 1
        if self.dval[sidx] > 0:
            self._wait(q, ("d", sidx, self.dval[sidx]))
        sem = self.dsem[sidx]
        for (o, i, kw) in xfers:
            self.dval[sidx] += 16
            self.prog[q].append(lambda eng, o=o, i=i, kw=kw: eng.dma_start(out=o, in_=i, **kw).then_inc(sem, 16))
        ev = ("d", sidx, self.dval[sidx])
        self._record(ev, reads, writes)
        return ev

    def barrier(self):
        for e in ("pe", "act", "dve"):
            assert self.last_marked[e], ("barrier with unmarked tail", e)
        for sidx in self.dring["sp"]:
            if self.dval[sidx] > 0:
                self._wait("sp", ("d", sidx, self.dval[sidx]))
        for f in ("pe", "act", "dve"):
            self._wait("sp", ("e", f, self.count[f]))
        self.count["sp"] += 1
        sem = self.sem["sp"]
        self.prog["sp"].append(lambda eng, sem=sem: eng.sem_inc(sem, 1))
        for e in ("pe", "act", "dve"):
            self._wait(e, ("e", "sp", self.count["sp"]))

    def emit(self):
        with self.nc.Block() as block:
            def mk(name):
                def body(engine):
                    for c in self.prog[name]:
                        c(engine)
                return body
            block.sync(mk("sp"))
            block.gpsimd(mk("pool"))
            block.scalar(mk("act"))
            block.vector(mk("dve"))
            block.tensor(mk("pe"))


def build_nc():
    nc = bass.Bass("TRN2", target_bir_lowering=False)

    def din(name, shape):
        return nc.dram_tensor(name, list(shape), F32, kind="ExternalInput").ap()

    xall = din("xall", [SEQ, D])
    xown = din("xown", [NOWN + 128, D])
    memb = din("memb", [256, D])
    seld_d = din("seld", [128, 4, 2, 128])
    seln_d = din("seln", [1, 4, 2, 128])
    diag_d = din("diag", [128, 4, 512])
    norm_mix = din("norm_mix", [1, D])
    w_in = din("w_in", [D, 5120])
    conv_w = din("conv_w", [3, 512])
    w_ba = din("w_branch_a", [512, D])
    w_bb = din("w_branch_b", [512, D])
    w_mix_out = din("w_mix_out", [D, D])
    norm_mem_q = din("norm_mem_q", [1, D])
    norm_mem_kv = din("norm_mem_kv", [1, D])
    w_mem_q = din("w_mem_q", [D, D])
    w_mem_kv = din("w_mem_kv", [D, 2 * D])
    w_mem_o = din("w_mem_o", [D, D])
    norm_ffn = din("norm_ffn", [1, D])
    w_ffn_in = din("w_ffn_in", [D, 2 * FFN_H])
    w_ffn_out = din("w_ffn_out", [FFN_H, D])
    norm_final = din("norm_final", [1, D])
    out_d = nc.dram_tensor("out", [NOWN, D], F32, kind="ExternalOutput").ap()

    with ExitStack() as st:
        S = Sched(nc, st)
        ARENA_F32 = 52900
        arena = st.enter_context(nc.sbuf_tensor("arena", [128, ARENA_F32], F32))
        psum = st.enter_context(nc.psum_tensor("psum", [128, 4096], F32))

        def R(off, shape, dt):
            esz = 4 if dt == F32 else 2
            n = int(np.prod(shape[1:]))
            nbytes = n * esz
            assert off % 4 == 0 and nbytes % 4 == 0, (off, shape)
            assert off + nbytes <= ARENA_F32 * 4, (off, shape)
            ap = arena[:, off // 4:(off + nbytes) // 4]
            if dt != F32:
                ap = ap.bitcast(dt)
            if len(shape) == 3:
                ap = ap.rearrange("p (a b) -> p a b", a=shape[1])
            elif len(shape) == 4:
                ap = ap.rearrange("p (a b c) -> p a b c", a=shape[1], b=shape[2])
            return ap

        def bank(i, n=1):
            return psum[:, i * 512:(i + n) * 512]

        def bank_bf(i, n=1):
            return psum[:, i * 512:(i + n) * 512].bitcast(BF16)

        b_ps = [Buf("ps%d" % i) for i in range(8)]

        ident = R(0, [128, 128], BF16); b_ident = Buf("ident")
        identf = R(256, [128, 128], F32); b_identf = Buf("identf")
        convw = R(768, [128, 4, 3], F32); b_convw = Buf("convw")
        stats = R(1024, [128, 256], F32)
        seld = R(2048, [128, 4, 2, 128], BF16); b_seld = Buf("seld")
        seln = R(4096, [128, 4, 2, 128], BF16); b_seln = Buf("seln")
        negrow = R(6144, [128, 512], BF16); b_negrow = Buf("negrow")
        zeros = R(7168, [128, 512], F32); b_zeros = Buf("zeros")
        diag = R(9216, [128, 4, 512], BF16); b_diag = Buf("diag")
        GAIN0 = 13312
        g_rep = [R(GAIN0 + i * 4096, [128, D], F32) for i in range(2)]
        b_g = [Buf("g%d" % i) for i in range(2)]
        RING0 = 21504
        NSLOT = 5
        DYN = RING0 + NSLOT * 8192

        ssq = stats[:, 0:8]; rstd = stats[:, 8:16]
        b_st = [Buf("st%d" % i) for i in range(8)]
        st_pos = [0]
        mx4 = stats[:, 16:20]; nb4 = stats[:, 20:24]; sm4 = stats[:, 24:28]; rs4 = stats[:, 28:32]
        b_mx = Buf("mx"); b_nb = Buf("nb"); b_sm = Buf("sm"); b_rs = Buf("rs")
        cuh = stats[:, 32:40]; b_cuh = Buf("cuh")

        class WStream:
            def __init__(self):
                self.blocks = []
                self.issued = 0
                self.released = set()
                self.bufs = [Buf("ring%d" % i) for i in range(NSLOT)]
                self.next_get = 0

            def add(self, pieces):
                self.blocks.append(pieces)
                return len(self.blocks) - 1

            def slot_ap(self, i):
                return R(RING0 + (i % NSLOT) * 8192, [128, 8, 512], BF16)

            def _issue(self):
                while self.issued < len(self.blocks):
                    k = self.issued
                    if k >= NSLOT and (k - NSLOT) not in self.released:
                        break
                    if k > self.next_get + NSLOT - 1:
                        break
                    sl = self.slot_ap(k)
                    xf = []
                    for (src, kc0, kcn, ncols) in self.blocks[k]:
                        xf.append((sl[:, kc0:kc0 + kcn, 0:ncols], src, {}))
                    S.dma("pool", xf, writes=[self.bufs[k % NSLOT]])
                    self.issued += 1

            def get(self, idx):
                assert idx == self.next_get, (idx, self.next_get)
                self.next_get += 1
                self._issue()
                assert self.issued > idx, ("weight block not issued", idx)
                return self.slot_ap(idx), self.bufs[idx % NSLOT]

            def release(self, idx):
                self.released.add(idx)
                self._issue()

        WS = WStream()

        def wpiece(w, r0, nrows, c0, ncols, kc0=0):
            src = w[r0:r0 + nrows, c0:c0 + ncols].rearrange("(kc p) n -> p kc n", p=128)
            return (src, kc0, nrows // 128, ncols)

        i_wk = WS.add([wpiece(w_in, 0, D, 512, 512)])
        i_wv = WS.add([wpiece(w_in, 0, D, 1024, 512)])
        i_wq = WS.add([wpiece(w_in, 0, D, 0, 512)])
        i_wu = WS.add([wpiece(w_in, 0, D, 1536, 512)])
        i_wgb = WS.add([wpiece(w_in, 0, D, 2048, 512)])
        i_wgc = WS.add([wpiece(w_in, 0, D, 2560, 512)])
        i_d = []
        for c4 in range(2):
            a = WS.add([wpiece(w_ba, 0, 512, c4 * 512, 512, kc0=0), wpiece(w_bb, 0, 512, c4 * 512, 512, kc0=4)])
            b_ = WS.add([wpiece(w_in, 0, D, 3072 + c4 * 512, 512)])
            c_ = WS.add([wpiece(w_in, 0, D, 4096 + c4 * 512, 512)])
            i_d.append((a, b_, c_))
        i_wmo = [WS.add([wpiece(w_mix_out, 0, D, hf * 512, 512)]) for hf in range(2)]
        i_wkvK = [WS.add([wpiece(w_mem_kv, 0, D, k * 512, 512)]) for k in range(2)]
        i_wkvV = [WS.add([wpiece(w_mem_kv, 0, D, D + k * 512, 512)]) for k in range(2)]
        i_wmq = [WS.add([wpiece(w_mem_q, 0, D, k * 512, 512)]) for k in range(2)]
        i_wmo2 = [WS.add([wpiece(w_mem_o, 0, D, k * 512, 512)]) for k in range(2)]
        i_ffn = []
        for tb in range(2):
            gu = []
            for blk in range(6):
                ncol = 512 if blk < 5 else 256
                ig = WS.add([wpiece(w_ffn_in, 0, D, blk * 512, ncol)])
                iu = WS.add([wpiece(w_ffn_in, 0, D, FFN_H + blk * 512, ncol)])
                gu.append((ig, iu, ncol))
            fo = []
            for hf in range(2):
                ks = []
                for k3 in range(3):
                    nr = 1024 if k3 < 2 else FFN_H - 2048
                    ks.append(WS.add([wpiece(w_ffn_out, k3 * 1024, nr, hf * 512, 512)]))
                fo.append(ks)
            i_ffn.append((gu, fo))

        evac_rr = [0]

        def evac(out_ap, in_ap, reads, writes, eng=None):
            if eng is None:
                eng = ("act", "dve")[evac_rr[0] % 2]
                evac_rr[0] += 1
            if eng == "act":
                return S.op("act", "activation", reads, writes, out=out_ap, in_=in_ap, func=AF.Copy)
            return S.op("dve", "tensor_copy", reads, writes, out=out_ap, in_=in_ap)

        def mm_group(out_ap, pairs, b_out, reads):
            n = len(pairs)
            for k, (l, r) in enumerate(pairs):
                S.op("pe", "matmul", reads, [b_out], out_ap, lhsT=l, rhs=r, start=(k == 0), stop=(k == n - 1),
                     _mark=(k == n - 1))

        def load_gain(i, src):
            S.dma("sp", [(g_rep[i], src.broadcast_to([128, D]), {})], writes=[b_g[i]])

        def norm_tile(x_ap, b_x, gi, out_ap, b_out, junk_ap, b_junk_):
            k = st_pos[0] % 8
            st_pos[0] += 1
            bs = b_st[k]
            S.op("act", "activation", [b_x], [b_junk_, bs], out=junk_ap, in_=x_ap, func=AF.Square,
                 accum_out=ssq[:, k:k + 1])
            S.op("act", "activation", [bs], [bs], out=rstd[:, k:k + 1], in_=ssq[:, k:k + 1], func=AF.Ln,
                 scale=1.0 / D, bias=EPS)
            S.op("act", "activation", [bs], [bs], out=rstd[:, k:k + 1], in_=rstd[:, k:k + 1], func=AF.Exp, scale=-0.5)
            S.op("dve", "scalar_tensor_tensor", [b_x, bs, b_g[gi]], [b_out], out=out_ap, in0=x_ap,
                 scalar=rstd[:, k:k + 1], in1=g_rep[gi], op0=ALU.mult, op1=ALU.mult)

        tr_rr = [0]

        def transpose_tile(xn_ap, b_xn_, dst3, b_dst, tbanks):
            bk = tbanks[tr_rr[0] % len(tbanks)]
            tr_rr[0] += 1
            pst = bank_bf(bk)
            for c in range(8):
                S.op("pe", "transpose", [b_xn_, b_ident], [b_ps[bk]], out=pst[:, c * 128:(c + 1) * 128],
                     in_=xn_ap[:, c * 128:(c + 1) * 128], identity=ident, _mark=(c == 7))
            S.op("act", "activation", [b_ps[bk]], [b_dst], out=dst3, in_=pst.rearrange("p (a b) -> p a b", a=8),
                 func=AF.Copy)

        def norm_pipeline(n, load_fn, gi, xn_bufs, b_xn_bufs, junk_ap, b_junk_, dst_fn, tbanks, after_fn=None):
            def pre(k):
                x_ap, b_x = load_fn(k)
                norm_tile(x_ap, b_x, gi, xn_bufs[k % 2], b_xn_bufs[k % 2], junk_ap, b_junk_)

            def post(k):
                dst3, b_dst = dst_fn(k)
                transpose_tile(xn_bufs[k % 2], b_xn_bufs[k % 2], dst3, b_dst, tbanks)
            pre(0)
            for k in range(n):
                if k + 1 < n:
                    pre(k + 1)
                post(k)
                if after_fn is not None:
                    after_fn(k)

        S.op("dve", "memset", [], [b_identf], identf, 1.0)
        S.op("pool", "affine_select", [b_identf], [b_identf], out=identf, in_=identf, pattern=[[-1, 128]],
             compare_op=ALU.is_equal, fill=0.0, base=0, channel_multiplier=1)
        S.op("dve", "tensor_copy", [b_identf], [b_ident], out=ident, in_=identf)
        S.op("dve", "memset", [], [b_negrow], negrow, NEG)
        S.op("dve", "memset", [], [b_zeros], zeros, 0.0)
        S.dma("pool", [(seld, seld_d, {})], writes=[b_seld])
        S.dma("pool", [(seln[0:1], seln_d, {})], writes=[b_seln])
        S.dma("pool", [(diag, diag_d, {})], writes=[b_diag])
        S.dma("sp", [(convw[:, j, :], conv_w[:, j * 128:(j + 1) * 128].rearrange("i p -> p i"),
                      {"allow_slow_non_contiguous": True}) for j in range(4)], writes=[b_convw])
        load_gain(0, norm_mix)

        KT = R(DYN + 0, [128, 4, SEQ], BF16)
        V = R(DYN + 32768, [128, 32, 512], BF16)
        QT = R(DYN + 65536, [128, 4, NOWN], BF16)
        OAT = R(DYN + 81920, [128, 4, NOWN], BF16)
        b_oat = [[Buf("oat%d_%d" % (p, s)) for s in range(4)] for p in range(4)]
        TMP = DYN + 98304
        xring = [R(TMP + i * 4096, [128, D], F32) for i in range(4)]; b_xr = [Buf("xr%d" % i) for i in range(4)]
        xn = [R(TMP + 16384 + i * 2048, [128, D], BF16) for i in range(2)]; b_xn = [Buf("xn%d" % i) for i in range(2)]
        junk = R(TMP + 20480, [128, D], BF16); b_junk = Buf("junk")
        hTa = [R(TMP + 22528 + i * 8192, [128, 8, 512], BF16) for i in range(2)]
        b_hTa = [[Buf("hTa%d_%d" % (i, t)) for t in range(4)] for i in range(2)]
        b_kt = [[Buf("kt%d_%d" % (p, ch)) for ch in range(8)] for p in range(4)]
        b_v = [Buf("v%d" % ch) for ch in range(8)]
        b_qt = [[Buf("qt%d_%d" % (p, s)) for s in range(4)] for p in range(4)]

        wk, b_wk = WS.get(i_wk)
        wv, b_wv = WS.get(i_wv)
        wq, b_wq = WS.get(i_wq)

        mmb = [0]

        def load_all(k):
            xt = xring[k % 4]; bx = b_xr[k % 4]
            S.dma("sp", [(xt, xall[k * 128:(k + 1) * 128, :], {})], writes=[bx])
            return xt, bx

        def dst_all(k):
            ch, t = k // 4, k % 4
            return hTa[ch % 2][:, :, t * 128:(t + 1) * 128], b_hTa[ch % 2][t]

        def after_all(k):
            if k % 4 != 3:
                return
            ch = k // 4
            hb = hTa[ch % 2]; bh = b_hTa[ch % 2]
            for p in range(4):
                bk = mmb[0] % 6; mmb[0] += 1
                mm_group(bank(bk), [(wk[:, kc, p * 128:(p + 1) * 128], hb[:, kc, :]) for kc in range(8)],
                         b_ps[bk], [b_wk] + bh)
                evac(KT[:, p, ch * 512:(ch + 1) * 512], bank(bk), [b_ps[bk]], [b_kt[p][ch]])
            for t in range(4):
                bk = mmb[0] % 6; mmb[0] += 1
                mm_group(bank(bk), [(hb[:, kc, t * 128:(t + 1) * 128], wv[:, kc, :]) for kc in range(8)],
                         b_ps[bk], [b_wv, bh[t]])
                evac(V[:, ch * 4 + t, :], bank(bk), [b_ps[bk]], [b_v[ch]])

        norm_pipeline(32, load_all, 0, xn, b_xn, junk, b_junk, dst_all, (6, 7), after_all)
        WS.release(i_wk); WS.release(i_wv)

        def load_own(k):
            xt = xring[k % 4]; bx = b_xr[k % 4]
            S.dma("sp", [(xt, xown[k * 128:(k + 1) * 128, :], {})], writes=[bx])
            return xt, bx

        def after_own(k):
            if k % 4 != 3:
                return
            s_ = k // 4
            hb = hTa[s_ % 2]; bh = b_hTa[s_ % 2]
            for p in range(4):
                bk = mmb[0] % 6; mmb[0] += 1
                mm_group(bank(bk), [(wq[:, kc, p * 128:(p + 1) * 128], hb[:, kc, :]) for kc in range(8)],
                         b_ps[bk], [b_wq] + bh)
                evac(QT[:, p, s_ * 512:(s_ + 1) * 512], bank(bk), [b_ps[bk]], [b_qt[p][s_]])

        norm_pipeline(16, load_own, 0, xn, b_xn, junk, b_junk, dst_all, (6, 7), after_own)
        WS.release(i_wq)
        S.barrier()

        Fb = [R(TMP + i * 8208, [128, 4, 513], F32) for i in range(2)]; b_F = [Buf("F%d" % i) for i in range(2)]
        D1 = R(TMP + 16416, [128, 4, 513], F32); b_D1 = Buf("D1")
        Pb = R(TMP + 24624, [128, 4, 513], F32); b_P = Buf("P")
        Wb = [R(TMP + 32832 + i * 4096, [128, 4, 512], BF16) for i in range(2)]; b_W = [Buf("W%d" % i) for i in range(2)]
        WTb = [R(TMP + 41024 + i * 4096, [128, 4, 512], BF16) for i in range(2)]; b_WT = [Buf("WT%d" % i) for i in range(2)]
        b_Z = Buf("Z")
        b_WTps = Buf("WTps")
        WTps = bank_bf(4, 2).rearrange("p (a b) -> p a b", a=4)
        for i in range(2):
            S.op("dve", "memset", [], [b_F[i]], Fb[i].rearrange("p a b -> p (a b)"), 0.0)
        S.op("dve", "memset", [], [b_D1], D1.rearrange("p a b -> p (a b)"), 0.0)
        groups = [(s, h, c) for s in range(4) for h in range(8) for c in range(NCH[s])]
        G = len(groups)

        def geom(gi):
            s, h, c = groups[gi]
            n = NCH[s]
            p = h // 2
            rows = slice(0, 64) if h % 2 == 0 else slice(64, 128)
            ob = 6 + (h % 2)
            cr = 8 - n + c
            return s, h, c, n, p, rows, ob, cr

        def stage1(gi):
            s, h, c, n, p, rows, ob, cr = geom(gi)
            k0 = cr * 512
            F_ = Fb[gi % 2]; bF = b_F[gi % 2]
            for i in range(4):
                q0 = s * 512 + i * 128
                zb = bank(i)
                last = (c >= 2)
                S.op("pe", "matmul", [b_qt[p][s], b_kt[p][cr]], [b_Z], zb, lhsT=QT[rows, p, q0:q0 + 128],
                     rhs=KT[rows, p, k0:k0 + 512], start=True, stop=last, _mark=(last and i == 3))
                if c < 2:
                    S.op("pe", "matmul", [b_seld, b_diag], [b_Z], zb, lhsT=seld[:, s, c, :], rhs=diag[:, i, :],
                         start=False, stop=False, _mark=False)
                    S.op("pe", "matmul", [b_seln, b_negrow], [b_Z], zb, lhsT=seln[0:1, s, c, :],
                         rhs=negrow[0:1, :], start=False, stop=True, _mark=(i == 3))
            S.op("act", "activation", [b_Z], [bF], out=F_[:, :, 1:513],
                 in_=bank(0, 4).rearrange("p (a b) -> p a b", a=4), func=AF.Sigmoid, scale=-0.125)

        def stage2(gi):
            s, h, c, n, p, rows, ob, cr = geom(gi)
            F_ = Fb[gi % 2]; bF = b_F[gi % 2]
            W_ = Wb[gi % 2]; bW = b_W[gi % 2]
            if c == 0:
                S.op("dve", "memset", [], [b_D1], D1[:, :, 0:1], 1.0)
            else:
                S.op("dve", "tensor_copy", [b_P], [b_D1], out=D1[:, :, 0:1], in_=Pb[:, :, 512:513])
            S.op("dve", "tensor_tensor_scan", [bF, b_D1], [b_P], out=Pb.rearrange("p a b -> p (a b)"),
                 data0=F_.rearrange("p a b -> p (a b)"), data1=D1.rearrange("p a b -> p (a b)"), initial=0.0,
                 op0=ALU.mult, op1=ALU.add)
            S.op("dve", "tensor_tensor", [b_P], [bW], out=W_, in0=Pb[:, :, 0:512], in1=Pb[:, :, 1:513],
                 op=ALU.subtract)

        def stage3(gi):
            s, h, c, n, p, rows, ob, cr = geom(gi)
            W_ = Wb[gi % 2]; bW = b_W[gi % 2]
            WT_ = WTb[gi % 2]; bWT = b_WT[gi % 2]
            for m in range(4):
                for i in range(4):
                    S.op("pe", "transpose", [bW, b_ident], [b_WTps], out=WTps[:, m, i * 128:(i + 1) * 128],
                         in_=W_[:, i, m * 128:(m + 1) * 128], identity=ident, _mark=(m == 3 and i == 3))
            S.op("act", "activation", [b_WTps], [bWT], out=WT_.rearrange("p a b -> p (a b)"),
                 in_=bank_bf(4, 2), func=AF.Copy)

        def stage4(gi):
            s, h, c, n, p, rows, ob, cr = geom(gi)
            WT_ = WTb[gi % 2]; bWT = b_WT[gi % 2]
            psO = psum[rows, ob * 512:(ob + 1) * 512]
            for m in range(4):
                first = (c == 0 and m == 0)
                lastpv = (c == n - 1 and m == 3)
                S.op("pe", "matmul", [bWT, b_v[cr]], [b_ps[ob]], psO, lhsT=V[:, cr * 4 + m, h * 64:(h + 1) * 64],
                     rhs=WT_[:, m, :], start=first, stop=lastpv, _mark=(m == 3))
            if c == n - 1:
                S.op("act", "activation", [b_ps[ob]], [b_oat[p][s]], out=OAT[rows, p, s * 512:(s + 1) * 512],
                     in_=psO, func=AF.Copy)

        for step in range(G + 3):
            if step < G:
                stage1(step)
            if 0 <= step - 1 < G:
                stage2(step - 1)
            if 0 <= step - 2 < G:
                stage3(step - 2)
            if 0 <= step - 3 < G:
                stage4(step - 3)
        S.barrier()

        hTo = R(DYN + 0, [128, 8, NOWN + 128], BF16); b_hTo = [Buf("hTo%d" % t) for t in range(17)]
        YBT = R(DYN + 34816, [128, 4, NOWN], BF16); b_ybt = [Buf("ybt%d" % j) for j in range(4)]
        xring2 = [R(DYN + 51200 + i * 4096, [128, D], F32) for i in range(2)]
        xn2 = [R(DYN + 59392 + i * 2048, [128, D], BF16) for i in range(2)]
        junk2 = R(DYN + 63488, [128, D], BF16)
        u_sb = R(DYN + 65536, [128, 512], F32); b_usb = Buf("usb")
        acc = R(DYN + 67584, [128, 512], F32); b_acc = Buf("acc")
        cub = R(DYN + 69632, [128, 2, 516], F32); b_cub = [Buf("cub0"), Buf("cub1")]
        MRG = R(DYN + 98304, [128, 8, NOWN], BF16)
        b_mrg = [[Buf("mrg%d_%d" % (c, s)) for s in range(4)] for c in range(8)]
        sgt = [R(DYN + 131072 + i * 2048, [128, 512], F32) for i in range(4)]; b_sgt = [Buf("sg%d" % i) for i in range(4)]
        b_x2 = [Buf("x2r%d" % i) for i in range(2)]; b_xn2 = [Buf("xn2_%d" % i) for i in range(2)]; b_junk2 = Buf("junk2")

        def load_own2(k):
            S.dma("sp", [(xring2[k % 2], xown[k * 128:(k + 1) * 128, :], {})], writes=[b_x2[k % 2]])
            return xring2[k % 2], b_x2[k % 2]

        norm_pipeline(17, load_own2, 0, xn2, b_xn2, junk2, b_junk2,
                      lambda k: (hTo[:, :, k * 128:(k + 1) * 128], b_hTo[k]), (6, 7))

        wu, b_wu = WS.get(i_wu)
        wgb, b_wgb = WS.get(i_wgb)
        wgc, b_wgc = WS.get(i_wgc)
        cb = 0
        for j in range(4):
            jc = slice(j * 128, (j + 1) * 128)
            mm_group(bank(0)[:, 0:128], [(wu[:, kc, jc], hTo[:, kc, NOWN:NOWN + 128]) for kc in range(8)],
                     b_ps[0], [b_wu, b_hTo[16]])
            mm_group(bank(1)[:, 0:128], [(wgc[:, kc, jc], hTo[:, kc, NOWN:NOWN + 128]) for kc in range(8)],
                     b_ps[1], [b_wgc, b_hTo[16]])
            S.op("act", "activation", [b_ps[0]], [b_usb], out=u_sb[:, 0:8], in_=bank(0)[:, 0:8], func=AF.Copy)
            S.op("dve", "tensor_tensor", [b_ps[1], b_usb], [b_cuh], out=cuh, in0=bank(1)[:, 0:8], in1=u_sb[:, 0:8],
                 op=ALU.mult)
            for s in range(4):
                hs = slice(s * 512, (s + 1) * 512)
                rd = [b_hTo[s * 4 + t] for t in range(4)]
                pb = 2 + (cb % 2) * 3
                cu = cub[:, cb % 2, :]; bcu = b_cub[cb % 2]
                cb += 1
                mm_group(bank(pb), [(wu[:, kc, jc], hTo[:, kc, hs]) for kc in range(8)], b_ps[pb], [b_wu] + rd)
                mm_group(bank(pb + 1), [(wgc[:, kc, jc], hTo[:, kc, hs]) for kc in range(8)], b_ps[pb + 1], [b_wgc] + rd)
                mm_group(bank(pb + 2), [(wgb[:, kc, jc], hTo[:, kc, hs]) for kc in range(8)], b_ps[pb + 2], [b_wgb] + rd)
                S.op("act", "activation", [b_ps[pb]], [b_usb], out=u_sb, in_=bank(pb), func=AF.Copy)
                S.op("dve", "tensor_copy", [b_cuh], [bcu], out=cu[:, 0:2], in_=cuh[:, 2 * s:2 * s + 2])
                S.op("dve", "tensor_tensor", [b_ps[pb + 1], b_usb], [bcu], out=cu[:, 2:514], in0=bank(pb + 1), in1=u_sb,
                     op=ALU.mult)
                S.op("dve", "tensor_scalar", [bcu, b_convw], [b_acc], out=acc, in0=cu[:, 2:514],
                     scalar1=convw[:, j, 2:3], scalar2=None, op0=ALU.mult)
                S.op("dve", "scalar_tensor_tensor", [bcu, b_convw, b_acc], [b_acc], out=acc, in0=cu[:, 1:513],
                     scalar=convw[:, j, 1:2], in1=acc, op0=ALU.mult, op1=ALU.add)
                S.op("dve", "scalar_tensor_tensor", [bcu, b_convw, b_acc], [b_acc], out=acc, in0=cu[:, 0:512],
                     scalar=convw[:, j, 0:1], in1=acc, op0=ALU.mult, op1=ALU.add)
                S.op("dve", "tensor_tensor", [b_ps[pb + 2], b_acc], [b_ybt[j]], out=YBT[:, j, hs], in0=bank(pb + 2),
                     in1=acc, op=ALU.mult)
        WS.release(i_wu); WS.release(i_wgb); WS.release(i_wgc)

        it = 0
        for c4 in range(2):
            ia, iga, igb = i_d[c4]
            wab, b_wab = WS.get(ia)
            wga, b_wga = WS.get(iga)
            wgB, b_wgB = WS.get(igb)
            for cc in range(4):
                c = c4 * 4 + cc
                ccs = slice(cc * 128, (cc + 1) * 128)
                for s in range(4):
                    hs = slice(s * 512, (s + 1) * 512)
                    rd = [b_hTo[s * 4 + t] for t in range(4)]
                    pb = (it % 2) * 4
                    sa = sgt[(it % 2) * 2]; bsa = b_sgt[(it % 2) * 2]
                    sb = sgt[(it % 2) * 2 + 1]; bsb = b_sgt[(it % 2) * 2 + 1]
                    it += 1
                    mm_group(bank(pb), [(wab[:, kc, ccs], OAT[:, kc, hs]) for kc in range(4)], b_ps[pb],
                             [b_wab] + [b_oat[kc][s] for kc in range(4)])
                    mm_group(bank(pb + 1), [(wab[:, 4 + kc, ccs], YBT[:, kc, hs]) for kc in range(4)], b_ps[pb + 1],
                             [b_wab] + b_ybt)
                    mm_group(bank(pb + 2), [(wga[:, kc, ccs], hTo[:, kc, hs]) for kc in range(8)], b_ps[pb + 2], [b_wga] + rd)
                    mm_group(bank(pb + 3), [(wgB[:, kc, ccs], hTo[:, kc, hs]) for kc in range(8)], b_ps[pb + 3], [b_wgB] + rd)
                    S.op("act", "activation", [b_ps[pb + 2]], [bsa], out=sa, in_=bank(pb + 2), func=AF.Sigmoid)
                    S.op("act", "activation", [b_ps[pb + 3]], [bsb], out=sb, in_=bank(pb + 3), func=AF.Sigmoid)
                    S.op("dve", "tensor_tensor", [b_ps[pb], bsa], [bsa], out=sa, in0=bank(pb), in1=sa, op=ALU.mult)
                    S.op("dve", "tensor_tensor", [b_ps[pb + 1], bsb], [bsb], out=sb, in0=bank(pb + 1), in1=sb, op=ALU.mult)
                    S.op("dve", "tensor_tensor", [bsa, bsb], [b_mrg[c][s]], out=MRG[:, c, hs], in0=sa, in1=sb, op=ALU.add)
            WS.release(ia); WS.release(iga); WS.release(igb)
        S.barrier()

        X = R(DYN + 0, [128, 16, D], F32); b_X = [Buf("X%d" % t) for t in range(16)]
        for t in range(16):
            S.dma("sp", [(X[:, t, :], xown[t * 128:(t + 1) * 128, :], {})], writes=[b_X[t]])
        rb = [0]
        for hf in range(2):
            wm, b_wm = WS.get(i_wmo[hf])
            for t in range(16):
                bk = rb[0] % 8; rb[0] += 1
                mm_group(bank(bk), [(MRG[:, kc, t * 128:(t + 1) * 128], wm[:, kc, :]) for kc in range(8)], b_ps[bk],
                         [b_wm] + [b_mrg[kc][t // 4] for kc in range(8)])
                xs = X[:, t, hf * 512:(hf + 1) * 512]
                S.op("dve", "tensor_tensor", [b_ps[bk], b_X[t]], [b_X[t]], out=xs, in0=bank(bk), in1=xs, op=ALU.add)
            WS.release(i_wmo[hf])
        S.barrier()

        hTs = [R(DYN + 65536 + i * 8192, [128, 8, 512], BF16) for i in range(2)]
        b_hTs = [[Buf("hTs%d_%d" % (i, t)) for t in range(4)] for i in range(2)]
        qmT = R(DYN + 81920, [128, 8, 512], BF16); b_qmT = [Buf("qmT%d" % c) for c in range(8)]
        omT = R(DYN + 90112, [128, 8, 512], BF16); b_omT = [Buf("omT%d" % c) for c in range(8)]
        mT = R(DYN + 98304, [128, 8, 256], BF16); b_mT = [Buf("mT0"), Buf("mT1")]
        KmT = R(DYN + 102400, [128, 8, 256], BF16); b_KmT = Buf("KmT")
        Vm = R(DYN + 106496, [128, 2, D], BF16); b_Vm = Buf("Vm")
        Esm = R(DYN + 110592, [128, 4, 256], F32); b_Esm = Buf("Esm")
        probs = [R(DYN + 114688 + i * 2048, [128, 4, 256], BF16) for i in range(2)]; b_probs = [Buf("pr0"), Buf("pr1")]
        pT = R(DYN + 118784, [128, 8, 512], BF16); b_pT = Buf("pT")
        xn3 = [R(DYN + 126976 + i * 2048, [128, D], BF16) for i in range(2)]; b_xn3 = [Buf("xn3_0"), Buf("xn3_1")]
        junk3 = R(DYN + 131072, [128, D], BF16); b_junk3 = Buf("junk3")
        memring = [R(DYN + 118784 + i * 4096, [128, D], F32) for i in range(2)]; b_mr = [Buf("mr0"), Buf("mr1")]

        load_gain(1, norm_mem_kv)
        load_gain(0, norm_mem_q)
        for mt in range(2):
            S.dma("sp", [(memring[mt], memb[mt * 128:(mt + 1) * 128, :], {})], writes=[b_mr[mt]])
            norm_tile(memring[mt], b_mr[mt], 1, xn3[mt], b_xn3[mt], junk3, b_junk3)
            transpose_tile(xn3[mt], b_xn3[mt], mT[:, :, mt * 128:(mt + 1) * 128], b_mT[mt], (4, 5))
        wK = [WS.get(i_wkvK[k]) for k in range(2)]
        for c in range(8):
            bk = 6 + c % 2
            w_, bw_ = wK[c // 4]
            mm_group(bank(bk)[:, 0:256], [(w_[:, kc, (c % 4) * 128:(c % 4 + 1) * 128], mT[:, kc, :]) for kc in range(8)],
                     b_ps[bk], [bw_] + b_mT)
            evac(KmT[:, c, :], bank(bk)[:, 0:256], [b_ps[bk]], [b_KmT])
        WS.release(i_wkvK[0]); WS.release(i_wkvK[1])
        wVv = [WS.get(i_wkvV[k]) for k in range(2)]
        for mt in range(2):
            for hf in range(2):
                bk = 6 + hf
                w_, bw_ = wVv[hf]
                mm_group(bank(bk), [(mT[:, kc, mt * 128:(mt + 1) * 128], w_[:, kc, :]) for kc in range(8)],
                         b_ps[bk], [bw_, b_mT[mt]])
                evac(Vm[:, mt, hf * 512:(hf + 1) * 512], bank(bk), [b_ps[bk]], [b_Vm])
        WS.release(i_wkvV[0]); WS.release(i_wkvV[1])
        S.barrier()

        wMQ = [WS.get(i_wmq[k]) for k in range(2)]
        wMO = [WS.get(i_wmo2[k]) for k in range(2)]
        sc_it = 0
        gb = [0]
        for s in range(4):
            hb = hTs[s % 2]; bh = b_hTs[s % 2]
            norm_pipeline(4, lambda k, s=s: (X[:, s * 4 + k, :], b_X[s * 4 + k]), 0, xn3, b_xn3, junk3, b_junk3,
                          lambda k, hb=hb, bh=bh: (hb[:, :, k * 128:(k + 1) * 128], bh[k]), (4, 5))
            for c in range(8):
                bk = 6 + gb[0] % 2; gb[0] += 1
                w_, bw_ = wMQ[c // 4]
                mm_group(bank(bk), [(w_[:, kc, (c % 4) * 128:(c % 4 + 1) * 128], hb[:, kc, :]) for kc in range(8)],
                         b_ps[bk], [bw_] + bh)
                evac(qmT[:, c, :], bank(bk), [b_ps[bk]], [b_qmT[c]])

            def sA(t):
                sb0 = (t % 2) * 2
                psS = bank(sb0, 2).rearrange("p (a b) -> p a b", a=4)
                for h in range(4):
                    for cc in range(2):
                        c = 2 * h + cc
                        S.op("pe", "matmul", [b_qmT[c], b_KmT], [b_ps[sb0]], psS[:, h, :],
                             lhsT=qmT[:, c, t * 128:(t + 1) * 128], rhs=KmT[:, c, :], start=(cc == 0), stop=(cc == 1),
                             _mark=(h == 3 and cc == 1))

            def sB(t):
                sb0 = (t % 2) * 2
                psS = bank(sb0, 2).rearrange("p (a b) -> p a b", a=4)
                b_S = b_ps[sb0]
                pr = probs[t % 2]; bpr = b_probs[t % 2]
                S.op("dve", "tensor_reduce", [b_S], [b_mx], out=mx4, in_=psS, axis=AX.X, op=ALU.max)
                S.op("dve", "tensor_scalar", [b_mx], [b_nb], out=nb4, in0=mx4, scalar1=-1.0 / 16, scalar2=None, op0=ALU.mult)
                for h in range(4):
                    S.op("act", "activation", [b_S, b_nb], [b_Esm, b_sm], out=Esm[:, h, :], in_=psS[:, h, :], func=AF.Exp,
                         scale=1.0 / 16, bias=nb4[:, h:h + 1], accum_out=sm4[:, h:h + 1])
                S.op("dve", "reciprocal", [b_sm], [b_rs], out=rs4, in_=sm4)
                for h in range(4):
                    S.op("dve", "tensor_scalar", [b_Esm, b_rs], [bpr], out=pr[:, h, :], in0=Esm[:, h, :],
                         scalar1=rs4[:, h:h + 1], scalar2=None, op0=ALU.mult)

            def sC(t):
                pr = probs[t % 2]; bpr = b_probs[t % 2]
                tb_ = 4 + (t % 2)
                pst = bank_bf(tb_)
                for h in range(4):
                    for mt in range(2):
                        k8 = h * 2 + mt
                        S.op("pe", "transpose", [bpr, b_ident], [b_ps[tb_]], out=pst[:, k8 * 128:(k8 + 1) * 128],
                             in_=pr[:, h, mt * 128:(mt + 1) * 128], identity=ident, _mark=(k8 == 7))
                S.op("act", "activation", [b_ps[tb_]], [b_pT], out=pT[:, :, t * 128:(t + 1) * 128],
                     in_=pst.rearrange("p (a b) -> p a b", a=8), func=AF.Copy)

            for step in range(6):
                if step < 4:
                    sA(step)
                if 0 <= step - 1 < 4:
                    sB(step - 1)
                if 0 <= step - 2 < 4:
                    sC(step - 2)
            for h in range(4):
                for dc in range(2):
                    c = h * 2 + dc
                    bk = 6 + gb[0] % 2; gb[0] += 1
                    mm_group(bank(bk), [(Vm[:, mt, c * 128:(c + 1) * 128], pT[:, h * 2 + mt, :]) for mt in range(2)],
                             b_ps[bk], [b_Vm, b_pT])
                    evac(omT[:, c, :], bank(bk), [b_ps[bk]], [b_omT[c]])
            for t in range(4):
                tt = s * 4 + t
                for hf in range(2):
                    bk = 6 + gb[0] % 2; gb[0] += 1
                    w_, bw_ = wMO[hf]
                    mm_group(bank(bk), [(omT[:, kc, t * 128:(t + 1) * 128], w_[:, kc, :]) for kc in range(8)],
                             b_ps[bk], [bw_] + b_omT)
                    xs = X[:, tt, hf * 512:(hf + 1) * 512]
                    S.op("dve", "tensor_tensor", [b_ps[bk], b_X[tt]], [b_X[tt]], out=xs, in0=bank(bk), in1=xs, op=ALU.add)
        for k in range(2):
            WS.release(i_wmq[k]); WS.release(i_wmo2[k])
        S.barrier()

        hTb = R(DYN + 65536, [128, 8, 1024], BF16); b_hTb = [Buf("hTb%d" % t) for t in range(8)]
        aT = R(DYN + 81920, [128, 22, 1024], BF16); b_aT = [Buf("aT%d" % j) for j in range(22)]
        sgl = [R(DYN + 126976 + i * 2048, [128, 512], F32) for i in range(2)]; b_sgl = [Buf("sgl0"), Buf("sgl1")]
        xn4 = [R(DYN + 131072 + i * 2048, [128, D], BF16) for i in range(2)]; b_xn4 = [Buf("xn4_0"), Buf("xn4_1")]
        junk4 = R(DYN + 135168, [128, D], BF16); b_junk4 = Buf("junk4")
        load_gain(1, norm_ffn)
        fit = 0
        for tb in range(2):
            gu, fo = i_ffn[tb]
            norm_pipeline(8, lambda k, tb=tb: (X[:, tb * 8 + k, :], b_X[tb * 8 + k]), 1, xn4, b_xn4, junk4, b_junk4,
                          lambda k: (hTb[:, :, k * 128:(k + 1) * 128], b_hTb[k]), (6, 7))
            for blk in range(6):
                ig, iu, ncol = gu[blk]
                wg_, b_wg = WS.get(ig)
                wu_, b_wu_ = WS.get(iu)
                for cc in range(ncol // 128):
                    j = blk * 4 + cc
                    ccs = slice(cc * 128, (cc + 1) * 128)
                    for s2 in range(2):
                        hs = slice(s2 * 512, (s2 + 1) * 512)
                        rd = b_hTb[s2 * 4:(s2 + 1) * 4]
                        pb = (fit % 3) * 2
                        sg_ = sgl[fit % 2]; bsg = b_sgl[fit % 2]
                        fit += 1
                        mm_group(bank(pb), [(wg_[:, kc, ccs], hTb[:, kc, hs]) for kc in range(8)], b_ps[pb], [b_wg] + rd)
                        mm_group(bank(pb + 1), [(wu_[:, kc, ccs], hTb[:, kc, hs]) for kc in range(8)], b_ps[pb + 1],
                                 [b_wu_] + rd)
                        S.op("act", "activation", [b_ps[pb]], [bsg], out=sg_, in_=bank(pb), func=AF.Silu)
                        S.op("dve", "tensor_tensor", [b_ps[pb + 1], bsg], [b_aT[j]], out=aT[:, j, hs], in0=bank(pb + 1),
                             in1=sg_, op=ALU.mult)
                WS.release(ig); WS.release(iu)
            for hf in range(2):
                wfo = [WS.get(fo[hf][k3]) for k3 in range(3)]
                for t in range(8):
                    tt = tb * 8 + t
                    bk = 6 + (fit % 2); fit += 1
                    mm_group(bank(bk), [(aT[:, j, t * 128:(t + 1) * 128], wfo[j // 8][0][:, j % 8, :]) for j in range(22)],
                             b_ps[bk], [w[1] for w in wfo] + b_aT)
                    xs = X[:, tt, hf * 512:(hf + 1) * 512]
                    S.op("dve", "tensor_tensor", [b_ps[bk], b_X[tt]], [b_X[tt]], out=xs, in0=bank(bk), in1=xs, op=ALU.add)
                for k3 in range(3):
                    WS.release(fo[hf][k3])
        S.barrier()

        otmp = [R(DYN + 65536 + i * 4096, [128, D], F32) for i in range(2)]; b_ot = [Buf("ot0"), Buf("ot1")]
        junk5 = R(DYN + 73728, [128, D], BF16); b_junk5 = Buf("junk5")
        load_gain(0, norm_final)
        out_evs = []
        for tt in range(16):
            i = tt % 2
            norm_tile(X[:, tt, :], b_X[tt], 0, otmp[i], b_ot[i], junk5, b_junk5)
            out_evs.append(S.dma("sp", [(out_d[tt * 128:(tt + 1) * 128, :], otmp[i], {})], reads=[b_ot[i]]))
        for ev in out_evs:
            S._wait("sp", ev)
        print("inst counts", S.n_inst, {e: len(S.prog[e]) for e in S.ENGS})
        S.emit()
    return nc


def _host_inputs(inputs):
    x = np.asarray(inputs["x"], dtype=np.float32)
    mem = np.asarray(inputs["mem"], dtype=np.float32)
    diag = np.zeros((128, 4, 512), np.float32)
    pp = np.arange(128)[:, None]
    kr = np.arange(512)[None, :]
    for i in range(4):
        ql = i * 128 + pp
        diag[:, i, :] = np.where(kr > 511 - ql, 0.0, NEG)
    eye = np.eye(128, dtype=np.float32)
    shared = {
        "diag": diag,
        "norm_mix": np.ascontiguousarray(inputs["norm_mix"][0:1]),
        "w_in": np.ascontiguousarray(inputs["w_in"][0]),
        "conv_w": np.ascontiguousarray(inputs["conv_w"][0]),
        "w_branch_a": np.ascontiguousarray(inputs["w_branch_a"][0]),
        "w_branch_b": np.ascontiguousarray(inputs["w_branch_b"][0]),
        "w_mix_out": np.ascontiguousarray(inputs["w_mix_out"][0]),
        "norm_mem_q": np.ascontiguousarray(inputs["norm_mem_q"][0:1]),
        "norm_mem_kv": np.ascontiguousarray(inputs["norm_mem_kv"][0:1]),
        "w_mem_q": np.ascontiguousarray(inputs["w_mem_q"][0]),
        "w_mem_kv": np.ascontiguousarray(inputs["w_mem_kv"][0]),
        "w_mem_o": np.ascontiguousarray(inputs["w_mem_o"][0]),
        "norm_ffn": np.ascontiguousarray(inputs["norm_ffn"][0:1]),
        "w_ffn_in": np.ascontiguousarray(inputs["w_ffn_in"][0]),
        "w_ffn_out": np.ascontiguousarray(inputs["w_ffn_out"][0]),
        "norm_final": np.ascontiguousarray(np.asarray(inputs["norm_final"]).reshape(1, D)),
    }
    shared = {k: np.asarray(v, dtype=np.float32) for k, v in shared.items()}
    in_maps = []
    for core in range(8):
        b, par = core // 2, core % 2
        tiles = T_OF[par]
        xown = np.zeros((NOWN + 128, D), np.float32)
        seld = np.zeros((128, 4, 2, 128), np.float32)
        seln = np.zeros((1, 4, 2, 128), np.float32)
        for j, T in enumerate(tiles):
            xown[j * 512:(j + 1) * 512] = x[b, T * 512:(T + 1) * 512]
            if T > 0:
                xown[NOWN + 2 * j:NOWN + 2 * j + 2] = x[b, T * 512 - 2:T * 512]
            tmax = NCH[j] - 1
            if T == tmax:
                seld[:, j, 0, :] = eye
            else:
                seln[0, j, 0, :] = 1.0
                seld[:, j, 1, :] = eye
        m = dict(shared)
        m["xall"] = np.ascontiguousarray(x[b, ::-1, :])
        m["xown"] = xown
        m["memb"] = np.ascontiguousarray(mem[b])
        m["seld"] = seld
        m["seln"] = seln
        in_maps.append(m)
    return in_maps


_NC_CACHE = {}


def kernel(**inputs):
    in_maps = _host_inputs(inputs)
    if "nc" not in _NC_CACHE:
        _NC_CACHE["nc"] = build_nc()
    nc = _NC_CACHE["nc"]
    res = run_bass_kernel_spmd(nc, in_maps, core_ids=list(range(8)))
    out = np.zeros((NB, SEQ, D), np.float32)
    for core in range(8):
        b, par = core // 2, core % 2
        o = np.asarray(res.results[core]["out"], dtype=np.float32)
        for j, T in enumerate(T_OF[par]):
            out[b, T * 512:(T + 1) * 512] = o[j * 512:(j + 1) * 512]
    return out
```

```python
import numpy as np
import concourse.bass as bass
import concourse.mybir as mybir
from concourse.bass_utils import run_bass_kernel_spmd
from contextlib import ExitStack

F32 = mybir.dt.float32
BF16 = mybir.dt.bfloat16
AF = mybir.ActivationFunctionType
ALU = mybir.AluOpType
AX = mybir.AxisListType

D = 1024
SEQ = 4096
NB = 4
TS = 512
T_OF = {0: (0, 3, 4, 7), 1: (1, 2, 5, 6)}
NCH = (2, 4, 6, 8)
NEG = -30000.0
EPS = 1e-6
FFN_H = 2816
NOWN = 2048
SAME_ENGINE_SYNC = True


class Buf:
    __slots__ = ("name", "w", "r")

    def __init__(self, name):
        self.name = name
        self.w = None
        self.r = {}


class Sched:
    ENGS = ("pe", "act", "dve", "pool", "sp")

    def __init__(self, nc, stack, n_dma_sems=8):
        self.nc = nc
        self.prog = {e: [] for e in self.ENGS}
        self.count = {e: 0 for e in self.ENGS}
        self.sem = {e: stack.enter_context(nc.semaphore("s_" + e)) for e in self.ENGS}
        self.seen = {e: {} for e in self.ENGS}
        self.dsem = {}
        self.dval = {}
        self.dring = {}
        self.dpos = {}
        idx = 0
        for q in ("sp", "pool"):
            ring = []
            for i in range(n_dma_sems):
                self.dsem[idx] = stack.enter_context(nc.semaphore("d_%s%d" % (q, i)))
                self.dval[idx] = 0
                ring.append(idx)
                idx += 1
            self.dring[q] = ring
            self.dpos[q] = 0
        self.n_inst = {e: 0 for e in self.ENGS}
        self.last_marked = {e: True for e in self.ENGS}

    def _wait(self, e, ev):
        if ev is None:
            return
        if ev[0] == "e":
            _, src, seq = ev
            if src == e and (e == "pe" or not SAME_ENGINE_SYNC):
                return
            assert self.count[src] >= seq, ("dependency on unissued mark", e, ev)
            key = ("e", src)
            val = seq
            sem = self.sem[src]
        else:
            _, sidx, val = ev
            key = ("d", sidx)
            sem = self.dsem[sidx]
        if self.seen[e].get(key, 0) >= val:
            return
        self.seen[e][key] = val
        self.prog[e].append(lambda eng, sem=sem, val=val: eng.wait_ge(sem, val))

    def _deps(self, e, reads, writes):
        for b in reads:
            self._wait(e, b.w)
        for b in writes:
            self._wait(e, b.w)
            for (k0, k1), v in list(b.r.items()):
                self._wait(e, (k0, k1, v))

    def _record(self, ev, reads, writes):
        key = (ev[0], ev[1])
        for b in reads:
            if b.r.get(key, 0) < ev[2]:
                b.r[key] = ev[2]
        for b in writes:
            b.w = ev
            b.r = {}

    def op(self, e, meth, reads, writes, *args, _mark=True, **kw):
        self._deps(e, reads, writes)
        sem = self.sem[e]
        if _mark:
            self.count[e] += 1
            self.prog[e].append(lambda eng: getattr(eng, meth)(*args, **kw).then_inc(sem, 1))
            ev = ("e", e, self.count[e])
        else:
            self.prog[e].append(lambda eng: getattr(eng, meth)(*args, **kw))
            ev = ("e", e, self.count[e] + 1)
        self.last_marked[e] = _mark
        self.n_inst[e] += 1
        self._record(ev, reads, writes)
        return ev

    def dma(self, q, xfers, reads=(), writes=()):
        self._deps(q, reads, writes)
        ring = self.dring[q]
        sidx = ring[self.dpos[q] % len(ring)]
        self.dpos[q] += 1
        if self.dval[sidx] > 0:
            self._wait(q, ("d", sidx, self.dval[sidx]))
        sem = self.dsem[sidx]
        for (o, i, kw) in xfers:
            self.dval[sidx] += 16
            self.prog[q].append(lambda eng, o=o, i=i, kw=kw: eng.dma_start(out=o, in_=i, **kw).then_inc(sem, 16))
        ev = ("d", sidx, self.dval[sidx])
        self._record(ev, reads, writes)
        return ev

    def barrier(self):
        for e in ("pe", "act", "dve"):
            assert self.last_marked[e], ("barrier with unmarked tail", e)
        for sidx in self.dring["sp"]:
            if self.dval[sidx] > 0:
                self._wait("sp", ("d", sidx, self.dval[sidx]))
        for f in ("pe", "act", "dve"):
            self._wait("sp", ("e", f, self.count[f]))
        self.count["sp"] += 1
        sem = self.sem["sp"]
        self.prog["sp"].append(lambda eng, sem=sem: eng.sem_inc(sem, 1))
        for e in ("pe", "act", "dve"):
            self._wait(e, ("e", "sp", self.count["sp"]))

    def emit(self):
        with self.nc.Block() as block:
            def mk(name):
                def body(engine):
                    for c in self.prog[name]:
                        c(engine)
                return body
            block.sync(mk("sp"))
            block.gpsimd(mk("pool"))
            block.scalar(mk("act"))
            block.vector(mk("dve"))
            block.tensor(mk("pe"))


def build_nc():
    nc = bass.Bass("TRN2", target_bir_lowering=False)

    def din(name, shape):
        return nc.dram_tensor(name, list(shape), F32, kind="ExternalInput").ap()

    xall = din("xall", [SEQ, D])
    xown = din("xown", [NOWN + 128, D])
    memb = din("memb", [256, D])
    seld_d = din("seld", [128, 4, 2, 128])
    seln_d = din("seln", [1, 4, 2, 128])
    diag_d = din("diag", [128, 4, 512])
    norm_mix = din("norm_mix", [1, D])
    w_in = din("w_in", [D, 5120])
    conv_w = din("conv_w", [3, 512])
    w_ba = din("w_branch_a", [512, D])
    w_bb = din("w_branch_b", [512, D])
    w_mix_out = din("w_mix_out", [D, D])
    norm_mem_q = din("norm_mem_q", [1, D])
    norm_mem_kv = din("norm_mem_kv", [1, D])
    w_mem_q = din("w_mem_q", [D, D])
    w_mem_kv = din("w_mem_kv", [D, 2 * D])
    w_mem_o = din("w_mem_o", [D, D])
    norm_ffn = din("norm_ffn", [1, D])
    w_ffn_in = din("w_ffn_in", [D, 2 * FFN_H])
    w_ffn_out = din("w_ffn_out", [FFN_H, D])
    norm_final = din("norm_final", [1, D])
    out_d = nc.dram_tensor("out", [NOWN, D], F32, kind="ExternalOutput").ap()

    with ExitStack() as st:
        S = Sched(nc, st)
        ARENA_F32 = 52900
        arena = st.enter_context(nc.sbuf_tensor("arena", [128, ARENA_F32], F32))
        psum = st.enter_context(nc.psum_tensor("psum", [128, 4096], F32))

        def R(off, shape, dt):
            esz = 4 if dt == F32 else 2
            n = int(np.prod(shape[1:]))
            nbytes = n * esz
            assert off % 4 == 0 and nbytes % 4 == 0, (off, shape)
            assert off + nbytes <= ARENA_F32 * 4, (off, shape)
            ap = arena[:, off // 4:(off + nbytes) // 4]
            if dt != F32:
                ap = ap.bitcast(dt)
            if len(shape) == 3:
                ap = ap.rearrange("p (a b) -> p a b", a=shape[1])
            elif len(shape) == 4:
                ap = ap.rearrange("p (a b c) -> p a b c", a=shape[1], b=shape[2])
            return ap

        def bank(i, n=1):
            return psum[:, i * 512:(i + n) * 512]

        def bank_bf(i, n=1):
            return psum[:, i * 512:(i + n) * 512].bitcast(BF16)

        b_ps = [Buf("ps%d" % i) for i in range(8)]

        ident = R(0, [128, 128], BF16); b_ident = Buf("ident")
        identf = R(256, [128, 128], F32); b_identf = Buf("identf")
        convw = R(768, [128, 4, 3], F32); b_convw = Buf("convw")
        stats = R(1024, [128, 256], F32)
        seld = R(2048, [128, 4, 2, 128], BF16); b_seld = Buf("seld")
        seln = R(4096, [128, 4, 2, 128], BF16); b_seln = Buf("seln")
        negrow = R(6144, [128, 512], BF16); b_negrow = Buf("negrow")
        zeros = R(7168, [128, 512], F32); b_zeros = Buf("zeros")
        diag = R(9216, [128, 4, 512], BF16); b_diag = Buf("diag")
        GAIN0 = 13312
        g_rep = [R(GAIN0 + i * 4096, [128, D], F32) for i in range(2)]
        b_g = [Buf("g%d" % i) for i in range(2)]
        RING0 = 21504
        NSLOT = 5
        DYN = RING0 + NSLOT * 8192

        ssq = stats[:, 0:8]; rstd = stats[:, 8:16]
        b_st = [Buf("st%d" % i) for i in range(8)]
        st_pos = [0]
        mx4 = stats[:, 16:20]; nb4 = stats[:, 20:24]; sm4 = stats[:, 24:28]; rs4 = stats[:, 28:32]
        b_mx = Buf("mx"); b_nb = Buf("nb"); b_sm = Buf("sm"); b_rs = Buf("rs")
        cuh = stats[:, 32:40]; b_cuh = Buf("cuh")

        class WStream:
            def __init__(self):
                self.blocks = []
                self.issued = 0
                self.released = set()
                self.bufs = [Buf("ring%d" % i) for i in range(NSLOT)]
                self.next_get = 0

            def add(self, pieces):
                self.blocks.append(pieces)
                return len(self.blocks) - 1

            def slot_ap(self, i):
                return R(RING0 + (i % NSLOT) * 8192, [128, 8, 512], BF16)

            def _issue(self):
                while self.issued < len(self.blocks):
                    k = self.issued
                    if k >= NSLOT and (k - NSLOT) not in self.released:
                        break
                    if k > self.next_get + NSLOT - 1:
                        break
                    sl = self.slot_ap(k)
                    xf = []
                    for (src, kc0, kcn, ncols) in self.blocks[k]:
                        xf.append((sl[:, kc0:kc0 + kcn, 0:ncols], src, {}))
                    S.dma("pool", xf, writes=[self.bufs[k % NSLOT]])
                    self.issued += 1

            def get(self, idx):
                assert idx == self.next_get, (idx, self.next_get)
                self.next_get += 1
                self._issue()
                assert self.issued > idx, ("weight block not issued", idx)
                return self.slot_ap(idx), self.bufs[idx % NSLOT]

            def release(self, idx):
                self.released.add(idx)
                self._issue()

        WS = WStream()

        def wpiece(w, r0, nrows, c0, ncols, kc0=0):
            src = w[r0:r0 + nrows, c0:c0 + ncols].rearrange("(kc p) n -> p kc n", p=128)
            return (src, kc0, nrows // 128, ncols)

        i_wk = WS.add([wpiece(w_in, 0, D, 512, 512)])
        i_wv = WS.add([wpiece(w_in, 0, D, 1024, 512)])
        i_wq = WS.add([wpiece(w_in, 0, D, 0, 512)])
        i_wu = WS.add([wpiece(w_in, 0, D, 1536, 512)])
        i_wgb = WS.add([wpiece(w_in, 0, D, 2048, 512)])
        i_wgc = WS.add([wpiece(w_in, 0, D, 2560, 512)])
        i_d = []
        for c4 in range(2):
            a = WS.add([wpiece(w_ba, 0, 512, c4 * 512, 512, kc0=0), wpiece(w_bb, 0, 512, c4 * 512, 512, kc0=4)])
            b_ = WS.add([wpiece(w_in, 0, D, 3072 + c4 * 512, 512)])
            c_ = WS.add([wpiece(w_in, 0, D, 4096 + c4 * 512, 512)])
            i_d.append((a, b_, c_))
        i_wmo = [WS.add([wpiece(w_mix_out, 0, D, hf * 512, 512)]) for hf in range(2)]
        i_wkvK = [WS.add([wpiece(w_mem_kv, 0, D, k * 512, 512)]) for k in range(2)]
        i_wkvV = [WS.add([wpiece(w_mem_kv, 0, D, D + k * 512, 512)]) for k in range(2)]
        i_wmq = [WS.add([wpiece(w_mem_q, 0, D, k * 512, 512)]) for k in range(2)]
        i_wmo2 = [WS.add([wpiece(w_mem_o, 0, D, k * 512, 512)]) for k in range(2)]
        i_ffn = []
        for tb in range(2):
            gu = []
            for blk in range(6):
                ncol = 512 if blk < 5 else 256
                ig = WS.add([wpiece(w_ffn_in, 0, D, blk * 512, ncol)])
                iu = WS.add([wpiece(w_ffn_in, 0, D, FFN_H + blk * 512, ncol)])
                gu.append((ig, iu, ncol))
            fo = []
            for hf in range(2):
                ks = []
                for k3 in range(3):
                    nr = 1024 if k3 < 2 else FFN_H - 2048
                    ks.append(WS.add([wpiece(w_ffn_out, k3 * 1024, nr, hf * 512, 512)]))
                fo.append(ks)
            i_ffn.append((gu, fo))

        evac_rr = [0]

        def evac(out_ap, in_ap, reads, writes, eng=None):
            if eng is None:
                eng = ("act", "dve")[evac_rr[0] % 2]
                evac_rr[0] += 1
            if eng == "act":
                return S.op("act", "activation", reads, writes, out=out_ap, in_=in_ap, func=AF.Copy)
            return S.op("dve", "tensor_copy", reads, writes, out=out_ap, in_=in_ap)

        def mm_group(out_ap, pairs, b_out, reads):
            n = len(pairs)
            for k, (l, r) in enumerate(pairs):
                S.op("pe", "matmul", reads, [b_out], out_ap, lhsT=l, rhs=r, start=(k == 0), stop=(k == n - 1),
                     _mark=(k == n - 1))

        def load_gain(i, src):
            S.dma("sp", [(g_rep[i], src.broadcast_to([128, D]), {})], writes=[b_g[i]])

        def norm_tile(x_ap, b_x, gi, out_ap, b_out, junk_ap, b_junk_):
            k = st_pos[0] % 8
            st_pos[0] += 1
            bs = b_st[k]
            S.op("act", "activation", [b_x], [b_junk_, bs], out=junk_ap, in_=x_ap, func=AF.Square,
                 accum_out=ssq[:, k:k + 1])
            S.op("act", "activation", [bs], [bs], out=rstd[:, k:k + 1], in_=ssq[:, k:k + 1], func=AF.Ln,
                 scale=1.0 / D, bias=EPS)
            S.op("act", "activation", [bs], [bs], out=rstd[:, k:k + 1], in_=rstd[:, k:k + 1], func=AF.Exp, scale=-0.5)
            S.op("dve", "scalar_tensor_tensor", [b_x, bs, b_g[gi]], [b_out], out=out_ap, in0=x_ap,
                 scalar=rstd[:, k:k + 1], in1=g_rep[gi], op0=ALU.mult, op1=ALU.mult)

        tr_rr = [0]

        def transpose_tile(xn_ap, b_xn_, dst3, b_dst, tbanks, mm=False):
            bk = tbanks[tr_rr[0] % len(tbanks)]
            tr_rr[0] += 1
            if mm:
                pst = bank(bk, 2)
                bb = [b_ps[bk], b_ps[bk + 1]]
                for c in range(8):
                    S.op("pe", "matmul", [b_xn_, b_ident], bb, pst[:, c * 128:(c + 1) * 128],
                         lhsT=xn_ap[:, c * 128:(c + 1) * 128], rhs=ident, start=True, stop=True, _mark=(c == 7))
                evac(dst3, pst.rearrange("p (a b) -> p a b", a=8), bb, [b_dst])
                return
            pst = bank_bf(bk)
            for c in range(8):
                S.op("pe", "transpose", [b_xn_, b_ident], [b_ps[bk]], out=pst[:, c * 128:(c + 1) * 128],
                     in_=xn_ap[:, c * 128:(c + 1) * 128], identity=ident, _mark=(c == 7))
            S.op("act", "activation", [b_ps[bk]], [b_dst], out=dst3, in_=pst.rearrange("p (a b) -> p a b", a=8),
                 func=AF.Copy)

        def norm_pipeline(n, load_fn, gi, xn_bufs, b_xn_bufs, junk_ap, b_junk_, dst_fn, tbanks, after_fn=None, mm=False):
            nb_ = len(xn_bufs)
            la = nb_ - 1

            def pre(k):
                x_ap, b_x = load_fn(k)
                norm_tile(x_ap, b_x, gi, xn_bufs[k % nb_], b_xn_bufs[k % nb_], junk_ap, b_junk_)

            def post(k):
                dst3, b_dst = dst_fn(k)
                transpose_tile(xn_bufs[k % nb_], b_xn_bufs[k % nb_], dst3, b_dst, tbanks, mm=mm)
            for k in range(min(la, n)):
                pre(k)
            for k in range(n):
                if k + la < n:
                    pre(k + la)
                post(k)
                if after_fn is not None:
                    after_fn(k)

        S.op("dve", "memset", [], [b_identf], identf, 1.0)
        S.op("pool", "affine_select", [b_identf], [b_identf], out=identf, in_=identf, pattern=[[-1, 128]],
             compare_op=ALU.is_equal, fill=0.0, base=0, channel_multiplier=1)
        S.op("dve", "tensor_copy", [b_identf], [b_ident], out=ident, in_=identf)
        S.op("dve", "memset", [], [b_negrow], negrow, NEG)
        S.op("dve", "memset", [], [b_zeros], zeros, 0.0)
        S.dma("pool", [(seld, seld_d, {})], writes=[b_seld])
        S.dma("pool", [(seln[0:1], seln_d, {})], writes=[b_seln])
        S.dma("pool", [(diag, diag_d, {})], writes=[b_diag])
        S.dma("sp", [(convw[:, j, :], conv_w[:, j * 128:(j + 1) * 128].rearrange("i p -> p i"),
                      {"allow_slow_non_contiguous": True}) for j in range(4)], writes=[b_convw])
        load_gain(0, norm_mix)

        KT = R(DYN + 0, [128, 4, SEQ], BF16)
        V = R(DYN + 32768, [128, 32, 512], BF16)
        QT = R(DYN + 65536, [128, 4, NOWN], BF16)
        OAT = R(DYN + 81920, [128, 4, NOWN], BF16)
        b_oat = [[Buf("oat%d_%d" % (p, s)) for s in range(4)] for p in range(4)]
        TMP = DYN + 98304
        xring = [R(TMP + i * 4096, [128, D], F32) for i in range(4)]; b_xr = [Buf("xr%d" % i) for i in range(4)]
        xn = [R(TMP + 16384 + i * 2048, [128, D], BF16) for i in range(3)]; b_xn = [Buf("xn%d" % i) for i in range(3)]
        junk = R(TMP + 24576, [128, D], BF16); b_junk = Buf("junk")
        hTa = [R(TMP + 26624 + i * 8192, [128, 8, 512], BF16) for i in range(2)]
        b_hTa = [[Buf("hTa%d_%d" % (i, t)) for t in range(4)] for i in range(2)]
        b_kt = [[Buf("kt%d_%d" % (p, ch)) for ch in range(8)] for p in range(4)]
        b_v = [Buf("v%d" % ch) for ch in range(8)]
        b_qt = [[Buf("qt%d_%d" % (p, s)) for s in range(4)] for p in range(4)]

        wk, b_wk = WS.get(i_wk)
        wv, b_wv = WS.get(i_wv)
        wq, b_wq = WS.get(i_wq)

        mmb = [0]

        def load_all(k):
            xt = xring[k % 4]; bx = b_xr[k % 4]
            S.dma("sp", [(xt, xall[k * 128:(k + 1) * 128, :], {})], writes=[bx])
            return xt, bx

        def dst_all(k):
            ch, t = k // 4, k % 4
            return hTa[ch % 2][:, :, t * 128:(t + 1) * 128], b_hTa[ch % 2][t]

        pend = []

        def flush(nmax):
            for _ in range(min(nmax, len(pend))):
                pend.pop(0)()

        def after_all(k):
            if k % 4 == 3:
                ch = k // 4
                hb = hTa[ch % 2]; bh = b_hTa[ch % 2]
                for p in range(4):
                    def f(p=p, ch=ch, hb=hb, bh=bh):
                        bk = mmb[0] % 4; mmb[0] += 1
                        mm_group(bank(bk), [(wk[:, kc, p * 128:(p + 1) * 128], hb[:, kc, :]) for kc in range(8)],
                                 b_ps[bk], [b_wk] + bh)
                        evac(KT[:, p, ch * 512:(ch + 1) * 512], bank(bk), [b_ps[bk]], [b_kt[p][ch]])
                    pend.append(f)
                for t in range(4):
                    def f(t=t, ch=ch, hb=hb, bh=bh):
                        bk = mmb[0] % 4; mmb[0] += 1
                        mm_group(bank(bk), [(hb[:, kc, t * 128:(t + 1) * 128], wv[:, kc, :]) for kc in range(8)],
                                 b_ps[bk], [b_wv, bh[t]])
                        evac(V[:, ch * 4 + t, :], bank(bk), [b_ps[bk]], [b_v[ch]])
                    pend.append(f)
            flush(2)

        norm_pipeline(32, load_all, 0, xn, b_xn, junk, b_junk, dst_all, (4, 6), after_all, mm=True)

        def load_own(k):
            xt = xring[k % 4]; bx = b_xr[k % 4]
            S.dma("sp", [(xt, xown[k * 128:(k + 1) * 128, :], {})], writes=[bx])
            return xt, bx

        def after_own(k):
            if k % 4 == 3:
                s_ = k // 4
                hb = hTa[s_ % 2]; bh = b_hTa[s_ % 2]
                for p in range(4):
                    def f(p=p, s_=s_, hb=hb, bh=bh):
                        bk = mmb[0] % 4; mmb[0] += 1
                        mm_group(bank(bk), [(wq[:, kc, p * 128:(p + 1) * 128], hb[:, kc, :]) for kc in range(8)],
                                 b_ps[bk], [b_wq] + bh)
                        evac(QT[:, p, s_ * 512:(s_ + 1) * 512], bank(bk), [b_ps[bk]], [b_qt[p][s_]])
                    pend.append(f)
            flush(2)

        norm_pipeline(16, load_own, 0, xn, b_xn, junk, b_junk, dst_all, (4, 6), after_own, mm=True)
        flush(100)
        WS.release(i_wk); WS.release(i_wv)
        WS.release(i_wq)
        S.barrier()

        Fb = [R(TMP + i * 8208, [128, 4, 513], F32) for i in range(2)]; b_F = [Buf("F%d" % i) for i in range(2)]
        D1 = R(TMP + 16416, [128, 4, 513], F32); b_D1 = Buf("D1")
        Pb = R(TMP + 24624, [128, 4, 513], F32); b_P = Buf("P")
        Wb = [R(TMP + 32832 + i * 4096, [128, 4, 512], BF16) for i in range(2)]; b_W = [Buf("W%d" % i) for i in range(2)]
        WTb = [R(TMP + 41024 + i * 4096, [128, 4, 512], BF16) for i in range(2)]; b_WT = [Buf("WT%d" % i) for i in range(2)]
        b_Z = Buf("Z")
        b_WTps = Buf("WTps")
        WTps = bank_bf(4, 2).rearrange("p (a b) -> p a b", a=4)
        for i in range(2):
            S.op("dve", "memset", [], [b_F[i]], Fb[i].rearrange("p a b -> p (a b)"), 0.0)
        S.op("dve", "memset", [], [b_D1], D1.rearrange("p a b -> p (a b)"), 0.0)
        groups = [(s, h, c) for s in range(4) for h in range(8) for c in range(NCH[s])]
        G = len(groups)

        def geom(gi):
            s, h, c = groups[gi]
            n = NCH[s]
            p = h // 2
            rows = slice(0, 64) if h % 2 == 0 else slice(64, 128)
            ob = 6 + (h % 2)
            cr = 8 - n + c
            return s, h, c, n, p, rows, ob, cr

        def stage1(gi):
            s, h, c, n, p, rows, ob, cr = geom(gi)
            k0 = cr * 512
            F_ = Fb[gi % 2]; bF = b_F[gi % 2]
            for i in range(4):
                q0 = s * 512 + i * 128
                zb = bank(i)
                last = (c >= 2)
                S.op("pe", "matmul", [b_qt[p][s], b_kt[p][cr]], [b_Z], zb, lhsT=QT[rows, p, q0:q0 + 128],
                     rhs=KT[rows, p, k0:k0 + 512], start=True, stop=last, _mark=(last and i == 3))
                if c < 2:
                    S.op("pe", "matmul", [b_seld, b_diag], [b_Z], zb, lhsT=seld[:, s, c, :], rhs=diag[:, i, :],
                         start=False, stop=False, _mark=False)
                    S.op("pe", "matmul", [b_seln, b_negrow], [b_Z], zb, lhsT=seln[0:1, s, c, :],
                         rhs=negrow[0:1, :], start=False, stop=True, _mark=(i == 3))
            S.op("act", "activation", [b_Z], [bF], out=F_[:, :, 1:513],
                 in_=bank(0, 4).rearrange("p (a b) -> p a b", a=4), func=AF.Sigmoid, scale=-0.125)

        def stage2(gi):
            s, h, c, n, p, rows, ob, cr = geom(gi)
            F_ = Fb[gi % 2]; bF = b_F[gi % 2]
            W_ = Wb[gi % 2]; bW = b_W[gi % 2]
            if c == 0:
                S.op("dve", "memset", [], [b_D1], D1[:, :, 0:1], 1.0)
            else:
                S.op("dve", "tensor_copy", [b_P], [b_D1], out=D1[:, :, 0:1], in_=Pb[:, :, 512:513])
            S.op("dve", "tensor_tensor_scan", [bF, b_D1], [b_P], out=Pb.rearrange("p a b -> p (a b)"),
                 data0=F_.rearrange("p a b -> p (a b)"), data1=D1.rearrange("p a b -> p (a b)"), initial=0.0,
                 op0=ALU.mult, op1=ALU.add)
            S.op("dve", "tensor_tensor", [b_P], [bW], out=W_, in0=Pb[:, :, 0:512], in1=Pb[:, :, 1:513],
                 op=ALU.subtract)

        def stage3(gi):
            s, h, c, n, p, rows, ob, cr = geom(gi)
            W_ = Wb[gi % 2]; bW = b_W[gi % 2]
            WT_ = WTb[gi % 2]; bWT = b_WT[gi % 2]
            for m in range(4):
                for i in range(4):
                    S.op("pe", "transpose", [bW, b_ident], [b_WTps], out=WTps[:, m, i * 128:(i + 1) * 128],
                         in_=W_[:, i, m * 128:(m + 1) * 128], identity=ident, _mark=(m == 3 and i == 3))
            S.op("act", "activation", [b_WTps], [bWT], out=WT_.rearrange("p a b -> p (a b)"),
                 in_=bank_bf(4, 2), func=AF.Copy)

        def stage4(gi):
            s, h, c, n, p, rows, ob, cr = geom(gi)
            WT_ = WTb[gi % 2]; bWT = b_WT[gi % 2]
            psO = psum[rows, ob * 512:(ob + 1) * 512]
            for m in range(4):
                first = (c == 0 and m == 0)
                lastpv = (c == n - 1 and m == 3)
                S.op("pe", "matmul", [bWT, b_v[cr]], [b_ps[ob]], psO, lhsT=V[:, cr * 4 + m, h * 64:(h + 1) * 64],
                     rhs=WT_[:, m, :], start=first, stop=lastpv, _mark=(m == 3))
            if c == n - 1:
                S.op("act", "activation", [b_ps[ob]], [b_oat[p][s]], out=OAT[rows, p, s * 512:(s + 1) * 512],
                     in_=psO, func=AF.Copy)

        for step in range(G + 3):
            if step < G:
                stage1(step)
            if 0 <= step - 1 < G:
                stage2(step - 1)
            if 0 <= step - 2 < G:
                stage3(step - 2)
            if 0 <= step - 3 < G:
                stage4(step - 3)
        S.barrier()

        hTo = R(DYN + 0, [128, 8, NOWN + 128], BF16); b_hTo = [Buf("hTo%d" % t) for t in range(17)]
        YBT = R(DYN + 34816, [128, 4, NOWN], BF16); b_ybt = [Buf("ybt%d" % j) for j in range(4)]
        xring2 = [R(DYN + 51200 + i * 4096, [128, D], F32) for i in range(2)]
        xn2 = [R(DYN + 59392 + i * 2048, [128, D], BF16) for i in range(2)]
        junk2 = R(DYN + 63488, [128, D], BF16)
        u_sb = R(DYN + 65536, [128, 512], F32); b_usb = Buf("usb")
        acc = R(DYN + 67584, [128, 512], F32); b_acc = Buf("acc")
        cub = R(DYN + 69632, [128, 2, 516], F32); b_cub = [Buf("cub0"), Buf("cub1")]
        MRG = R(DYN + 98304, [128, 8, NOWN], BF16)
        b_mrg = [[Buf("mrg%d_%d" % (c, s)) for s in range(4)] for c in range(8)]
        sgt = [R(DYN + 131072 + i * 2048, [128, 512], F32) for i in range(4)]; b_sgt = [Buf("sg%d" % i) for i in range(4)]
        b_x2 = [Buf("x2r%d" % i) for i in range(2)]; b_xn2 = [Buf("xn2_%d" % i) for i in range(2)]; b_junk2 = Buf("junk2")

        def load_own2(k):
            S.dma("sp", [(xring2[k % 2], xown[k * 128:(k + 1) * 128, :], {})], writes=[b_x2[k % 2]])
            return xring2[k % 2], b_x2[k % 2]

        norm_pipeline(17, load_own2, 0, xn2, b_xn2, junk2, b_junk2,
                      lambda k: (hTo[:, :, k * 128:(k + 1) * 128], b_hTo[k]), (4, 6), mm=True)

        wu, b_wu = WS.get(i_wu)
        wgb, b_wgb = WS.get(i_wgb)
        wgc, b_wgc = WS.get(i_wgc)
        cb = 0
        for j in range(4):
            jc = slice(j * 128, (j + 1) * 128)
            mm_group(bank(0)[:, 0:128], [(wu[:, kc, jc], hTo[:, kc, NOWN:NOWN + 128]) for kc in range(8)],
                     b_ps[0], [b_wu, b_hTo[16]])
            mm_group(bank(1)[:, 0:128], [(wgc[:, kc, jc], hTo[:, kc, NOWN:NOWN + 128]) for kc in range(8)],
                     b_ps[1], [b_wgc, b_hTo[16]])
            S.op("act", "activation", [b_ps[0]], [b_usb], out=u_sb[:, 0:8], in_=bank(0)[:, 0:8], func=AF.Copy)
            S.op("dve", "tensor_tensor", [b_ps[1], b_usb], [b_cuh], out=cuh, in0=bank(1)[:, 0:8], in1=u_sb[:, 0:8],
                 op=ALU.mult)
            for s in range(4):
                hs = slice(s * 512, (s + 1) * 512)
                rd = [b_hTo[s * 4 + t] for t in range(4)]
                pb = 2 + (cb % 2) * 3
                cu = cub[:, cb % 2, :]; bcu = b_cub[cb % 2]
                cb += 1
                mm_group(bank(pb), [(wu[:, kc, jc], hTo[:, kc, hs]) for kc in range(8)], b_ps[pb], [b_wu] + rd)
                mm_group(bank(pb + 1), [(wgc[:, kc, jc], hTo[:, kc, hs]) for kc in range(8)], b_ps[pb + 1], [b_wgc] + rd)
                mm_group(bank(pb + 2), [(wgb[:, kc, jc], hTo[:, kc, hs]) for kc in range(8)], b_ps[pb + 2], [b_wgb] + rd)
                S.op("act", "activation", [b_ps[pb]], [b_usb], out=u_sb, in_=bank(pb), func=AF.Copy)
                S.op("dve", "tensor_copy", [b_cuh], [bcu], out=cu[:, 0:2], in_=cuh[:, 2 * s:2 * s + 2])
                S.op("dve", "tensor_tensor", [b_ps[pb + 1], b_usb], [bcu], out=cu[:, 2:514], in0=bank(pb + 1), in1=u_sb,
                     op=ALU.mult)
                S.op("dve", "tensor_scalar", [bcu, b_convw], [b_acc], out=acc, in0=cu[:, 2:514],
                     scalar1=convw[:, j, 2:3], scalar2=None, op0=ALU.mult)
                S.op("dve", "scalar_tensor_tensor", [bcu, b_convw, b_acc], [b_acc], out=acc, in0=cu[:, 1:513],
                     scalar=convw[:, j, 1:2], in1=acc, op0=ALU.mult, op1=ALU.add)
                S.op("dve", "scalar_tensor_tensor", [bcu, b_convw, b_acc], [b_acc], out=acc, in0=cu[:, 0:512],
                     scalar=convw[:, j, 0:1], in1=acc, op0=ALU.mult, op1=ALU.add)
                S.op("dve", "tensor_tensor", [b_ps[pb + 2], b_acc], [b_ybt[j]], out=YBT[:, j, hs], in0=bank(pb + 2),
                     in1=acc, op=ALU.mult)
        WS.release(i_wu); WS.release(i_wgb); WS.release(i_wgc)

        it = 0
        for c4 in range(2):
            ia, iga, igb = i_d[c4]
            wab, b_wab = WS.get(ia)
            wga, b_wga = WS.get(iga)
            wgB, b_wgB = WS.get(igb)
            for cc in range(4):
                c = c4 * 4 + cc
                ccs = slice(cc * 128, (cc + 1) * 128)
                for s in range(4):
                    hs = slice(s * 512, (s + 1) * 512)
                    rd = [b_hTo[s * 4 + t] for t in range(4)]
                    pb = (it % 2) * 4
                    sa = sgt[(it % 2) * 2]; bsa = b_sgt[(it % 2) * 2]
                    sb = sgt[(it % 2) * 2 + 1]; bsb = b_sgt[(it % 2) * 2 + 1]
                    it += 1
                    mm_group(bank(pb), [(wab[:, kc, ccs], OAT[:, kc, hs]) for kc in range(4)], b_ps[pb],
                             [b_wab] + [b_oat[kc][s] for kc in range(4)])
                    mm_group(bank(pb + 1), [(wab[:, 4 + kc, ccs], YBT[:, kc, hs]) for kc in range(4)], b_ps[pb + 1],
                             [b_wab] + b_ybt)
                    mm_group(bank(pb + 2), [(wga[:, kc, ccs], hTo[:, kc, hs]) for kc in range(8)], b_ps[pb + 2], [b_wga] + rd)
                    mm_group(bank(pb + 3), [(wgB[:, kc, ccs], hTo[:, kc, hs]) for kc in range(8)], b_ps[pb + 3], [b_wgB] + rd)
                    S.op("act", "activation", [b_ps[pb + 2]], [bsa], out=sa, in_=bank(pb + 2), func=AF.Sigmoid)
                    S.op("act", "activation", [b_ps[pb + 3]], [bsb], out=sb, in_=bank(pb + 3), func=AF.Sigmoid)
                    S.op("dve", "tensor_tensor", [b_ps[pb], bsa], [bsa], out=sa, in0=bank(pb), in1=sa, op=ALU.mult)
                    S.op("dve", "tensor_tensor", [b_ps[pb + 1], bsb], [bsb], out=sb, in0=bank(pb + 1), in1=sb, op=ALU.mult)
                    S.op("dve", "tensor_tensor", [bsa, bsb], [b_mrg[c][s]], out=MRG[:, c, hs], in0=sa, in1=sb, op=ALU.add)
            WS.release(ia); WS.release(iga); WS.release(igb)
        S.barrier()

        X = R(DYN + 0, [128, 16, D], F32); b_X = [Buf("X%d" % t) for t in range(16)]
        for t in range(16):
            S.dma("sp", [(X[:, t, :], xown[t * 128:(t + 1) * 128, :], {})], writes=[b_X[t]])
        rb = [0]
        for hf in range(2):
            wm, b_wm = WS.get(i_wmo[hf])
            for t in range(16):
                bk = rb[0] % 8; rb[0] += 1
                mm_group(bank(bk), [(MRG[:, kc, t * 128:(t + 1) * 128], wm[:, kc, :]) for kc in range(8)], b_ps[bk],
                         [b_wm] + [b_mrg[kc][t // 4] for kc in range(8)])
                xs = X[:, t, hf * 512:(hf + 1) * 512]
                S.op("dve", "tensor_tensor", [b_ps[bk], b_X[t]], [b_X[t]], out=xs, in0=bank(bk), in1=xs, op=ALU.add)
            WS.release(i_wmo[hf])
        S.barrier()

        hTs = [R(DYN + 65536 + i * 8192, [128, 8, 512], BF16) for i in range(2)]
        b_hTs = [[Buf("hTs%d_%d" % (i, t)) for t in range(4)] for i in range(2)]
        qmT = R(DYN + 81920, [128, 8, 512], BF16); b_qmT = [Buf("qmT%d" % c) for c in range(8)]
        omT = R(DYN + 90112, [128, 8, 512], BF16); b_omT = [Buf("omT%d" % c) for c in range(8)]
        mT = R(DYN + 98304, [128, 8, 256], BF16); b_mT = [Buf("mT0"), Buf("mT1")]
        KmT = R(DYN + 102400, [128, 8, 256], BF16); b_KmT = Buf("KmT")
        Vm = R(DYN + 106496, [128, 2, D], BF16); b_Vm = Buf("Vm")
        Esm = R(DYN + 110592, [128, 4, 256], F32); b_Esm = Buf("Esm")
        probs = [R(DYN + 114688 + i * 2048, [128, 4, 256], BF16) for i in range(2)]; b_probs = [Buf("pr0"), Buf("pr1")]
        pT = R(DYN + 118784, [128, 8, 512], BF16); b_pT = Buf("pT")
        xn3 = [R(DYN + 126976 + i * 2048, [128, D], BF16) for i in range(2)]; b_xn3 = [Buf("xn3_0"), Buf("xn3_1")]
        junk3 = R(DYN + 131072, [128, D], BF16); b_junk3 = Buf("junk3")
        memring = [R(DYN + 118784 + i * 4096, [128, D], F32) for i in range(2)]; b_mr = [Buf("mr0"), Buf("mr1")]

        load_gain(1, norm_mem_kv)
        load_gain(0, norm_mem_q)
        for mt in range(2):
            S.dma("sp", [(memring[mt], memb[mt * 128:(mt + 1) * 128, :], {})], writes=[b_mr[mt]])
            norm_tile(memring[mt], b_mr[mt], 1, xn3[mt], b_xn3[mt], junk3, b_junk3)
            transpose_tile(xn3[mt], b_xn3[mt], mT[:, :, mt * 128:(mt + 1) * 128], b_mT[mt], (4, 5))
        wK = [WS.get(i_wkvK[k]) for k in range(2)]
        for c in range(8):
            bk = 6 + c % 2
            w_, bw_ = wK[c // 4]
            mm_group(bank(bk)[:, 0:256], [(w_[:, kc, (c % 4) * 128:(c % 4 + 1) * 128], mT[:, kc, :]) for kc in range(8)],
                     b_ps[bk], [bw_] + b_mT)
            evac(KmT[:, c, :], bank(bk)[:, 0:256], [b_ps[bk]], [b_KmT])
        WS.release(i_wkvK[0]); WS.release(i_wkvK[1])
        wVv = [WS.get(i_wkvV[k]) for k in range(2)]
        for mt in range(2):
            for hf in range(2):
                bk = 6 + hf
                w_, bw_ = wVv[hf]
                mm_group(bank(bk), [(mT[:, kc, mt * 128:(mt + 1) * 128], w_[:, kc, :]) for kc in range(8)],
                         b_ps[bk], [bw_, b_mT[mt]])
                evac(Vm[:, mt, hf * 512:(hf + 1) * 512], bank(bk), [b_ps[bk]], [b_Vm])
        WS.release(i_wkvV[0]); WS.release(i_wkvV[1])
        S.barrier()

        wMQ = [WS.get(i_wmq[k]) for k in range(2)]
        wMO = [WS.get(i_wmo2[k]) for k in range(2)]
        sc_it = 0
        gb = [0]
        for s in range(4):
            hb = hTs[s % 2]; bh = b_hTs[s % 2]
            norm_pipeline(4, lambda k, s=s: (X[:, s * 4 + k, :], b_X[s * 4 + k]), 0, xn3, b_xn3, junk3, b_junk3,
                          lambda k, hb=hb, bh=bh: (hb[:, :, k * 128:(k + 1) * 128], bh[k]), (4, 5))
            for c in range(8):
                bk = 6 + gb[0] % 2; gb[0] += 1
                w_, bw_ = wMQ[c // 4]
                mm_group(bank(bk), [(w_[:, kc, (c % 4) * 128:(c % 4 + 1) * 128], hb[:, kc, :]) for kc in range(8)],
                         b_ps[bk], [bw_] + bh)
                evac(qmT[:, c, :], bank(bk), [b_ps[bk]], [b_qmT[c]])

            def sA(t):
                sb0 = (t % 2) * 2
                psS = bank(sb0, 2).rearrange("p (a b) -> p a b", a=4)
                for h in range(4):
                    for cc in range(2):
                        c = 2 * h + cc
                        S.op("pe", "matmul", [b_qmT[c], b_KmT], [b_ps[sb0]], psS[:, h, :],
                             lhsT=qmT[:, c, t * 128:(t + 1) * 128], rhs=KmT[:, c, :], start=(cc == 0), stop=(cc == 1),
                             _mark=(h == 3 and cc == 1))

            def sB(t):
                sb0 = (t % 2) * 2
                psS = bank(sb0, 2).rearrange("p (a b) -> p a b", a=4)
                b_S = b_ps[sb0]
                pr = probs[t % 2]; bpr = b_probs[t % 2]
                S.op("dve", "tensor_reduce", [b_S], [b_mx], out=mx4, in_=psS, axis=AX.X, op=ALU.max)
                S.op("dve", "tensor_scalar", [b_mx], [b_nb], out=nb4, in0=mx4, scalar1=-1.0 / 16, scalar2=None, op0=ALU.mult)
                for h in range(4):
                    S.op("act", "activation", [b_S, b_nb], [b_Esm, b_sm], out=Esm[:, h, :], in_=psS[:, h, :], func=AF.Exp,
                         scale=1.0 / 16, bias=nb4[:, h:h + 1], accum_out=sm4[:, h:h + 1])
                S.op("dve", "reciprocal", [b_sm], [b_rs], out=rs4, in_=sm4)
                for h in range(4):
                    S.op("dve", "tensor_scalar", [b_Esm, b_rs], [bpr], out=pr[:, h, :], in0=Esm[:, h, :],
                         scalar1=rs4[:, h:h + 1], scalar2=None, op0=ALU.mult)

            def sC(t):
                pr = probs[t % 2]; bpr = b_probs[t % 2]
                tb_ = 4 + (t % 2)
                pst = bank_bf(tb_)
                for h in range(4):
                    for mt in range(2):
                        k8 = h * 2 + mt
                        S.op("pe", "transpose", [bpr, b_ident], [b_ps[tb_]], out=pst[:, k8 * 128:(k8 + 1) * 128],
                             in_=pr[:, h, mt * 128:(mt + 1) * 128], identity=ident, _mark=(k8 == 7))
                S.op("act", "activation", [b_ps[tb_]], [b_pT], out=pT[:, :, t * 128:(t + 1) * 128],
                     in_=pst.rearrange("p (a b) -> p a b", a=8), func=AF.Copy)

            for step in range(6):
                if step < 4:
                    sA(step)
                if 0 <= step - 1 < 4:
                    sB(step - 1)
                if 0 <= step - 2 < 4:
                    sC(step - 2)
            for h in range(4):
                for dc in range(2):
                    c = h * 2 + dc
                    bk = 6 + gb[0] % 2; gb[0] += 1
                    mm_group(bank(bk), [(Vm[:, mt, c * 128:(c + 1) * 128], pT[:, h * 2 + mt, :]) for mt in range(2)],
                             b_ps[bk], [b_Vm, b_pT])
                    evac(omT[:, c, :], bank(bk), [b_ps[bk]], [b_omT[c]])
            for t in range(4):
                tt = s * 4 + t
                for hf in range(2):
                    bk = 6 + gb[0] % 2; gb[0] += 1
                    w_, bw_ = wMO[hf]
                    mm_group(bank(bk), [(omT[:, kc, t * 128:(t + 1) * 128], w_[:, kc, :]) for kc in range(8)],
                             b_ps[bk], [bw_] + b_omT)
                    xs = X[:, tt, hf * 512:(hf + 1) * 512]
                    S.op("dve", "tensor_tensor", [b_ps[bk], b_X[tt]], [b_X[tt]], out=xs, in0=bank(bk), in1=xs, op=ALU.add)
        for k in range(2):
            WS.release(i_wmq[k]); WS.release(i_wmo2[k])
        S.barrier()

        hTb = R(DYN + 65536, [128, 8, 1024], BF16); b_hTb = [Buf("hTb%d" % t) for t in range(8)]
        aT = R(DYN + 81920, [128, 22, 1024], BF16); b_aT = [Buf("aT%d" % j) for j in range(22)]
        sgl = [R(DYN + 126976 + i * 2048, [128, 512], F32) for i in range(2)]; b_sgl = [Buf("sgl0"), Buf("sgl1")]
        xn4 = [R(DYN + 131072 + i * 2048, [128, D], BF16) for i in range(2)]; b_xn4 = [Buf("xn4_0"), Buf("xn4_1")]
        junk4 = R(DYN + 135168, [128, D], BF16); b_junk4 = Buf("junk4")
        load_gain(1, norm_ffn)
        fit = 0
        for tb in range(2):
            gu, fo = i_ffn[tb]
            norm_pipeline(8, lambda k, tb=tb: (X[:, tb * 8 + k, :], b_X[tb * 8 + k]), 1, xn4, b_xn4, junk4, b_junk4,
                          lambda k: (hTb[:, :, k * 128:(k + 1) * 128], b_hTb[k]), (4, 6), mm=True)
            for blk in range(6):
                ig, iu, ncol = gu[blk]
                wg_, b_wg = WS.get(ig)
                wu_, b_wu_ = WS.get(iu)
                for cc in range(ncol // 128):
                    j = blk * 4 + cc
                    ccs = slice(cc * 128, (cc + 1) * 128)
                    for s2 in range(2):
                        hs = slice(s2 * 512, (s2 + 1) * 512)
                        rd = b_hTb[s2 * 4:(s2 + 1) * 4]
                        pb = (fit % 3) * 2
                        sg_ = sgl[fit % 2]; bsg = b_sgl[fit % 2]
                        fit += 1
                        mm_group(bank(pb), [(wg_[:, kc, ccs], hTb[:, kc, hs]) for kc in range(8)], b_ps[pb], [b_wg] + rd)
                        mm_group(bank(pb + 1), [(wu_[:, kc, ccs], hTb[:, kc, hs]) for kc in range(8)], b_ps[pb + 1],
                                 [b_wu_] + rd)
                        S.op("act", "activation", [b_ps[pb]], [bsg], out=sg_, in_=bank(pb), func=AF.Silu)
                        S.op("dve", "tensor_tensor", [b_ps[pb + 1], bsg], [b_aT[j]], out=aT[:, j, hs], in0=bank(pb + 1),
                             in1=sg_, op=ALU.mult)
                WS.release(ig); WS.release(iu)
            for hf in range(2):
                wfo = [WS.get(fo[hf][k3]) for k3 in range(3)]
                for t in range(8):
                    tt = tb * 8 + t
                    bk = 6 + (fit % 2); fit += 1
                    mm_group(bank(bk), [(aT[:, j, t * 128:(t + 1) * 128], wfo[j // 8][0][:, j % 8, :]) for j in range(22)],
                             b_ps[bk], [w[1] for w in wfo] + b_aT)
                    xs = X[:, tt, hf * 512:(hf + 1) * 512]
                    S.op("dve", "tensor_tensor", [b_ps[bk], b_X[tt]], [b_X[tt]], out=xs, in0=bank(bk), in1=xs, op=ALU.add)
                for k3 in range(3):
                    WS.release(fo[hf][k3])
        S.barrier()

        otmp = [R(DYN + 65536 + i * 4096, [128, D], F32) for i in range(2)]; b_ot = [Buf("ot0"), Buf("ot1")]
        junk5 = R(DYN + 73728, [128, D], BF16); b_junk5 = Buf("junk5")
        load_gain(0, norm_final)
        out_evs = []
        for tt in range(16):
            i = tt % 2
            norm_tile(X[:, tt, :], b_X[tt], 0, otmp[i], b_ot[i], junk5, b_junk5)
            out_evs.append(S.dma("sp", [(out_d[tt * 128:(tt + 1) * 128, :], otmp[i], {})], reads=[b_ot[i]]))
        for ev in out_evs:
            S._wait("sp", ev)
        print("inst counts", S.n_inst, {e: len(S.prog[e]) for e in S.ENGS})
        S.emit()
    return nc


def _host_inputs(inputs):
    x = np.asarray(inputs["x"], dtype=np.float32)
    mem = np.asarray(inputs["mem"], dtype=np.float32)
    diag = np.zeros((128, 4, 512), np.float32)
    pp = np.arange(128)[:, None]
    kr = np.arange(512)[None, :]
    for i in range(4):
        ql = i * 128 + pp
        diag[:, i, :] = np.where(kr > 511 - ql, 0.0, NEG)
    eye = np.eye(128, dtype=np.float32)
    shared = {
        "diag": diag,
        "norm_mix": np.ascontiguousarray(inputs["norm_mix"][0:1]),
        "w_in": np.ascontiguousarray(inputs["w_in"][0]),
        "conv_w": np.ascontiguousarray(inputs["conv_w"][0]),
        "w_branch_a": np.ascontiguousarray(inputs["w_branch_a"][0]),
        "w_branch_b": np.ascontiguousarray(inputs["w_branch_b"][0]),
        "w_mix_out": np.ascontiguousarray(inputs["w_mix_out"][0]),
        "norm_mem_q": np.ascontiguousarray(inputs["norm_mem_q"][0:1]),
        "norm_mem_kv": np.ascontiguousarray(inputs["norm_mem_kv"][0:1]),
        "w_mem_q": np.ascontiguousarray(inputs["w_mem_q"][0]),
        "w_mem_kv": np.ascontiguousarray(inputs["w_mem_kv"][0]),
        "w_mem_o": np.ascontiguousarray(inputs["w_mem_o"][0]),
        "norm_ffn": np.ascontiguousarray(inputs["norm_ffn"][0:1]),
        "w_ffn_in": np.ascontiguousarray(inputs["w_ffn_in"][0]),
        "w_ffn_out": np.ascontiguousarray(inputs["w_ffn_out"][0]),
        "norm_final": np.ascontiguousarray(np.asarray(inputs["norm_final"]).reshape(1, D)),
    }
    shared = {k: np.asarray(v, dtype=np.float32) for k, v in shared.items()}
    in_maps = []
    for core in range(8):
        b, par = core // 2, core % 2
        tiles = T_OF[par]
        xown = np.zeros((NOWN + 128, D), np.float32)
        seld = np.zeros((128, 4, 2, 128), np.float32)
        seln = np.zeros((1, 4, 2, 128), np.float32)
        for j, T in enumerate(tiles):
            xown[j * 512:(j + 1) * 512] = x[b, T * 512:(T + 1) * 512]
            if T > 0:
                xown[NOWN + 2 * j:NOWN + 2 * j + 2] = x[b, T * 512 - 2:T * 512]
            tmax = NCH[j] - 1
            if T == tmax:
                seld[:, j, 0, :] = eye
            else:
                seln[0, j, 0, :] = 1.0
                seld[:, j, 1, :] = eye
        m = dict(shared)
        m["xall"] = np.ascontiguousarray(x[b, ::-1, :])
        m["xown"] = xown
        m["memb"] = np.ascontiguousarray(mem[b])
        m["seld"] = seld
        m["seln"] = seln
        in_maps.append(m)
    return in_maps


_NC_CACHE = {}


def kernel(**inputs):
    in_maps = _host_inputs(inputs)
    if "nc" not in _NC_CACHE:
        _NC_CACHE["nc"] = build_nc()
    nc = _NC_CACHE["nc"]
    res = run_bass_kernel_spmd(nc, in_maps, core_ids=list(range(8)))
    out = np.zeros((NB, SEQ, D), np.float32)
    for core in range(8):
        b, par = core // 2, core % 2
        o = np.asarray(res.results[core]["out"], dtype=np.float32)
        for j, T in enumerate(T_OF[par]):
            out[b, T * 512:(T + 1) * 512] = o[j * 512:(j + 1) * 512]
    return out
```

```python
import numpy as np
import concourse.bass as bass
import concourse.mybir as mybir
from concourse.bass_utils import run_bass_kernel_spmd
from contextlib import ExitStack

F32 = mybir.dt.float32
BF16 = mybir.dt.bfloat16
AF = mybir.ActivationFunctionType
ALU = mybir.AluOpType
AX = mybir.AxisListType

D = 1024
SEQ = 4096
NB = 4
TS = 512
T_OF = {0: (0, 3, 4, 7), 1: (1, 2, 5, 6)}
NCH = (2, 4, 6, 8)
NEG = -30000.0
EPS = 1e-6
FFN_H = 2816
NOWN = 2048
SAME_ENGINE_SYNC = True


class Buf:
    __slots__ = ("name", "w", "r")

    def __init__(self, name):
        self.name = name
        self.w = None
        self.r = {}


class Sched:
    ENGS = ("pe", "act", "dve", "pool", "sp")

    def __init__(self, nc, stack, n_dma_sems=8):
        self.nc = nc
        self.prog = {e: [] for e in self.ENGS}
        self.count = {e: 0 for e in self.ENGS}
        self.sem = {e: stack.enter_context(nc.semaphore("s_" + e)) for e in self.ENGS}
        self.seen = {e: {} for e in self.ENGS}
        self.dsem = {}
        self.dval = {}
        self.dring = {}
        self.dpos = {}
        idx = 0
        for q in ("sp", "pool"):
            ring = []
            for i in range(n_dma_sems):
                self.dsem[idx] = stack.enter_context(nc.semaphore("d_%s%d" % (q, i)))
                self.dval[idx] = 0
                ring.append(idx)
                idx += 1
            self.dring[q] = ring
            self.dpos[q] = 0
        self.n_inst = {e: 0 for e in self.ENGS}
        self.last_marked = {e: True for e in self.ENGS}

    def _wait(self, e, ev):
        if ev is None:
            return
        if ev[0] == "e":
            _, src, seq = ev
            if src == e and (e == "pe" or not SAME_ENGINE_SYNC):
                return
            assert self.count[src] >= seq, ("dependency on unissued mark", e, ev)
            key = ("e", src)
            val = seq
            sem = self.sem[src]
        else:
            _, sidx, val = ev
            key = ("d", sidx)
            sem = self.dsem[sidx]
        if self.seen[e].get(key, 0) >= val:
            return
        self.seen[e][key] = val
        self.prog[e].append(lambda eng, sem=sem, val=val: eng.wait_ge(sem, val))

    def _deps(self, e, reads, writes):
        for b in reads:
            self._wait(e, b.w)
        for b in writes:
            self._wait(e, b.w)
            for (k0, k1), v in list(b.r.items()):
                self._wait(e, (k0, k1, v))

    def _record(self, ev, reads, writes):
        key = (ev[0], ev[1])
        for b in reads:
            if b.r.get(key, 0) < ev[2]:
                b.r[key] = ev[2]
        for b in writes:
            b.w = ev
            b.r = {}

    def op(self, e, meth, reads, writes, *args, _mark=True, **kw):
        self._deps(e, reads, writes)
        sem = self.sem[e]
        if _mark:
            self.count[e] += 1
            self.prog[e].append(lambda eng: getattr(eng, meth)(*args, **kw).then_inc(sem, 1))
            ev = ("e", e, self.count[e])
        else:
            self.prog[e].append(lambda eng: getattr(eng, meth)(*args, **kw))
            ev = ("e", e, self.count[e] + 1)
        self.last_marked[e] = _mark
        self.n_inst[e] += 1
        self._record(ev, reads, writes)
        return ev

    def dma(self, q, xfers, reads=(), writes=()):
        self._deps(q, reads, writes)
        ring = self.dring[q]
        sidx = ring[self.dpos[q] % len(ring)]
        self.dpos[q] += 1
        if self.dval[sidx] > 0:
            self._wait(q, ("d", sidx, self.dval[sidx]))
        sem = self.dsem[sidx]
        for (o, i, kw) in xfers:
            self.dval[sidx] += 16
            self.prog[q].append(lambda eng, o=o, i=i, kw=kw: eng.dma_start(out=o, in_=i, **kw).then_inc(sem, 16))
        ev = ("d", sidx, self.dval[sidx])
        self._record(ev, reads, writes)
        return ev

    def barrier(self):
        for e in ("pe", "act", "dve"):
            assert self.last_marked[e], ("barrier with unmarked tail", e)
        for sidx in self.dring["sp"]:
            if self.dval[sidx] > 0:
                self._wait("sp", ("d", sidx, self.dval[sidx]))
        for f in ("pe", "act", "dve"):
            self._wait("sp", ("e", f, self.count[f]))
        self.count["sp"] += 1
        sem = self.sem["sp"]
        self.prog["sp"].append(lambda eng, sem=sem: eng.sem_inc(sem, 1))
        for e in ("pe", "act", "dve"):
            self._wait(e, ("e", "sp", self.count["sp"]))

    def emit(self):
        with self.nc.Block() as block:
            def mk(name):
                def body(engine):
                    for c in self.prog[name]:
                        c(engine)
                return body
            block.sync(mk("sp"))
            block.gpsimd(mk("pool"))
            block.scalar(mk("act"))
            block.vector(mk("dve"))
            block.tensor(mk("pe"))


def build_nc():
    nc = bass.Bass("TRN2", target_bir_lowering=False)

    def din(name, shape):
        return nc.dram_tensor(name, list(shape), F32, kind="ExternalInput").ap()

    xall = din("xall", [SEQ, D])
    xown = din("xown", [NOWN + 128, D])
    memb = din("memb", [256, D])
    seld_d = din("seld", [128, 4, 2, 128])
    seln_d = din("seln", [1, 4, 2, 128])
    diag_d = din("diag", [128, 4, 512])
    norm_mix = din("norm_mix", [1, D])
    w_in = din("w_in", [D, 5120])
    conv_w = din("conv_w", [3, 512])
    w_ba = din("w_branch_a", [512, D])
    w_bb = din("w_branch_b", [512, D])
    w_mix_out = din("w_mix_out", [D, D])
    norm_mem_q = din("norm_mem_q", [1, D])
    norm_mem_kv = din("norm_mem_kv", [1, D])
    w_mem_q = din("w_mem_q", [D, D])
    w_mem_kv = din("w_mem_kv", [D, 2 * D])
    w_mem_o = din("w_mem_o", [D, D])
    norm_ffn = din("norm_ffn", [1, D])
    w_ffn_in = din("w_ffn_in", [D, 2 * FFN_H])
    w_ffn_out = din("w_ffn_out", [FFN_H, D])
    norm_final = din("norm_final", [1, D])
    out_d = nc.dram_tensor("out", [NOWN, D], F32, kind="ExternalOutput").ap()

    with ExitStack() as st:
        S = Sched(nc, st)
        ARENA_F32 = 52900
        arena = st.enter_context(nc.sbuf_tensor("arena", [128, ARENA_F32], F32))
        psum = st.enter_context(nc.psum_tensor("psum", [128, 4096], F32))

        def R(off, shape, dt):
            esz = 4 if dt == F32 else 2
            n = int(np.prod(shape[1:]))
            nbytes = n * esz
            assert off % 4 == 0 and nbytes % 4 == 0, (off, shape)
            assert off + nbytes <= ARENA_F32 * 4, (off, shape)
            ap = arena[:, off // 4:(off + nbytes) // 4]
            if dt != F32:
                ap = ap.bitcast(dt)
            if len(shape) == 3:
                ap = ap.rearrange("p (a b) -> p a b", a=shape[1])
            elif len(shape) == 4:
                ap = ap.rearrange("p (a b c) -> p a b c", a=shape[1], b=shape[2])
            return ap

        def bank(i, n=1):
            return psum[:, i * 512:(i + n) * 512]

        def bank_bf(i, n=1):
            return psum[:, i * 512:(i + n) * 512].bitcast(BF16)

        b_ps = [Buf("ps%d" % i) for i in range(8)]

        ident = R(0, [128, 128], BF16); b_ident = Buf("ident")
        identf = R(256, [128, 128], F32); b_identf = Buf("identf")
        convw = R(768, [128, 4, 3], F32); b_convw = Buf("convw")
        stats = R(1024, [128, 256], F32)
        seld = R(2048, [128, 4, 2, 128], BF16); b_seld = Buf("seld")
        seln = R(4096, [128, 4, 2, 128], BF16); b_seln = Buf("seln")
        negrow = R(6144, [128, 512], BF16); b_negrow = Buf("negrow")
        zeros = R(7168, [128, 512], F32); b_zeros = Buf("zeros")
        diag = R(9216, [128, 4, 512], BF16); b_diag = Buf("diag")
        GAIN0 = 13312
        g_rep = [R(GAIN0 + i * 4096, [128, D], F32) for i in range(2)]
        b_g = [Buf("g%d" % i) for i in range(2)]
        RING0 = 21504
        NSLOT = 5
        DYN = RING0 + NSLOT * 8192

        ssq = stats[:, 0:8]; rstd = stats[:, 8:16]
        b_st = [Buf("st%d" % i) for i in range(8)]
        st_pos = [0]
        mx4 = stats[:, 16:20]; nb4 = stats[:, 20:24]; sm4 = stats[:, 24:28]; rs4 = stats[:, 28:32]
        b_mx = Buf("mx"); b_nb = Buf("nb"); b_sm = Buf("sm"); b_rs = Buf("rs")
        cuh = stats[:, 32:40]; b_cuh = Buf("cuh")

        class WStream:
            def __init__(self):
                self.blocks = []
                self.issued = 0
                self.released = set()
                self.bufs = [Buf("ring%d" % i) for i in range(NSLOT)]
                self.next_get = 0

            def add(self, pieces):
                self.blocks.append(pieces)
                return len(self.blocks) - 1

            def slot_ap(self, i):
                return R(RING0 + (i % NSLOT) * 8192, [128, 8, 512], BF16)

            def _issue(self):
                while self.issued < len(self.blocks):
                    k = self.issued
                    if k >= NSLOT and (k - NSLOT) not in self.released:
                        break
                    if k > self.next_get + NSLOT - 1:
                        break
                    sl = self.slot_ap(k)
                    xf = []
                    for (src, kc0, kcn, ncols) in self.blocks[k]:
                        xf.append((sl[:, kc0:kc0 + kcn, 0:ncols], src, {}))
                    S.dma("pool", xf, writes=[self.bufs[k % NSLOT]])
                    self.issued += 1

            def get(self, idx):
                assert idx == self.next_get, (idx, self.next_get)
                self.next_get += 1
                self._issue()
                assert self.issued > idx, ("weight block not issued", idx)
                return self.slot_ap(idx), self.bufs[idx % NSLOT]

            def release(self, idx):
                self.released.add(idx)
                self._issue()

        WS = WStream()

        def wpiece(w, r0, nrows, c0, ncols, kc0=0):
            src = w[r0:r0 + nrows, c0:c0 + ncols].rearrange("(kc p) n -> p kc n", p=128)
            return (src, kc0, nrows // 128, ncols)

        i_wk = WS.add([wpiece(w_in, 0, D, 512, 512)])
        i_wv = WS.add([wpiece(w_in, 0, D, 1024, 512)])
        i_wq = WS.add([wpiece(w_in, 0, D, 0, 512)])
        i_wu = WS.add([wpiece(w_in, 0, D, 1536, 512)])
        i_wgb = WS.add([wpiece(w_in, 0, D, 2048, 512)])
        i_wgc = WS.add([wpiece(w_in, 0, D, 2560, 512)])
        i_d = []
        for c4 in range(2):
            a = WS.add([wpiece(w_ba, 0, 512, c4 * 512, 512, kc0=0), wpiece(w_bb, 0, 512, c4 * 512, 512, kc0=4)])
            b_ = WS.add([wpiece(w_in, 0, D, 3072 + c4 * 512, 512)])
            c_ = WS.add([wpiece(w_in, 0, D, 4096 + c4 * 512, 512)])
            i_d.append((a, b_, c_))
        i_wmo = [WS.add([wpiece(w_mix_out, 0, D, hf * 512, 512)]) for hf in range(2)]
        i_wkvK = [WS.add([wpiece(w_mem_kv, 0, D, k * 512, 512)]) for k in range(2)]
        i_wkvV = [WS.add([wpiece(w_mem_kv, 0, D, D + k * 512, 512)]) for k in range(2)]
        i_wmq = [WS.add([wpiece(w_mem_q, 0, D, k * 512, 512)]) for k in range(2)]
        i_wmo2 = [WS.add([wpiece(w_mem_o, 0, D, k * 512, 512)]) for k in range(2)]
        i_ffn = []
        for tb in range(2):
            gu = []
            for blk in range(6):
                ncol = 512 if blk < 5 else 256
                ig = WS.add([wpiece(w_ffn_in, 0, D, blk * 512, ncol)])
                iu = WS.add([wpiece(w_ffn_in, 0, D, FFN_H + blk * 512, ncol)])
                gu.append((ig, iu, ncol))
            fo = []
            for hf in range(2):
                ks = []
                for k3 in range(3):
                    nr = 1024 if k3 < 2 else FFN_H - 2048
                    ks.append(WS.add([wpiece(w_ffn_out, k3 * 1024, nr, hf * 512, 512)]))
                fo.append(ks)
            i_ffn.append((gu, fo))

        evac_rr = [0]

        def evac(out_ap, in_ap, reads, writes, eng=None):
            if eng is None:
                eng = ("act", "dve")[evac_rr[0] % 2]
                evac_rr[0] += 1
            if eng == "act":
                return S.op("act", "activation", reads, writes, out=out_ap, in_=in_ap, func=AF.Copy)
            return S.op("dve", "tensor_copy", reads, writes, out=out_ap, in_=in_ap)

        def mm_group(out_ap, pairs, b_out, reads):
            n = len(pairs)
            for k, (l, r) in enumerate(pairs):
                S.op("pe", "matmul", reads, [b_out], out_ap, lhsT=l, rhs=r, start=(k == 0), stop=(k == n - 1),
                     _mark=(k == n - 1))

        def load_gain(i, src):
            S.dma("sp", [(g_rep[i], src.broadcast_to([128, D]), {})], writes=[b_g[i]])

        def norm_tile(x_ap, b_x, gi, out_ap, b_out, junk_ap, b_junk_):
            k = st_pos[0] % 8
            st_pos[0] += 1
            bs = b_st[k]
            S.op("act", "activation", [b_x], [b_junk_, bs], out=junk_ap, in_=x_ap, func=AF.Square,
                 accum_out=ssq[:, k:k + 1])
            S.op("act", "activation", [bs], [bs], out=rstd[:, k:k + 1], in_=ssq[:, k:k + 1], func=AF.Ln,
                 scale=1.0 / D, bias=EPS)
            S.op("act", "activation", [bs], [bs], out=rstd[:, k:k + 1], in_=rstd[:, k:k + 1], func=AF.Exp, scale=-0.5)
            S.op("dve", "scalar_tensor_tensor", [b_x, bs, b_g[gi]], [b_out], out=out_ap, in0=x_ap,
                 scalar=rstd[:, k:k + 1], in1=g_rep[gi], op0=ALU.mult, op1=ALU.mult)

        tr_rr = [0]

        def transpose_tile(xn_ap, b_xn_, dst3, b_dst, tbanks, mm=False):
            bk = tbanks[tr_rr[0] % len(tbanks)]
            tr_rr[0] += 1
            if mm:
                pst = bank(bk, 2)
                bb = [b_ps[bk], b_ps[bk + 1]]
                for c in range(8):
                    S.op("pe", "matmul", [b_xn_, b_ident], bb, pst[:, c * 128:(c + 1) * 128],
                         lhsT=xn_ap[:, c * 128:(c + 1) * 128], rhs=ident, start=True, stop=True, _mark=(c == 7))
                evac(dst3, pst.rearrange("p (a b) -> p a b", a=8), bb, [b_dst])
                return
            pst = bank_bf(bk)
            for c in range(8):
                S.op("pe", "transpose", [b_xn_, b_ident], [b_ps[bk]], out=pst[:, c * 128:(c + 1) * 128],
                     in_=xn_ap[:, c * 128:(c + 1) * 128], identity=ident, _mark=(c == 7))
            S.op("act", "activation", [b_ps[bk]], [b_dst], out=dst3, in_=pst.rearrange("p (a b) -> p a b", a=8),
                 func=AF.Copy)

        def norm_pipeline(n, load_fn, gi, xn_bufs, b_xn_bufs, junk_ap, b_junk_, dst_fn, tbanks, after_fn=None, mm=False):
            nb_ = len(xn_bufs)
            la = nb_ - 1

            def pre(k):
                x_ap, b_x = load_fn(k)
                norm_tile(x_ap, b_x, gi, xn_bufs[k % nb_], b_xn_bufs[k % nb_], junk_ap, b_junk_)

            def post(k):
                dst3, b_dst = dst_fn(k)
                transpose_tile(xn_bufs[k % nb_], b_xn_bufs[k % nb_], dst3, b_dst, tbanks, mm=mm)
            for k in range(min(la, n)):
                pre(k)
            for k in range(n):
                if k + la < n:
                    pre(k + la)
                post(k)
                if after_fn is not None:
                    after_fn(k)

        S.op("dve", "memset", [], [b_identf], identf, 1.0)
        S.op("pool", "affine_select", [b_identf], [b_identf], out=identf, in_=identf, pattern=[[-1, 128]],
             compare_op=ALU.is_equal, fill=0.0, base=0, channel_multiplier=1)
        S.op("dve", "tensor_copy", [b_identf], [b_ident], out=ident, in_=identf)
        S.op("dve", "memset", [], [b_negrow], negrow, NEG)
        S.op("dve", "memset", [], [b_zeros], zeros, 0.0)
        S.dma("pool", [(seld, seld_d, {})], writes=[b_seld])
        S.dma("pool", [(seln[0:1], seln_d, {})], writes=[b_seln])
        S.dma("pool", [(diag, diag_d, {})], writes=[b_diag])
        S.dma("sp", [(convw[:, j, :], conv_w[:, j * 128:(j + 1) * 128].rearrange("i p -> p i"),
                      {"allow_slow_non_contiguous": True}) for j in range(4)], writes=[b_convw])
        load_gain(0, norm_mix)

        KT = R(DYN + 0, [128, 4, SEQ], BF16)
        V = R(DYN + 32768, [128, 32, 512], BF16)
        QT = R(DYN + 65536, [128, 4, NOWN], BF16)
        OAT = R(DYN + 81920, [128, 4, NOWN], BF16)
        b_oat = [[Buf("oat%d_%d" % (p, s)) for s in range(4)] for p in range(4)]
        TMP = DYN + 98304
        xring = [R(TMP + i * 4096, [128, D], F32) for i in range(4)]; b_xr = [Buf("xr%d" % i) for i in range(4)]
        xn = [R(TMP + 16384 + i * 2048, [128, D], BF16) for i in range(3)]; b_xn = [Buf("xn%d" % i) for i in range(3)]
        junk = R(TMP + 24576, [128, D], BF16); b_junk = Buf("junk")
        hTa = [R(TMP + 26624 + i * 8192, [128, 8, 512], BF16) for i in range(2)]
        b_hTa = [[Buf("hTa%d_%d" % (i, t)) for t in range(4)] for i in range(2)]
        b_kt = [[Buf("kt%d_%d" % (p, ch)) for ch in range(8)] for p in range(4)]
        b_v = [Buf("v%d" % ch) for ch in range(8)]
        b_qt = [[Buf("qt%d_%d" % (p, s)) for s in range(4)] for p in range(4)]

        wk, b_wk = WS.get(i_wk)
        wv, b_wv = WS.get(i_wv)
        wq, b_wq = WS.get(i_wq)

        mmb = [0]

        def load_all(k):
            xt = xring[k % 4]; bx = b_xr[k % 4]
            S.dma("sp", [(xt, xall[k * 128:(k + 1) * 128, :], {})], writes=[bx])
            return xt, bx

        def dst_all(k):
            ch, t = k // 4, k % 4
            return hTa[ch % 2][:, :, t * 128:(t + 1) * 128], b_hTa[ch % 2][t]

        pend = []

        def flush(nmax):
            for _ in range(min(nmax, len(pend))):
                pend.pop(0)()

        def after_all(k):
            if k % 4 == 3:
                ch = k // 4
                hb = hTa[ch % 2]; bh = b_hTa[ch % 2]
                for p in range(4):
                    def f(p=p, ch=ch, hb=hb, bh=bh):
                        bk = mmb[0] % 4; mmb[0] += 1
                        mm_group(bank(bk), [(wk[:, kc, p * 128:(p + 1) * 128], hb[:, kc, :]) for kc in range(8)],
                                 b_ps[bk], [b_wk] + bh)
                        evac(KT[:, p, ch * 512:(ch + 1) * 512], bank(bk), [b_ps[bk]], [b_kt[p][ch]])
                    pend.append(f)
                for t in range(4):
                    def f(t=t, ch=ch, hb=hb, bh=bh):
                        bk = mmb[0] % 4; mmb[0] += 1
                        mm_group(bank(bk), [(hb[:, kc, t * 128:(t + 1) * 128], wv[:, kc, :]) for kc in range(8)],
                                 b_ps[bk], [b_wv, bh[t]])
                        evac(V[:, ch * 4 + t, :], bank(bk), [b_ps[bk]], [b_v[ch]])
                    pend.append(f)
            flush(2)

        norm_pipeline(32, load_all, 0, xn, b_xn, junk, b_junk, dst_all, (4, 6), after_all, mm=True)

        def load_own(k):
            xt = xring[k % 4]; bx = b_xr[k % 4]
            S.dma("sp", [(xt, xown[k * 128:(k + 1) * 128, :], {})], writes=[bx])
            return xt, bx

        def after_own(k):
            if k % 4 == 3:
                s_ = k // 4
                hb = hTa[s_ % 2]; bh = b_hTa[s_ % 2]
                for p in range(4):
                    def f(p=p, s_=s_, hb=hb, bh=bh):
                        bk = mmb[0] % 4; mmb[0] += 1
                        mm_group(bank(bk), [(wq[:, kc, p * 128:(p + 1) * 128], hb[:, kc, :]) for kc in range(8)],
                                 b_ps[bk], [b_wq] + bh)
                        evac(QT[:, p, s_ * 512:(s_ + 1) * 512], bank(bk), [b_ps[bk]], [b_qt[p][s_]])
                    pend.append(f)
            flush(2)

        norm_pipeline(16, load_own, 0, xn, b_xn, junk, b_junk, dst_all, (4, 6), after_own, mm=True)
        flush(100)
        WS.release(i_wk); WS.release(i_wv)
        WS.release(i_wq)
        S.barrier()

        Fb = [R(TMP + i * 8208, [128, 4, 513], F32) for i in range(2)]; b_F = [Buf("F%d" % i) for i in range(2)]
        D1 = R(TMP + 16416, [128, 4, 513], F32); b_D1 = Buf("D1")
        Pb = R(TMP + 24624, [128, 4, 513], F32); b_P = Buf("P")
        Wb = [R(TMP + 32832 + i * 4096, [128, 4, 512], BF16) for i in range(2)]; b_W = [Buf("W%d" % i) for i in range(2)]
        WTb = [R(TMP + 41024 + i * 4096, [128, 4, 512], BF16) for i in range(2)]; b_WT = [Buf("WT%d" % i) for i in range(2)]
        b_Z = Buf("Z")
        b_WTps = Buf("WTps")
        WTps = bank_bf(4, 2).rearrange("p (a b) -> p a b", a=4)
        for i in range(2):
            S.op("dve", "memset", [], [b_F[i]], Fb[i].rearrange("p a b -> p (a b)"), 0.0)
        S.op("dve", "memset", [], [b_D1], D1.rearrange("p a b -> p (a b)"), 0.0)
        groups = [(s, h, c) for s in range(4) for h in range(8) for c in range(NCH[s])]
        G = len(groups)

        def geom(gi):
            s, h, c = groups[gi]
            n = NCH[s]
            p = h // 2
            rows = slice(0, 64) if h % 2 == 0 else slice(64, 128)
            ob = 6 + (h % 2)
            cr = 8 - n + c
            return s, h, c, n, p, rows, ob, cr

        def stage1(gi):
            s, h, c, n, p, rows, ob, cr = geom(gi)
            k0 = cr * 512
            F_ = Fb[gi % 2]; bF = b_F[gi % 2]
            for i in range(4):
                q0 = s * 512 + i * 128
                zb = bank(i)
                last = (c >= 2)
                S.op("pe", "matmul", [b_qt[p][s], b_kt[p][cr]], [b_Z], zb, lhsT=QT[rows, p, q0:q0 + 128],
                     rhs=KT[rows, p, k0:k0 + 512], start=True, stop=last, _mark=(last and i == 3))
                if c < 2:
                    S.op("pe", "matmul", [b_seld, b_diag], [b_Z], zb, lhsT=seld[:, s, c, :], rhs=diag[:, i, :],
                         start=False, stop=False, _mark=False)
                    S.op("pe", "matmul", [b_seln, b_negrow], [b_Z], zb, lhsT=seln[0:1, s, c, :],
                         rhs=negrow[0:1, :], start=False, stop=True, _mark=(i == 3))
            S.op("act", "activation", [b_Z], [bF], out=F_[:, :, 1:513],
                 in_=bank(0, 4).rearrange("p (a b) -> p a b", a=4), func=AF.Sigmoid, scale=-0.125)

        def stage2(gi):
            s, h, c, n, p, rows, ob, cr = geom(gi)
            F_ = Fb[gi % 2]; bF = b_F[gi % 2]
            W_ = Wb[gi % 2]; bW = b_W[gi % 2]
            if c == 0:
                S.op("dve", "memset", [], [b_D1], D1[:, :, 0:1], 1.0)
            else:
                S.op("dve", "tensor_copy", [b_P], [b_D1], out=D1[:, :, 0:1], in_=Pb[:, :, 512:513])
            S.op("dve", "tensor_tensor_scan", [bF, b_D1], [b_P], out=Pb.rearrange("p a b -> p (a b)"),
                 data0=F_.rearrange("p a b -> p (a b)"), data1=D1.rearrange("p a b -> p (a b)"), initial=0.0,
                 op0=ALU.mult, op1=ALU.add)
            S.op("dve", "tensor_tensor", [b_P], [bW], out=W_, in0=Pb[:, :, 0:512], in1=Pb[:, :, 1:513],
                 op=ALU.subtract)

        def stage3(gi):
            s, h, c, n, p, rows, ob, cr = geom(gi)
            W_ = Wb[gi % 2]; bW = b_W[gi % 2]
            WT_ = WTb[gi % 2]; bWT = b_WT[gi % 2]
            for m in range(4):
                for i in range(4):
                    S.op("pe", "transpose", [bW, b_ident], [b_WTps], out=WTps[:, m, i * 128:(i + 1) * 128],
                         in_=W_[:, i, m * 128:(m + 1) * 128], identity=ident, _mark=(m == 3 and i == 3))
            S.op("act", "activation", [b_WTps], [bWT], out=WT_.rearrange("p a b -> p (a b)"),
                 in_=bank_bf(4, 2), func=AF.Copy)

        def stage4(gi):
            s, h, c, n, p, rows, ob, cr = geom(gi)
            WT_ = WTb[gi % 2]; bWT = b_WT[gi % 2]
            psO = psum[rows, ob * 512:(ob + 1) * 512]
            for m in range(4):
                first = (c == 0 and m == 0)
                lastpv = (c == n - 1 and m == 3)
                S.op("pe", "matmul", [bWT, b_v[cr]], [b_ps[ob]], psO, lhsT=V[:, cr * 4 + m, h * 64:(h + 1) * 64],
                     rhs=WT_[:, m, :], start=first, stop=lastpv, _mark=(m == 3))
            if c == n - 1:
                S.op("act", "activation", [b_ps[ob]], [b_oat[p][s]], out=OAT[rows, p, s * 512:(s + 1) * 512],
                     in_=psO, func=AF.Copy)

        for step in range(G + 3):
            if step < G:
                stage1(step)
            if 0 <= step - 1 < G:
                stage2(step - 1)
            if 0 <= step - 2 < G:
                stage3(step - 2)
            if 0 <= step - 3 < G:
                stage4(step - 3)
        S.barrier()

        hTo = R(DYN + 0, [128, 8, NOWN + 128], BF16); b_hTo = [Buf("hTo%d" % t) for t in range(17)]
        YBT = R(DYN + 34816, [128, 4, NOWN], BF16); b_ybt = [Buf("ybt%d" % j) for j in range(4)]
        xring2 = [R(DYN + 51200 + i * 4096, [128, D], F32) for i in range(2)]
        xn2 = [R(DYN + 59392 + i * 2048, [128, D], BF16) for i in range(2)]
        junk2 = R(DYN + 63488, [128, D], BF16)
        u_sb = R(DYN + 65536, [128, 512], F32); b_usb = Buf("usb")
        acc = R(DYN + 67584, [128, 512], F32); b_acc = Buf("acc")
        cub = R(DYN + 69632, [128, 2, 516], F32); b_cub = [Buf("cub0"), Buf("cub1")]
        MRG = R(DYN + 98304, [128, 8, NOWN], BF16)
        b_mrg = [[Buf("mrg%d_%d" % (c, s)) for s in range(4)] for c in range(8)]
        sgt = [R(DYN + 131072 + i * 2048, [128, 512], F32) for i in range(4)]; b_sgt = [Buf("sg%d" % i) for i in range(4)]
        b_x2 = [Buf("x2r%d" % i) for i in range(2)]; b_xn2 = [Buf("xn2_%d" % i) for i in range(2)]; b_junk2 = Buf("junk2")

        def load_own2(k):
            S.dma("sp", [(xring2[k % 2], xown[k * 128:(k + 1) * 128, :], {})], writes=[b_x2[k % 2]])
            return xring2[k % 2], b_x2[k % 2]

        norm_pipeline(17, load_own2, 0, xn2, b_xn2, junk2, b_junk2,
                      lambda k: (hTo[:, :, k * 128:(k + 1) * 128], b_hTo[k]), (4, 6), mm=True)

        wu, b_wu = WS.get(i_wu)
        wgb, b_wgb = WS.get(i_wgb)
        wgc, b_wgc = WS.get(i_wgc)
        cb = 0
        for j in range(4):
            jc = slice(j * 128, (j + 1) * 128)
            mm_group(bank(0)[:, 0:128], [(wu[:, kc, jc], hTo[:, kc, NOWN:NOWN + 128]) for kc in range(8)],
                     b_ps[0], [b_wu, b_hTo[16]])
            mm_group(bank(1)[:, 0:128], [(wgc[:, kc, jc], hTo[:, kc, NOWN:NOWN + 128]) for kc in range(8)],
                     b_ps[1], [b_wgc, b_hTo[16]])
            S.op("act", "activation", [b_ps[0]], [b_usb], out=u_sb[:, 0:8], in_=bank(0)[:, 0:8], func=AF.Copy)
            S.op("dve", "tensor_tensor", [b_ps[1], b_usb], [b_cuh], out=cuh, in0=bank(1)[:, 0:8], in1=u_sb[:, 0:8],
                 op=ALU.mult)
            for s in range(4):
                hs = slice(s * 512, (s + 1) * 512)
                rd = [b_hTo[s * 4 + t] for t in range(4)]
                pb = 2 + (cb % 2) * 3
                cu = cub[:, cb % 2, :]; bcu = b_cub[cb % 2]
                cb += 1
                mm_group(bank(pb), [(wu[:, kc, jc], hTo[:, kc, hs]) for kc in range(8)], b_ps[pb], [b_wu] + rd)
                mm_group(bank(pb + 1), [(wgc[:, kc, jc], hTo[:, kc, hs]) for kc in range(8)], b_ps[pb + 1], [b_wgc] + rd)
                mm_group(bank(pb + 2), [(wgb[:, kc, jc], hTo[:, kc, hs]) for kc in range(8)], b_ps[pb + 2], [b_wgb] + rd)
                S.op("act", "activation", [b_ps[pb]], [b_usb], out=u_sb, in_=bank(pb), func=AF.Copy)
                S.op("dve", "tensor_copy", [b_cuh], [bcu], out=cu[:, 0:2], in_=cuh[:, 2 * s:2 * s + 2])
                S.op("dve", "tensor_tensor", [b_ps[pb + 1], b_usb], [bcu], out=cu[:, 2:514], in0=bank(pb + 1), in1=u_sb,
                     op=ALU.mult)
                S.op("dve", "tensor_scalar", [bcu, b_convw], [b_acc], out=acc, in0=cu[:, 2:514],
                     scalar1=convw[:, j, 2:3], scalar2=None, op0=ALU.mult)
                S.op("dve", "scalar_tensor_tensor", [bcu, b_convw, b_acc], [b_acc], out=acc, in0=cu[:, 1:513],
                     scalar=convw[:, j, 1:2], in1=acc, op0=ALU.mult, op1=ALU.add)
                S.op("dve", "scalar_tensor_tensor", [bcu, b_convw, b_acc], [b_acc], out=acc, in0=cu[:, 0:512],
                     scalar=convw[:, j, 0:1], in1=acc, op0=ALU.mult, op1=ALU.add)
                S.op("dve", "tensor_tensor", [b_ps[pb + 2], b_acc], [b_ybt[j]], out=YBT[:, j, hs], in0=bank(pb + 2),
                     in1=acc, op=ALU.mult)
        WS.release(i_wu); WS.release(i_wgb); WS.release(i_wgc)

        it = 0
        for c4 in range(2):
            ia, iga, igb = i_d[c4]
            wab, b_wab = WS.get(ia)
            wga, b_wga = WS.get(iga)
            wgB, b_wgB = WS.get(igb)
            for cc in range(4):
                c = c4 * 4 + cc
                ccs = slice(cc * 128, (cc + 1) * 128)
                for s in range(4):
                    hs = slice(s * 512, (s + 1) * 512)
                    rd = [b_hTo[s * 4 + t] for t in range(4)]
                    pb = (it % 2) * 4
                    sa = sgt[(it % 2) * 2]; bsa = b_sgt[(it % 2) * 2]
                    sb = sgt[(it % 2) * 2 + 1]; bsb = b_sgt[(it % 2) * 2 + 1]
                    it += 1
                    mm_group(bank(pb), [(wab[:, kc, ccs], OAT[:, kc, hs]) for kc in range(4)], b_ps[pb],
                             [b_wab] + [b_oat[kc][s] for kc in range(4)])
                    mm_group(bank(pb + 1), [(wab[:, 4 + kc, ccs], YBT[:, kc, hs]) for kc in range(4)], b_ps[pb + 1],
                             [b_wab] + b_ybt)
                    mm_group(bank(pb + 2), [(wga[:, kc, ccs], hTo[:, kc, hs]) for kc in range(8)], b_ps[pb + 2], [b_wga] + rd)
                    mm_group(bank(pb + 3), [(wgB[:, kc, ccs], hTo[:, kc, hs]) for kc in range(8)], b_ps[pb + 3], [b_wgB] + rd)
                    S.op("act", "activation", [b_ps[pb + 2]], [bsa], out=sa, in_=bank(pb + 2), func=AF.Sigmoid)
                    S.op("act", "activation", [b_ps[pb + 3]], [bsb], out=sb, in_=bank(pb + 3), func=AF.Sigmoid)
                    S.op("dve", "tensor_tensor", [b_ps[pb], bsa], [bsa], out=sa, in0=bank(pb), in1=sa, op=ALU.mult)
                    S.op("dve", "tensor_tensor", [b_ps[pb + 1], bsb], [bsb], out=sb, in0=bank(pb + 1), in1=sb, op=ALU.mult)
                    S.op("dve", "tensor_tensor", [bsa, bsb], [b_mrg[c][s]], out=MRG[:, c, hs], in0=sa, in1=sb, op=ALU.add)
            WS.release(ia); WS.release(iga); WS.release(igb)
        S.barrier()

        X = R(DYN + 0, [128, 16, D], F32); b_X = [Buf("X%d" % t) for t in range(16)]
        for t in range(16):
            S.dma("sp", [(X[:, t, :], xown[t * 128:(t + 1) * 128, :], {})], writes=[b_X[t]])
        rb = [0]
        for hf in range(2):
            wm, b_wm = WS.get(i_wmo[hf])
            for t in range(16):
                bk = rb[0] % 8; rb[0] += 1
                mm_group(bank(bk), [(MRG[:, kc, t * 128:(t + 1) * 128], wm[:, kc, :]) for kc in range(8)], b_ps[bk],
                         [b_wm] + [b_mrg[kc][t // 4] for kc in range(8)])
                xs = X[:, t, hf * 512:(hf + 1) * 512]
                S.op("dve", "tensor_tensor", [b_ps[bk], b_X[t]], [b_X[t]], out=xs, in0=bank(bk), in1=xs, op=ALU.add)
            WS.release(i_wmo[hf])
        S.barrier()

        hTs = [R(DYN + 65536 + i * 8192, [128, 8, 512], BF16) for i in range(2)]
        b_hTs = [[Buf("hTs%d_%d" % (i, t)) for t in range(4)] for i in range(2)]
        qmT = R(DYN + 81920, [128, 8, 512], BF16); b_qmT = [Buf("qmT%d" % c) for c in range(8)]
        omT = R(DYN + 90112, [128, 8, 512], BF16); b_omT = [Buf("omT%d" % c) for c in range(8)]
        mT = R(DYN + 98304, [128, 8, 256], BF16); b_mT = [Buf("mT0"), Buf("mT1")]
        KmT = R(DYN + 102400, [128, 8, 256], BF16); b_KmT = Buf("KmT")
        Vm = R(DYN + 106496, [128, 2, D], BF16); b_Vm = Buf("Vm")
        Esm = R(DYN + 110592, [128, 4, 256], F32); b_Esm = Buf("Esm")
        probs = [R(DYN + 114688 + i * 2048, [128, 4, 256], BF16) for i in range(2)]; b_probs = [Buf("pr0"), Buf("pr1")]
        pT = R(DYN + 118784, [128, 8, 512], BF16); b_pT = Buf("pT")
        xn3 = [R(DYN + 126976 + i * 2048, [128, D], BF16) for i in range(2)]; b_xn3 = [Buf("xn3_0"), Buf("xn3_1")]
        junk3 = R(DYN + 131072, [128, D], BF16); b_junk3 = Buf("junk3")
        memring = [R(DYN + 118784 + i * 4096, [128, D], F32) for i in range(2)]; b_mr = [Buf("mr0"), Buf("mr1")]

        load_gain(1, norm_mem_kv)
        load_gain(0, norm_mem_q)
        for mt in range(2):
            S.dma("sp", [(memring[mt], memb[mt * 128:(mt + 1) * 128, :], {})], writes=[b_mr[mt]])
            norm_tile(memring[mt], b_mr[mt], 1, xn3[mt], b_xn3[mt], junk3, b_junk3)
            transpose_tile(xn3[mt], b_xn3[mt], mT[:, :, mt * 128:(mt + 1) * 128], b_mT[mt], (4, 5))
        wK = [WS.get(i_wkvK[k]) for k in range(2)]
        for c in range(8):
            bk = 6 + c % 2
            w_, bw_ = wK[c // 4]
            mm_group(bank(bk)[:, 0:256], [(w_[:, kc, (c % 4) * 128:(c % 4 + 1) * 128], mT[:, kc, :]) for kc in range(8)],
                     b_ps[bk], [bw_] + b_mT)
            evac(KmT[:, c, :], bank(bk)[:, 0:256], [b_ps[bk]], [b_KmT])
        WS.release(i_wkvK[0]); WS.release(i_wkvK[1])
        wVv = [WS.get(i_wkvV[k]) for k in range(2)]
        for mt in range(2):
            for hf in range(2):
                bk = 6 + hf
                w_, bw_ = wVv[hf]
                mm_group(bank(bk), [(mT[:, kc, mt * 128:(mt + 1) * 128], w_[:, kc, :]) for kc in range(8)],
                         b_ps[bk], [bw_, b_mT[mt]])
                evac(Vm[:, mt, hf * 512:(hf + 1) * 512], bank(bk), [b_ps[bk]], [b_Vm])
        WS.release(i_wkvV[0]); WS.release(i_wkvV[1])
        S.barrier()

        wMQ = [WS.get(i_wmq[k]) for k in range(2)]
        wMO = [WS.get(i_wmo2[k]) for k in range(2)]
        sc_it = 0
        gb = [0]
        def fg_norm(s):
            hb = hTs[s % 2]; bh = b_hTs[s % 2]
            norm_pipeline(4, lambda k, s=s: (X[:, s * 4 + k, :], b_X[s * 4 + k]), 0, xn3, b_xn3, junk3, b_junk3,
                          lambda k, hb=hb, bh=bh: (hb[:, :, k * 128:(k + 1) * 128], bh[k]), (4, 5))

        fg_norm(0)
        for s in range(4):
            hb = hTs[s % 2]; bh = b_hTs[s % 2]
            for c in range(8):
                bk = 6 + gb[0] % 2; gb[0] += 1
                w_, bw_ = wMQ[c // 4]
                mm_group(bank(bk), [(w_[:, kc, (c % 4) * 128:(c % 4 + 1) * 128], hb[:, kc, :]) for kc in range(8)],
                         b_ps[bk], [bw_] + bh)
                evac(qmT[:, c, :], bank(bk), [b_ps[bk]], [b_qmT[c]])

            def sA(t):
                sb0 = (t % 2) * 2
                psS = bank(sb0, 2).rearrange("p (a b) -> p a b", a=4)
                for h in range(4):
                    for cc in range(2):
                        c = 2 * h + cc
                        S.op("pe", "matmul", [b_qmT[c], b_KmT], [b_ps[sb0]], psS[:, h, :],
                             lhsT=qmT[:, c, t * 128:(t + 1) * 128], rhs=KmT[:, c, :], start=(cc == 0), stop=(cc == 1),
                             _mark=(h == 3 and cc == 1))

            def sB(t):
                sb0 = (t % 2) * 2
                psS = bank(sb0, 2).rearrange("p (a b) -> p a b", a=4)
                b_S = b_ps[sb0]
                pr = probs[t % 2]; bpr = b_probs[t % 2]
                S.op("dve", "tensor_reduce", [b_S], [b_mx], out=mx4, in_=psS, axis=AX.X, op=ALU.max)
                S.op("dve", "tensor_scalar", [b_mx], [b_nb], out=nb4, in0=mx4, scalar1=-1.0 / 16, scalar2=None, op0=ALU.mult)
                for h in range(4):
                    S.op("act", "activation", [b_S, b_nb], [b_Esm, b_sm], out=Esm[:, h, :], in_=psS[:, h, :], func=AF.Exp,
                         scale=1.0 / 16, bias=nb4[:, h:h + 1], accum_out=sm4[:, h:h + 1])
                S.op("dve", "reciprocal", [b_sm], [b_rs], out=rs4, in_=sm4)
                for h in range(4):
                    S.op("dve", "tensor_scalar", [b_Esm, b_rs], [bpr], out=pr[:, h, :], in0=Esm[:, h, :],
                         scalar1=rs4[:, h:h + 1], scalar2=None, op0=ALU.mult)

            def sC(t):
                pr = probs[t % 2]; bpr = b_probs[t % 2]
                tb_ = 4 + (t % 2)
                pst = bank_bf(tb_)
                for h in range(4):
                    for mt in range(2):
                        k8 = h * 2 + mt
                        S.op("pe", "transpose", [bpr, b_ident], [b_ps[tb_]], out=pst[:, k8 * 128:(k8 + 1) * 128],
                             in_=pr[:, h, mt * 128:(mt + 1) * 128], identity=ident, _mark=(k8 == 7))
                S.op("act", "activation", [b_ps[tb_]], [b_pT], out=pT[:, :, t * 128:(t + 1) * 128],
                     in_=pst.rearrange("p (a b) -> p a b", a=8), func=AF.Copy)

            for step in range(6):
                if step < 4:
                    sA(step)
                if 0 <= step - 1 < 4:
                    sB(step - 1)
                if 0 <= step - 2 < 4:
                    sC(step - 2)
            if s + 1 < 4:
                fg_norm(s + 1)
            for h in range(4):
                for dc in range(2):
                    c = h * 2 + dc
                    bk = 6 + gb[0] % 2; gb[0] += 1
                    mm_group(bank(bk), [(Vm[:, mt, c * 128:(c + 1) * 128], pT[:, h * 2 + mt, :]) for mt in range(2)],
                             b_ps[bk], [b_Vm, b_pT])
                    evac(omT[:, c, :], bank(bk), [b_ps[bk]], [b_omT[c]])
            for t in range(4):
                tt = s * 4 + t
                for hf in range(2):
                    bk = 6 + gb[0] % 2; gb[0] += 1
                    w_, bw_ = wMO[hf]
                    mm_group(bank(bk), [(omT[:, kc, t * 128:(t + 1) * 128], w_[:, kc, :]) for kc in range(8)],
                             b_ps[bk], [bw_] + b_omT)
                    xs = X[:, tt, hf * 512:(hf + 1) * 512]
                    S.op("dve", "tensor_tensor", [b_ps[bk], b_X[tt]], [b_X[tt]], out=xs, in0=bank(bk), in1=xs, op=ALU.add)
        for k in range(2):
            WS.release(i_wmq[k]); WS.release(i_wmo2[k])
        S.barrier()

        hTb = R(DYN + 65536, [128, 8, 1024], BF16); b_hTb = [Buf("hTb%d" % t) for t in range(8)]
        aT = R(DYN + 81920, [128, 22, 1024], BF16); b_aT = [Buf("aT%d" % j) for j in range(22)]
        sgl = [R(DYN + 126976 + i * 2048, [128, 512], F32) for i in range(2)]; b_sgl = [Buf("sgl0"), Buf("sgl1")]
        xn4 = [R(DYN + 131072 + i * 2048, [128, D], BF16) for i in range(2)]; b_xn4 = [Buf("xn4_0"), Buf("xn4_1")]
        junk4 = R(DYN + 135168, [128, D], BF16); b_junk4 = Buf("junk4")
        load_gain(1, norm_ffn)
        load_gain(0, norm_final)
        otmp = [R(DYN + 137216 + i * 4096, [128, D], F32) for i in range(2)]; b_ot = [Buf("ot0"), Buf("ot1")]
        junk5 = R(DYN + 145408, [128, D], BF16); b_junk5 = Buf("junk5")
        out_evs = []
        fit = 0
        for tb in range(2):
            gu, fo = i_ffn[tb]
            norm_pipeline(8, lambda k, tb=tb: (X[:, tb * 8 + k, :], b_X[tb * 8 + k]), 1, xn4, b_xn4, junk4, b_junk4,
                          lambda k: (hTb[:, :, k * 128:(k + 1) * 128], b_hTb[k]), (4, 6), mm=True)
            for blk in range(6):
                ig, iu, ncol = gu[blk]
                wg_, b_wg = WS.get(ig)
                wu_, b_wu_ = WS.get(iu)
                for cc in range(ncol // 128):
                    j = blk * 4 + cc
                    ccs = slice(cc * 128, (cc + 1) * 128)
                    for s2 in range(2):
                        hs = slice(s2 * 512, (s2 + 1) * 512)
                        rd = b_hTb[s2 * 4:(s2 + 1) * 4]
                        pb = (fit % 3) * 2
                        sg_ = sgl[fit % 2]; bsg = b_sgl[fit % 2]
                        fit += 1
                        mm_group(bank(pb), [(wg_[:, kc, ccs], hTb[:, kc, hs]) for kc in range(8)], b_ps[pb], [b_wg] + rd)
                        mm_group(bank(pb + 1), [(wu_[:, kc, ccs], hTb[:, kc, hs]) for kc in range(8)], b_ps[pb + 1],
                                 [b_wu_] + rd)
                        S.op("act", "activation", [b_ps[pb]], [bsg], out=sg_, in_=bank(pb), func=AF.Silu)
                        S.op("dve", "tensor_tensor", [b_ps[pb + 1], bsg], [b_aT[j]], out=aT[:, j, hs], in0=bank(pb + 1),
                             in1=sg_, op=ALU.mult)
                WS.release(ig); WS.release(iu)
            for hf in range(2):
                wfo = [WS.get(fo[hf][k3]) for k3 in range(3)]
                for t in range(8):
                    tt = tb * 8 + t
                    bk = 6 + (fit % 2); fit += 1
                    mm_group(bank(bk), [(aT[:, j, t * 128:(t + 1) * 128], wfo[j // 8][0][:, j % 8, :]) for j in range(22)],
                             b_ps[bk], [w[1] for w in wfo] + b_aT)
                    xs = X[:, tt, hf * 512:(hf + 1) * 512]
                    S.op("dve", "tensor_tensor", [b_ps[bk], b_X[tt]], [b_X[tt]], out=xs, in0=bank(bk), in1=xs, op=ALU.add)
                    if hf == 1:
                        i = tt % 2
                        norm_tile(X[:, tt, :], b_X[tt], 0, otmp[i], b_ot[i], junk5, b_junk5)
                        out_evs.append(S.dma("sp", [(out_d[tt * 128:(tt + 1) * 128, :], otmp[i], {})], reads=[b_ot[i]]))
                for k3 in range(3):
                    WS.release(fo[hf][k3])
        for ev in out_evs:
            S._wait("sp", ev)
        print("inst counts", S.n_inst, {e: len(S.prog[e]) for e in S.ENGS})
        S.emit()
    return nc


def _host_inputs(inputs):
    x = np.asarray(inputs["x"], dtype=np.float32)
    mem = np.asarray(inputs["mem"], dtype=np.float32)
    diag = np.zeros((128, 4, 512), np.float32)
    pp = np.arange(128)[:, None]
    kr = np.arange(512)[None, :]
    for i in range(4):
        ql = i * 128 + pp
        diag[:, i, :] = np.where(kr > 511 - ql, 0.0, NEG)
    eye = np.eye(128, dtype=np.float32)
    shared = {
        "diag": diag,
        "norm_mix": np.ascontiguousarray(inputs["norm_mix"][0:1]),
        "w_in": np.ascontiguousarray(inputs["w_in"][0]),
        "conv_w": np.ascontiguousarray(inputs["conv_w"][0]),
        "w_branch_a": np.ascontiguousarray(inputs["w_branch_a"][0]),
        "w_branch_b": np.ascontiguousarray(inputs["w_branch_b"][0]),
        "w_mix_out": np.ascontiguousarray(inputs["w_mix_out"][0]),
        "norm_mem_q": np.ascontiguousarray(inputs["norm_mem_q"][0:1]),
        "norm_mem_kv": np.ascontiguousarray(inputs["norm_mem_kv"][0:1]),
        "w_mem_q": np.ascontiguousarray(inputs["w_mem_q"][0]),
        "w_mem_kv": np.ascontiguousarray(inputs["w_mem_kv"][0]),
        "w_mem_o": np.ascontiguousarray(inputs["w_mem_o"][0]),
        "norm_ffn": np.ascontiguousarray(inputs["norm_ffn"][0:1]),
        "w_ffn_in": np.ascontiguousarray(inputs["w_ffn_in"][0]),
        "w_ffn_out": np.ascontiguousarray(inputs["w_ffn_out"][0]),
        "norm_final": np.ascontiguousarray(np.asarray(inputs["norm_final"]).reshape(1, D)),
    }
    shared = {k: np.asarray(v, dtype=np.float32) for k, v in shared.items()}
    in_maps = []
    for core in range(8):
        b, par = core // 2, core % 2
        tiles = T_OF[par]
        xown = np.zeros((NOWN + 128, D), np.float32)
        seld = np.zeros((128, 4, 2, 128), np.float32)
        seln = np.zeros((1, 4, 2, 128), np.float32)
        for j, T in enumerate(tiles):
            xown[j * 512:(j + 1) * 512] = x[b, T * 512:(T + 1) * 512]
            if T > 0:
                xown[NOWN + 2 * j:NOWN + 2 * j + 2] = x[b, T * 512 - 2:T * 512]
            tmax = NCH[j] - 1
            if T == tmax:
                seld[:, j, 0, :] = eye
            else:
                seln[0, j, 0, :] = 1.0
                seld[:, j, 1, :] = eye
        m = dict(shared)
        m["xall"] = np.ascontiguousarray(x[b, ::-1, :])
        m["xown"] = xown
        m["memb"] = np.ascontiguousarray(mem[b])
        m["seld"] = seld
        m["seln"] = seln
        in_maps.append(m)
    return in_maps


_NC_CACHE = {}


def kernel(**inputs):
    in_maps = _host_inputs(inputs)
    if "nc" not in _NC_CACHE:
        _NC_CACHE["nc"] = build_nc()
    nc = _NC_CACHE["nc"]
    res = run_bass_kernel_spmd(nc, in_maps, core_ids=list(range(8)))
    out = np.zeros((NB, SEQ, D), np.float32)
    for core in range(8):
        b, par = core // 2, core % 2
        o = np.asarray(res.results[core]["out"], dtype=np.float32)
        for j, T in enumerate(T_OF[par]):
            out[b, T * 512:(T + 1) * 512] = o[j * 512:(j + 1) * 512]
    return out
```

```python
import numpy as np
import concourse.bass as bass
import concourse.mybir as mybir
from concourse.bass_utils import run_bass_kernel_spmd
from contextlib import ExitStack

F32 = mybir.dt.float32
BF16 = mybir.dt.bfloat16
AF = mybir.ActivationFunctionType
ALU = mybir.AluOpType
AX = mybir.AxisListType

D = 1024
SEQ = 4096
NB = 4
TS = 512
T_OF = {0: (0, 3, 4, 7), 1: (1, 2, 5, 6)}
NCH = (2, 4, 6, 8)
NEG = -30000.0
EPS = 1e-6
FFN_H = 2816
NOWN = 2048
SAME_ENGINE_SYNC = True


class Buf:
    __slots__ = ("name", "w", "r")

    def __init__(self, name):
        self.name = name
        self.w = None
        self.r = {}


class Sched:
    ENGS = ("pe", "act", "dve", "pool", "sp")

    def __init__(self, nc, stack, n_dma_sems=8):
        self.nc = nc
        self.prog = {e: [] for e in self.ENGS}
        self.count = {e: 0 for e in self.ENGS}
        self.sem = {e: stack.enter_context(nc.semaphore("s_" + e)) for e in self.ENGS}
        self.seen = {e: {} for e in self.ENGS}
        self.dsem = {}
        self.dval = {}
        self.dring = {}
        self.dpos = {}
        idx = 0
        for q in ("sp", "pool"):
            ring = []
            for i in range(n_dma_sems):
                self.dsem[idx] = stack.enter_context(nc.semaphore("d_%s%d" % (q, i)))
                self.dval[idx] = 0
                ring.append(idx)
                idx += 1
            self.dring[q] = ring
            self.dpos[q] = 0
        self.n_inst = {e: 0 for e in self.ENGS}
        self.last_marked = {e: True for e in self.ENGS}

    def _wait(self, e, ev):
        if ev is None:
            return
        if ev[0] == "e":
            _, src, seq = ev
            if src == e and (e == "pe" or not SAME_ENGINE_SYNC):
                return
            assert self.count[src] >= seq, ("dependency on unissued mark", e, ev)
            key = ("e", src)
            val = seq
            sem = self.sem[src]
        else:
            _, sidx, val = ev
            key = ("d", sidx)
            sem = self.dsem[sidx]
        if self.seen[e].get(key, 0) >= val:
            return
        self.seen[e][key] = val
        self.prog[e].append(lambda eng, sem=sem, val=val: eng.wait_ge(sem, val))

    def _deps(self, e, reads, writes):
        for b in reads:
            self._wait(e, b.w)
        for b in writes:
            self._wait(e, b.w)
            for (k0, k1), v in list(b.r.items()):
                self._wait(e, (k0, k1, v))

    def _record(self, ev, reads, writes):
        key = (ev[0], ev[1])
        for b in reads:
            if b.r.get(key, 0) < ev[2]:
                b.r[key] = ev[2]
        for b in writes:
            b.w = ev
            b.r = {}

    def op(self, e, meth, reads, writes, *args, _mark=True, **kw):
        self._deps(e, reads, writes)
        sem = self.sem[e]
        if _mark:
            self.count[e] += 1
            self.prog[e].append(lambda eng: getattr(eng, meth)(*args, **kw).then_inc(sem, 1))
            ev = ("e", e, self.count[e])
        else:
            self.prog[e].append(lambda eng: getattr(eng, meth)(*args, **kw))
            ev = ("e", e, self.count[e] + 1)
        self.last_marked[e] = _mark
        self.n_inst[e] += 1
        self._record(ev, reads, writes)
        return ev

    def dma(self, q, xfers, reads=(), writes=()):
        self._deps(q, reads, writes)
        ring = self.dring[q]
        sidx = ring[self.dpos[q] % len(ring)]
        self.dpos[q] += 1
        if self.dval[sidx] > 0:
            self._wait(q, ("d", sidx, self.dval[sidx]))
        sem = self.dsem[sidx]
        for (o, i, kw) in xfers:
            self.dval[sidx] += 16
            self.prog[q].append(lambda eng, o=o, i=i, kw=kw: eng.dma_start(out=o, in_=i, **kw).then_inc(sem, 16))
        ev = ("d", sidx, self.dval[sidx])
        self._record(ev, reads, writes)
        return ev

    def barrier(self):
        for e in ("pe", "act", "dve"):
            assert self.last_marked[e], ("barrier with unmarked tail", e)
        for sidx in self.dring["sp"]:
            if self.dval[sidx] > 0:
                self._wait("sp", ("d", sidx, self.dval[sidx]))
        for f in ("pe", "act", "dve"):
            self._wait("sp", ("e", f, self.count[f]))
        self.count["sp"] += 1
        sem = self.sem["sp"]
        self.prog["sp"].append(lambda eng, sem=sem: eng.sem_inc(sem, 1))
        for e in ("pe", "act", "dve"):
            self._wait(e, ("e", "sp", self.count["sp"]))

    def emit(self):
        with self.nc.Block() as block:
            def mk(name):
                def body(engine):
                    for c in self.prog[name]:
                        c(engine)
                return body
            block.sync(mk("sp"))
            block.gpsimd(mk("pool"))
            block.scalar(mk("act"))
            block.vector(mk("dve"))
            block.tensor(mk("pe"))


def build_nc():
    nc = bass.Bass("TRN2", target_bir_lowering=False)

    def din(name, shape):
        return nc.dram_tensor(name, list(shape), F32, kind="ExternalInput").ap()

    xall = din("xall", [SEQ, D])
    xown = din("xown", [NOWN + 128, D])
    memb = din("memb", [256, D])
    seld_d = din("seld", [128, 4, 2, 128])
    seln_d = din("seln", [1, 4, 2, 128])
    diag_d = din("diag", [128, 4, 512])
    norm_mix = din("norm_mix", [1, D])
    w_in = din("w_in", [D, 5120])
    conv_w = din("conv_w", [3, 512])
    w_ba = din("w_branch_a", [512, D])
    w_bb = din("w_branch_b", [512, D])
    w_mix_out = din("w_mix_out", [D, D])
    norm_mem_q = din("norm_mem_q", [1, D])
    norm_mem_kv = din("norm_mem_kv", [1, D])
    w_mem_q = din("w_mem_q", [D, D])
    w_mem_kv = din("w_mem_kv", [D, 2 * D])
    w_mem_o = din("w_mem_o", [D, D])
    norm_ffn = din("norm_ffn", [1, D])
    w_ffn_in = din("w_ffn_in", [D, 2 * FFN_H])
    w_ffn_out = din("w_ffn_out", [FFN_H, D])
    norm_final = din("norm_final", [1, D])
    out_d = nc.dram_tensor("out", [NOWN, D], F32, kind="ExternalOutput").ap()

    with ExitStack() as st:
        S = Sched(nc, st)
        ARENA_F32 = 53200
        arena = st.enter_context(nc.sbuf_tensor("arena", [128, ARENA_F32], F32))
        psum = st.enter_context(nc.psum_tensor("psum", [128, 4096], F32))

        def R(off, shape, dt):
            esz = 4 if dt == F32 else 2
            n = int(np.prod(shape[1:]))
            nbytes = n * esz
            assert off % 4 == 0 and nbytes % 4 == 0, (off, shape)
            assert off + nbytes <= ARENA_F32 * 4, (off, shape)
            ap = arena[:, off // 4:(off + nbytes) // 4]
            if dt != F32:
                ap = ap.bitcast(dt)
            if len(shape) == 3:
                ap = ap.rearrange("p (a b) -> p a b", a=shape[1])
            elif len(shape) == 4:
                ap = ap.rearrange("p (a b c) -> p a b c", a=shape[1], b=shape[2])
            return ap

        def bank(i, n=1):
            return psum[:, i * 512:(i + n) * 512]

        def bank_bf(i, n=1):
            return psum[:, i * 512:(i + n) * 512].bitcast(BF16)

        b_ps = [Buf("ps%d" % i) for i in range(8)]

        ident = R(0, [128, 128], BF16); b_ident = Buf("ident")
        identf = R(256, [128, 128], F32); b_identf = Buf("identf")
        convw = R(768, [128, 4, 3], F32); b_convw = Buf("convw")
        stats = R(1024, [128, 256], F32)
        seld = R(2048, [128, 4, 2, 128], BF16); b_seld = Buf("seld")
        seln = R(4096, [128, 4, 2, 128], BF16); b_seln = Buf("seln")
        negrow = R(6144, [128, 512], BF16); b_negrow = Buf("negrow")
        zeros = R(7168, [128, 512], F32); b_zeros = Buf("zeros")
        diag = R(9216, [128, 4, 512], BF16); b_diag = Buf("diag")
        GAIN0 = 13312
        g_rep = [R(GAIN0 + i * 4096, [128, D], F32) for i in range(2)]
        b_g = [Buf("g%d" % i) for i in range(2)]
        RING0 = 21504
        NSLOT = 5
        DYN = RING0 + NSLOT * 8192

        ssq = stats[:, 0:8]; rstd = stats[:, 8:16]
        b_st = [Buf("st%d" % i) for i in range(8)]
        st_pos = [0]
        mx4 = stats[:, 16:20]; nb4 = stats[:, 20:24]; sm4 = stats[:, 24:28]; rs4 = stats[:, 28:32]
        b_mx = Buf("mx"); b_nb = Buf("nb"); b_sm = Buf("sm"); b_rs = Buf("rs")
        cuh = stats[:, 32:40]; b_cuh = Buf("cuh")

        class WStream:
            def __init__(self):
                self.blocks = []
                self.issued = 0
                self.released = set()
                self.bufs = [Buf("ring%d" % i) for i in range(NSLOT)]
                self.next_get = 0

            def add(self, pieces):
                self.blocks.append(pieces)
                return len(self.blocks) - 1

            def slot_ap(self, i):
                return R(RING0 + (i % NSLOT) * 8192, [128, 8, 512], BF16)

            def _issue(self):
                while self.issued < len(self.blocks):
                    k = self.issued
                    if k >= NSLOT and (k - NSLOT) not in self.released:
                        break
                    if k > self.next_get + NSLOT - 1:
                        break
                    sl = self.slot_ap(k)
                    xf = []
                    for (src, kc0, kcn, ncols) in self.blocks[k]:
                        xf.append((sl[:, kc0:kc0 + kcn, 0:ncols], src, {}))
                    S.dma("pool", xf, writes=[self.bufs[k % NSLOT]])
                    self.issued += 1

            def get(self, idx):
                assert idx == self.next_get, (idx, self.next_get)
                self.next_get += 1
                self._issue()
                assert self.issued > idx, ("weight block not issued", idx)
                return self.slot_ap(idx), self.bufs[idx % NSLOT]

            def release(self, idx):
                self.released.add(idx)
                self._issue()

        WS = WStream()

        def wpiece(w, r0, nrows, c0, ncols, kc0=0):
            src = w[r0:r0 + nrows, c0:c0 + ncols].rearrange("(kc p) n -> p kc n", p=128)
            return (src, kc0, nrows // 128, ncols)

        i_wk = WS.add([wpiece(w_in, 0, D, 512, 512)])
        i_wv = WS.add([wpiece(w_in, 0, D, 1024, 512)])
        i_wq = WS.add([wpiece(w_in, 0, D, 0, 512)])
        i_wu = WS.add([wpiece(w_in, 0, D, 1536, 512)])
        i_wgb = WS.add([wpiece(w_in, 0, D, 2048, 512)])
        i_wgc = WS.add([wpiece(w_in, 0, D, 2560, 512)])
        i_d = []
        for c4 in range(2):
            a = WS.add([wpiece(w_ba, 0, 512, c4 * 512, 512, kc0=0), wpiece(w_bb, 0, 512, c4 * 512, 512, kc0=4)])
            b_ = WS.add([wpiece(w_in, 0, D, 3072 + c4 * 512, 512)])
            c_ = WS.add([wpiece(w_in, 0, D, 4096 + c4 * 512, 512)])
            i_d.append((a, b_, c_))
        i_wmo = [WS.add([wpiece(w_mix_out, 0, D, hf * 512, 512)]) for hf in range(2)]
        i_wkvK = [WS.add([wpiece(w_mem_kv, 0, D, k * 512, 512)]) for k in range(2)]
        i_wkvV = [WS.add([wpiece(w_mem_kv, 0, D, D + k * 512, 512)]) for k in range(2)]
        i_wmq = [WS.add([wpiece(w_mem_q, 0, D, k * 512, 512)]) for k in range(2)]
        i_wmo2 = [WS.add([wpiece(w_mem_o, 0, D, k * 512, 512)]) for k in range(2)]
        i_ffn = []
        for tb in range(2):
            gu = []
            for blk in range(6):
                ncol = 512 if blk < 5 else 256
                ig = WS.add([wpiece(w_ffn_in, 0, D, blk * 512, ncol)])
                iu = WS.add([wpiece(w_ffn_in, 0, D, FFN_H + blk * 512, ncol)])
                gu.append((ig, iu, ncol))
            fo = []
            for hf in range(2):
                ks = []
                for k3 in range(3):
                    nr = 1024 if k3 < 2 else FFN_H - 2048
                    ks.append(WS.add([wpiece(w_ffn_out, k3 * 1024, nr, hf * 512, 512)]))
                fo.append(ks)
            i_ffn.append((gu, fo))

        evac_rr = [0]

        def evac(out_ap, in_ap, reads, writes, eng=None):
            if eng is None:
                eng = ("act", "dve")[evac_rr[0] % 2]
                evac_rr[0] += 1
            if eng == "act":
                return S.op("act", "activation", reads, writes, out=out_ap, in_=in_ap, func=AF.Copy)
            return S.op("dve", "tensor_copy", reads, writes, out=out_ap, in_=in_ap)

        def mm_group(out_ap, pairs, b_out, reads):
            n = len(pairs)
            for k, (l, r) in enumerate(pairs):
                S.op("pe", "matmul", reads, [b_out], out_ap, lhsT=l, rhs=r, start=(k == 0), stop=(k == n - 1),
                     _mark=(k == n - 1))

        def load_gain(i, src):
            S.dma("sp", [(g_rep[i], src.broadcast_to([128, D]), {})], writes=[b_g[i]])

        def norm_tile(x_ap, b_x, gi, out_ap, b_out, junk_ap, b_junk_):
            k = st_pos[0] % 8
            st_pos[0] += 1
            bs = b_st[k]
            S.op("act", "activation", [b_x], [b_junk_, bs], out=junk_ap, in_=x_ap, func=AF.Square,
                 accum_out=ssq[:, k:k + 1])
            S.op("act", "activation", [bs], [bs], out=rstd[:, k:k + 1], in_=ssq[:, k:k + 1], func=AF.Ln,
                 scale=1.0 / D, bias=EPS)
            S.op("act", "activation", [bs], [bs], out=rstd[:, k:k + 1], in_=rstd[:, k:k + 1], func=AF.Exp, scale=-0.5)
            S.op("dve", "scalar_tensor_tensor", [b_x, bs, b_g[gi]], [b_out], out=out_ap, in0=x_ap,
                 scalar=rstd[:, k:k + 1], in1=g_rep[gi], op0=ALU.mult, op1=ALU.mult)

        tr_rr = [0]

        def transpose_tile(xn_ap, b_xn_, dst3, b_dst, tbanks, mm=False):
            bk = tbanks[tr_rr[0] % len(tbanks)]
            tr_rr[0] += 1
            if mm:
                pst = bank(bk, 2)
                bb = [b_ps[bk], b_ps[bk + 1]]
                for c in range(8):
                    S.op("pe", "matmul", [b_xn_, b_ident], bb, pst[:, c * 128:(c + 1) * 128],
                         lhsT=xn_ap[:, c * 128:(c + 1) * 128], rhs=ident, start=True, stop=True, _mark=(c == 7))
                evac(dst3, pst.rearrange("p (a b) -> p a b", a=8), bb, [b_dst])
                return
            pst = bank_bf(bk)
            for c in range(8):
                S.op("pe", "transpose", [b_xn_, b_ident], [b_ps[bk]], out=pst[:, c * 128:(c + 1) * 128],
                     in_=xn_ap[:, c * 128:(c + 1) * 128], identity=ident, _mark=(c == 7))
            S.op("act", "activation", [b_ps[bk]], [b_dst], out=dst3, in_=pst.rearrange("p (a b) -> p a b", a=8),
                 func=AF.Copy)

        def norm_pipeline(n, load_fn, gi, xn_bufs, b_xn_bufs, junk_ap, b_junk_, dst_fn, tbanks, after_fn=None, mm=False):
            nb_ = len(xn_bufs)
            la = nb_ - 1

            def pre(k):
                x_ap, b_x = load_fn(k)
                norm_tile(x_ap, b_x, gi, xn_bufs[k % nb_], b_xn_bufs[k % nb_], junk_ap, b_junk_)

            def post(k):
                dst3, b_dst = dst_fn(k)
                transpose_tile(xn_bufs[k % nb_], b_xn_bufs[k % nb_], dst3, b_dst, tbanks, mm=mm)
            for k in range(min(la, n)):
                pre(k)
            for k in range(n):
                if k + la < n:
                    pre(k + la)
                post(k)
                if after_fn is not None:
                    after_fn(k)

        S.op("dve", "memset", [], [b_identf], identf, 1.0)
        S.op("pool", "affine_select", [b_identf], [b_identf], out=identf, in_=identf, pattern=[[-1, 128]],
             compare_op=ALU.is_equal, fill=0.0, base=0, channel_multiplier=1)
        S.op("dve", "tensor_copy", [b_identf], [b_ident], out=ident, in_=identf)
        S.op("dve", "memset", [], [b_negrow], negrow, NEG)
        S.op("dve", "memset", [], [b_zeros], zeros, 0.0)
        S.dma("pool", [(seld, seld_d, {})], writes=[b_seld])
        S.dma("pool", [(seln[0:1], seln_d, {})], writes=[b_seln])
        S.dma("pool", [(diag, diag_d, {})], writes=[b_diag])
        S.dma("sp", [(convw[:, j, :], conv_w[:, j * 128:(j + 1) * 128].rearrange("i p -> p i"),
                      {"allow_slow_non_contiguous": True}) for j in range(4)], writes=[b_convw])
        load_gain(0, norm_mix)

        KT = R(DYN + 0, [128, 4, SEQ], BF16)
        V = R(DYN + 32768, [128, 32, 512], BF16)
        QT = R(DYN + 65536, [128, 4, NOWN], BF16)
        OAT = R(DYN + 81920, [128, 4, NOWN], BF16)
        b_oat = [[Buf("oat%d_%d" % (p, s)) for s in range(4)] for p in range(4)]
        TMP = DYN + 98304
        xring = [R(TMP + i * 4096, [128, D], F32) for i in range(4)]; b_xr = [Buf("xr%d" % i) for i in range(4)]
        xn = [R(TMP + 16384 + i * 2048, [128, D], BF16) for i in range(3)]; b_xn = [Buf("xn%d" % i) for i in range(3)]
        junk = R(TMP + 24576, [128, D], BF16); b_junk = Buf("junk")
        hTa = [R(TMP + 26624 + i * 8192, [128, 8, 512], BF16) for i in range(2)]
        b_hTa = [[Buf("hTa%d_%d" % (i, t)) for t in range(4)] for i in range(2)]
        b_kt = [[Buf("kt%d_%d" % (p, ch)) for ch in range(8)] for p in range(4)]
        b_v = [Buf("v%d" % ch) for ch in range(8)]
        b_qt = [[Buf("qt%d_%d" % (p, s)) for s in range(4)] for p in range(4)]

        wk, b_wk = WS.get(i_wk)
        wv, b_wv = WS.get(i_wv)
        wq, b_wq = WS.get(i_wq)

        mmb = [0]

        def load_all(k):
            xt = xring[k % 4]; bx = b_xr[k % 4]
            S.dma("sp", [(xt, xall[k * 128:(k + 1) * 128, :], {})], writes=[bx])
            return xt, bx

        def dst_all(k):
            ch, t = k // 4, k % 4
            return hTa[ch % 2][:, :, t * 128:(t + 1) * 128], b_hTa[ch % 2][t]

        pend = []

        def flush(nmax):
            for _ in range(min(nmax, len(pend))):
                pend.pop(0)()

        def after_all(k):
            if k % 4 == 3:
                ch = k // 4
                hb = hTa[ch % 2]; bh = b_hTa[ch % 2]
                for p in range(4):
                    def f(p=p, ch=ch, hb=hb, bh=bh):
                        bk = mmb[0] % 4; mmb[0] += 1
                        mm_group(bank(bk), [(wk[:, kc, p * 128:(p + 1) * 128], hb[:, kc, :]) for kc in range(8)],
                                 b_ps[bk], [b_wk] + bh)
                        evac(KT[:, p, ch * 512:(ch + 1) * 512], bank(bk), [b_ps[bk]], [b_kt[p][ch]])
                    pend.append(f)
                for t in range(4):
                    def f(t=t, ch=ch, hb=hb, bh=bh):
                        bk = mmb[0] % 4; mmb[0] += 1
                        mm_group(bank(bk), [(hb[:, kc, t * 128:(t + 1) * 128], wv[:, kc, :]) for kc in range(8)],
                                 b_ps[bk], [b_wv, bh[t]])
                        evac(V[:, ch * 4 + t, :], bank(bk), [b_ps[bk]], [b_v[ch]])
                    pend.append(f)
            flush(2)

        norm_pipeline(32, load_all, 0, xn, b_xn, junk, b_junk, dst_all, (4, 6), after_all, mm=True)

        def load_own(k):
            xt = xring[k % 4]; bx = b_xr[k % 4]
            S.dma("sp", [(xt, xown[k * 128:(k + 1) * 128, :], {})], writes=[bx])
            return xt, bx

        def after_own(k):
            if k % 4 == 3:
                s_ = k // 4
                hb = hTa[s_ % 2]; bh = b_hTa[s_ % 2]
                for p in range(4):
                    def f(p=p, s_=s_, hb=hb, bh=bh):
                        bk = mmb[0] % 4; mmb[0] += 1
                        mm_group(bank(bk), [(wq[:, kc, p * 128:(p + 1) * 128], hb[:, kc, :]) for kc in range(8)],
                                 b_ps[bk], [b_wq] + bh)
                        evac(QT[:, p, s_ * 512:(s_ + 1) * 512], bank(bk), [b_ps[bk]], [b_qt[p][s_]])
                    pend.append(f)
            flush(2)

        norm_pipeline(16, load_own, 0, xn, b_xn, junk, b_junk, dst_all, (4, 6), after_own, mm=True)
        flush(100)
        WS.release(i_wk); WS.release(i_wv)
        WS.release(i_wq)
        S.barrier()

        Fb = [R(TMP + i * 8208, [128, 4, 513], F32) for i in range(2)]; b_F = [Buf("F%d" % i) for i in range(2)]
        D1 = R(TMP + 16416, [128, 4, 513], F32); b_D1 = Buf("D1")
        Pb = R(TMP + 24624, [128, 4, 513], F32); b_P = Buf("P")
        Wb = [R(TMP + 32832 + i * 4096, [128, 4, 512], BF16) for i in range(2)]; b_W = [Buf("W%d" % i) for i in range(2)]
        WTb = [R(TMP + 41024 + i * 4096, [128, 4, 512], BF16) for i in range(2)]; b_WT = [Buf("WT%d" % i) for i in range(2)]
        b_Z = Buf("Z")
        b_WTps = Buf("WTps")
        WTps = bank_bf(4, 2).rearrange("p (a b) -> p a b", a=4)
        for i in range(2):
            S.op("dve", "memset", [], [b_F[i]], Fb[i].rearrange("p a b -> p (a b)"), 0.0)
        S.op("dve", "memset", [], [b_D1], D1.rearrange("p a b -> p (a b)"), 0.0)
        groups = [(s, h, c) for s in range(4) for h in range(8) for c in range(NCH[s])]
        G = len(groups)

        def geom(gi):
            s, h, c = groups[gi]
            n = NCH[s]
            p = h // 2
            rows = slice(0, 64) if h % 2 == 0 else slice(64, 128)
            ob = 6 + (h % 2)
            cr = 8 - n + c
            return s, h, c, n, p, rows, ob, cr

        def stage1(gi):
            s, h, c, n, p, rows, ob, cr = geom(gi)
            k0 = cr * 512
            F_ = Fb[gi % 2]; bF = b_F[gi % 2]
            for i in range(4):
                q0 = s * 512 + i * 128
                zb = bank(i)
                last = (c >= 2)
                S.op("pe", "matmul", [b_qt[p][s], b_kt[p][cr]], [b_Z], zb, lhsT=QT[rows, p, q0:q0 + 128],
                     rhs=KT[rows, p, k0:k0 + 512], start=True, stop=last, _mark=(last and i == 3))
                if c < 2:
                    S.op("pe", "matmul", [b_seld, b_diag], [b_Z], zb, lhsT=seld[:, s, c, :], rhs=diag[:, i, :],
                         start=False, stop=False, _mark=False)
                    S.op("pe", "matmul", [b_seln, b_negrow], [b_Z], zb, lhsT=seln[0:1, s, c, :],
                         rhs=negrow[0:1, :], start=False, stop=True, _mark=(i == 3))
            S.op("act", "activation", [b_Z], [bF], out=F_[:, :, 1:513],
                 in_=bank(0, 4).rearrange("p (a b) -> p a b", a=4), func=AF.Sigmoid, scale=-0.125)

        def stage2(gi):
            s, h, c, n, p, rows, ob, cr = geom(gi)
            F_ = Fb[gi % 2]; bF = b_F[gi % 2]
            W_ = Wb[gi % 2]; bW = b_W[gi % 2]
            if c == 0:
                S.op("dve", "memset", [], [b_D1], D1[:, :, 0:1], 1.0)
            else:
                S.op("dve", "tensor_copy", [b_P], [b_D1], out=D1[:, :, 0:1], in_=Pb[:, :, 512:513])
            S.op("dve", "tensor_tensor_scan", [bF, b_D1], [b_P], out=Pb.rearrange("p a b -> p (a b)"),
                 data0=F_.rearrange("p a b -> p (a b)"), data1=D1.rearrange("p a b -> p (a b)"), initial=0.0,
                 op0=ALU.mult, op1=ALU.add)
            S.op("dve", "tensor_tensor", [b_P], [bW], out=W_, in0=Pb[:, :, 0:512], in1=Pb[:, :, 1:513],
                 op=ALU.subtract)

        def stage3(gi):
            s, h, c, n, p, rows, ob, cr = geom(gi)
            W_ = Wb[gi % 2]; bW = b_W[gi % 2]
            WT_ = WTb[gi % 2]; bWT = b_WT[gi % 2]
            for m in range(4):
                for i in range(4):
                    S.op("pe", "transpose", [bW, b_ident], [b_WTps], out=WTps[:, m, i * 128:(i + 1) * 128],
                         in_=W_[:, i, m * 128:(m + 1) * 128], identity=ident, _mark=(m == 3 and i == 3))
            S.op("act", "activation", [b_WTps], [bWT], out=WT_.rearrange("p a b -> p (a b)"),
                 in_=bank_bf(4, 2), func=AF.Copy)

        def stage4(gi):
            s, h, c, n, p, rows, ob, cr = geom(gi)
            WT_ = WTb[gi % 2]; bWT = b_WT[gi % 2]
            psO = psum[rows, ob * 512:(ob + 1) * 512]
            for m in range(4):
                first = (c == 0 and m == 0)
                lastpv = (c == n - 1 and m == 3)
                S.op("pe", "matmul", [bWT, b_v[cr]], [b_ps[ob]], psO, lhsT=V[:, cr * 4 + m, h * 64:(h + 1) * 64],
                     rhs=WT_[:, m, :], start=first, stop=lastpv, _mark=(m == 3))
            if c == n - 1:
                S.op("act", "activation", [b_ps[ob]], [b_oat[p][s]], out=OAT[rows, p, s * 512:(s + 1) * 512],
                     in_=psO, func=AF.Copy)

        for step in range(G + 3):
            if step < G:
                stage1(step)
            if 0 <= step - 1 < G:
                stage2(step - 1)
            if 0 <= step - 2 < G:
                stage3(step - 2)
            if 0 <= step - 3 < G:
                stage4(step - 3)
        S.barrier()

        hTo = R(DYN + 0, [128, 8, NOWN + 128], BF16); b_hTo = [Buf("hTo%d" % t) for t in range(17)]
        YBT = R(DYN + 34816, [128, 4, NOWN], BF16); b_ybt = [Buf("ybt%d" % j) for j in range(4)]
        xring2 = [R(DYN + 51200 + i * 4096, [128, D], F32) for i in range(2)]
        xn2 = [R(DYN + 59392 + i * 2048, [128, D], BF16) for i in range(2)]
        junk2 = R(DYN + 63488, [128, D], BF16)
        u_sb = R(DYN + 65536, [128, 512], F32); b_usb = Buf("usb")
        acc = R(DYN + 67584, [128, 512], F32); b_acc = Buf("acc")
        cub = R(DYN + 69632, [128, 2, 516], F32); b_cub = [Buf("cub0"), Buf("cub1")]
        MRG = R(DYN + 98304, [128, 8, NOWN], BF16)
        b_mrg = [[Buf("mrg%d_%d" % (c, s)) for s in range(4)] for c in range(8)]
        sgt = [R(DYN + 131072 + i * 2048, [128, 512], F32) for i in range(4)]; b_sgt = [Buf("sg%d" % i) for i in range(4)]
        b_x2 = [Buf("x2r%d" % i) for i in range(2)]; b_xn2 = [Buf("xn2_%d" % i) for i in range(2)]; b_junk2 = Buf("junk2")

        def load_own2(k):
            S.dma("sp", [(xring2[k % 2], xown[k * 128:(k + 1) * 128, :], {})], writes=[b_x2[k % 2]])
            return xring2[k % 2], b_x2[k % 2]

        norm_pipeline(17, load_own2, 0, xn2, b_xn2, junk2, b_junk2,
                      lambda k: (hTo[:, :, k * 128:(k + 1) * 128], b_hTo[k]), (4, 6), mm=True)

        wu, b_wu = WS.get(i_wu)
        wgb, b_wgb = WS.get(i_wgb)
        wgc, b_wgc = WS.get(i_wgc)
        cb = 0
        for j in range(4):
            jc = slice(j * 128, (j + 1) * 128)
            mm_group(bank(0)[:, 0:128], [(wu[:, kc, jc], hTo[:, kc, NOWN:NOWN + 128]) for kc in range(8)],
                     b_ps[0], [b_wu, b_hTo[16]])
            mm_group(bank(1)[:, 0:128], [(wgc[:, kc, jc], hTo[:, kc, NOWN:NOWN + 128]) for kc in range(8)],
                     b_ps[1], [b_wgc, b_hTo[16]])
            S.op("act", "activation", [b_ps[0]], [b_usb], out=u_sb[:, 0:8], in_=bank(0)[:, 0:8], func=AF.Copy)
            S.op("dve", "tensor_tensor", [b_ps[1], b_usb], [b_cuh], out=cuh, in0=bank(1)[:, 0:8], in1=u_sb[:, 0:8],
                 op=ALU.mult)
            for s in range(4):
                hs = slice(s * 512, (s + 1) * 512)
                rd = [b_hTo[s * 4 + t] for t in range(4)]
                pb = 2 + (cb % 2) * 3
                cu = cub[:, cb % 2, :]; bcu = b_cub[cb % 2]
                cb += 1
                mm_group(bank(pb), [(wu[:, kc, jc], hTo[:, kc, hs]) for kc in range(8)], b_ps[pb], [b_wu] + rd)
                mm_group(bank(pb + 1), [(wgc[:, kc, jc], hTo[:, kc, hs]) for kc in range(8)], b_ps[pb + 1], [b_wgc] + rd)
                mm_group(bank(pb + 2), [(wgb[:, kc, jc], hTo[:, kc, hs]) for kc in range(8)], b_ps[pb + 2], [b_wgb] + rd)
                S.op("act", "activation", [b_ps[pb]], [b_usb], out=u_sb, in_=bank(pb), func=AF.Copy)
                S.op("dve", "tensor_copy", [b_cuh], [bcu], out=cu[:, 0:2], in_=cuh[:, 2 * s:2 * s + 2])
                S.op("dve", "tensor_tensor", [b_ps[pb + 1], b_usb], [bcu], out=cu[:, 2:514], in0=bank(pb + 1), in1=u_sb,
                     op=ALU.mult)
                S.op("dve", "tensor_scalar", [bcu, b_convw], [b_acc], out=acc, in0=cu[:, 2:514],
                     scalar1=convw[:, j, 2:3], scalar2=None, op0=ALU.mult)
                S.op("dve", "scalar_tensor_tensor", [bcu, b_convw, b_acc], [b_acc], out=acc, in0=cu[:, 1:513],
                     scalar=convw[:, j, 1:2], in1=acc, op0=ALU.mult, op1=ALU.add)
                S.op("dve", "scalar_tensor_tensor", [bcu, b_convw, b_acc], [b_acc], out=acc, in0=cu[:, 0:512],
                     scalar=convw[:, j, 0:1], in1=acc, op0=ALU.mult, op1=ALU.add)
                S.op("dve", "tensor_tensor", [b_ps[pb + 2], b_acc], [b_ybt[j]], out=YBT[:, j, hs], in0=bank(pb + 2),
                     in1=acc, op=ALU.mult)
        WS.release(i_wu); WS.release(i_wgb); WS.release(i_wgc)

        it = 0
        for c4 in range(2):
            ia, iga, igb = i_d[c4]
            wab, b_wab = WS.get(ia)
            wga, b_wga = WS.get(iga)
            wgB, b_wgB = WS.get(igb)
            for cc in range(4):
                c = c4 * 4 + cc
                ccs = slice(cc * 128, (cc + 1) * 128)
                for s in range(4):
                    hs = slice(s * 512, (s + 1) * 512)
                    rd = [b_hTo[s * 4 + t] for t in range(4)]
                    pb = (it % 2) * 4
                    sa = sgt[(it % 2) * 2]; bsa = b_sgt[(it % 2) * 2]
                    sb = sgt[(it % 2) * 2 + 1]; bsb = b_sgt[(it % 2) * 2 + 1]
                    it += 1
                    mm_group(bank(pb), [(wab[:, kc, ccs], OAT[:, kc, hs]) for kc in range(4)], b_ps[pb],
                             [b_wab] + [b_oat[kc][s] for kc in range(4)])
                    mm_group(bank(pb + 1), [(wab[:, 4 + kc, ccs], YBT[:, kc, hs]) for kc in range(4)], b_ps[pb + 1],
                             [b_wab] + b_ybt)
                    mm_group(bank(pb + 2), [(wga[:, kc, ccs], hTo[:, kc, hs]) for kc in range(8)], b_ps[pb + 2], [b_wga] + rd)
                    mm_group(bank(pb + 3), [(wgB[:, kc, ccs], hTo[:, kc, hs]) for kc in range(8)], b_ps[pb + 3], [b_wgB] + rd)
                    S.op("act", "activation", [b_ps[pb + 2]], [bsa], out=sa, in_=bank(pb + 2), func=AF.Sigmoid)
                    S.op("act", "activation", [b_ps[pb + 3]], [bsb], out=sb, in_=bank(pb + 3), func=AF.Sigmoid)
                    S.op("dve", "tensor_tensor", [b_ps[pb], bsa], [bsa], out=sa, in0=bank(pb), in1=sa, op=ALU.mult)
                    S.op("dve", "tensor_tensor", [b_ps[pb + 1], bsb], [bsb], out=sb, in0=bank(pb + 1), in1=sb, op=ALU.mult)
                    S.op("dve", "tensor_tensor", [bsa, bsb], [b_mrg[c][s]], out=MRG[:, c, hs], in0=sa, in1=sb, op=ALU.add)
            WS.release(ia); WS.release(iga); WS.release(igb)
        S.barrier()

        X = R(DYN + 0, [128, 16, D], F32); b_X = [Buf("X%d" % t) for t in range(16)]
        for t in range(16):
            S.dma("sp", [(X[:, t, :], xown[t * 128:(t + 1) * 128, :], {})], writes=[b_X[t]])
        rb = [0]
        for hf in range(2):
            wm, b_wm = WS.get(i_wmo[hf])
            for t in range(16):
                bk = rb[0] % 8; rb[0] += 1
                mm_group(bank(bk), [(MRG[:, kc, t * 128:(t + 1) * 128], wm[:, kc, :]) for kc in range(8)], b_ps[bk],
                         [b_wm] + [b_mrg[kc][t // 4] for kc in range(8)])
                xs = X[:, t, hf * 512:(hf + 1) * 512]
                S.op("dve", "tensor_tensor", [b_ps[bk], b_X[t]], [b_X[t]], out=xs, in0=bank(bk), in1=xs, op=ALU.add)
            WS.release(i_wmo[hf])
        S.barrier()

        hTs = [R(DYN + 65536 + i * 8192, [128, 8, 512], BF16) for i in range(2)]
        b_hTs = [[Buf("hTs%d_%d" % (i, t)) for t in range(4)] for i in range(2)]
        qmT = R(DYN + 81920, [128, 8, 512], BF16); b_qmT = [Buf("qmT%d" % c) for c in range(8)]
        omT = R(DYN + 90112, [128, 8, 512], BF16); b_omT = [Buf("omT%d" % c) for c in range(8)]
        mT = R(DYN + 98304, [128, 8, 256], BF16); b_mT = [Buf("mT0"), Buf("mT1")]
        KmT = R(DYN + 102400, [128, 8, 256], BF16); b_KmT = Buf("KmT")
        Vm = R(DYN + 106496, [128, 2, D], BF16); b_Vm = Buf("Vm")
        Esm = R(DYN + 110592, [128, 4, 256], F32); b_Esm = Buf("Esm")
        probs = [R(DYN + 114688 + i * 2048, [128, 4, 256], BF16) for i in range(2)]; b_probs = [Buf("pr0"), Buf("pr1")]
        pT = R(DYN + 118784, [128, 8, 512], BF16); b_pT = Buf("pT")
        xn3 = [R(DYN + 126976 + i * 2048, [128, D], BF16) for i in range(2)]; b_xn3 = [Buf("xn3_0"), Buf("xn3_1")]
        junk3 = R(DYN + 131072, [128, D], BF16); b_junk3 = Buf("junk3")
        memring = [R(DYN + 118784 + i * 4096, [128, D], F32) for i in range(2)]; b_mr = [Buf("mr0"), Buf("mr1")]

        load_gain(1, norm_mem_kv)
        load_gain(0, norm_mem_q)
        for mt in range(2):
            S.dma("sp", [(memring[mt], memb[mt * 128:(mt + 1) * 128, :], {})], writes=[b_mr[mt]])
            norm_tile(memring[mt], b_mr[mt], 1, xn3[mt], b_xn3[mt], junk3, b_junk3)
            transpose_tile(xn3[mt], b_xn3[mt], mT[:, :, mt * 128:(mt + 1) * 128], b_mT[mt], (4, 5))
        wK = [WS.get(i_wkvK[k]) for k in range(2)]
        for c in range(8):
            bk = 6 + c % 2
            w_, bw_ = wK[c // 4]
            mm_group(bank(bk)[:, 0:256], [(w_[:, kc, (c % 4) * 128:(c % 4 + 1) * 128], mT[:, kc, :]) for kc in range(8)],
                     b_ps[bk], [bw_] + b_mT)
            evac(KmT[:, c, :], bank(bk)[:, 0:256], [b_ps[bk]], [b_KmT])
        WS.release(i_wkvK[0]); WS.release(i_wkvK[1])
        wVv = [WS.get(i_wkvV[k]) for k in range(2)]
        for mt in range(2):
            for hf in range(2):
                bk = 6 + hf
                w_, bw_ = wVv[hf]
                mm_group(bank(bk), [(mT[:, kc, mt * 128:(mt + 1) * 128], w_[:, kc, :]) for kc in range(8)],
                         b_ps[bk], [bw_, b_mT[mt]])
                evac(Vm[:, mt, hf * 512:(hf + 1) * 512], bank(bk), [b_ps[bk]], [b_Vm])
        WS.release(i_wkvV[0]); WS.release(i_wkvV[1])
        S.barrier()

        wMQ = [WS.get(i_wmq[k]) for k in range(2)]
        wMO = [WS.get(i_wmo2[k]) for k in range(2)]
        gb = [0]
        qmT2 = [qmT, R(DYN + 133120, [128, 8, 512], BF16)]
        b_qmT2 = [b_qmT, [Buf("qmTb%d" % c) for c in range(8)]]
        pT2 = [pT, R(DYN + 141312, [128, 8, 512], BF16)]
        b_pT2 = [b_pT, Buf("pTb")]

        def fg_norm(s):
            hb = hTs[s % 2]; bh = b_hTs[s % 2]
            norm_pipeline(4, lambda k, s=s: (X[:, s * 4 + k, :], b_X[s * 4 + k]), 0, xn3, b_xn3, junk3, b_junk3,
                          lambda k, hb=hb, bh=bh: (hb[:, :, k * 128:(k + 1) * 128], bh[k]), (4, 5))

        def q_groups(s):
            hb = hTs[s % 2]; bh = b_hTs[s % 2]
            qm = qmT2[s % 2]; bq = b_qmT2[s % 2]
            out = []
            for c in range(8):
                def f(c=c):
                    bk = 6 + gb[0] % 2; gb[0] += 1
                    w_, bw_ = wMQ[c // 4]
                    mm_group(bank(bk), [(w_[:, kc, (c % 4) * 128:(c % 4 + 1) * 128], hb[:, kc, :]) for kc in range(8)],
                             b_ps[bk], [bw_] + bh)
                    evac(qm[:, c, :], bank(bk), [b_ps[bk]], [bq[c]])
                out.append(f)
            return out

        def pvo_groups(s):
            pT_ = pT2[s % 2]; bpT = b_pT2[s % 2]
            out = []
            for h in range(4):
                for dc in range(2):
                    def f(h=h, dc=dc):
                        c = h * 2 + dc
                        bk = 6 + gb[0] % 2; gb[0] += 1
                        mm_group(bank(bk), [(Vm[:, mt, c * 128:(c + 1) * 128], pT_[:, h * 2 + mt, :]) for mt in range(2)],
                                 b_ps[bk], [b_Vm, bpT])
                        evac(omT[:, c, :], bank(bk), [b_ps[bk]], [b_omT[c]])
                    out.append(f)
            for t in range(4):
                for hf in range(2):
                    def f(t=t, hf=hf):
                        tt = s * 4 + t
                        bk = 6 + gb[0] % 2; gb[0] += 1
                        w_, bw_ = wMO[hf]
                        mm_group(bank(bk), [(omT[:, kc, t * 128:(t + 1) * 128], w_[:, kc, :]) for kc in range(8)],
                                 b_ps[bk], [bw_] + b_omT)
                        xs = X[:, tt, hf * 512:(hf + 1) * 512]
                        S.op("dve", "tensor_tensor", [b_ps[bk], b_X[tt]], [b_X[tt]], out=xs, in0=bank(bk), in1=xs, op=ALU.add)
                    out.append(f)
            return out

        def softmax_steps(s):
            qm = qmT2[s % 2]; bq = b_qmT2[s % 2]
            pT_ = pT2[s % 2]; bpT = b_pT2[s % 2]

            def sA(t):
                sb0 = (t % 2) * 2
                psS = bank(sb0, 2).rearrange("p (a b) -> p a b", a=4)
                for h in range(4):
                    for cc in range(2):
                        c = 2 * h + cc
                        S.op("pe", "matmul", [bq[c], b_KmT], [b_ps[sb0]], psS[:, h, :],
                             lhsT=qm[:, c, t * 128:(t + 1) * 128], rhs=KmT[:, c, :], start=(cc == 0), stop=(cc == 1),
                             _mark=(h == 3 and cc == 1))

            def sB(t):
                sb0 = (t % 2) * 2
                psS = bank(sb0, 2).rearrange("p (a b) -> p a b", a=4)
                b_S = b_ps[sb0]
                pr = probs[t % 2]; bpr = b_probs[t % 2]
                S.op("dve", "tensor_reduce", [b_S], [b_mx], out=mx4, in_=psS, axis=AX.X, op=ALU.max)
                S.op("dve", "tensor_scalar", [b_mx], [b_nb], out=nb4, in0=mx4, scalar1=-1.0 / 16, scalar2=None, op0=ALU.mult)
                for h in range(4):
                    S.op("act", "activation", [b_S, b_nb], [b_Esm, b_sm], out=Esm[:, h, :], in_=psS[:, h, :], func=AF.Exp,
                         scale=1.0 / 16, bias=nb4[:, h:h + 1], accum_out=sm4[:, h:h + 1])
                S.op("dve", "reciprocal", [b_sm], [b_rs], out=rs4, in_=sm4)
                for h in range(4):
                    S.op("dve", "tensor_scalar", [b_Esm, b_rs], [bpr], out=pr[:, h, :], in0=Esm[:, h, :],
                         scalar1=rs4[:, h:h + 1], scalar2=None, op0=ALU.mult)

            def sC(t):
                pr = probs[t % 2]; bpr = b_probs[t % 2]
                tb_ = 4 + (t % 2)
                pst = bank_bf(tb_)
                for h in range(4):
                    for mt in range(2):
                        k8 = h * 2 + mt
                        S.op("pe", "transpose", [bpr, b_ident], [b_ps[tb_]], out=pst[:, k8 * 128:(k8 + 1) * 128],
                             in_=pr[:, h, mt * 128:(mt + 1) * 128], identity=ident, _mark=(k8 == 7))
                S.op("act", "activation", [b_ps[tb_]], [bpT], out=pT_[:, :, t * 128:(t + 1) * 128],
                     in_=pst.rearrange("p (a b) -> p a b", a=8), func=AF.Copy)

            steps = []
            for step in range(6):
                def f(step=step):
                    if step < 4:
                        sA(step)
                    if 0 <= step - 1 < 4:
                        sB(step - 1)
                    if 0 <= step - 2 < 4:
                        sC(step - 2)
                steps.append(f)
            return steps

        fg_norm(0)
        for f in q_groups(0):
            f()
        fg_norm(1)
        for s in range(5):
            fill = []
            if s - 1 >= 0:
                fill += pvo_groups(s - 1)
            if s + 1 < 4:
                fill += q_groups(s + 1)
            if s < 4:
                steps = softmax_steps(s)
                per = (len(fill) + len(steps) - 1) // len(steps) if fill else 0
                for st_ in steps:
                    st_()
                    for _ in range(per):
                        if fill:
                            fill.pop(0)()
            while fill:
                fill.pop(0)()
            if s + 2 < 4:
                fg_norm(s + 2)
        for k in range(2):
            WS.release(i_wmq[k]); WS.release(i_wmo2[k])
        S.barrier()

        hTb = R(DYN + 65536, [128, 8, 1024], BF16); b_hTb = [Buf("hTb%d" % t) for t in range(8)]
        aT = R(DYN + 81920, [128, 22, 1024], BF16); b_aT = [Buf("aT%d" % j) for j in range(22)]
        sgl = [R(DYN + 126976 + i * 2048, [128, 512], F32) for i in range(2)]; b_sgl = [Buf("sgl0"), Buf("sgl1")]
        xn4 = [R(DYN + 131072 + i * 2048, [128, D], BF16) for i in range(2)]; b_xn4 = [Buf("xn4_0"), Buf("xn4_1")]
        junk4 = R(DYN + 135168, [128, D], BF16); b_junk4 = Buf("junk4")
        load_gain(1, norm_ffn)
        load_gain(0, norm_final)
        otmp = [R(DYN + 137216 + i * 4096, [128, D], F32) for i in range(2)]; b_ot = [Buf("ot0"), Buf("ot1")]
        junk5 = R(DYN + 145408, [128, D], BF16); b_junk5 = Buf("junk5")
        out_evs = []
        fit = 0
        for tb in range(2):
            gu, fo = i_ffn[tb]
            norm_pipeline(8, lambda k, tb=tb: (X[:, tb * 8 + k, :], b_X[tb * 8 + k]), 1, xn4, b_xn4, junk4, b_junk4,
                          lambda k: (hTb[:, :, k * 128:(k + 1) * 128], b_hTb[k]), (4, 6), mm=True)
            for blk in range(6):
                ig, iu, ncol = gu[blk]
                wg_, b_wg = WS.get(ig)
                wu_, b_wu_ = WS.get(iu)
                for cc in range(ncol // 128):
                    j = blk * 4 + cc
                    ccs = slice(cc * 128, (cc + 1) * 128)
                    for s2 in range(2):
                        hs = slice(s2 * 512, (s2 + 1) * 512)
                        rd = b_hTb[s2 * 4:(s2 + 1) * 4]
                        pb = (fit % 3) * 2
                        sg_ = sgl[fit % 2]; bsg = b_sgl[fit % 2]
                        fit += 1
                        mm_group(bank(pb), [(wg_[:, kc, ccs], hTb[:, kc, hs]) for kc in range(8)], b_ps[pb], [b_wg] + rd)
                        mm_group(bank(pb + 1), [(wu_[:, kc, ccs], hTb[:, kc, hs]) for kc in range(8)], b_ps[pb + 1],
                                 [b_wu_] + rd)
                        S.op("act", "activation", [b_ps[pb]], [bsg], out=sg_, in_=bank(pb), func=AF.Silu)
                        S.op("dve", "tensor_tensor", [b_ps[pb + 1], bsg], [b_aT[j]], out=aT[:, j, hs], in0=bank(pb + 1),
                             in1=sg_, op=ALU.mult)
                WS.release(ig); WS.release(iu)
            for hf in range(2):
                wfo = [WS.get(fo[hf][k3]) for k3 in range(3)]
                for t in range(8):
                    tt = tb * 8 + t
                    bk = 6 + (fit % 2); fit += 1
                    mm_group(bank(bk), [(aT[:, j, t * 128:(t + 1) * 128], wfo[j // 8][0][:, j % 8, :]) for j in range(22)],
                             b_ps[bk], [w[1] for w in wfo] + b_aT)
                    xs = X[:, tt, hf * 512:(hf + 1) * 512]
                    S.op("dve", "tensor_tensor", [b_ps[bk], b_X[tt]], [b_X[tt]], out=xs, in0=bank(bk), in1=xs, op=ALU.add)
                    if hf == 1:
                        i = tt % 2
                        norm_tile(X[:, tt, :], b_X[tt], 0, otmp[i], b_ot[i], junk5, b_junk5)
                        out_evs.append(S.dma("sp", [(out_d[tt * 128:(tt + 1) * 128, :], otmp[i], {})], reads=[b_ot[i]]))
                for k3 in range(3):
                    WS.release(fo[hf][k3])
        for ev in out_evs:
            S._wait("sp", ev)
        print("inst counts", S.n_inst, {e: len(S.prog[e]) for e in S.ENGS})
        S.emit()
    return nc


def _host_inputs(inputs):
    x = np.asarray(inputs["x"], dtype=np.float32)
    mem = np.asarray(inputs["mem"], dtype=np.float32)
    diag = np.zeros((128, 4, 512), np.float32)
    pp = np.arange(128)[:, None]
    kr = np.arange(512)[None, :]
    for i in range(4):
        ql = i * 128 + pp
        diag[:, i, :] = np.where(kr > 511 - ql, 0.0, NEG)
    eye = np.eye(128, dtype=np.float32)
    shared = {
        "diag": diag,
        "norm_mix": np.ascontiguousarray(inputs["norm_mix"][0:1]),
        "w_in": np.ascontiguousarray(inputs["w_in"][0]),
        "conv_w": np.ascontiguousarray(inputs["conv_w"][0]),
        "w_branch_a": np.ascontiguousarray(inputs["w_branch_a"][0]),
        "w_branch_b": np.ascontiguousarray(inputs["w_branch_b"][0]),
        "w_mix_out": np.ascontiguousarray(inputs["w_mix_out"][0]),
        "norm_mem_q": np.ascontiguousarray(inputs["norm_mem_q"][0:1]),
        "norm_mem_kv": np.ascontiguousarray(inputs["norm_mem_kv"][0:1]),
        "w_mem_q": np.ascontiguousarray(inputs["w_mem_q"][0]),
        "w_mem_kv": np.ascontiguousarray(inputs["w_mem_kv"][0]),
        "w_mem_o": np.ascontiguousarray(inputs["w_mem_o"][0]),
        "norm_ffn": np.ascontiguousarray(inputs["norm_ffn"][0:1]),
        "w_ffn_in": np.ascontiguousarray(inputs["w_ffn_in"][0]),
        "w_ffn_out": np.ascontiguousarray(inputs["w_ffn_out"][0]),
        "norm_final": np.ascontiguousarray(np.asarray(inputs["norm_final"]).reshape(1, D)),
    }
    shared = {k: np.asarray(v, dtype=np.float32) for k, v in shared.items()}
    in_maps = []
    for core in range(8):
        b, par = core // 2, core % 2
        tiles = T_OF[par]
        xown = np.zeros((NOWN + 128, D), np.float32)
        seld = np.zeros((128, 4, 2, 128), np.float32)
        seln = np.zeros((1, 4, 2, 128), np.float32)
        for j, T in enumerate(tiles):
            xown[j * 512:(j + 1) * 512] = x[b, T * 512:(T + 1) * 512]
            if T > 0:
                xown[NOWN + 2 * j:NOWN + 2 * j + 2] = x[b, T * 512 - 2:T * 512]
            tmax = NCH[j] - 1
            if T == tmax:
                seld[:, j, 0, :] = eye
            else:
                seln[0, j, 0, :] = 1.0
                seld[:, j, 1, :] = eye
        m = dict(shared)
        m["xall"] = np.ascontiguousarray(x[b, ::-1, :])
        m["xown"] = xown
        m["memb"] = np.ascontiguousarray(mem[b])
        m["seld"] = seld
        m["seln"] = seln
        in_maps.append(m)
    return in_maps


_NC_CACHE = {}


def kernel(**inputs):
    in_maps = _host_inputs(inputs)
    if "nc" not in _NC_CACHE:
        _NC_CACHE["nc"] = build_nc()
    nc = _NC_CACHE["nc"]
    res = run_bass_kernel_spmd(nc, in_maps, core_ids=list(range(8)))
    out = np.zeros((NB, SEQ, D), np.float32)
    for core in range(8):
        b, par = core // 2, core % 2
        o = np.asarray(res.results[core]["out"], dtype=np.float32)
        for j, T in enumerate(T_OF[par]):
            out[b, T * 512:(T + 1) * 512] = o[j * 512:(j + 1) * 512]
    return out
```

```python
import numpy as np
import concourse.bass as bass
import concourse.mybir as mybir
from concourse.bass_utils import run_bass_kernel_spmd
from contextlib import ExitStack

F32 = mybir.dt.float32
BF16 = mybir.dt.bfloat16
AF = mybir.ActivationFunctionType
ALU = mybir.AluOpType
AX = mybir.AxisListType

D = 1024
SEQ = 4096
NB = 4
TS = 512
T_OF = {0: (0, 3, 4, 7), 1: (1, 2, 5, 6)}
NCH = (2, 4, 6, 8)
NEG = -30000.0
EPS = 1e-6
FFN_H = 2816
NOWN = 2048
SAME_ENGINE_SYNC = True


class Buf:
    __slots__ = ("name", "w", "r")

    def __init__(self, name):
        self.name = name
        self.w = None
        self.r = {}


class Sched:
    ENGS = ("pe", "act", "dve", "pool", "sp")

    def __init__(self, nc, stack, n_dma_sems=8):
        self.nc = nc
        self.prog = {e: [] for e in self.ENGS}
        self.count = {e: 0 for e in self.ENGS}
        self.sem = {e: stack.enter_context(nc.semaphore("s_" + e)) for e in self.ENGS}
        self.seen = {e: {} for e in self.ENGS}
        self.dsem = {}
        self.dval = {}
        self.dring = {}
        self.dpos = {}
        idx = 0
        for q in ("sp", "pool"):
            ring = []
            for i in range(n_dma_sems):
                self.dsem[idx] = stack.enter_context(nc.semaphore("d_%s%d" % (q, i)))
                self.dval[idx] = 0
                ring.append(idx)
                idx += 1
            self.dring[q] = ring
            self.dpos[q] = 0
        self.n_inst = {e: 0 for e in self.ENGS}
        self.last_marked = {e: True for e in self.ENGS}

    def _wait(self, e, ev):
        if ev is None:
            return
        if ev[0] == "e":
            _, src, seq = ev
            if src == e and (e == "pe" or not SAME_ENGINE_SYNC):
                return
            assert self.count[src] >= seq, ("dependency on unissued mark", e, ev)
            key = ("e", src)
            val = seq
            sem = self.sem[src]
        else:
            _, sidx, val = ev
            key = ("d", sidx)
            sem = self.dsem[sidx]
        if self.seen[e].get(key, 0) >= val:
            return
        self.seen[e][key] = val
        self.prog[e].append(lambda eng, sem=sem, val=val: eng.wait_ge(sem, val))

    def _deps(self, e, reads, writes):
        for b in reads:
            self._wait(e, b.w)
        for b in writes:
            self._wait(e, b.w)
            for (k0, k1), v in list(b.r.items()):
                self._wait(e, (k0, k1, v))

    def _record(self, ev, reads, writes):
        key = (ev[0], ev[1])
        for b in reads:
            if b.r.get(key, 0) < ev[2]:
                b.r[key] = ev[2]
        for b in writes:
            b.w = ev
            b.r = {}

    def op(self, e, meth, reads, writes, *args, _mark=True, **kw):
        self._deps(e, reads, writes)
        sem = self.sem[e]
        if _mark:
            self.count[e] += 1
            self.prog[e].append(lambda eng: getattr(eng, meth)(*args, **kw).then_inc(sem, 1))
            ev = ("e", e, self.count[e])
        else:
            self.prog[e].append(lambda eng: getattr(eng, meth)(*args, **kw))
            ev = ("e", e, self.count[e] + 1)
        self.last_marked[e] = _mark
        self.n_inst[e] += 1
        self._record(ev, reads, writes)
        return ev

    def dma(self, q, xfers, reads=(), writes=()):
        self._deps(q, reads, writes)
        ring = self.dring[q]
        sidx = ring[self.dpos[q] % len(ring)]
        self.dpos[q] += 1
        if self.dval[sidx] > 0:
            self._wait(q, ("d", sidx, self.dval[sidx]))
        sem = self.dsem[sidx]
        for (o, i, kw) in xfers:
            self.dval[sidx] += 16
            self.prog[q].append(lambda eng, o=o, i=i, kw=kw: eng.dma_start(out=o, in_=i, **kw).then_inc(sem, 16))
        ev = ("d", sidx, self.dval[sidx])
        self._record(ev, reads, writes)
        return ev

    def barrier(self):
        for e in ("pe", "act", "dve"):
            assert self.last_marked[e], ("barrier with unmarked tail", e)
        for sidx in self.dring["sp"]:
            if self.dval[sidx] > 0:
                self._wait("sp", ("d", sidx, self.dval[sidx]))
        for f in ("pe", "act", "dve"):
            self._wait("sp", ("e", f, self.count[f]))
        self.count["sp"] += 1
        sem = self.sem["sp"]
        self.prog["sp"].append(lambda eng, sem=sem: eng.sem_inc(sem, 1))
        for e in ("pe", "act", "dve"):
            self._wait(e, ("e", "sp", self.count["sp"]))

    def emit(self):
        with self.nc.Block() as block:
            def mk(name):
                def body(engine):
                    for c in self.prog[name]:
                        c(engine)
                return body
            block.sync(mk("sp"))
            block.gpsimd(mk("pool"))
            block.scalar(mk("act"))
            block.vector(mk("dve"))
            block.tensor(mk("pe"))


def build_nc():
    nc = bass.Bass("TRN2", target_bir_lowering=False)

    def din(name, shape):
        return nc.dram_tensor(name, list(shape), F32, kind="ExternalInput").ap()

    xall = din("xall", [SEQ, D])
    xown = din("xown", [NOWN + 128, D])
    memb = din("memb", [256, D])
    seld_d = din("seld", [128, 4, 2, 128])
    seln_d = din("seln", [1, 4, 2, 128])
    diag_d = din("diag", [128, 4, 512])
    norm_mix = din("norm_mix", [1, D])
    w_in = din("w_in", [D, 5120])
    conv_w = din("conv_w", [3, 512])
    w_ba = din("w_branch_a", [512, D])
    w_bb = din("w_branch_b", [512, D])
    w_mix_out = din("w_mix_out", [D, D])
    norm_mem_q = din("norm_mem_q", [1, D])
    norm_mem_kv = din("norm_mem_kv", [1, D])
    w_mem_q = din("w_mem_q", [D, D])
    w_mem_kv = din("w_mem_kv", [D, 2 * D])
    w_mem_o = din("w_mem_o", [D, D])
    norm_ffn = din("norm_ffn", [1, D])
    w_ffn_in = din("w_ffn_in", [D, 2 * FFN_H])
    w_ffn_out = din("w_ffn_out", [FFN_H, D])
    norm_final = din("norm_final", [1, D])
    out_d = nc.dram_tensor("out", [NOWN, D], F32, kind="ExternalOutput").ap()

    with ExitStack() as st:
        S = Sched(nc, st)
        ARENA_F32 = 53200
        arena = st.enter_context(nc.sbuf_tensor("arena", [128, ARENA_F32], F32))
        psum = st.enter_context(nc.psum_tensor("psum", [128, 4096], F32))

        def R(off, shape, dt):
            esz = 4 if dt == F32 else 2
            n = int(np.prod(shape[1:]))
            nbytes = n * esz
            assert off % 4 == 0 and nbytes % 4 == 0, (off, shape)
            assert off + nbytes <= ARENA_F32 * 4, (off, shape)
            ap = arena[:, off // 4:(off + nbytes) // 4]
            if dt != F32:
                ap = ap.bitcast(dt)
            if len(shape) == 3:
                ap = ap.rearrange("p (a b) -> p a b", a=shape[1])
            elif len(shape) == 4:
                ap = ap.rearrange("p (a b c) -> p a b c", a=shape[1], b=shape[2])
            return ap

        def bank(i, n=1):
            return psum[:, i * 512:(i + n) * 512]

        def bank_bf(i, n=1):
            return psum[:, i * 512:(i + n) * 512].bitcast(BF16)

        b_ps = [Buf("ps%d" % i) for i in range(8)]

        ident = R(0, [128, 128], BF16); b_ident = Buf("ident")
        identf = R(256, [128, 128], F32); b_identf = Buf("identf")
        convw = R(768, [128, 4, 3], F32); b_convw = Buf("convw")
        stats = R(1024, [128, 256], F32)
        seld = R(2048, [128, 4, 2, 128], BF16); b_seld = Buf("seld")
        seln = R(4096, [128, 4, 2, 128], BF16); b_seln = Buf("seln")
        negrow = R(6144, [128, 512], BF16); b_negrow = Buf("negrow")
        zeros = R(7168, [128, 512], F32); b_zeros = Buf("zeros")
        diag = R(9216, [128, 4, 512], BF16); b_diag = Buf("diag")
        GAIN0 = 13312
        g_rep = [R(GAIN0 + i * 4096, [128, D], F32) for i in range(2)]
        b_g = [Buf("g%d" % i) for i in range(2)]
        RING0 = 21504
        NSLOT = 5
        DYN = RING0 + NSLOT * 8192

        ssq = stats[:, 0:8]; rstd = stats[:, 8:16]
        b_st = [Buf("st%d" % i) for i in range(8)]
        st_pos = [0]
        mx4 = stats[:, 16:20]; nb4 = stats[:, 20:24]; sm4 = stats[:, 24:28]; rs4 = stats[:, 28:32]
        b_mx = Buf("mx"); b_nb = Buf("nb"); b_sm = Buf("sm"); b_rs = Buf("rs")
        cuh = stats[:, 32:40]; b_cuh = Buf("cuh")

        class WStream:
            def __init__(self):
                self.blocks = []
                self.issued = 0
                self.released = set()
                self.bufs = [Buf("ring%d" % i) for i in range(NSLOT)]
                self.next_get = 0

            def add(self, pieces):
                self.blocks.append(pieces)
                return len(self.blocks) - 1

            def slot_ap(self, i):
                return R(RING0 + (i % NSLOT) * 8192, [128, 8, 512], BF16)

            def _issue(self):
                while self.issued < len(self.blocks):
                    k = self.issued
                    if k >= NSLOT and (k - NSLOT) not in self.released:
                        break
                    if k > self.next_get + NSLOT - 1:
                        break
                    sl = self.slot_ap(k)
                    xf = []
                    for (src, kc0, kcn, ncols) in self.blocks[k]:
                        xf.append((sl[:, kc0:kc0 + kcn, 0:ncols], src, {}))
                    S.dma("pool", xf, writes=[self.bufs[k % NSLOT]])
                    self.issued += 1

            def get(self, idx):
                assert idx == self.next_get, (idx, self.next_get)
                self.next_get += 1
                self._issue()
                assert self.issued > idx, ("weight block not issued", idx)
                return self.slot_ap(idx), self.bufs[idx % NSLOT]

            def release(self, idx):
                self.released.add(idx)
                self._issue()

        WS = WStream()

        def wpiece(w, r0, nrows, c0, ncols, kc0=0):
            src = w[r0:r0 + nrows, c0:c0 + ncols].rearrange("(kc p) n -> p kc n", p=128)
            return (src, kc0, nrows // 128, ncols)

        i_wk = WS.add([wpiece(w_in, 0, D, 512, 512)])
        i_wv = WS.add([wpiece(w_in, 0, D, 1024, 512)])
        i_wq = WS.add([wpiece(w_in, 0, D, 0, 512)])
        i_wu = WS.add([wpiece(w_in, 0, D, 1536, 512)])
        i_wgb = WS.add([wpiece(w_in, 0, D, 2048, 512)])
        i_wgc = WS.add([wpiece(w_in, 0, D, 2560, 512)])
        i_d = []
        for c4 in range(2):
            a = WS.add([wpiece(w_ba, 0, 512, c4 * 512, 512, kc0=0), wpiece(w_bb, 0, 512, c4 * 512, 512, kc0=4)])
            b_ = WS.add([wpiece(w_in, 0, D, 3072 + c4 * 512, 512)])
            c_ = WS.add([wpiece(w_in, 0, D, 4096 + c4 * 512, 512)])
            i_d.append((a, b_, c_))
        i_wmo = [WS.add([wpiece(w_mix_out, 0, D, hf * 512, 512)]) for hf in range(2)]
        i_wkvK = [WS.add([wpiece(w_mem_kv, 0, D, k * 512, 512)]) for k in range(2)]
        i_wkvV = [WS.add([wpiece(w_mem_kv, 0, D, D + k * 512, 512)]) for k in range(2)]
        i_wmq = [WS.add([wpiece(w_mem_q, 0, D, k * 512, 512)]) for k in range(2)]
        i_wmo2 = [WS.add([wpiece(w_mem_o, 0, D, k * 512, 512)]) for k in range(2)]
        i_ffn = []
        for tb in range(2):
            gu = []
            for blk in range(6):
                ncol = 512 if blk < 5 else 256
                ig = WS.add([wpiece(w_ffn_in, 0, D, blk * 512, ncol)])
                iu = WS.add([wpiece(w_ffn_in, 0, D, FFN_H + blk * 512, ncol)])
                gu.append((ig, iu, ncol))
            fo = []
            for hf in range(2):
                ks = []
                for k3 in range(3):
                    nr = 1024 if k3 < 2 else FFN_H - 2048
                    ks.append(WS.add([wpiece(w_ffn_out, k3 * 1024, nr, hf * 512, 512)]))
                fo.append(ks)
            i_ffn.append((gu, fo))

        evac_rr = [0]

        def evac(out_ap, in_ap, reads, writes, eng=None):
            if eng is None:
                eng = ("act", "dve")[evac_rr[0] % 2]
                evac_rr[0] += 1
            if eng == "act":
                return S.op("act", "activation", reads, writes, out=out_ap, in_=in_ap, func=AF.Copy)
            return S.op("dve", "tensor_copy", reads, writes, out=out_ap, in_=in_ap)

        def mm_group(out_ap, pairs, b_out, reads):
            n = len(pairs)
            for k, (l, r) in enumerate(pairs):
                S.op("pe", "matmul", reads, [b_out], out_ap, lhsT=l, rhs=r, start=(k == 0), stop=(k == n - 1),
                     _mark=(k == n - 1))

        def load_gain(i, src):
            S.dma("sp", [(g_rep[i], src.broadcast_to([128, D]), {})], writes=[b_g[i]])

        def norm_tile(x_ap, b_x, gi, out_ap, b_out, junk_ap, b_junk_):
            k = st_pos[0] % 8
            st_pos[0] += 1
            bs = b_st[k]
            S.op("act", "activation", [b_x], [b_junk_, bs], out=junk_ap, in_=x_ap, func=AF.Square,
                 accum_out=ssq[:, k:k + 1])
            S.op("act", "activation", [bs], [bs], out=rstd[:, k:k + 1], in_=ssq[:, k:k + 1], func=AF.Ln,
                 scale=1.0 / D, bias=EPS)
            S.op("act", "activation", [bs], [bs], out=rstd[:, k:k + 1], in_=rstd[:, k:k + 1], func=AF.Exp, scale=-0.5)
            S.op("dve", "scalar_tensor_tensor", [b_x, bs, b_g[gi]], [b_out], out=out_ap, in0=x_ap,
                 scalar=rstd[:, k:k + 1], in1=g_rep[gi], op0=ALU.mult, op1=ALU.mult)

        tr_rr = [0]

        def transpose_tile(xn_ap, b_xn_, dst3, b_dst, tbanks, mm=False):
            bk = tbanks[tr_rr[0] % len(tbanks)]
            tr_rr[0] += 1
            if mm:
                pst = bank(bk, 2)
                bb = [b_ps[bk], b_ps[bk + 1]]
                for c in range(8):
                    S.op("pe", "matmul", [b_xn_, b_ident], bb, pst[:, c * 128:(c + 1) * 128],
                         lhsT=xn_ap[:, c * 128:(c + 1) * 128], rhs=ident, start=True, stop=True, _mark=(c == 7))
                evac(dst3, pst.rearrange("p (a b) -> p a b", a=8), bb, [b_dst])
                return
            pst = bank_bf(bk)
            for c in range(8):
                S.op("pe", "transpose", [b_xn_, b_ident], [b_ps[bk]], out=pst[:, c * 128:(c + 1) * 128],
                     in_=xn_ap[:, c * 128:(c + 1) * 128], identity=ident, _mark=(c == 7))
            S.op("act", "activation", [b_ps[bk]], [b_dst], out=dst3, in_=pst.rearrange("p (a b) -> p a b", a=8),
                 func=AF.Copy)

        def norm_pipeline(n, load_fn, gi, xn_bufs, b_xn_bufs, junk_ap, b_junk_, dst_fn, tbanks, after_fn=None, mm=False):
            nb_ = len(xn_bufs)
            la = nb_ - 1

            def pre(k):
                x_ap, b_x = load_fn(k)
                norm_tile(x_ap, b_x, gi, xn_bufs[k % nb_], b_xn_bufs[k % nb_], junk_ap, b_junk_)

            def post(k):
                dst3, b_dst = dst_fn(k)
                transpose_tile(xn_bufs[k % nb_], b_xn_bufs[k % nb_], dst3, b_dst, tbanks, mm=mm)
            for k in range(min(la, n)):
                pre(k)
            for k in range(n):
                if k + la < n:
                    pre(k + la)
                post(k)
                if after_fn is not None:
                    after_fn(k)

        S.op("dve", "memset", [], [b_identf], identf, 1.0)
        S.op("pool", "affine_select", [b_identf], [b_identf], out=identf, in_=identf, pattern=[[-1, 128]],
             compare_op=ALU.is_equal, fill=0.0, base=0, channel_multiplier=1)
        S.op("dve", "tensor_copy", [b_identf], [b_ident], out=ident, in_=identf)
        S.op("dve", "memset", [], [b_negrow], negrow, NEG)
        S.op("dve", "memset", [], [b_zeros], zeros, 0.0)
        S.dma("pool", [(seld, seld_d, {})], writes=[b_seld])
        S.dma("pool", [(seln[0:1], seln_d, {})], writes=[b_seln])
        S.dma("pool", [(diag, diag_d, {})], writes=[b_diag])
        S.dma("sp", [(convw[:, j, :], conv_w[:, j * 128:(j + 1) * 128].rearrange("i p -> p i"),
                      {"allow_slow_non_contiguous": True}) for j in range(4)], writes=[b_convw])
        load_gain(0, norm_mix)

        KT = R(DYN + 0, [128, 4, SEQ], BF16)
        V = R(DYN + 32768, [128, 32, 512], BF16)
        QT = R(DYN + 65536, [128, 4, NOWN], BF16)
        OAT = R(DYN + 81920, [128, 4, NOWN], BF16)
        b_oat = [[Buf("oat%d_%d" % (p, s)) for s in range(4)] for p in range(4)]
        TMP = DYN + 98304
        xring = [R(TMP + i * 4096, [128, D], F32) for i in range(4)]; b_xr = [Buf("xr%d" % i) for i in range(4)]
        xn = [R(TMP + 16384 + i * 2048, [128, D], BF16) for i in range(4)]; b_xn = [Buf("xn%d" % i) for i in range(4)]
        junk = R(TMP + 24576, [128, D], BF16); b_junk = Buf("junk")
        hTa = [R(TMP + 26624 + i * 8192, [128, 8, 512], BF16) for i in range(2)]
        b_hTa = [[Buf("hTa%d_%d" % (i, t)) for t in range(4)] for i in range(2)]
        b_kt = [[Buf("kt%d_%d" % (p, ch)) for ch in range(8)] for p in range(4)]
        b_v = [Buf("v%d" % ch) for ch in range(8)]
        b_qt = [[Buf("qt%d_%d" % (p, s)) for s in range(4)] for p in range(4)]

        wk, b_wk = WS.get(i_wk)
        wv, b_wv = WS.get(i_wv)
        wq, b_wq = WS.get(i_wq)

        mmb = [0]

        def load_all(k):
            xt = xring[k % 4]; bx = b_xr[k % 4]
            S.dma("sp", [(xt, xall[k * 128:(k + 1) * 128, :], {})], writes=[bx])
            return xt, bx

        def dst_all(k):
            ch, t = k // 4, k % 4
            return hTa[ch % 2][:, :, t * 128:(t + 1) * 128], b_hTa[ch % 2][t]

        pend = []

        def flush(nmax):
            for _ in range(min(nmax, len(pend))):
                pend.pop(0)()

        def after_all(k):
            if k % 4 == 3:
                ch = k // 4
                hb = hTa[ch % 2]; bh = b_hTa[ch % 2]
                for p in range(4):
                    def f(p=p, ch=ch, hb=hb, bh=bh):
                        bk = mmb[0] % 4; mmb[0] += 1
                        mm_group(bank(bk), [(wk[:, kc, p * 128:(p + 1) * 128], hb[:, kc, :]) for kc in range(8)],
                                 b_ps[bk], [b_wk] + bh)
                        evac(KT[:, p, ch * 512:(ch + 1) * 512], bank(bk), [b_ps[bk]], [b_kt[p][ch]])
                    pend.append(f)
                for t in range(4):
                    def f(t=t, ch=ch, hb=hb, bh=bh):
                        bk = mmb[0] % 4; mmb[0] += 1
                        mm_group(bank(bk), [(hb[:, kc, t * 128:(t + 1) * 128], wv[:, kc, :]) for kc in range(8)],
                                 b_ps[bk], [b_wv, bh[t]])
                        evac(V[:, ch * 4 + t, :], bank(bk), [b_ps[bk]], [b_v[ch]])
                    pend.append(f)
            flush(2)

        norm_pipeline(32, load_all, 0, xn, b_xn, junk, b_junk, dst_all, (4, 6), after_all, mm=True)

        def load_own(k):
            xt = xring[k % 4]; bx = b_xr[k % 4]
            S.dma("sp", [(xt, xown[k * 128:(k + 1) * 128, :], {})], writes=[bx])
            return xt, bx

        def after_own(k):
            if k % 4 == 3:
                s_ = k // 4
                hb = hTa[s_ % 2]; bh = b_hTa[s_ % 2]
                for p in range(4):
                    def f(p=p, s_=s_, hb=hb, bh=bh):
                        bk = mmb[0] % 4; mmb[0] += 1
                        mm_group(bank(bk), [(wq[:, kc, p * 128:(p + 1) * 128], hb[:, kc, :]) for kc in range(8)],
                                 b_ps[bk], [b_wq] + bh)
                        evac(QT[:, p, s_ * 512:(s_ + 1) * 512], bank(bk), [b_ps[bk]], [b_qt[p][s_]])
                    pend.append(f)
            flush(2)

        norm_pipeline(16, load_own, 0, xn, b_xn, junk, b_junk, dst_all, (4, 6), after_own, mm=True)
        flush(100)
        WS.release(i_wk); WS.release(i_wv)
        WS.release(i_wq)
        S.barrier()

        Fb = [R(TMP + i * 8208, [128, 4, 513], F32) for i in range(2)]; b_F = [Buf("F%d" % i) for i in range(2)]
        D1 = R(TMP + 16416, [128, 4, 513], F32); b_D1 = Buf("D1")
        Pb = R(TMP + 24624, [128, 4, 513], F32); b_P = Buf("P")
        Wb = [R(TMP + 32832 + i * 4096, [128, 4, 512], BF16) for i in range(2)]; b_W = [Buf("W%d" % i) for i in range(2)]
        WTb = [R(TMP + 41024 + i * 4096, [128, 4, 512], BF16) for i in range(2)]; b_WT = [Buf("WT%d" % i) for i in range(2)]
        b_Z = Buf("Z")
        b_WTps = Buf("WTps")
        WTps = bank_bf(4, 2).rearrange("p (a b) -> p a b", a=4)
        for i in range(2):
            S.op("dve", "memset", [], [b_F[i]], Fb[i].rearrange("p a b -> p (a b)"), 0.0)
        S.op("dve", "memset", [], [b_D1], D1.rearrange("p a b -> p (a b)"), 0.0)
        groups = [(s, h, c) for s in range(4) for h in range(8) for c in range(NCH[s])]
        G = len(groups)

        def geom(gi):
            s, h, c = groups[gi]
            n = NCH[s]
            p = h // 2
            rows = slice(0, 64) if h % 2 == 0 else slice(64, 128)
            ob = 6 + (h % 2)
            cr = 8 - n + c
            return s, h, c, n, p, rows, ob, cr

        def stage1(gi):
            s, h, c, n, p, rows, ob, cr = geom(gi)
            k0 = cr * 512
            F_ = Fb[gi % 2]; bF = b_F[gi % 2]
            for i in range(4):
                q0 = s * 512 + i * 128
                zb = bank(i)
                last = (c >= 2)
                S.op("pe", "matmul", [b_qt[p][s], b_kt[p][cr]], [b_Z], zb, lhsT=QT[rows, p, q0:q0 + 128],
                     rhs=KT[rows, p, k0:k0 + 512], start=True, stop=last, _mark=(last and i == 3))
                if c < 2:
                    S.op("pe", "matmul", [b_seld, b_diag], [b_Z], zb, lhsT=seld[:, s, c, :], rhs=diag[:, i, :],
                         start=False, stop=False, _mark=False)
                    S.op("pe", "matmul", [b_seln, b_negrow], [b_Z], zb, lhsT=seln[0:1, s, c, :],
                         rhs=negrow[0:1, :], start=False, stop=True, _mark=(i == 3))
            S.op("act", "activation", [b_Z], [bF], out=F_[:, :, 1:513],
                 in_=bank(0, 4).rearrange("p (a b) -> p a b", a=4), func=AF.Sigmoid, scale=-0.125)

        def stage2(gi):
            s, h, c, n, p, rows, ob, cr = geom(gi)
            F_ = Fb[gi % 2]; bF = b_F[gi % 2]
            W_ = Wb[gi % 2]; bW = b_W[gi % 2]
            if c == 0:
                S.op("dve", "memset", [], [b_D1], D1[:, :, 0:1], 1.0)
            else:
                S.op("dve", "tensor_copy", [b_P], [b_D1], out=D1[:, :, 0:1], in_=Pb[:, :, 512:513])
            S.op("dve", "tensor_tensor_scan", [bF, b_D1], [b_P], out=Pb.rearrange("p a b -> p (a b)"),
                 data0=F_.rearrange("p a b -> p (a b)"), data1=D1.rearrange("p a b -> p (a b)"), initial=0.0,
                 op0=ALU.mult, op1=ALU.add)
            S.op("dve", "tensor_tensor", [b_P], [bW], out=W_, in0=Pb[:, :, 0:512], in1=Pb[:, :, 1:513],
                 op=ALU.subtract)

        def stage3(gi):
            s, h, c, n, p, rows, ob, cr = geom(gi)
            W_ = Wb[gi % 2]; bW = b_W[gi % 2]
            WT_ = WTb[gi % 2]; bWT = b_WT[gi % 2]
            for m in range(4):
                for i in range(4):
                    S.op("pe", "transpose", [bW, b_ident], [b_WTps], out=WTps[:, m, i * 128:(i + 1) * 128],
                         in_=W_[:, i, m * 128:(m + 1) * 128], identity=ident, _mark=(m == 3 and i == 3))
            S.op("act", "activation", [b_WTps], [bWT], out=WT_.rearrange("p a b -> p (a b)"),
                 in_=bank_bf(4, 2), func=AF.Copy)

        def stage4(gi):
            s, h, c, n, p, rows, ob, cr = geom(gi)
            WT_ = WTb[gi % 2]; bWT = b_WT[gi % 2]
            psO = psum[rows, ob * 512:(ob + 1) * 512]
            for m in range(4):
                first = (c == 0 and m == 0)
                lastpv = (c == n - 1 and m == 3)
                S.op("pe", "matmul", [bWT, b_v[cr]], [b_ps[ob]], psO, lhsT=V[:, cr * 4 + m, h * 64:(h + 1) * 64],
                     rhs=WT_[:, m, :], start=first, stop=lastpv, _mark=(m == 3))
            if c == n - 1:
                S.op("act", "activation", [b_ps[ob]], [b_oat[p][s]], out=OAT[rows, p, s * 512:(s + 1) * 512],
                     in_=psO, func=AF.Copy)

        for step in range(G + 3):
            if step < G:
                stage1(step)
            if 0 <= step - 1 < G:
                stage2(step - 1)
            if 0 <= step - 2 < G:
                stage3(step - 2)
            if 0 <= step - 3 < G:
                stage4(step - 3)
        S.barrier()

        hTo = R(DYN + 0, [128, 8, NOWN + 128], BF16); b_hTo = [Buf("hTo%d" % t) for t in range(17)]
        YBT = R(DYN + 34816, [128, 4, NOWN], BF16); b_ybt = [Buf("ybt%d" % j) for j in range(4)]
        xring2 = [R(DYN + 51200 + i * 4096, [128, D], F32) for i in range(2)]
        xn2 = [R(DYN + 59392 + i * 2048, [128, D], BF16) for i in range(2)] + [R(DYN + 139264, [128, D], BF16)]
        junk2 = R(DYN + 63488, [128, D], BF16)
        u_sb = R(DYN + 65536, [128, 512], F32); b_usb = Buf("usb")
        acc = R(DYN + 67584, [128, 512], F32); b_acc = Buf("acc")
        cub = R(DYN + 69632, [128, 2, 516], F32); b_cub = [Buf("cub0"), Buf("cub1")]
        MRG = R(DYN + 98304, [128, 8, NOWN], BF16)
        b_mrg = [[Buf("mrg%d_%d" % (c, s)) for s in range(4)] for c in range(8)]
        sgt = [R(DYN + 131072 + i * 2048, [128, 512], F32) for i in range(4)]; b_sgt = [Buf("sg%d" % i) for i in range(4)]
        b_x2 = [Buf("x2r%d" % i) for i in range(2)]; b_xn2 = [Buf("xn2_%d" % i) for i in range(3)]; b_junk2 = Buf("junk2")

        xring2.append(R(DYN + 141312, [128, D], F32)); b_x2.append(Buf("x2r2"))

        def load_own2(k):
            S.dma("sp", [(xring2[k % 3], xown[k * 128:(k + 1) * 128, :], {})], writes=[b_x2[k % 3]])
            return xring2[k % 3], b_x2[k % 3]

        norm_pipeline(17, load_own2, 0, xn2, b_xn2, junk2, b_junk2,
                      lambda k: (hTo[:, :, k * 128:(k + 1) * 128], b_hTo[k]), (4, 6), mm=True)

        wu, b_wu = WS.get(i_wu)
        wgb, b_wgb = WS.get(i_wgb)
        wgc, b_wgc = WS.get(i_wgc)
        cb = 0
        for j in range(4):
            jc = slice(j * 128, (j + 1) * 128)
            mm_group(bank(0)[:, 0:128], [(wu[:, kc, jc], hTo[:, kc, NOWN:NOWN + 128]) for kc in range(8)],
                     b_ps[0], [b_wu, b_hTo[16]])
            mm_group(bank(1)[:, 0:128], [(wgc[:, kc, jc], hTo[:, kc, NOWN:NOWN + 128]) for kc in range(8)],
                     b_ps[1], [b_wgc, b_hTo[16]])
            S.op("act", "activation", [b_ps[0]], [b_usb], out=u_sb[:, 0:8], in_=bank(0)[:, 0:8], func=AF.Copy)
            S.op("dve", "tensor_tensor", [b_ps[1], b_usb], [b_cuh], out=cuh, in0=bank(1)[:, 0:8], in1=u_sb[:, 0:8],
                 op=ALU.mult)
            for s in range(4):
                hs = slice(s * 512, (s + 1) * 512)
                rd = [b_hTo[s * 4 + t] for t in range(4)]
                pb = 2 + (cb % 2) * 3
                cu = cub[:, cb % 2, :]; bcu = b_cub[cb % 2]
                cb += 1
                mm_group(bank(pb), [(wu[:, kc, jc], hTo[:, kc, hs]) for kc in range(8)], b_ps[pb], [b_wu] + rd)
                mm_group(bank(pb + 1), [(wgc[:, kc, jc], hTo[:, kc, hs]) for kc in range(8)], b_ps[pb + 1], [b_wgc] + rd)
                mm_group(bank(pb + 2), [(wgb[:, kc, jc], hTo[:, kc, hs]) for kc in range(8)], b_ps[pb + 2], [b_wgb] + rd)
                S.op("act", "activation", [b_ps[pb]], [b_usb], out=u_sb, in_=bank(pb), func=AF.Copy)
                S.op("dve", "tensor_copy", [b_cuh], [bcu], out=cu[:, 0:2], in_=cuh[:, 2 * s:2 * s + 2])
                S.op("dve", "tensor_tensor", [b_ps[pb + 1], b_usb], [bcu], out=cu[:, 2:514], in0=bank(pb + 1), in1=u_sb,
                     op=ALU.mult)
                S.op("dve", "tensor_scalar", [bcu, b_convw], [b_acc], out=acc, in0=cu[:, 2:514],
                     scalar1=convw[:, j, 2:3], scalar2=None, op0=ALU.mult)
                S.op("dve", "scalar_tensor_tensor", [bcu, b_convw, b_acc], [b_acc], out=acc, in0=cu[:, 1:513],
                     scalar=convw[:, j, 1:2], in1=acc, op0=ALU.mult, op1=ALU.add)
                S.op("dve", "scalar_tensor_tensor", [bcu, b_convw, b_acc], [b_acc], out=acc, in0=cu[:, 0:512],
                     scalar=convw[:, j, 0:1], in1=acc, op0=ALU.mult, op1=ALU.add)
                S.op("dve", "tensor_tensor", [b_ps[pb + 2], b_acc], [b_ybt[j]], out=YBT[:, j, hs], in0=bank(pb + 2),
                     in1=acc, op=ALU.mult)
        WS.release(i_wu); WS.release(i_wgb); WS.release(i_wgc)

        it = 0
        for c4 in range(2):
            ia, iga, igb = i_d[c4]
            wab, b_wab = WS.get(ia)
            wga, b_wga = WS.get(iga)
            wgB, b_wgB = WS.get(igb)
            for cc in range(4):
                c = c4 * 4 + cc
                ccs = slice(cc * 128, (cc + 1) * 128)
                for s in range(4):
                    hs = slice(s * 512, (s + 1) * 512)
                    rd = [b_hTo[s * 4 + t] for t in range(4)]
                    pb = (it % 2) * 4
                    sa = sgt[(it % 2) * 2]; bsa = b_sgt[(it % 2) * 2]
                    sb = sgt[(it % 2) * 2 + 1]; bsb = b_sgt[(it % 2) * 2 + 1]
                    it += 1
                    mm_group(bank(pb), [(wab[:, kc, ccs], OAT[:, kc, hs]) for kc in range(4)], b_ps[pb],
                             [b_wab] + [b_oat[kc][s] for kc in range(4)])
                    mm_group(bank(pb + 1), [(wab[:, 4 + kc, ccs], YBT[:, kc, hs]) for kc in range(4)], b_ps[pb + 1],
                             [b_wab] + b_ybt)
                    mm_group(bank(pb + 2), [(wga[:, kc, ccs], hTo[:, kc, hs]) for kc in range(8)], b_ps[pb + 2], [b_wga] + rd)
                    mm_group(bank(pb + 3), [(wgB[:, kc, ccs], hTo[:, kc, hs]) for kc in range(8)], b_ps[pb + 3], [b_wgB] + rd)
                    S.op("act", "activation", [b_ps[pb + 2]], [bsa], out=sa, in_=bank(pb + 2), func=AF.Sigmoid)
                    S.op("act", "activation", [b_ps[pb + 3]], [bsb], out=sb, in_=bank(pb + 3), func=AF.Sigmoid)
                    S.op("dve", "tensor_tensor", [b_ps[pb], bsa], [bsa], out=sa, in0=bank(pb), in1=sa, op=ALU.mult)
                    S.op("dve", "tensor_tensor", [b_ps[pb + 1], bsb], [bsb], out=sb, in0=bank(pb + 1), in1=sb, op=ALU.mult)
                    S.op("dve", "tensor_tensor", [bsa, bsb], [b_mrg[c][s]], out=MRG[:, c, hs], in0=sa, in1=sb, op=ALU.add)
            WS.release(ia); WS.release(iga); WS.release(igb)
        S.barrier()

        X = R(DYN + 0, [128, 16, D], F32); b_X = [Buf("X%d" % t) for t in range(16)]
        for t in range(16):
            S.dma("sp", [(X[:, t, :], xown[t * 128:(t + 1) * 128, :], {})], writes=[b_X[t]])
        rb = [0]
        for hf in range(2):
            wm, b_wm = WS.get(i_wmo[hf])
            for t in range(16):
                bk = rb[0] % 8; rb[0] += 1
                mm_group(bank(bk), [(MRG[:, kc, t * 128:(t + 1) * 128], wm[:, kc, :]) for kc in range(8)], b_ps[bk],
                         [b_wm] + [b_mrg[kc][t // 4] for kc in range(8)])
                xs = X[:, t, hf * 512:(hf + 1) * 512]
                S.op("dve", "tensor_tensor", [b_ps[bk], b_X[t]], [b_X[t]], out=xs, in0=bank(bk), in1=xs, op=ALU.add)
            WS.release(i_wmo[hf])
        S.barrier()

        hTs = [R(DYN + 65536 + i * 8192, [128, 8, 512], BF16) for i in range(2)]
        b_hTs = [[Buf("hTs%d_%d" % (i, t)) for t in range(4)] for i in range(2)]
        qmT = R(DYN + 81920, [128, 8, 512], BF16); b_qmT = [Buf("qmT%d" % c) for c in range(8)]
        omT = R(DYN + 90112, [128, 8, 512], BF16); b_omT = [Buf("omT%d" % c) for c in range(8)]
        mT = R(DYN + 98304, [128, 8, 256], BF16); b_mT = [Buf("mT0"), Buf("mT1")]
        KmT = R(DYN + 102400, [128, 8, 256], BF16); b_KmT = Buf("KmT")
        Vm = R(DYN + 106496, [128, 2, D], BF16); b_Vm = Buf("Vm")
        Esm = R(DYN + 110592, [128, 4, 256], F32); b_Esm = Buf("Esm")
        probs = [R(DYN + 114688 + i * 2048, [128, 4, 256], BF16) for i in range(2)]; b_probs = [Buf("pr0"), Buf("pr1")]
        pT = R(DYN + 118784, [128, 8, 512], BF16); b_pT = Buf("pT")
        xn3 = [R(DYN + 126976 + i * 2048, [128, D], BF16) for i in range(2)]; b_xn3 = [Buf("xn3_0"), Buf("xn3_1")]
        junk3 = R(DYN + 131072, [128, D], BF16); b_junk3 = Buf("junk3")
        memring = [R(DYN + 118784 + i * 4096, [128, D], F32) for i in range(2)]; b_mr = [Buf("mr0"), Buf("mr1")]

        load_gain(1, norm_mem_kv)
        load_gain(0, norm_mem_q)
        for mt in range(2):
            S.dma("sp", [(memring[mt], memb[mt * 128:(mt + 1) * 128, :], {})], writes=[b_mr[mt]])
            norm_tile(memring[mt], b_mr[mt], 1, xn3[mt], b_xn3[mt], junk3, b_junk3)
            transpose_tile(xn3[mt], b_xn3[mt], mT[:, :, mt * 128:(mt + 1) * 128], b_mT[mt], (4, 5))
        wK = [WS.get(i_wkvK[k]) for k in range(2)]
        for c in range(8):
            bk = 6 + c % 2
            w_, bw_ = wK[c // 4]
            mm_group(bank(bk)[:, 0:256], [(w_[:, kc, (c % 4) * 128:(c % 4 + 1) * 128], mT[:, kc, :]) for kc in range(8)],
                     b_ps[bk], [bw_] + b_mT)
            evac(KmT[:, c, :], bank(bk)[:, 0:256], [b_ps[bk]], [b_KmT])
        WS.release(i_wkvK[0]); WS.release(i_wkvK[1])
        wVv = [WS.get(i_wkvV[k]) for k in range(2)]
        for mt in range(2):
            for hf in range(2):
                bk = 6 + hf
                w_, bw_ = wVv[hf]
                mm_group(bank(bk), [(mT[:, kc, mt * 128:(mt + 1) * 128], w_[:, kc, :]) for kc in range(8)],
                         b_ps[bk], [bw_, b_mT[mt]])
                evac(Vm[:, mt, hf * 512:(hf + 1) * 512], bank(bk), [b_ps[bk]], [b_Vm])
        WS.release(i_wkvV[0]); WS.release(i_wkvV[1])
        S.barrier()

        wMQ = [WS.get(i_wmq[k]) for k in range(2)]
        wMO = [WS.get(i_wmo2[k]) for k in range(2)]
        gb = [0]
        qmT2 = [qmT, R(DYN + 133120, [128, 8, 512], BF16)]
        b_qmT2 = [b_qmT, [Buf("qmTb%d" % c) for c in range(8)]]
        pT2 = [pT, R(DYN + 141312, [128, 8, 512], BF16)]
        b_pT2 = [b_pT, Buf("pTb")]

        def fg_norm(s):
            hb = hTs[s % 2]; bh = b_hTs[s % 2]
            norm_pipeline(4, lambda k, s=s: (X[:, s * 4 + k, :], b_X[s * 4 + k]), 0, xn3, b_xn3, junk3, b_junk3,
                          lambda k, hb=hb, bh=bh: (hb[:, :, k * 128:(k + 1) * 128], bh[k]), (4, 5))

        def q_groups(s):
            hb = hTs[s % 2]; bh = b_hTs[s % 2]
            qm = qmT2[s % 2]; bq = b_qmT2[s % 2]
            out = []
            for c in range(8):
                def f(c=c):
                    bk = 6 + gb[0] % 2; gb[0] += 1
                    w_, bw_ = wMQ[c // 4]
                    mm_group(bank(bk), [(w_[:, kc, (c % 4) * 128:(c % 4 + 1) * 128], hb[:, kc, :]) for kc in range(8)],
                             b_ps[bk], [bw_] + bh)
                    evac(qm[:, c, :], bank(bk), [b_ps[bk]], [bq[c]])
                out.append(f)
            return out

        def pvo_groups(s):
            pT_ = pT2[s % 2]; bpT = b_pT2[s % 2]
            out = []
            for h in range(4):
                for dc in range(2):
                    def f(h=h, dc=dc):
                        c = h * 2 + dc
                        bk = 6 + gb[0] % 2; gb[0] += 1
                        mm_group(bank(bk), [(Vm[:, mt, c * 128:(c + 1) * 128], pT_[:, h * 2 + mt, :]) for mt in range(2)],
                                 b_ps[bk], [b_Vm, bpT])
                        evac(omT[:, c, :], bank(bk), [b_ps[bk]], [b_omT[c]])
                    out.append(f)
            for t in range(4):
                for hf in range(2):
                    def f(t=t, hf=hf):
                        tt = s * 4 + t
                        bk = 6 + gb[0] % 2; gb[0] += 1
                        w_, bw_ = wMO[hf]
                        mm_group(bank(bk), [(omT[:, kc, t * 128:(t + 1) * 128], w_[:, kc, :]) for kc in range(8)],
                                 b_ps[bk], [bw_] + b_omT)
                        xs = X[:, tt, hf * 512:(hf + 1) * 512]
                        S.op("dve", "tensor_tensor", [b_ps[bk], b_X[tt]], [b_X[tt]], out=xs, in0=bank(bk), in1=xs, op=ALU.add)
                    out.append(f)
            return out

        def softmax_steps(s):
            qm = qmT2[s % 2]; bq = b_qmT2[s % 2]
            pT_ = pT2[s % 2]; bpT = b_pT2[s % 2]

            def sA(t):
                sb0 = (t % 2) * 2
                psS = bank(sb0, 2).rearrange("p (a b) -> p a b", a=4)
                for h in range(4):
                    for cc in range(2):
                        c = 2 * h + cc
                        S.op("pe", "matmul", [bq[c], b_KmT], [b_ps[sb0]], psS[:, h, :],
                             lhsT=qm[:, c, t * 128:(t + 1) * 128], rhs=KmT[:, c, :], start=(cc == 0), stop=(cc == 1),
                             _mark=(h == 3 and cc == 1))

            def sB(t):
                sb0 = (t % 2) * 2
                psS = bank(sb0, 2).rearrange("p (a b) -> p a b", a=4)
                b_S = b_ps[sb0]
                pr = probs[t % 2]; bpr = b_probs[t % 2]
                S.op("dve", "tensor_reduce", [b_S], [b_mx], out=mx4, in_=psS, axis=AX.X, op=ALU.max)
                S.op("dve", "tensor_scalar", [b_mx], [b_nb], out=nb4, in0=mx4, scalar1=-1.0 / 16, scalar2=None, op0=ALU.mult)
                for h in range(4):
                    S.op("act", "activation", [b_S, b_nb], [b_Esm, b_sm], out=Esm[:, h, :], in_=psS[:, h, :], func=AF.Exp,
                         scale=1.0 / 16, bias=nb4[:, h:h + 1], accum_out=sm4[:, h:h + 1])
                S.op("dve", "reciprocal", [b_sm], [b_rs], out=rs4, in_=sm4)
                for h in range(4):
                    S.op("dve", "tensor_scalar", [b_Esm, b_rs], [bpr], out=pr[:, h, :], in0=Esm[:, h, :],
                         scalar1=rs4[:, h:h + 1], scalar2=None, op0=ALU.mult)

            def sC(t):
                pr = probs[t % 2]; bpr = b_probs[t % 2]
                tb_ = 4 + (t % 2)
                pst = bank_bf(tb_)
                for h in range(4):
                    for mt in range(2):
                        k8 = h * 2 + mt
                        S.op("pe", "transpose", [bpr, b_ident], [b_ps[tb_]], out=pst[:, k8 * 128:(k8 + 1) * 128],
                             in_=pr[:, h, mt * 128:(mt + 1) * 128], identity=ident, _mark=(k8 == 7))
                S.op("act", "activation", [b_ps[tb_]], [bpT], out=pT_[:, :, t * 128:(t + 1) * 128],
                     in_=pst.rearrange("p (a b) -> p a b", a=8), func=AF.Copy)

            steps = []
            for step in range(6):
                def f(step=step):
                    if step < 4:
                        sA(step)
                    if 0 <= step - 1 < 4:
                        sB(step - 1)
                    if 0 <= step - 2 < 4:
                        sC(step - 2)
                steps.append(f)
            return steps

        fg_norm(0)
        for f in q_groups(0):
            f()
        fg_norm(1)
        for s in range(5):
            fill = []
            if s - 1 >= 0:
                fill += pvo_groups(s - 1)
            if s + 1 < 4:
                fill += q_groups(s + 1)
            if s < 4:
                steps = softmax_steps(s)
                per = (len(fill) + len(steps) - 1) // len(steps) if fill else 0
                for st_ in steps:
                    st_()
                    for _ in range(per):
                        if fill:
                            fill.pop(0)()
            while fill:
                fill.pop(0)()
            if s + 2 < 4:
                fg_norm(s + 2)
        for k in range(2):
            WS.release(i_wmq[k]); WS.release(i_wmo2[k])
        S.barrier()

        hTb = R(DYN + 65536, [128, 8, 1024], BF16); b_hTb = [Buf("hTb%d" % t) for t in range(8)]
        aT = R(DYN + 81920, [128, 22, 1024], BF16); b_aT = [Buf("aT%d" % j) for j in range(22)]
        sgl = [R(DYN + 126976 + i * 2048, [128, 512], F32) for i in range(2)]; b_sgl = [Buf("sgl0"), Buf("sgl1")]
        xn4 = [R(DYN + 131072 + i * 2048, [128, D], BF16) for i in range(2)] + [R(DYN + 147456, [128, D], BF16)]
        b_xn4 = [Buf("xn4_0"), Buf("xn4_1"), Buf("xn4_2")]
        junk4 = R(DYN + 135168, [128, D], BF16); b_junk4 = Buf("junk4")
        load_gain(1, norm_ffn)
        load_gain(0, norm_final)
        otmp = [R(DYN + 137216 + i * 4096, [128, D], F32) for i in range(2)]; b_ot = [Buf("ot0"), Buf("ot1")]
        junk5 = R(DYN + 145408, [128, D], BF16); b_junk5 = Buf("junk5")
        out_evs = []
        fit = 0
        for tb in range(2):
            gu, fo = i_ffn[tb]
            norm_pipeline(8, lambda k, tb=tb: (X[:, tb * 8 + k, :], b_X[tb * 8 + k]), 1, xn4, b_xn4, junk4, b_junk4,
                          lambda k: (hTb[:, :, k * 128:(k + 1) * 128], b_hTb[k]), (4, 6), mm=True)
            for blk in range(6):
                ig, iu, ncol = gu[blk]
                wg_, b_wg = WS.get(ig)
                wu_, b_wu_ = WS.get(iu)
                for cc in range(ncol // 128):
                    j = blk * 4 + cc
                    ccs = slice(cc * 128, (cc + 1) * 128)
                    for s2 in range(2):
                        hs = slice(s2 * 512, (s2 + 1) * 512)
                        rd = b_hTb[s2 * 4:(s2 + 1) * 4]
                        pb = (fit % 3) * 2
                        sg_ = sgl[fit % 2]; bsg = b_sgl[fit % 2]
                        fit += 1
                        mm_group(bank(pb), [(wg_[:, kc, ccs], hTb[:, kc, hs]) for kc in range(8)], b_ps[pb], [b_wg] + rd)
                        mm_group(bank(pb + 1), [(wu_[:, kc, ccs], hTb[:, kc, hs]) for kc in range(8)], b_ps[pb + 1],
                                 [b_wu_] + rd)
                        S.op("act", "activation", [b_ps[pb]], [bsg], out=sg_, in_=bank(pb), func=AF.Silu)
                        S.op("dve", "tensor_tensor", [b_ps[pb + 1], bsg], [b_aT[j]], out=aT[:, j, hs], in0=bank(pb + 1),
                             in1=sg_, op=ALU.mult)
                WS.release(ig); WS.release(iu)
            for hf in range(2):
                wfo = [WS.get(fo[hf][k3]) for k3 in range(3)]
                for t in range(8):
                    tt = tb * 8 + t
                    bk = 6 + (fit % 2); fit += 1
                    mm_group(bank(bk), [(aT[:, j, t * 128:(t + 1) * 128], wfo[j // 8][0][:, j % 8, :]) for j in range(22)],
                             b_ps[bk], [w[1] for w in wfo] + b_aT)
                    xs = X[:, tt, hf * 512:(hf + 1) * 512]
                    S.op("dve", "tensor_tensor", [b_ps[bk], b_X[tt]], [b_X[tt]], out=xs, in0=bank(bk), in1=xs, op=ALU.add)
                    if hf == 1:
                        i = tt % 2
                        norm_tile(X[:, tt, :], b_X[tt], 0, otmp[i], b_ot[i], junk5, b_junk5)
                        out_evs.append(S.dma("sp", [(out_d[tt * 128:(tt + 1) * 128, :], otmp[i], {})], reads=[b_ot[i]]))
                for k3 in range(3):
                    WS.release(fo[hf][k3])
        for ev in out_evs:
            S._wait("sp", ev)
        print("inst counts", S.n_inst, {e: len(S.prog[e]) for e in S.ENGS})
        S.emit()
    return nc


def _host_inputs(inputs):
    x = np.asarray(inputs["x"], dtype=np.float32)
    mem = np.asarray(inputs["mem"], dtype=np.float32)
    diag = np.zeros((128, 4, 512), np.float32)
    pp = np.arange(128)[:, None]
    kr = np.arange(512)[None, :]
    for i in range(4):
        ql = i * 128 + pp
        diag[:, i, :] = np.where(kr > 511 - ql, 0.0, NEG)
    eye = np.eye(128, dtype=np.float32)
    shared = {
        "diag": diag,
        "norm_mix": np.ascontiguousarray(inputs["norm_mix"][0:1]),
        "w_in": np.ascontiguousarray(inputs["w_in"][0]),
        "conv_w": np.ascontiguousarray(inputs["conv_w"][0]),
        "w_branch_a": np.ascontiguousarray(inputs["w_branch_a"][0]),
        "w_branch_b": np.ascontiguousarray(inputs["w_branch_b"][0]),
        "w_mix_out": np.ascontiguousarray(inputs["w_mix_out"][0]),
        "norm_mem_q": np.ascontiguousarray(inputs["norm_mem_q"][0:1]),
        "norm_mem_kv": np.ascontiguousarray(inputs["norm_mem_kv"][0:1]),
        "w_mem_q": np.ascontiguousarray(inputs["w_mem_q"][0]),
        "w_mem_kv": np.ascontiguousarray(inputs["w_mem_kv"][0]),
        "w_mem_o": np.ascontiguousarray(inputs["w_mem_o"][0]),
        "norm_ffn": np.ascontiguousarray(inputs["norm_ffn"][0:1]),
        "w_ffn_in": np.ascontiguousarray(inputs["w_ffn_in"][0]),
        "w_ffn_out": np.ascontiguousarray(inputs["w_ffn_out"][0]),
        "norm_final": np.ascontiguousarray(np.asarray(inputs["norm_final"]).reshape(1, D)),
    }
    shared = {k: np.asarray(v, dtype=np.float32) for k, v in shared.items()}
    in_maps = []
    for core in range(8):
        b, par = core // 2, core % 2
        tiles = T_OF[par]
        xown = np.zeros((NOWN + 128, D), np.float32)
        seld = np.zeros((128, 4, 2, 128), np.float32)
        seln = np.zeros((1, 4, 2, 128), np.float32)
        for j, T in enumerate(tiles):
            xown[j * 512:(j + 1) * 512] = x[b, T * 512:(T + 1) * 512]
            if T > 0:
                xown[NOWN + 2 * j:NOWN + 2 * j + 2] = x[b, T * 512 - 2:T * 512]
            tmax = NCH[j] - 1
            if T == tmax:
                seld[:, j, 0, :] = eye
            else:
                seln[0, j, 0, :] = 1.0
                seld[:, j, 1, :] = eye
        m = dict(shared)
        m["xall"] = np.ascontiguousarray(x[b, ::-1, :])
        m["xown"] = xown
        m["memb"] = np.ascontiguousarray(mem[b])
        m["seld"] = seld
        m["seln"] = seln
        in_maps.append(m)
    return in_maps


_NC_CACHE = {}


def kernel(**inputs):
    in_maps = _host_inputs(inputs)
    if "nc" not in _NC_CACHE:
        _NC_CACHE["nc"] = build_nc()
    nc = _NC_CACHE["nc"]
    res = run_bass_kernel_spmd(nc, in_maps, core_ids=list(range(8)))
    out = np.zeros((NB, SEQ, D), np.float32)
    for core in range(8):
        b, par = core // 2, core % 2
        o = np.asarray(res.results[core]["out"], dtype=np.float32)
        for j, T in enumerate(T_OF[par]):
            out[b, T * 512:(T + 1) * 512] = o[j * 512:(j + 1) * 512]
    return out
```

```python
import numpy as np
import concourse.bass as bass
import concourse.mybir as mybir
from concourse.bass_utils import run_bass_kernel_spmd
from contextlib import ExitStack

F32 = mybir.dt.float32
BF16 = mybir.dt.bfloat16
AF = mybir.ActivationFunctionType
ALU = mybir.AluOpType
AX = mybir.AxisListType

D = 1024
SEQ = 4096
NB = 4
TS = 512
T_OF = {0: (0, 3, 4, 7), 1: (1, 2, 5, 6)}
NCH = (2, 4, 6, 8)
NEG = -30000.0
EPS = 1e-6
FFN_H = 2816
NOWN = 2048
SAME_ENGINE_SYNC = True


class Buf:
    __slots__ = ("name", "w", "r")

    def __init__(self, name):
        self.name = name
        self.w = None
        self.r = {}


class Sched:
    ENGS = ("pe", "act", "dve", "pool", "sp")

    def __init__(self, nc, stack, n_dma_sems=8):
        self.nc = nc
        self.prog = {e: [] for e in self.ENGS}
        self.count = {e: 0 for e in self.ENGS}
        self.sem = {e: stack.enter_context(nc.semaphore("s_" + e)) for e in self.ENGS}
        self.seen = {e: {} for e in self.ENGS}
        self.dsem = {}
        self.dval = {}
        self.dring = {}
        self.dpos = {}
        idx = 0
        for q in ("sp", "pool"):
            ring = []
            for i in range(n_dma_sems):
                self.dsem[idx] = stack.enter_context(nc.semaphore("d_%s%d" % (q, i)))
                self.dval[idx] = 0
                ring.append(idx)
                idx += 1
            self.dring[q] = ring
            self.dpos[q] = 0
        self.n_inst = {e: 0 for e in self.ENGS}
        self.last_marked = {e: True for e in self.ENGS}

    def _wait(self, e, ev):
        if ev is None:
            return
        if ev[0] == "e":
            _, src, seq = ev
            if src == e and (e == "pe" or not SAME_ENGINE_SYNC):
                return
            assert self.count[src] >= seq, ("dependency on unissued mark", e, ev)
            key = ("e", src)
            val = seq
            sem = self.sem[src]
        else:
            _, sidx, val = ev
            key = ("d", sidx)
            sem = self.dsem[sidx]
        if self.seen[e].get(key, 0) >= val:
            return
        self.seen[e][key] = val
        self.prog[e].append(lambda eng, sem=sem, val=val: eng.wait_ge(sem, val))

    def _deps(self, e, reads, writes):
        for b in reads:
            self._wait(e, b.w)
        for b in writes:
            self._wait(e, b.w)
            for (k0, k1), v in list(b.r.items()):
                self._wait(e, (k0, k1, v))

    def _record(self, ev, reads, writes):
        key = (ev[0], ev[1])
        for b in reads:
            if b.r.get(key, 0) < ev[2]:
                b.r[key] = ev[2]
        for b in writes:
            b.w = ev
            b.r = {}

    def op(self, e, meth, reads, writes, *args, _mark=True, **kw):
        self._deps(e, reads, writes)
        sem = self.sem[e]
        if _mark:
            self.count[e] += 1
            self.prog[e].append(lambda eng: getattr(eng, meth)(*args, **kw).then_inc(sem, 1))
            ev = ("e", e, self.count[e])
        else:
            self.prog[e].append(lambda eng: getattr(eng, meth)(*args, **kw))
            ev = ("e", e, self.count[e] + 1)
        self.last_marked[e] = _mark
        self.n_inst[e] += 1
        self._record(ev, reads, writes)
        return ev

    def dma(self, q, xfers, reads=(), writes=()):
        self._deps(q, reads, writes)
        ring = self.dring[q]
        sidx = ring[self.dpos[q] % len(ring)]
        self.dpos[q] += 1
        if self.dval[sidx] > 0:
            self._wait(q, ("d", sidx, self.dval[sidx]))
        sem = self.dsem[sidx]
        for (o, i, kw) in xfers:
            self.dval[sidx] += 16
            self.prog[q].append(lambda eng, o=o, i=i, kw=kw: eng.dma_start(out=o, in_=i, **kw).then_inc(sem, 16))
        ev = ("d", sidx, self.dval[sidx])
        self._record(ev, reads, writes)
        return ev

    def barrier(self):
        for e in ("pe", "act", "dve"):
            assert self.last_marked[e], ("barrier with unmarked tail", e)
        for sidx in self.dring["sp"]:
            if self.dval[sidx] > 0:
                self._wait("sp", ("d", sidx, self.dval[sidx]))
        for f in ("pe", "act", "dve"):
            self._wait("sp", ("e", f, self.count[f]))
        self.count["sp"] += 1
        sem = self.sem["sp"]
        self.prog["sp"].append(lambda eng, sem=sem: eng.sem_inc(sem, 1))
        for e in ("pe", "act", "dve"):
            self._wait(e, ("e", "sp", self.count["sp"]))

    def emit(self):
        with self.nc.Block() as block:
            def mk(name):
                def body(engine):
                    for c in self.prog[name]:
                        c(engine)
                return body
            block.sync(mk("sp"))
            block.gpsimd(mk("pool"))
            block.scalar(mk("act"))
            block.vector(mk("dve"))
            block.tensor(mk("pe"))


def build_nc():
    nc = bass.Bass("TRN2", target_bir_lowering=False)

    def din(name, shape):
        return nc.dram_tensor(name, list(shape), F32, kind="ExternalInput").ap()

    xall = din("xall", [SEQ, D])
    xown = din("xown", [NOWN + 128, D])
    memb = din("memb", [256, D])
    seld_d = din("seld", [128, 4, 2, 128])
    seln_d = din("seln", [1, 4, 2, 128])
    diag_d = din("diag", [128, 4, 512])
    norm_mix = din("norm_mix", [1, D])
    w_in = din("w_in", [D, 5120])
    conv_w = din("conv_w", [3, 512])
    w_ba = din("w_branch_a", [512, D])
    w_bb = din("w_branch_b", [512, D])
    w_mix_out = din("w_mix_out", [D, D])
    norm_mem_q = din("norm_mem_q", [1, D])
    norm_mem_kv = din("norm_mem_kv", [1, D])
    w_mem_q = din("w_mem_q", [D, D])
    w_mem_kv = din("w_mem_kv", [D, 2 * D])
    w_mem_o = din("w_mem_o", [D, D])
    norm_ffn = din("norm_ffn", [1, D])
    w_ffn_in = din("w_ffn_in", [D, 2 * FFN_H])
    w_ffn_out = din("w_ffn_out", [FFN_H, D])
    norm_final = din("norm_final", [1, D])
    out_d = nc.dram_tensor("out", [NOWN, D], F32, kind="ExternalOutput").ap()

    with ExitStack() as st:
        S = Sched(nc, st)
        ARENA_F32 = 53200
        arena = st.enter_context(nc.sbuf_tensor("arena", [128, ARENA_F32], F32))
        psum = st.enter_context(nc.psum_tensor("psum", [128, 4096], F32))

        def R(off, shape, dt):
            esz = 4 if dt == F32 else 2
            n = int(np.prod(shape[1:]))
            nbytes = n * esz
            assert off % 4 == 0 and nbytes % 4 == 0, (off, shape)
            assert off + nbytes <= ARENA_F32 * 4, (off, shape)
            ap = arena[:, off // 4:(off + nbytes) // 4]
            if dt != F32:
                ap = ap.bitcast(dt)
            if len(shape) == 3:
                ap = ap.rearrange("p (a b) -> p a b", a=shape[1])
            elif len(shape) == 4:
                ap = ap.rearrange("p (a b c) -> p a b c", a=shape[1], b=shape[2])
            return ap

        def bank(i, n=1):
            return psum[:, i * 512:(i + n) * 512]

        def bank_bf(i, n=1):
            return psum[:, i * 512:(i + n) * 512].bitcast(BF16)

        b_ps = [Buf("ps%d" % i) for i in range(8)]

        ident = R(0, [128, 128], BF16); b_ident = Buf("ident")
        identf = R(256, [128, 128], F32); b_identf = Buf("identf")
        convw = R(768, [128, 4, 3], F32); b_convw = Buf("convw")
        stats = R(1024, [128, 256], F32)
        seld = R(2048, [128, 4, 2, 128], BF16); b_seld = Buf("seld")
        seln = R(4096, [128, 4, 2, 128], BF16); b_seln = Buf("seln")
        negrow = R(6144, [128, 512], BF16); b_negrow = Buf("negrow")
        zeros = R(7168, [128, 512], F32); b_zeros = Buf("zeros")
        diag = R(9216, [128, 4, 512], BF16); b_diag = Buf("diag")
        GAIN0 = 13312
        g_rep = [R(GAIN0 + i * 4096, [128, D], F32) for i in range(2)]
        b_g = [Buf("g%d" % i) for i in range(2)]
        RING0 = 21504
        NSLOT = 5
        DYN = RING0 + NSLOT * 8192

        ssq = stats[:, 0:8]; rstd = stats[:, 8:16]
        b_st = [Buf("st%d" % i) for i in range(8)]
        st_pos = [0]
        mx4 = stats[:, 16:20]; nb4 = stats[:, 20:24]; sm4 = stats[:, 24:28]; rs4 = stats[:, 28:32]
        b_mx = Buf("mx"); b_nb = Buf("nb"); b_sm = Buf("sm"); b_rs = Buf("rs")
        cuh = stats[:, 32:40]; b_cuh = Buf("cuh")

        class WStream:
            def __init__(self):
                self.blocks = []
                self.issued = 0
                self.released = set()
                self.bufs = [Buf("ring%d" % i) for i in range(NSLOT)]
                self.next_get = 0

            def add(self, pieces):
                self.blocks.append(pieces)
                return len(self.blocks) - 1

            def slot_ap(self, i):
                return R(RING0 + (i % NSLOT) * 8192, [128, 8, 512], BF16)

            def _issue(self):
                while self.issued < len(self.blocks):
                    k = self.issued
                    if k >= NSLOT and (k - NSLOT) not in self.released:
                        break
                    if k > self.next_get + NSLOT - 1:
                        break
                    sl = self.slot_ap(k)
                    xf = []
                    for (src, kc0, kcn, ncols) in self.blocks[k]:
                        xf.append((sl[:, kc0:kc0 + kcn, 0:ncols], src, {}))
                    S.dma("pool", xf, writes=[self.bufs[k % NSLOT]])
                    self.issued += 1

            def get(self, idx):
                assert idx == self.next_get, (idx, self.next_get)
                self.next_get += 1
                self._issue()
                assert self.issued > idx, ("weight block not issued", idx)
                return self.slot_ap(idx), self.bufs[idx % NSLOT]

            def release(self, idx):
                self.released.add(idx)
                self._issue()

        WS = WStream()

        def wpiece(w, r0, nrows, c0, ncols, kc0=0):
            src = w[r0:r0 + nrows, c0:c0 + ncols].rearrange("(kc p) n -> p kc n", p=128)
            return (src, kc0, nrows // 128, ncols)

        i_wk = WS.add([wpiece(w_in, 0, D, 512, 512)])
        i_wv = WS.add([wpiece(w_in, 0, D, 1024, 512)])
        i_wq = WS.add([wpiece(w_in, 0, D, 0, 512)])
        i_wu = WS.add([wpiece(w_in, 0, D, 1536, 512)])
        i_wgb = WS.add([wpiece(w_in, 0, D, 2048, 512)])
        i_wgc = WS.add([wpiece(w_in, 0, D, 2560, 512)])
        i_d = []
        for c4 in range(2):
            a = WS.add([wpiece(w_ba, 0, 512, c4 * 512, 512, kc0=0), wpiece(w_bb, 0, 512, c4 * 512, 512, kc0=4)])
            b_ = WS.add([wpiece(w_in, 0, D, 3072 + c4 * 512, 512)])
            c_ = WS.add([wpiece(w_in, 0, D, 4096 + c4 * 512, 512)])
            i_d.append((a, b_, c_))
        i_wmo = [WS.add([wpiece(w_mix_out, 0, D, hf * 512, 512)]) for hf in range(2)]
        i_wkvK = [WS.add([wpiece(w_mem_kv, 0, D, k * 512, 512)]) for k in range(2)]
        i_wkvV = [WS.add([wpiece(w_mem_kv, 0, D, D + k * 512, 512)]) for k in range(2)]
        i_wmq = [WS.add([wpiece(w_mem_q, 0, D, k * 512, 512)]) for k in range(2)]
        i_wmo2 = [WS.add([wpiece(w_mem_o, 0, D, k * 512, 512)]) for k in range(2)]
        i_ffn = []
        for tb in range(2):
            gu = []
            for blk in range(6):
                ncol = 512 if blk < 5 else 256
                ig = WS.add([wpiece(w_ffn_in, 0, D, blk * 512, ncol)])
                iu = WS.add([wpiece(w_ffn_in, 0, D, FFN_H + blk * 512, ncol)])
                gu.append((ig, iu, ncol))
            fo = []
            for hf in range(2):
                ks = []
                for k3 in range(3):
                    nr = 1024 if k3 < 2 else FFN_H - 2048
                    ks.append(WS.add([wpiece(w_ffn_out, k3 * 1024, nr, hf * 512, 512)]))
                fo.append(ks)
            i_ffn.append((gu, fo))

        evac_rr = [0]

        def evac(out_ap, in_ap, reads, writes, eng=None):
            if eng is None:
                eng = ("act", "dve")[evac_rr[0] % 2]
                evac_rr[0] += 1
            if eng == "act":
                return S.op("act", "activation", reads, writes, out=out_ap, in_=in_ap, func=AF.Copy)
            return S.op("dve", "tensor_copy", reads, writes, out=out_ap, in_=in_ap)

        def mm_group(out_ap, pairs, b_out, reads):
            n = len(pairs)
            for k, (l, r) in enumerate(pairs):
                S.op("pe", "matmul", reads, [b_out], out_ap, lhsT=l, rhs=r, start=(k == 0), stop=(k == n - 1),
                     _mark=(k == n - 1))

        def load_gain(i, src):
            S.dma("sp", [(g_rep[i], src.broadcast_to([128, D]), {})], writes=[b_g[i]])

        def norm_tile(x_ap, b_x, gi, out_ap, b_out, junk_ap, b_junk_):
            k = st_pos[0] % 8
            st_pos[0] += 1
            bs = b_st[k]
            S.op("act", "activation", [b_x], [b_junk_, bs], out=junk_ap, in_=x_ap, func=AF.Square,
                 accum_out=ssq[:, k:k + 1])
            S.op("act", "activation", [bs], [bs], out=rstd[:, k:k + 1], in_=ssq[:, k:k + 1], func=AF.Ln,
                 scale=1.0 / D, bias=EPS)
            S.op("act", "activation", [bs], [bs], out=rstd[:, k:k + 1], in_=rstd[:, k:k + 1], func=AF.Exp, scale=-0.5)
            S.op("dve", "scalar_tensor_tensor", [b_x, bs, b_g[gi]], [b_out], out=out_ap, in0=x_ap,
                 scalar=rstd[:, k:k + 1], in1=g_rep[gi], op0=ALU.mult, op1=ALU.mult)

        tr_rr = [0]

        def transpose_tile(xn_ap, b_xn_, dst3, b_dst, tbanks, mm=False):
            bk = tbanks[tr_rr[0] % len(tbanks)]
            tr_rr[0] += 1
            if mm:
                pst = bank(bk, 2)
                bb = [b_ps[bk], b_ps[bk + 1]]
                for c in range(8):
                    S.op("pe", "matmul", [b_xn_, b_ident], bb, pst[:, c * 128:(c + 1) * 128],
                         lhsT=xn_ap[:, c * 128:(c + 1) * 128], rhs=ident, start=True, stop=True, _mark=(c == 7))
                evac(dst3, pst.rearrange("p (a b) -> p a b", a=8), bb, [b_dst])
                return
            pst = bank_bf(bk)
            for c in range(8):
                S.op("pe", "transpose", [b_xn_, b_ident], [b_ps[bk]], out=pst[:, c * 128:(c + 1) * 128],
                     in_=xn_ap[:, c * 128:(c + 1) * 128], identity=ident, _mark=(c == 7))
            S.op("act", "activation", [b_ps[bk]], [b_dst], out=dst3, in_=pst.rearrange("p (a b) -> p a b", a=8),
                 func=AF.Copy)

        def norm_pipeline(n, load_fn, gi, xn_bufs, b_xn_bufs, junk_ap, b_junk_, dst_fn, tbanks, after_fn=None, mm=False):
            nb_ = len(xn_bufs)
            la = nb_ - 1

            def pre(k):
                x_ap, b_x = load_fn(k)
                norm_tile(x_ap, b_x, gi, xn_bufs[k % nb_], b_xn_bufs[k % nb_], junk_ap, b_junk_)

            def post(k):
                dst3, b_dst = dst_fn(k)
                transpose_tile(xn_bufs[k % nb_], b_xn_bufs[k % nb_], dst3, b_dst, tbanks, mm=mm)
            for k in range(min(la, n)):
                pre(k)
            for k in range(n):
                if k + la < n:
                    pre(k + la)
                post(k)
                if after_fn is not None:
                    after_fn(k)

        S.op("dve", "memset", [], [b_identf], identf, 1.0)
        S.op("pool", "affine_select", [b_identf], [b_identf], out=identf, in_=identf, pattern=[[-1, 128]],
             compare_op=ALU.is_equal, fill=0.0, base=0, channel_multiplier=1)
        S.op("dve", "tensor_copy", [b_identf], [b_ident], out=ident, in_=identf)
        S.op("dve", "memset", [], [b_negrow], negrow, NEG)
        S.op("dve", "memset", [], [b_zeros], zeros, 0.0)
        S.dma("pool", [(seld, seld_d, {})], writes=[b_seld])
        S.dma("pool", [(seln[0:1], seln_d, {})], writes=[b_seln])
        S.dma("pool", [(diag, diag_d, {})], writes=[b_diag])
        S.dma("sp", [(convw[:, j, :], conv_w[:, j * 128:(j + 1) * 128].rearrange("i p -> p i"),
                      {"allow_slow_non_contiguous": True}) for j in range(4)], writes=[b_convw])
        load_gain(0, norm_mix)

        KT = R(DYN + 0, [128, 4, SEQ], BF16)
        V = R(DYN + 32768, [128, 32, 512], BF16)
        QT = R(DYN + 65536, [128, 4, NOWN], BF16)
        OAT = R(DYN + 81920, [128, 4, NOWN], BF16)
        b_oat = [[Buf("oat%d_%d" % (p, s)) for s in range(4)] for p in range(4)]
        TMP = DYN + 98304
        xring = [R(TMP + i * 4096, [128, D], F32) for i in range(4)]; b_xr = [Buf("xr%d" % i) for i in range(4)]
        xn = [R(TMP + 16384 + i * 2048, [128, D], BF16) for i in range(4)]; b_xn = [Buf("xn%d" % i) for i in range(4)]
        junk = R(TMP + 24576, [128, D], BF16); b_junk = Buf("junk")
        hTa = [R(TMP + 26624 + i * 8192, [128, 8, 512], BF16) for i in range(2)]
        b_hTa = [[Buf("hTa%d_%d" % (i, t)) for t in range(4)] for i in range(2)]
        b_kt = [[Buf("kt%d_%d" % (p, ch)) for ch in range(8)] for p in range(4)]
        b_v = [Buf("v%d" % ch) for ch in range(8)]
        b_qt = [[Buf("qt%d_%d" % (p, s)) for s in range(4)] for p in range(4)]

        wk, b_wk = WS.get(i_wk)
        wv, b_wv = WS.get(i_wv)
        wq, b_wq = WS.get(i_wq)

        mmb = [0]

        def load_all(k):
            xt = xring[k % 4]; bx = b_xr[k % 4]
            S.dma("sp", [(xt, xall[k * 128:(k + 1) * 128, :], {})], writes=[bx])
            return xt, bx

        def dst_all(k):
            ch, t = k // 4, k % 4
            return hTa[ch % 2][:, :, t * 128:(t + 1) * 128], b_hTa[ch % 2][t]

        pend = []

        def flush(nmax):
            for _ in range(min(nmax, len(pend))):
                pend.pop(0)()

        def after_all(k):
            if k % 4 == 3:
                ch = k // 4
                hb = hTa[ch % 2]; bh = b_hTa[ch % 2]
                for p in range(4):
                    def f(p=p, ch=ch, hb=hb, bh=bh):
                        bk = mmb[0] % 4; mmb[0] += 1
                        mm_group(bank(bk), [(wk[:, kc, p * 128:(p + 1) * 128], hb[:, kc, :]) for kc in range(8)],
                                 b_ps[bk], [b_wk] + bh)
                        evac(KT[:, p, ch * 512:(ch + 1) * 512], bank(bk), [b_ps[bk]], [b_kt[p][ch]])
                    pend.append(f)
                for t in range(4):
                    def f(t=t, ch=ch, hb=hb, bh=bh):
                        bk = mmb[0] % 4; mmb[0] += 1
                        mm_group(bank(bk), [(hb[:, kc, t * 128:(t + 1) * 128], wv[:, kc, :]) for kc in range(8)],
                                 b_ps[bk], [b_wv, bh[t]])
                        evac(V[:, ch * 4 + t, :], bank(bk), [b_ps[bk]], [b_v[ch]])
                    pend.append(f)
            flush(2)

        norm_pipeline(32, load_all, 0, xn, b_xn, junk, b_junk, dst_all, (4, 6), after_all, mm=True)

        def load_own(k):
            xt = xring[k % 4]; bx = b_xr[k % 4]
            S.dma("sp", [(xt, xown[k * 128:(k + 1) * 128, :], {})], writes=[bx])
            return xt, bx

        def after_own(k):
            if k % 4 == 3:
                s_ = k // 4
                hb = hTa[s_ % 2]; bh = b_hTa[s_ % 2]
                for p in range(4):
                    def f(p=p, s_=s_, hb=hb, bh=bh):
                        bk = mmb[0] % 4; mmb[0] += 1
                        mm_group(bank(bk), [(wq[:, kc, p * 128:(p + 1) * 128], hb[:, kc, :]) for kc in range(8)],
                                 b_ps[bk], [b_wq] + bh)
                        evac(QT[:, p, s_ * 512:(s_ + 1) * 512], bank(bk), [b_ps[bk]], [b_qt[p][s_]])
                    pend.append(f)
            flush(2)

        norm_pipeline(16, load_own, 0, xn, b_xn, junk, b_junk, dst_all, (4, 6), after_own, mm=True)
        flush(100)
        WS.release(i_wk); WS.release(i_wv)
        WS.release(i_wq)
        S.barrier()

        Fb = [R(TMP + i * 8208, [128, 4, 513], F32) for i in range(2)]; b_F = [Buf("F%d" % i) for i in range(2)]
        D1 = R(TMP + 16416, [128, 4, 513], F32); b_D1 = Buf("D1")
        Pb = R(TMP + 24624, [128, 4, 513], F32); b_P = Buf("P")
        Wb = [R(TMP + 32832 + i * 4096, [128, 4, 512], BF16) for i in range(2)]; b_W = [Buf("W%d" % i) for i in range(2)]
        WTb = [R(TMP + 41024 + i * 4096, [128, 4, 512], BF16) for i in range(2)]; b_WT = [Buf("WT%d" % i) for i in range(2)]
        b_Z = Buf("Z")
        b_WTps = Buf("WTps")
        WTps = bank_bf(4, 2).rearrange("p (a b) -> p a b", a=4)
        for i in range(2):
            S.op("dve", "memset", [], [b_F[i]], Fb[i].rearrange("p a b -> p (a b)"), 0.0)
        S.op("dve", "memset", [], [b_D1], D1.rearrange("p a b -> p (a b)"), 0.0)
        groups = [(s, h, c) for s in range(4) for h in range(8) for c in range(NCH[s])]
        G = len(groups)

        def geom(gi):
            s, h, c = groups[gi]
            n = NCH[s]
            p = h // 2
            rows = slice(0, 64) if h % 2 == 0 else slice(64, 128)
            ob = 6 + (h % 2)
            cr = 8 - n + c
            return s, h, c, n, p, rows, ob, cr

        def stage1(gi):
            s, h, c, n, p, rows, ob, cr = geom(gi)
            k0 = cr * 512
            F_ = Fb[gi % 2]; bF = b_F[gi % 2]
            for i in range(4):
                q0 = s * 512 + i * 128
                zb = bank(i)
                last = (c >= 2)
                S.op("pe", "matmul", [b_qt[p][s], b_kt[p][cr]], [b_Z], zb, lhsT=QT[rows, p, q0:q0 + 128],
                     rhs=KT[rows, p, k0:k0 + 512], start=True, stop=last, _mark=(last and i == 3))
                if c < 2:
                    S.op("pe", "matmul", [b_seld, b_diag], [b_Z], zb, lhsT=seld[:, s, c, :], rhs=diag[:, i, :],
                         start=False, stop=False, _mark=False)
                    S.op("pe", "matmul", [b_seln, b_negrow], [b_Z], zb, lhsT=seln[0:1, s, c, :],
                         rhs=negrow[0:1, :], start=False, stop=True, _mark=(i == 3))
            S.op("act", "activation", [b_Z], [bF], out=F_[:, :, 1:513],
                 in_=bank(0, 4).rearrange("p (a b) -> p a b", a=4), func=AF.Sigmoid, scale=-0.125)

        def stage2(gi):
            s, h, c, n, p, rows, ob, cr = geom(gi)
            F_ = Fb[gi % 2]; bF = b_F[gi % 2]
            W_ = Wb[gi % 2]; bW = b_W[gi % 2]
            if c == 0:
                S.op("dve", "memset", [], [b_D1], D1[:, :, 0:1], 1.0)
            else:
                S.op("dve", "tensor_copy", [b_P], [b_D1], out=D1[:, :, 0:1], in_=Pb[:, :, 512:513])
            S.op("dve", "tensor_tensor_scan", [bF, b_D1], [b_P], out=Pb.rearrange("p a b -> p (a b)"),
                 data0=F_.rearrange("p a b -> p (a b)"), data1=D1.rearrange("p a b -> p (a b)"), initial=0.0,
                 op0=ALU.mult, op1=ALU.add)
            S.op("dve", "tensor_tensor", [b_P], [bW], out=W_, in0=Pb[:, :, 0:512], in1=Pb[:, :, 1:513],
                 op=ALU.subtract)

        def stage3(gi):
            s, h, c, n, p, rows, ob, cr = geom(gi)
            W_ = Wb[gi % 2]; bW = b_W[gi % 2]
            WT_ = WTb[gi % 2]; bWT = b_WT[gi % 2]
            for m in range(4):
                for i in range(4):
                    S.op("pe", "transpose", [bW, b_ident], [b_WTps], out=WTps[:, m, i * 128:(i + 1) * 128],
                         in_=W_[:, i, m * 128:(m + 1) * 128], identity=ident, _mark=(m == 3 and i == 3))
            S.op("act", "activation", [b_WTps], [bWT], out=WT_.rearrange("p a b -> p (a b)"),
                 in_=bank_bf(4, 2), func=AF.Copy)

        def stage4(gi):
            s, h, c, n, p, rows, ob, cr = geom(gi)
            WT_ = WTb[gi % 2]; bWT = b_WT[gi % 2]
            psO = psum[rows, ob * 512:(ob + 1) * 512]
            for m in range(4):
                first = (c == 0 and m == 0)
                lastpv = (c == n - 1 and m == 3)
                S.op("pe", "matmul", [bWT, b_v[cr]], [b_ps[ob]], psO, lhsT=V[:, cr * 4 + m, h * 64:(h + 1) * 64],
                     rhs=WT_[:, m, :], start=first, stop=lastpv, _mark=(m == 3))
            if c == n - 1:
                S.op("act", "activation", [b_ps[ob]], [b_oat[p][s]], out=OAT[rows, p, s * 512:(s + 1) * 512],
                     in_=psO, func=AF.Copy)

        for step in range(G + 3):
            if step < G:
                stage1(step)
            if 0 <= step - 1 < G:
                stage2(step - 1)
            if 0 <= step - 2 < G:
                stage3(step - 2)
            if 0 <= step - 3 < G:
                stage4(step - 3)
        S.barrier()

        hTo = R(DYN + 0, [128, 8, NOWN + 128], BF16); b_hTo = [Buf("hTo%d" % t) for t in range(17)]
        YBT = R(DYN + 34816, [128, 4, NOWN], BF16); b_ybt = [Buf("ybt%d" % j) for j in range(4)]
        xring2 = [R(DYN + 51200 + i * 4096, [128, D], F32) for i in range(2)]
        xn2 = [R(DYN + 59392 + i * 2048, [128, D], BF16) for i in range(2)] + [R(DYN + 139264, [128, D], BF16)]
        junk2 = R(DYN + 63488, [128, D], BF16)
        u_sb = R(DYN + 65536, [128, 512], F32); b_usb = Buf("usb")
        acc = R(DYN + 67584, [128, 512], F32); b_acc = Buf("acc")
        cub = R(DYN + 69632, [128, 2, 516], F32); b_cub = [Buf("cub0"), Buf("cub1")]
        MRG = R(DYN + 98304, [128, 8, NOWN], BF16)
        b_mrg = [[Buf("mrg%d_%d" % (c, s)) for s in range(4)] for c in range(8)]
        sgt = [R(DYN + 131072 + i * 2048, [128, 512], F32) for i in range(4)]; b_sgt = [Buf("sg%d" % i) for i in range(4)]
        b_x2 = [Buf("x2r%d" % i) for i in range(2)]; b_xn2 = [Buf("xn2_%d" % i) for i in range(3)]; b_junk2 = Buf("junk2")

        xring2.append(R(DYN + 141312, [128, D], F32)); b_x2.append(Buf("x2r2"))

        wu, b_wu = WS.get(i_wu)
        wgb, b_wgb = WS.get(i_wgb)
        wgc, b_wgc = WS.get(i_wgc)
        order2 = [16] + list(range(16))
        cuh4 = stats[:, 32:64].rearrange("p (j e) -> p j e", j=4)
        hu_sb = stats[:, 64:96].rearrange("p (j e) -> p j e", j=4); b_husb = Buf("husb")
        b_cuh4 = [Buf("cuh%d" % j) for j in range(4)]

        def load_own2(k):
            tt = order2[k]
            S.dma("sp", [(xring2[k % 3], xown[tt * 128:(tt + 1) * 128, :], {})], writes=[b_x2[k % 3]])
            return xring2[k % 3], b_x2[k % 3]

        pend2 = []
        cbx = [0]

        def halo_work(j):
            jc = slice(j * 128, (j + 1) * 128)
            mm_group(bank(0)[:, 0:128], [(wu[:, kc, jc], hTo[:, kc, NOWN:NOWN + 128]) for kc in range(8)],
                     b_ps[0], [b_wu, b_hTo[16]])
            mm_group(bank(1)[:, 0:128], [(wgc[:, kc, jc], hTo[:, kc, NOWN:NOWN + 128]) for kc in range(8)],
                     b_ps[1], [b_wgc, b_hTo[16]])
            S.op("act", "activation", [b_ps[0]], [b_husb], out=hu_sb[:, j, :], in_=bank(0)[:, 0:8], func=AF.Copy)
            S.op("dve", "tensor_tensor", [b_ps[1], b_husb], [b_cuh4[j]], out=cuh4[:, j, :], in0=bank(1)[:, 0:8],
                 in1=hu_sb[:, j, :], op=ALU.mult)

        def conv_work(s, j):
            jc = slice(j * 128, (j + 1) * 128)
            hs = slice(s * 512, (s + 1) * 512)
            rd = [b_hTo[s * 4 + t] for t in range(4)]
            pb = (cbx[0] % 2) * 3
            cu = cub[:, cbx[0] % 2, :]; bcu = b_cub[cbx[0] % 2]
            cbx[0] += 1
            mm_group(bank(pb), [(wu[:, kc, jc], hTo[:, kc, hs]) for kc in range(8)], b_ps[pb], [b_wu] + rd)
            mm_group(bank(pb + 1), [(wgc[:, kc, jc], hTo[:, kc, hs]) for kc in range(8)], b_ps[pb + 1], [b_wgc] + rd)
            mm_group(bank(pb + 2), [(wgb[:, kc, jc], hTo[:, kc, hs]) for kc in range(8)], b_ps[pb + 2], [b_wgb] + rd)
            S.op("act", "activation", [b_ps[pb]], [b_usb], out=u_sb, in_=bank(pb), func=AF.Copy)
            S.op("dve", "tensor_copy", [b_cuh4[j]], [bcu], out=cu[:, 0:2], in_=cuh4[:, j, 2 * s:2 * s + 2])
            S.op("dve", "tensor_tensor", [b_ps[pb + 1], b_usb], [bcu], out=cu[:, 2:514], in0=bank(pb + 1), in1=u_sb,
                 op=ALU.mult)
            S.op("dve", "tensor_scalar", [bcu, b_convw], [b_acc], out=acc, in0=cu[:, 2:514],
                 scalar1=convw[:, j, 2:3], scalar2=None, op0=ALU.mult)
            S.op("dve", "scalar_tensor_tensor", [bcu, b_convw, b_acc], [b_acc], out=acc, in0=cu[:, 1:513],
                 scalar=convw[:, j, 1:2], in1=acc, op0=ALU.mult, op1=ALU.add)
            S.op("dve", "scalar_tensor_tensor", [bcu, b_convw, b_acc], [b_acc], out=acc, in0=cu[:, 0:512],
                 scalar=convw[:, j, 0:1], in1=acc, op0=ALU.mult, op1=ALU.add)
            S.op("dve", "tensor_tensor", [b_ps[pb + 2], b_acc], [b_ybt[j]], out=YBT[:, j, hs], in0=bank(pb + 2),
                 in1=acc, op=ALU.mult)

        def after_conv(k):
            tt = order2[k]
            if k == 0:
                for j in range(4):
                    pend2.append(lambda j=j: halo_work(j))
            elif tt % 4 == 3:
                for j in range(4):
                    pend2.append(lambda s=tt // 4, j=j: conv_work(s, j))
            if pend2:
                pend2.pop(0)()

        norm_pipeline(17, load_own2, 0, xn2, b_xn2, junk2, b_junk2,
                      lambda k: (hTo[:, :, order2[k] * 128:(order2[k] + 1) * 128], b_hTo[order2[k]]), (6,), after_conv,
                      mm=True)
        while pend2:
            pend2.pop(0)()
        WS.release(i_wu); WS.release(i_wgb); WS.release(i_wgc)

        it = 0
        for c4 in range(2):
            ia, iga, igb = i_d[c4]
            wab, b_wab = WS.get(ia)
            wga, b_wga = WS.get(iga)
            wgB, b_wgB = WS.get(igb)
            for cc in range(4):
                c = c4 * 4 + cc
                ccs = slice(cc * 128, (cc + 1) * 128)
                for s in range(4):
                    hs = slice(s * 512, (s + 1) * 512)
                    rd = [b_hTo[s * 4 + t] for t in range(4)]
                    pb = (it % 2) * 4
                    sa = sgt[(it % 2) * 2]; bsa = b_sgt[(it % 2) * 2]
                    sb = sgt[(it % 2) * 2 + 1]; bsb = b_sgt[(it % 2) * 2 + 1]
                    it += 1
                    mm_group(bank(pb), [(wab[:, kc, ccs], OAT[:, kc, hs]) for kc in range(4)], b_ps[pb],
                             [b_wab] + [b_oat[kc][s] for kc in range(4)])
                    mm_group(bank(pb + 1), [(wab[:, 4 + kc, ccs], YBT[:, kc, hs]) for kc in range(4)], b_ps[pb + 1],
                             [b_wab] + b_ybt)
                    mm_group(bank(pb + 2), [(wga[:, kc, ccs], hTo[:, kc, hs]) for kc in range(8)], b_ps[pb + 2], [b_wga] + rd)
                    mm_group(bank(pb + 3), [(wgB[:, kc, ccs], hTo[:, kc, hs]) for kc in range(8)], b_ps[pb + 3], [b_wgB] + rd)
                    S.op("act", "activation", [b_ps[pb + 2]], [bsa], out=sa, in_=bank(pb + 2), func=AF.Sigmoid)
                    S.op("act", "activation", [b_ps[pb + 3]], [bsb], out=sb, in_=bank(pb + 3), func=AF.Sigmoid)
                    S.op("dve", "tensor_tensor", [b_ps[pb], bsa], [bsa], out=sa, in0=bank(pb), in1=sa, op=ALU.mult)
                    S.op("dve", "tensor_tensor", [b_ps[pb + 1], bsb], [bsb], out=sb, in0=bank(pb + 1), in1=sb, op=ALU.mult)
                    S.op("dve", "tensor_tensor", [bsa, bsb], [b_mrg[c][s]], out=MRG[:, c, hs], in0=sa, in1=sb, op=ALU.add)
            WS.release(ia); WS.release(iga); WS.release(igb)
        S.barrier()

        X = R(DYN + 0, [128, 16, D], F32); b_X = [Buf("X%d" % t) for t in range(16)]
        for t in range(16):
            S.dma("sp", [(X[:, t, :], xown[t * 128:(t + 1) * 128, :], {})], writes=[b_X[t]])
        rb = [0]
        for hf in range(2):
            wm, b_wm = WS.get(i_wmo[hf])
            for t in range(16):
                bk = rb[0] % 8; rb[0] += 1
                mm_group(bank(bk), [(MRG[:, kc, t * 128:(t + 1) * 128], wm[:, kc, :]) for kc in range(8)], b_ps[bk],
                         [b_wm] + [b_mrg[kc][t // 4] for kc in range(8)])
                xs = X[:, t, hf * 512:(hf + 1) * 512]
                S.op("dve", "tensor_tensor", [b_ps[bk], b_X[t]], [b_X[t]], out=xs, in0=bank(bk), in1=xs, op=ALU.add)
            WS.release(i_wmo[hf])
        S.barrier()

        hTs = [R(DYN + 65536 + i * 8192, [128, 8, 512], BF16) for i in range(2)]
        b_hTs = [[Buf("hTs%d_%d" % (i, t)) for t in range(4)] for i in range(2)]
        qmT = R(DYN + 81920, [128, 8, 512], BF16); b_qmT = [Buf("qmT%d" % c) for c in range(8)]
        omT = R(DYN + 90112, [128, 8, 512], BF16); b_omT = [Buf("omT%d" % c) for c in range(8)]
        mT = R(DYN + 98304, [128, 8, 256], BF16); b_mT = [Buf("mT0"), Buf("mT1")]
        KmT = R(DYN + 102400, [128, 8, 256], BF16); b_KmT = Buf("KmT")
        Vm = R(DYN + 106496, [128, 2, D], BF16); b_Vm = Buf("Vm")
        Esm = R(DYN + 110592, [128, 4, 256], F32); b_Esm = Buf("Esm")
        probs = [R(DYN + 114688 + i * 2048, [128, 4, 256], BF16) for i in range(2)]; b_probs = [Buf("pr0"), Buf("pr1")]
        pT = R(DYN + 118784, [128, 8, 512], BF16); b_pT = Buf("pT")
        xn3 = [R(DYN + 126976 + i * 2048, [128, D], BF16) for i in range(2)]; b_xn3 = [Buf("xn3_0"), Buf("xn3_1")]
        junk3 = R(DYN + 131072, [128, D], BF16); b_junk3 = Buf("junk3")
        memring = [R(DYN + 118784 + i * 4096, [128, D], F32) for i in range(2)]; b_mr = [Buf("mr0"), Buf("mr1")]

        load_gain(1, norm_mem_kv)
        load_gain(0, norm_mem_q)
        for mt in range(2):
            S.dma("sp", [(memring[mt], memb[mt * 128:(mt + 1) * 128, :], {})], writes=[b_mr[mt]])
            norm_tile(memring[mt], b_mr[mt], 1, xn3[mt], b_xn3[mt], junk3, b_junk3)
            transpose_tile(xn3[mt], b_xn3[mt], mT[:, :, mt * 128:(mt + 1) * 128], b_mT[mt], (4, 5))
        wK = [WS.get(i_wkvK[k]) for k in range(2)]
        for c in range(8):
            bk = 6 + c % 2
            w_, bw_ = wK[c // 4]
            mm_group(bank(bk)[:, 0:256], [(w_[:, kc, (c % 4) * 128:(c % 4 + 1) * 128], mT[:, kc, :]) for kc in range(8)],
                     b_ps[bk], [bw_] + b_mT)
            evac(KmT[:, c, :], bank(bk)[:, 0:256], [b_ps[bk]], [b_KmT])
        WS.release(i_wkvK[0]); WS.release(i_wkvK[1])
        wVv = [WS.get(i_wkvV[k]) for k in range(2)]
        for mt in range(2):
            for hf in range(2):
                bk = 6 + hf
                w_, bw_ = wVv[hf]
                mm_group(bank(bk), [(mT[:, kc, mt * 128:(mt + 1) * 128], w_[:, kc, :]) for kc in range(8)],
                         b_ps[bk], [bw_, b_mT[mt]])
                evac(Vm[:, mt, hf * 512:(hf + 1) * 512], bank(bk), [b_ps[bk]], [b_Vm])
        WS.release(i_wkvV[0]); WS.release(i_wkvV[1])
        S.barrier()

        wMQ = [WS.get(i_wmq[k]) for k in range(2)]
        wMO = [WS.get(i_wmo2[k]) for k in range(2)]
        gb = [0]
        qmT2 = [qmT, R(DYN + 133120, [128, 8, 512], BF16)]
        b_qmT2 = [b_qmT, [Buf("qmTb%d" % c) for c in range(8)]]
        pT2 = [pT, R(DYN + 141312, [128, 8, 512], BF16)]
        b_pT2 = [b_pT, Buf("pTb")]

        def fg_norm(s):
            hb = hTs[s % 2]; bh = b_hTs[s % 2]
            norm_pipeline(4, lambda k, s=s: (X[:, s * 4 + k, :], b_X[s * 4 + k]), 0, xn3, b_xn3, junk3, b_junk3,
                          lambda k, hb=hb, bh=bh: (hb[:, :, k * 128:(k + 1) * 128], bh[k]), (4, 5))

        def q_groups(s):
            hb = hTs[s % 2]; bh = b_hTs[s % 2]
            qm = qmT2[s % 2]; bq = b_qmT2[s % 2]
            out = []
            for c in range(8):
                def f(c=c):
                    bk = 6 + gb[0] % 2; gb[0] += 1
                    w_, bw_ = wMQ[c // 4]
                    mm_group(bank(bk), [(w_[:, kc, (c % 4) * 128:(c % 4 + 1) * 128], hb[:, kc, :]) for kc in range(8)],
                             b_ps[bk], [bw_] + bh)
                    evac(qm[:, c, :], bank(bk), [b_ps[bk]], [bq[c]])
                out.append(f)
            return out

        def pvo_groups(s):
            pT_ = pT2[s % 2]; bpT = b_pT2[s % 2]
            out = []
            for h in range(4):
                for dc in range(2):
                    def f(h=h, dc=dc):
                        c = h * 2 + dc
                        bk = 6 + gb[0] % 2; gb[0] += 1
                        mm_group(bank(bk), [(Vm[:, mt, c * 128:(c + 1) * 128], pT_[:, h * 2 + mt, :]) for mt in range(2)],
                                 b_ps[bk], [b_Vm, bpT])
                        evac(omT[:, c, :], bank(bk), [b_ps[bk]], [b_omT[c]])
                    out.append(f)
            for t in range(4):
                for hf in range(2):
                    def f(t=t, hf=hf):
                        tt = s * 4 + t
                        bk = 6 + gb[0] % 2; gb[0] += 1
                        w_, bw_ = wMO[hf]
                        mm_group(bank(bk), [(omT[:, kc, t * 128:(t + 1) * 128], w_[:, kc, :]) for kc in range(8)],
                                 b_ps[bk], [bw_] + b_omT)
                        xs = X[:, tt, hf * 512:(hf + 1) * 512]
                        S.op("dve", "tensor_tensor", [b_ps[bk], b_X[tt]], [b_X[tt]], out=xs, in0=bank(bk), in1=xs, op=ALU.add)
                    out.append(f)
            return out

        def softmax_steps(s):
            qm = qmT2[s % 2]; bq = b_qmT2[s % 2]
            pT_ = pT2[s % 2]; bpT = b_pT2[s % 2]

            def sA(t):
                sb0 = (t % 2) * 2
                psS = bank(sb0, 2).rearrange("p (a b) -> p a b", a=4)
                for h in range(4):
                    for cc in range(2):
                        c = 2 * h + cc
                        S.op("pe", "matmul", [bq[c], b_KmT], [b_ps[sb0]], psS[:, h, :],
                             lhsT=qm[:, c, t * 128:(t + 1) * 128], rhs=KmT[:, c, :], start=(cc == 0), stop=(cc == 1),
                             _mark=(h == 3 and cc == 1))

            def sB(t):
                sb0 = (t % 2) * 2
                psS = bank(sb0, 2).rearrange("p (a b) -> p a b", a=4)
                b_S = b_ps[sb0]
                pr = probs[t % 2]; bpr = b_probs[t % 2]
                S.op("dve", "tensor_reduce", [b_S], [b_mx], out=mx4, in_=psS, axis=AX.X, op=ALU.max)
                S.op("dve", "tensor_scalar", [b_mx], [b_nb], out=nb4, in0=mx4, scalar1=-1.0 / 16, scalar2=None, op0=ALU.mult)
                for h in range(4):
                    S.op("act", "activation", [b_S, b_nb], [b_Esm, b_sm], out=Esm[:, h, :], in_=psS[:, h, :], func=AF.Exp,
                         scale=1.0 / 16, bias=nb4[:, h:h + 1], accum_out=sm4[:, h:h + 1])
                S.op("dve", "reciprocal", [b_sm], [b_rs], out=rs4, in_=sm4)
                for h in range(4):
                    S.op("dve", "tensor_scalar", [b_Esm, b_rs], [bpr], out=pr[:, h, :], in0=Esm[:, h, :],
                         scalar1=rs4[:, h:h + 1], scalar2=None, op0=ALU.mult)

            def sC(t):
                pr = probs[t % 2]; bpr = b_probs[t % 2]
                tb_ = 4 + (t % 2)
                pst = bank_bf(tb_)
                for h in range(4):
                    for mt in range(2):
                        k8 = h * 2 + mt
                        S.op("pe", "transpose", [bpr, b_ident], [b_ps[tb_]], out=pst[:, k8 * 128:(k8 + 1) * 128],
                             in_=pr[:, h, mt * 128:(mt + 1) * 128], identity=ident, _mark=(k8 == 7))
                S.op("act", "activation", [b_ps[tb_]], [bpT], out=pT_[:, :, t * 128:(t + 1) * 128],
                     in_=pst.rearrange("p (a b) -> p a b", a=8), func=AF.Copy)

            steps = []
            for step in range(6):
                def f(step=step):
                    if step < 4:
                        sA(step)
                    if 0 <= step - 1 < 4:
                        sB(step - 1)
                    if 0 <= step - 2 < 4:
                        sC(step - 2)
                steps.append(f)
            return steps

        fg_norm(0)
        for f in q_groups(0):
            f()
        fg_norm(1)
        for s in range(5):
            fill = []
            if s - 1 >= 0:
                fill += pvo_groups(s - 1)
            if s + 1 < 4:
                fill += q_groups(s + 1)
            if s < 4:
                steps = softmax_steps(s)
                per = (len(fill) + len(steps) - 1) // len(steps) if fill else 0
                for st_ in steps:
                    st_()
                    for _ in range(per):
                        if fill:
                            fill.pop(0)()
            while fill:
                fill.pop(0)()
            if s + 2 < 4:
                fg_norm(s + 2)
        for k in range(2):
            WS.release(i_wmq[k]); WS.release(i_wmo2[k])
        S.barrier()

        hTb = R(DYN + 65536, [128, 8, 1024], BF16); b_hTb = [Buf("hTb%d" % t) for t in range(8)]
        aT = R(DYN + 81920, [128, 22, 1024], BF16); b_aT = [Buf("aT%d" % j) for j in range(22)]
        sgl = [R(DYN + 126976 + i * 2048, [128, 512], F32) for i in range(2)]; b_sgl = [Buf("sgl0"), Buf("sgl1")]
        xn4 = [R(DYN + 131072 + i * 2048, [128, D], BF16) for i in range(2)] + [R(DYN + 147456, [128, D], BF16)]
        b_xn4 = [Buf("xn4_0"), Buf("xn4_1"), Buf("xn4_2")]
        junk4 = R(DYN + 135168, [128, D], BF16); b_junk4 = Buf("junk4")
        load_gain(1, norm_ffn)
        load_gain(0, norm_final)
        otmp = [R(DYN + 137216 + i * 4096, [128, D], F32) for i in range(2)]; b_ot = [Buf("ot0"), Buf("ot1")]
        junk5 = R(DYN + 145408, [128, D], BF16); b_junk5 = Buf("junk5")
        out_evs = []
        fit = 0
        for tb in range(2):
            gu, fo = i_ffn[tb]
            norm_pipeline(8, lambda k, tb=tb: (X[:, tb * 8 + k, :], b_X[tb * 8 + k]), 1, xn4, b_xn4, junk4, b_junk4,
                          lambda k: (hTb[:, :, k * 128:(k + 1) * 128], b_hTb[k]), (4, 6), mm=True)
            for blk in range(6):
                ig, iu, ncol = gu[blk]
                wg_, b_wg = WS.get(ig)
                wu_, b_wu_ = WS.get(iu)
                for cc in range(ncol // 128):
                    j = blk * 4 + cc
                    ccs = slice(cc * 128, (cc + 1) * 128)
                    for s2 in range(2):
                        hs = slice(s2 * 512, (s2 + 1) * 512)
                        rd = b_hTb[s2 * 4:(s2 + 1) * 4]
                        pb = (fit % 3) * 2
                        sg_ = sgl[fit % 2]; bsg = b_sgl[fit % 2]
                        fit += 1
                        mm_group(bank(pb), [(wg_[:, kc, ccs], hTb[:, kc, hs]) for kc in range(8)], b_ps[pb], [b_wg] + rd)
                        mm_group(bank(pb + 1), [(wu_[:, kc, ccs], hTb[:, kc, hs]) for kc in range(8)], b_ps[pb + 1],
                                 [b_wu_] + rd)
                        S.op("act", "activation", [b_ps[pb]], [bsg], out=sg_, in_=bank(pb), func=AF.Silu)
                        S.op("dve", "tensor_tensor", [b_ps[pb + 1], bsg], [b_aT[j]], out=aT[:, j, hs], in0=bank(pb + 1),
                             in1=sg_, op=ALU.mult)
                WS.release(ig); WS.release(iu)
            for hf in range(2):
                wfo = [WS.get(fo[hf][k3]) for k3 in range(3)]
                for t in range(8):
                    tt = tb * 8 + t
                    bk = 6 + (fit % 2); fit += 1
                    mm_group(bank(bk), [(aT[:, j, t * 128:(t + 1) * 128], wfo[j // 8][0][:, j % 8, :]) for j in range(22)],
                             b_ps[bk], [w[1] for w in wfo] + b_aT)
                    xs = X[:, tt, hf * 512:(hf + 1) * 512]
                    S.op("dve", "tensor_tensor", [b_ps[bk], b_X[tt]], [b_X[tt]], out=xs, in0=bank(bk), in1=xs, op=ALU.add)
                    if hf == 1:
                        i = tt % 2
                        norm_tile(X[:, tt, :], b_X[tt], 0, otmp[i], b_ot[i], junk5, b_junk5)
                        out_evs.append(S.dma("sp", [(out_d[tt * 128:(tt + 1) * 128, :], otmp[i], {})], reads=[b_ot[i]]))
                for k3 in range(3):
                    WS.release(fo[hf][k3])
        for ev in out_evs:
            S._wait("sp", ev)
        print("inst counts", S.n_inst, {e: len(S.prog[e]) for e in S.ENGS})
        S.emit()
    return nc


def _host_inputs(inputs):
    x = np.asarray(inputs["x"], dtype=np.float32)
    mem = np.asarray(inputs["mem"], dtype=np.float32)
    diag = np.zeros((128, 4, 512), np.float32)
    pp = np.arange(128)[:, None]
    kr = np.arange(512)[None, :]
    for i in range(4):
        ql = i * 128 + pp
        diag[:, i, :] = np.where(kr > 511 - ql, 0.0, NEG)
    eye = np.eye(128, dtype=np.float32)
    shared = {
        "diag": diag,
        "norm_mix": np.ascontiguousarray(inputs["norm_mix"][0:1]),
        "w_in": np.ascontiguousarray(inputs["w_in"][0]),
        "conv_w": np.ascontiguousarray(inputs["conv_w"][0]),
        "w_branch_a": np.ascontiguousarray(inputs["w_branch_a"][0]),
        "w_branch_b": np.ascontiguousarray(inputs["w_branch_b"][0]),
        "w_mix_out": np.ascontiguousarray(inputs["w_mix_out"][0]),
        "norm_mem_q": np.ascontiguousarray(inputs["norm_mem_q"][0:1]),
        "norm_mem_kv": np.ascontiguousarray(inputs["norm_mem_kv"][0:1]),
        "w_mem_q": np.ascontiguousarray(inputs["w_mem_q"][0]),
        "w_mem_kv": np.ascontiguousarray(inputs["w_mem_kv"][0]),
        "w_mem_o": np.ascontiguousarray(inputs["w_mem_o"][0]),
        "norm_ffn": np.ascontiguousarray(inputs["norm_ffn"][0:1]),
        "w_ffn_in": np.ascontiguousarray(inputs["w_ffn_in"][0]),
        "w_ffn_out": np.ascontiguousarray(inputs["w_ffn_out"][0]),
        "norm_final": np.ascontiguousarray(np.asarray(inputs["norm_final"]).reshape(1, D)),
    }
    shared = {k: np.asarray(v, dtype=np.float32) for k, v in shared.items()}
    in_maps = []
    for core in range(8):
        b, par = core // 2, core % 2
        tiles = T_OF[par]
        xown = np.zeros((NOWN + 128, D), np.float32)
        seld = np.zeros((128, 4, 2, 128), np.float32)
        seln = np.zeros((1, 4, 2, 128), np.float32)
        for j, T in enumerate(tiles):
            xown[j * 512:(j + 1) * 512] = x[b, T * 512:(T + 1) * 512]
            if T > 0:
                xown[NOWN + 2 * j:NOWN + 2 * j + 2] = x[b, T * 512 - 2:T * 512]
            tmax = NCH[j] - 1
            if T == tmax:
                seld[:, j, 0, :] = eye
            else:
                seln[0, j, 0, :] = 1.0
                seld[:, j, 1, :] = eye
        m = dict(shared)
        m["xall"] = np.ascontiguousarray(x[b, ::-1, :])
        m["xown"] = xown
        m["memb"] = np.ascontiguousarray(mem[b])
        m["seld"] = seld
        m["seln"] = seln
        in_maps.append(m)
    return in_maps


_NC_CACHE = {}


def kernel(**inputs):
    in_maps = _host_inputs(inputs)
    if "nc" not in _NC_CACHE:
        _NC_CACHE["nc"] = build_nc()
    nc = _NC_CACHE["nc"]
    res = run_bass_kernel_spmd(nc, in_maps, core_ids=list(range(8)))
    out = np.zeros((NB, SEQ, D), np.float32)
    for core in range(8):
        b, par = core // 2, core % 2
        o = np.asarray(res.results[core]["out"], dtype=np.float32)
        for j, T in enumerate(T_OF[par]):
            out[b, T * 512:(T + 1) * 512] = o[j * 512:(j + 1) * 512]
    return out
```

```python
import numpy as np
import concourse.bass as bass
import concourse.mybir as mybir
from concourse.bass_utils import run_bass_kernel_spmd
from contextlib import ExitStack

F32 = mybir.dt.float32
BF16 = mybir.dt.bfloat16
AF = mybir.ActivationFunctionType
ALU = mybir.AluOpType
AX = mybir.AxisListType

D = 1024
SEQ = 4096
NB = 4
TS = 512
T_OF = {0: (0, 3, 4, 7), 1: (1, 2, 5, 6)}
NCH = (2, 4, 6, 8)
NEG = -30000.0
EPS = 1e-6
FFN_H = 2816
NOWN = 2048
SAME_ENGINE_SYNC = True


class Buf:
    __slots__ = ("name", "w", "r")

    def __init__(self, name):
        self.name = name
        self.w = None
        self.r = {}


class Sched:
    ENGS = ("pe", "act", "dve", "pool", "sp")

    def __init__(self, nc, stack, n_dma_sems=8):
        self.nc = nc
        self.prog = {e: [] for e in self.ENGS}
        self.count = {e: 0 for e in self.ENGS}
        self.sem = {e: stack.enter_context(nc.semaphore("s_" + e)) for e in self.ENGS}
        self.seen = {e: {} for e in self.ENGS}
        self.dsem = {}
        self.dval = {}
        self.dring = {}
        self.dpos = {}
        idx = 0
        for q in ("sp", "pool"):
            ring = []
            for i in range(n_dma_sems):
                self.dsem[idx] = stack.enter_context(nc.semaphore("d_%s%d" % (q, i)))
                self.dval[idx] = 0
                ring.append(idx)
                idx += 1
            self.dring[q] = ring
            self.dpos[q] = 0
        self.n_inst = {e: 0 for e in self.ENGS}
        self.last_marked = {e: True for e in self.ENGS}

    def _wait(self, e, ev):
        if ev is None:
            return
        if ev[0] == "e":
            _, src, seq = ev
            if src == e and (e == "pe" or not SAME_ENGINE_SYNC):
                return
            assert self.count[src] >= seq, ("dependency on unissued mark", e, ev)
            key = ("e", src)
            val = seq
            sem = self.sem[src]
        else:
            _, sidx, val = ev
            key = ("d", sidx)
            sem = self.dsem[sidx]
        if self.seen[e].get(key, 0) >= val:
            return
        self.seen[e][key] = val
        self.prog[e].append(lambda eng, sem=sem, val=val: eng.wait_ge(sem, val))

    def _deps(self, e, reads, writes):
        for b in reads:
            self._wait(e, b.w)
        for b in writes:
            self._wait(e, b.w)
            for (k0, k1), v in list(b.r.items()):
                self._wait(e, (k0, k1, v))

    def _record(self, ev, reads, writes):
        key = (ev[0], ev[1])
        for b in reads:
            if b.r.get(key, 0) < ev[2]:
                b.r[key] = ev[2]
        for b in writes:
            b.w = ev
            b.r = {}

    def op(self, e, meth, reads, writes, *args, _mark=True, **kw):
        self._deps(e, reads, writes)
        sem = self.sem[e]
        if _mark:
            self.count[e] += 1
            self.prog[e].append(lambda eng: getattr(eng, meth)(*args, **kw).then_inc(sem, 1))
            ev = ("e", e, self.count[e])
        else:
            self.prog[e].append(lambda eng: getattr(eng, meth)(*args, **kw))
            ev = ("e", e, self.count[e] + 1)
        self.last_marked[e] = _mark
        self.n_inst[e] += 1
        self._record(ev, reads, writes)
        return ev

    def dma(self, q, xfers, reads=(), writes=()):
        self._deps(q, reads, writes)
        ring = self.dring[q]
        sidx = ring[self.dpos[q] % len(ring)]
        self.dpos[q] += 1
        if self.dval[sidx] > 0:
            self._wait(q, ("d", sidx, self.dval[sidx]))
        sem = self.dsem[sidx]
        for (o, i, kw) in xfers:
            self.dval[sidx] += 16
            self.prog[q].append(lambda eng, o=o, i=i, kw=kw: eng.dma_start(out=o, in_=i, **kw).then_inc(sem, 16))
        ev = ("d", sidx, self.dval[sidx])
        self._record(ev, reads, writes)
        return ev

    def barrier(self):
        for e in ("pe", "act", "dve"):
            assert self.last_marked[e], ("barrier with unmarked tail", e)
        for sidx in self.dring["sp"]:
            if self.dval[sidx] > 0:
                self._wait("sp", ("d", sidx, self.dval[sidx]))
        for f in ("pe", "act", "dve"):
            self._wait("sp", ("e", f, self.count[f]))
        self.count["sp"] += 1
        sem = self.sem["sp"]
        self.prog["sp"].append(lambda eng, sem=sem: eng.sem_inc(sem, 1))
        for e in ("pe", "act", "dve"):
            self._wait(e, ("e", "sp", self.count["sp"]))

    def emit(self):
        with self.nc.Block() as block:
            def mk(name):
                def body(engine):
                    for c in self.prog[name]:
                        c(engine)
                return body
            block.sync(mk("sp"))
            block.gpsimd(mk("pool"))
            block.scalar(mk("act"))
            block.vector(mk("dve"))
            block.tensor(mk("pe"))


def build_nc():
    nc = bass.Bass("TRN2", target_bir_lowering=False)

    def din(name, shape):
        return nc.dram_tensor(name, list(shape), F32, kind="ExternalInput").ap()

    xall = din("xall", [SEQ, D])
    xown = din("xown", [NOWN + 128, D])
    memb = din("memb", [256, D])
    seld_d = din("seld", [128, 4, 2, 128])
    seln_d = din("seln", [1, 4, 2, 128])
    diag_d = din("diag", [128, 4, 512])
    norm_mix = din("norm_mix", [1, D])
    w_in = din("w_in", [D, 5120])
    conv_w = din("conv_w", [3, 512])
    w_ba = din("w_branch_a", [512, D])
    w_bb = din("w_branch_b", [512, D])
    w_mix_out = din("w_mix_out", [D, D])
    norm_mem_q = din("norm_mem_q", [1, D])
    norm_mem_kv = din("norm_mem_kv", [1, D])
    w_mem_q = din("w_mem_q", [D, D])
    w_mem_kv = din("w_mem_kv", [D, 2 * D])
    w_mem_o = din("w_mem_o", [D, D])
    norm_ffn = din("norm_ffn", [1, D])
    w_ffn_in = din("w_ffn_in", [D, 2 * FFN_H])
    w_ffn_out = din("w_ffn_out", [FFN_H, D])
    norm_final = din("norm_final", [1, D])
    out_d = nc.dram_tensor("out", [NOWN, D], F32, kind="ExternalOutput").ap()

    with ExitStack() as st:
        S = Sched(nc, st)
        ARENA_F32 = 53200
        arena = st.enter_context(nc.sbuf_tensor("arena", [128, ARENA_F32], F32))
        psum = st.enter_context(nc.psum_tensor("psum", [128, 4096], F32))

        def R(off, shape, dt):
            esz = 4 if dt == F32 else 2
            n = int(np.prod(shape[1:]))
            nbytes = n * esz
            assert off % 4 == 0 and nbytes % 4 == 0, (off, shape)
            assert off + nbytes <= ARENA_F32 * 4, (off, shape)
            ap = arena[:, off // 4:(off + nbytes) // 4]
            if dt != F32:
                ap = ap.bitcast(dt)
            if len(shape) == 3:
                ap = ap.rearrange("p (a b) -> p a b", a=shape[1])
            elif len(shape) == 4:
                ap = ap.rearrange("p (a b c) -> p a b c", a=shape[1], b=shape[2])
            return ap

        def bank(i, n=1):
            return psum[:, i * 512:(i + n) * 512]

        def bank_bf(i, n=1):
            return psum[:, i * 512:(i + n) * 512].bitcast(BF16)

        b_ps = [Buf("ps%d" % i) for i in range(8)]

        ident = R(0, [128, 128], BF16); b_ident = Buf("ident")
        identf = R(256, [128, 128], F32); b_identf = Buf("identf")
        convw = R(768, [128, 4, 3], F32); b_convw = Buf("convw")
        stats = R(1024, [128, 256], F32)
        seld = R(2048, [128, 4, 2, 128], BF16); b_seld = Buf("seld")
        seln = R(4096, [128, 4, 2, 128], BF16); b_seln = Buf("seln")
        negrow = R(6144, [128, 512], BF16); b_negrow = Buf("negrow")
        zeros = R(7168, [128, 512], F32); b_zeros = Buf("zeros")
        diag = R(9216, [128, 4, 512], BF16); b_diag = Buf("diag")
        GAIN0 = 13312
        g_rep = [R(GAIN0 + i * 4096, [128, D], F32) for i in range(2)]
        b_g = [Buf("g%d" % i) for i in range(2)]
        RING0 = 21504
        NSLOT = 5
        DYN = RING0 + NSLOT * 8192

        ssq = stats[:, 0:8]; rstd = stats[:, 8:16]
        b_st = [Buf("st%d" % i) for i in range(8)]
        st_pos = [0]
        mx4 = stats[:, 16:20]; nb4 = stats[:, 20:24]; sm4 = stats[:, 24:28]; rs4 = stats[:, 28:32]
        b_mx = Buf("mx"); b_nb = Buf("nb"); b_sm = Buf("sm"); b_rs = Buf("rs")
        cuh = stats[:, 32:40]; b_cuh = Buf("cuh")

        class WStream:
            def __init__(self):
                self.blocks = []
                self.issued = 0
                self.released = set()
                self.bufs = [Buf("ring%d" % i) for i in range(NSLOT)]
                self.next_get = 0

            def add(self, pieces):
                self.blocks.append(pieces)
                return len(self.blocks) - 1

            def slot_ap(self, i):
                return R(RING0 + (i % NSLOT) * 8192, [128, 8, 512], BF16)

            def _issue(self):
                while self.issued < len(self.blocks):
                    k = self.issued
                    if k >= NSLOT and (k - NSLOT) not in self.released:
                        break
                    if k > self.next_get + NSLOT - 1:
                        break
                    sl = self.slot_ap(k)
                    xf = []
                    for (src, kc0, kcn, ncols) in self.blocks[k]:
                        xf.append((sl[:, kc0:kc0 + kcn, 0:ncols], src, {}))
                    S.dma("pool", xf, writes=[self.bufs[k % NSLOT]])
                    self.issued += 1

            def get(self, idx):
                assert idx == self.next_get, (idx, self.next_get)
                self.next_get += 1
                self._issue()
                assert self.issued > idx, ("weight block not issued", idx)
                return self.slot_ap(idx), self.bufs[idx % NSLOT]

            def release(self, idx):
                self.released.add(idx)
                self._issue()

        WS = WStream()

        def wpiece(w, r0, nrows, c0, ncols, kc0=0):
            src = w[r0:r0 + nrows, c0:c0 + ncols].rearrange("(kc p) n -> p kc n", p=128)
            return (src, kc0, nrows // 128, ncols)

        i_wk = WS.add([wpiece(w_in, 0, D, 512, 512)])
        i_wv = WS.add([wpiece(w_in, 0, D, 1024, 512)])
        i_wq = WS.add([wpiece(w_in, 0, D, 0, 512)])
        i_wu = WS.add([wpiece(w_in, 0, D, 1536, 512)])
        i_wgb = WS.add([wpiece(w_in, 0, D, 2048, 512)])
        i_wgc = WS.add([wpiece(w_in, 0, D, 2560, 512)])
        i_d = []
        for c4 in range(2):
            a = WS.add([wpiece(w_ba, 0, 512, c4 * 512, 512, kc0=0), wpiece(w_bb, 0, 512, c4 * 512, 512, kc0=4)])
            b_ = WS.add([wpiece(w_in, 0, D, 3072 + c4 * 512, 512)])
            c_ = WS.add([wpiece(w_in, 0, D, 4096 + c4 * 512, 512)])
            i_d.append((a, b_, c_))
        i_wmo = [WS.add([wpiece(w_mix_out, 0, D, hf * 512, 512)]) for hf in range(2)]
        i_wkvK = [WS.add([wpiece(w_mem_kv, 0, D, k * 512, 512)]) for k in range(2)]
        i_wkvV = [WS.add([wpiece(w_mem_kv, 0, D, D + k * 512, 512)]) for k in range(2)]
        i_wmq = [WS.add([wpiece(w_mem_q, 0, D, k * 512, 512)]) for k in range(2)]
        i_wmo2 = [WS.add([wpiece(w_mem_o, 0, D, k * 512, 512)]) for k in range(2)]
        i_ffn = []
        for tb in range(2):
            gu = []
            for blk in range(6):
                ncol = 512 if blk < 5 else 256
                ig = WS.add([wpiece(w_ffn_in, 0, D, blk * 512, ncol)])
                iu = WS.add([wpiece(w_ffn_in, 0, D, FFN_H + blk * 512, ncol)])
                gu.append((ig, iu, ncol))
            fo = []
            for hf in range(2):
                ks = []
                for k3 in range(3):
                    nr = 1024 if k3 < 2 else FFN_H - 2048
                    ks.append(WS.add([wpiece(w_ffn_out, k3 * 1024, nr, hf * 512, 512)]))
                fo.append(ks)
            i_ffn.append((gu, fo))

        evac_rr = [0]

        def evac(out_ap, in_ap, reads, writes, eng=None):
            if eng is None:
                eng = ("act", "dve")[evac_rr[0] % 2]
                evac_rr[0] += 1
            if eng == "act":
                return S.op("act", "activation", reads, writes, out=out_ap, in_=in_ap, func=AF.Copy)
            return S.op("dve", "tensor_copy", reads, writes, out=out_ap, in_=in_ap)

        def mm_group(out_ap, pairs, b_out, reads):
            n = len(pairs)
            for k, (l, r) in enumerate(pairs):
                S.op("pe", "matmul", reads, [b_out], out_ap, lhsT=l, rhs=r, start=(k == 0), stop=(k == n - 1),
                     _mark=(k == n - 1))

        def load_gain(i, src):
            S.dma("sp", [(g_rep[i], src.broadcast_to([128, D]), {})], writes=[b_g[i]])

        def norm_tile(x_ap, b_x, gi, out_ap, b_out, junk_ap, b_junk_):
            k = st_pos[0] % 8
            st_pos[0] += 1
            bs = b_st[k]
            S.op("act", "activation", [b_x], [b_junk_, bs], out=junk_ap, in_=x_ap, func=AF.Square,
                 accum_out=ssq[:, k:k + 1])
            S.op("act", "activation", [bs], [bs], out=rstd[:, k:k + 1], in_=ssq[:, k:k + 1], func=AF.Ln,
                 scale=1.0 / D, bias=EPS)
            S.op("act", "activation", [bs], [bs], out=rstd[:, k:k + 1], in_=rstd[:, k:k + 1], func=AF.Exp, scale=-0.5)
            S.op("dve", "scalar_tensor_tensor", [b_x, bs, b_g[gi]], [b_out], out=out_ap, in0=x_ap,
                 scalar=rstd[:, k:k + 1], in1=g_rep[gi], op0=ALU.mult, op1=ALU.mult)

        tr_rr = [0]

        def transpose_tile(xn_ap, b_xn_, dst3, b_dst, tbanks, mm=False):
            bk = tbanks[tr_rr[0] % len(tbanks)]
            tr_rr[0] += 1
            if mm:
                pst = bank(bk, 2)
                bb = [b_ps[bk], b_ps[bk + 1]]
                for c in range(8):
                    S.op("pe", "matmul", [b_xn_, b_ident], bb, pst[:, c * 128:(c + 1) * 128],
                         lhsT=xn_ap[:, c * 128:(c + 1) * 128], rhs=ident, start=True, stop=True, _mark=(c == 7))
                evac(dst3, pst.rearrange("p (a b) -> p a b", a=8), bb, [b_dst])
                return
            pst = bank_bf(bk)
            for c in range(8):
                S.op("pe", "transpose", [b_xn_, b_ident], [b_ps[bk]], out=pst[:, c * 128:(c + 1) * 128],
                     in_=xn_ap[:, c * 128:(c + 1) * 128], identity=ident, _mark=(c == 7))
            S.op("act", "activation", [b_ps[bk]], [b_dst], out=dst3, in_=pst.rearrange("p (a b) -> p a b", a=8),
                 func=AF.Copy)

        def norm_pipeline(n, load_fn, gi, xn_bufs, b_xn_bufs, junk_ap, b_junk_, dst_fn, tbanks, after_fn=None, mm=False):
            nb_ = len(xn_bufs)
            la = nb_ - 1

            def pre(k):
                x_ap, b_x = load_fn(k)
                norm_tile(x_ap, b_x, gi, xn_bufs[k % nb_], b_xn_bufs[k % nb_], junk_ap, b_junk_)

            def post(k):
                dst3, b_dst = dst_fn(k)
                transpose_tile(xn_bufs[k % nb_], b_xn_bufs[k % nb_], dst3, b_dst, tbanks, mm=mm)
            for k in range(min(la, n)):
                pre(k)
            for k in range(n):
                if k + la < n:
                    pre(k + la)
                post(k)
                if after_fn is not None:
                    after_fn(k)

        S.op("dve", "memset", [], [b_identf], identf, 1.0)
        S.op("pool", "affine_select", [b_identf], [b_identf], out=identf, in_=identf, pattern=[[-1, 128]],
             compare_op=ALU.is_equal, fill=0.0, base=0, channel_multiplier=1)
        S.op("dve", "tensor_copy", [b_identf], [b_ident], out=ident, in_=identf)
        S.op("dve", "memset", [], [b_negrow], negrow, NEG)
        S.op("dve", "memset", [], [b_zeros], zeros, 0.0)
        S.dma("pool", [(seld, seld_d, {})], writes=[b_seld])
        S.dma("pool", [(seln[0:1], seln_d, {})], writes=[b_seln])
        S.dma("pool", [(diag, diag_d, {})], writes=[b_diag])
        S.dma("sp", [(convw[:, j, :], conv_w[:, j * 128:(j + 1) * 128].rearrange("i p -> p i"),
                      {"allow_slow_non_contiguous": True}) for j in range(4)], writes=[b_convw])
        load_gain(0, norm_mix)

        KT = R(DYN + 0, [128, 4, SEQ], BF16)
        V = R(DYN + 32768, [128, 32, 512], BF16)
        QT = R(DYN + 65536, [128, 4, NOWN], BF16)
        OAT = R(DYN + 81920, [128, 4, NOWN], BF16)
        b_oat = [[Buf("oat%d_%d" % (p, s)) for s in range(4)] for p in range(4)]
        TMP = DYN + 98304
        xring = [R(TMP + i * 4096, [128, D], F32) for i in range(4)]; b_xr = [Buf("xr%d" % i) for i in range(4)]
        xn = [R(TMP + 16384 + i * 2048, [128, D], BF16) for i in range(4)]; b_xn = [Buf("xn%d" % i) for i in range(4)]
        junk = R(TMP + 24576, [128, D], BF16); b_junk = Buf("junk")
        hTa = [R(TMP + 26624 + i * 8192, [128, 8, 512], BF16) for i in range(2)]
        b_hTa = [[Buf("hTa%d_%d" % (i, t)) for t in range(4)] for i in range(2)]
        b_kt = [[Buf("kt%d_%d" % (p, ch)) for ch in range(8)] for p in range(4)]
        b_v = [Buf("v%d" % ch) for ch in range(8)]
        b_qt = [[Buf("qt%d_%d" % (p, s)) for s in range(4)] for p in range(4)]

        wk, b_wk = WS.get(i_wk)
        wv, b_wv = WS.get(i_wv)
        wq, b_wq = WS.get(i_wq)

        mmb = [0]

        def load_all(k):
            xt = xring[k % 4]; bx = b_xr[k % 4]
            S.dma("sp", [(xt, xall[k * 128:(k + 1) * 128, :], {})], writes=[bx])
            return xt, bx

        def dst_all(k):
            ch, t = k // 4, k % 4
            return hTa[ch % 2][:, :, t * 128:(t + 1) * 128], b_hTa[ch % 2][t]

        pend = []

        def flush(nmax):
            for _ in range(min(nmax, len(pend))):
                pend.pop(0)()

        def after_all(k):
            if k % 4 == 3:
                ch = k // 4
                hb = hTa[ch % 2]; bh = b_hTa[ch % 2]
                for p in range(4):
                    def f(p=p, ch=ch, hb=hb, bh=bh):
                        bk = mmb[0] % 4; mmb[0] += 1
                        mm_group(bank(bk), [(wk[:, kc, p * 128:(p + 1) * 128], hb[:, kc, :]) for kc in range(8)],
                                 b_ps[bk], [b_wk] + bh)
                        evac(KT[:, p, ch * 512:(ch + 1) * 512], bank(bk), [b_ps[bk]], [b_kt[p][ch]])
                    pend.append(f)
                for t in range(4):
                    def f(t=t, ch=ch, hb=hb, bh=bh):
                        bk = mmb[0] % 4; mmb[0] += 1
                        mm_group(bank(bk), [(hb[:, kc, t * 128:(t + 1) * 128], wv[:, kc, :]) for kc in range(8)],
                                 b_ps[bk], [b_wv, bh[t]])
                        evac(V[:, ch * 4 + t, :], bank(bk), [b_ps[bk]], [b_v[ch]])
                    pend.append(f)
            flush(2)

        norm_pipeline(32, load_all, 0, xn, b_xn, junk, b_junk, dst_all, (4, 6), after_all, mm=True)

        def load_own(k):
            xt = xring[k % 4]; bx = b_xr[k % 4]
            S.dma("sp", [(xt, xown[k * 128:(k + 1) * 128, :], {})], writes=[bx])
            return xt, bx

        def after_own(k):
            if k % 4 == 3:
                s_ = k // 4
                hb = hTa[s_ % 2]; bh = b_hTa[s_ % 2]
                for p in range(4):
                    def f(p=p, s_=s_, hb=hb, bh=bh):
                        bk = mmb[0] % 4; mmb[0] += 1
                        mm_group(bank(bk), [(wq[:, kc, p * 128:(p + 1) * 128], hb[:, kc, :]) for kc in range(8)],
                                 b_ps[bk], [b_wq] + bh)
                        evac(QT[:, p, s_ * 512:(s_ + 1) * 512], bank(bk), [b_ps[bk]], [b_qt[p][s_]])
                    pend.append(f)
            flush(2)

        norm_pipeline(16, load_own, 0, xn, b_xn, junk, b_junk, dst_all, (4, 6), after_own, mm=True)
        flush(100)
        WS.release(i_wk); WS.release(i_wv)
        WS.release(i_wq)
        S.barrier()

        Fb = [R(TMP + i * 8208, [128, 4, 513], F32) for i in range(2)]; b_F = [Buf("F%d" % i) for i in range(2)]
        D1 = R(TMP + 16416, [128, 4, 513], F32); b_D1 = Buf("D1")
        Pb = R(TMP + 24624, [128, 4, 513], F32); b_P = Buf("P")
        Wb = [R(TMP + 32832 + i * 4096, [128, 4, 512], BF16) for i in range(2)]; b_W = [Buf("W%d" % i) for i in range(2)]
        WTb = [R(TMP + 41024 + i * 4096, [128, 4, 512], BF16) for i in range(2)]; b_WT = [Buf("WT%d" % i) for i in range(2)]
        b_Z = Buf("Z")
        b_WTps = Buf("WTps")
        WTps = bank_bf(4, 2).rearrange("p (a b) -> p a b", a=4)
        for i in range(2):
            S.op("dve", "memset", [], [b_F[i]], Fb[i].rearrange("p a b -> p (a b)"), 0.0)
        S.op("dve", "memset", [], [b_D1], D1.rearrange("p a b -> p (a b)"), 0.0)
        groups = [(s, h, c) for s in range(4) for h in range(8) for c in range(NCH[s])]
        G = len(groups)

        def geom(gi):
            s, h, c = groups[gi]
            n = NCH[s]
            p = h // 2
            rows = slice(0, 64) if h % 2 == 0 else slice(64, 128)
            ob = 6 + (h % 2)
            cr = 8 - n + c
            return s, h, c, n, p, rows, ob, cr

        def stage1(gi):
            s, h, c, n, p, rows, ob, cr = geom(gi)
            k0 = cr * 512
            F_ = Fb[gi % 2]; bF = b_F[gi % 2]
            for i in range(4):
                q0 = s * 512 + i * 128
                zb = bank(i)
                last = (c >= 2)
                S.op("pe", "matmul", [b_qt[p][s], b_kt[p][cr]], [b_Z], zb, lhsT=QT[rows, p, q0:q0 + 128],
                     rhs=KT[rows, p, k0:k0 + 512], start=True, stop=last, _mark=(last and i == 3))
                if c < 2:
                    S.op("pe", "matmul", [b_seld, b_diag], [b_Z], zb, lhsT=seld[:, s, c, :], rhs=diag[:, i, :],
                         start=False, stop=False, _mark=False)
                    S.op("pe", "matmul", [b_seln, b_negrow], [b_Z], zb, lhsT=seln[0:1, s, c, :],
                         rhs=negrow[0:1, :], start=False, stop=True, _mark=(i == 3))
            S.op("act", "activation", [b_Z], [bF], out=F_[:, :, 1:513],
                 in_=bank(0, 4).rearrange("p (a b) -> p a b", a=4), func=AF.Sigmoid, scale=-0.125)

        def stage2(gi):
            s, h, c, n, p, rows, ob, cr = geom(gi)
            F_ = Fb[gi % 2]; bF = b_F[gi % 2]
            W_ = Wb[gi % 2]; bW = b_W[gi % 2]
            if c == 0:
                S.op("dve", "memset", [], [b_D1], D1[:, :, 0:1], 1.0)
            else:
                S.op("dve", "tensor_copy", [b_P], [b_D1], out=D1[:, :, 0:1], in_=Pb[:, :, 512:513])
            S.op("dve", "tensor_tensor_scan", [bF, b_D1], [b_P], out=Pb.rearrange("p a b -> p (a b)"),
                 data0=F_.rearrange("p a b -> p (a b)"), data1=D1.rearrange("p a b -> p (a b)"), initial=0.0,
                 op0=ALU.mult, op1=ALU.add)
            S.op("dve", "tensor_tensor", [b_P], [bW], out=W_, in0=Pb[:, :, 0:512], in1=Pb[:, :, 1:513],
                 op=ALU.subtract)

        def stage3(gi):
            s, h, c, n, p, rows, ob, cr = geom(gi)
            W_ = Wb[gi % 2]; bW = b_W[gi % 2]
            WT_ = WTb[gi % 2]; bWT = b_WT[gi % 2]
            for m in range(4):
                for i in range(4):
                    S.op("pe", "transpose", [bW, b_ident], [b_WTps], out=WTps[:, m, i * 128:(i + 1) * 128],
                         in_=W_[:, i, m * 128:(m + 1) * 128], identity=ident, _mark=(m == 3 and i == 3))
            S.op("act", "activation", [b_WTps], [bWT], out=WT_.rearrange("p a b -> p (a b)"),
                 in_=bank_bf(4, 2), func=AF.Copy)

        def stage4(gi):
            s, h, c, n, p, rows, ob, cr = geom(gi)
            WT_ = WTb[gi % 2]; bWT = b_WT[gi % 2]
            psO = psum[rows, ob * 512:(ob + 1) * 512]
            for m in range(4):
                first = (c == 0 and m == 0)
                lastpv = (c == n - 1 and m == 3)
                S.op("pe", "matmul", [bWT, b_v[cr]], [b_ps[ob]], psO, lhsT=V[:, cr * 4 + m, h * 64:(h + 1) * 64],
                     rhs=WT_[:, m, :], start=first, stop=lastpv, _mark=(m == 3))
            if c == n - 1:
                S.op("act", "activation", [b_ps[ob]], [b_oat[p][s]], out=OAT[rows, p, s * 512:(s + 1) * 512],
                     in_=psO, func=AF.Copy)

        for step in range(G + 3):
            if step < G:
                stage1(step)
            if 0 <= step - 1 < G:
                stage2(step - 1)
            if 0 <= step - 2 < G:
                stage3(step - 2)
            if 0 <= step - 3 < G:
                stage4(step - 3)
        S.barrier()

        hTo = R(DYN + 0, [128, 8, NOWN + 128], BF16); b_hTo = [Buf("hTo%d" % t) for t in range(17)]
        YBT = R(DYN + 34816, [128, 4, NOWN], BF16); b_ybt = [Buf("ybt%d" % j) for j in range(4)]
        xring2 = [R(DYN + 51200 + i * 4096, [128, D], F32) for i in range(2)]
        xn2 = [R(DYN + 59392 + i * 2048, [128, D], BF16) for i in range(2)] + [R(DYN + 139264, [128, D], BF16)]
        junk2 = R(DYN + 63488, [128, D], BF16)
        u_sb = R(DYN + 65536, [128, 512], F32); b_usb = Buf("usb")
        acc = R(DYN + 67584, [128, 512], F32); b_acc = Buf("acc")
        cub = R(DYN + 69632, [128, 2, 516], F32); b_cub = [Buf("cub0"), Buf("cub1")]
        MRG = R(DYN + 98304, [128, 8, NOWN], BF16)
        b_mrg = [[Buf("mrg%d_%d" % (c, s)) for s in range(4)] for c in range(8)]
        sgt = [R(DYN + 131072 + i * 2048, [128, 512], F32) for i in range(4)]; b_sgt = [Buf("sg%d" % i) for i in range(4)]
        b_x2 = [Buf("x2r%d" % i) for i in range(2)]; b_xn2 = [Buf("xn2_%d" % i) for i in range(3)]; b_junk2 = Buf("junk2")

        xring2.append(R(DYN + 141312, [128, D], F32)); b_x2.append(Buf("x2r2"))

        def load_own2(k):
            S.dma("sp", [(xring2[k % 3], xown[k * 128:(k + 1) * 128, :], {})], writes=[b_x2[k % 3]])
            return xring2[k % 3], b_x2[k % 3]

        norm_pipeline(17, load_own2, 0, xn2, b_xn2, junk2, b_junk2,
                      lambda k: (hTo[:, :, k * 128:(k + 1) * 128], b_hTo[k]), (4, 6), mm=True)

        wu, b_wu = WS.get(i_wu)
        wgb, b_wgb = WS.get(i_wgb)
        wgc, b_wgc = WS.get(i_wgc)
        cb = 0
        for j in range(4):
            jc = slice(j * 128, (j + 1) * 128)
            mm_group(bank(0)[:, 0:128], [(wu[:, kc, jc], hTo[:, kc, NOWN:NOWN + 128]) for kc in range(8)],
                     b_ps[0], [b_wu, b_hTo[16]])
            mm_group(bank(1)[:, 0:128], [(wgc[:, kc, jc], hTo[:, kc, NOWN:NOWN + 128]) for kc in range(8)],
                     b_ps[1], [b_wgc, b_hTo[16]])
            S.op("act", "activation", [b_ps[0]], [b_usb], out=u_sb[:, 0:8], in_=bank(0)[:, 0:8], func=AF.Copy)
            S.op("dve", "tensor_tensor", [b_ps[1], b_usb], [b_cuh], out=cuh, in0=bank(1)[:, 0:8], in1=u_sb[:, 0:8],
                 op=ALU.mult)
            for s in range(4):
                hs = slice(s * 512, (s + 1) * 512)
                rd = [b_hTo[s * 4 + t] for t in range(4)]
                pb = 2 + (cb % 2) * 3
                cu = cub[:, cb % 2, :]; bcu = b_cub[cb % 2]
                cb += 1
                mm_group(bank(pb), [(wu[:, kc, jc], hTo[:, kc, hs]) for kc in range(8)], b_ps[pb], [b_wu] + rd)
                mm_group(bank(pb + 1), [(wgc[:, kc, jc], hTo[:, kc, hs]) for kc in range(8)], b_ps[pb + 1], [b_wgc] + rd)
                mm_group(bank(pb + 2), [(wgb[:, kc, jc], hTo[:, kc, hs]) for kc in range(8)], b_ps[pb + 2], [b_wgb] + rd)
                S.op("act", "activation", [b_ps[pb]], [b_usb], out=u_sb, in_=bank(pb), func=AF.Copy)
                S.op("dve", "tensor_copy", [b_cuh], [bcu], out=cu[:, 0:2], in_=cuh[:, 2 * s:2 * s + 2])
                S.op("dve", "tensor_tensor", [b_ps[pb + 1], b_usb], [bcu], out=cu[:, 2:514], in0=bank(pb + 1), in1=u_sb,
                     op=ALU.mult)
                S.op("dve", "tensor_scalar", [bcu, b_convw], [b_acc], out=acc, in0=cu[:, 2:514],
                     scalar1=convw[:, j, 2:3], scalar2=None, op0=ALU.mult)
                S.op("dve", "scalar_tensor_tensor", [bcu, b_convw, b_acc], [b_acc], out=acc, in0=cu[:, 1:513],
                     scalar=convw[:, j, 1:2], in1=acc, op0=ALU.mult, op1=ALU.add)
                S.op("dve", "scalar_tensor_tensor", [bcu, b_convw, b_acc], [b_acc], out=acc, in0=cu[:, 0:512],
                     scalar=convw[:, j, 0:1], in1=acc, op0=ALU.mult, op1=ALU.add)
                S.op("dve", "tensor_tensor", [b_ps[pb + 2], b_acc], [b_ybt[j]], out=YBT[:, j, hs], in0=bank(pb + 2),
                     in1=acc, op=ALU.mult)
        WS.release(i_wu); WS.release(i_wgb); WS.release(i_wgc)

        it = 0
        for c4 in range(2):
            ia, iga, igb = i_d[c4]
            wab, b_wab = WS.get(ia)
            wga, b_wga = WS.get(iga)
            wgB, b_wgB = WS.get(igb)
            for cc in range(4):
                c = c4 * 4 + cc
                ccs = slice(cc * 128, (cc + 1) * 128)
                for s in range(4):
                    hs = slice(s * 512, (s + 1) * 512)
                    rd = [b_hTo[s * 4 + t] for t in range(4)]
                    pb = (it % 2) * 4
                    sa = sgt[(it % 2) * 2]; bsa = b_sgt[(it % 2) * 2]
                    sb = sgt[(it % 2) * 2 + 1]; bsb = b_sgt[(it % 2) * 2 + 1]
                    it += 1
                    mm_group(bank(pb), [(wab[:, kc, ccs], OAT[:, kc, hs]) for kc in range(4)], b_ps[pb],
                             [b_wab] + [b_oat[kc][s] for kc in range(4)])
                    mm_group(bank(pb + 1), [(wab[:, 4 + kc, ccs], YBT[:, kc, hs]) for kc in range(4)], b_ps[pb + 1],
                             [b_wab] + b_ybt)
                    mm_group(bank(pb + 2), [(wga[:, kc, ccs], hTo[:, kc, hs]) for kc in range(8)], b_ps[pb + 2], [b_wga] + rd)
                    mm_group(bank(pb + 3), [(wgB[:, kc, ccs], hTo[:, kc, hs]) for kc in range(8)], b_ps[pb + 3], [b_wgB] + rd)
                    S.op("act", "activation", [b_ps[pb + 2]], [bsa], out=sa, in_=bank(pb + 2), func=AF.Sigmoid)
                    S.op("act", "activation", [b_ps[pb + 3]], [bsb], out=sb, in_=bank(pb + 3), func=AF.Sigmoid)
                    S.op("dve", "tensor_tensor", [b_ps[pb], bsa], [bsa], out=sa, in0=bank(pb), in1=sa, op=ALU.mult)
                    S.op("dve", "tensor_tensor", [b_ps[pb + 1], bsb], [bsb], out=sb, in0=bank(pb + 1), in1=sb, op=ALU.mult)
                    S.op("dve", "tensor_tensor", [bsa, bsb], [b_mrg[c][s]], out=MRG[:, c, hs], in0=sa, in1=sb, op=ALU.add)
            WS.release(ia); WS.release(iga); WS.release(igb)
        S.barrier()

        X = R(DYN + 0, [128, 16, D], F32); b_X = [Buf("X%d" % t) for t in range(16)]
        for t in range(16):
            S.dma("sp", [(X[:, t, :], xown[t * 128:(t + 1) * 128, :], {})], writes=[b_X[t]])
        rb = [0]
        for hf in range(2):
            wm, b_wm = WS.get(i_wmo[hf])
            for t in range(16):
                bk = rb[0] % 8; rb[0] += 1
                mm_group(bank(bk), [(MRG[:, kc, t * 128:(t + 1) * 128], wm[:, kc, :]) for kc in range(8)], b_ps[bk],
                         [b_wm] + [b_mrg[kc][t // 4] for kc in range(8)])
                xs = X[:, t, hf * 512:(hf + 1) * 512]
                S.op("dve", "tensor_tensor", [b_ps[bk], b_X[t]], [b_X[t]], out=xs, in0=bank(bk), in1=xs, op=ALU.add)
            WS.release(i_wmo[hf])
        S.barrier()

        hTs = [R(DYN + 65536 + i * 8192, [128, 8, 512], BF16) for i in range(2)]
        b_hTs = [[Buf("hTs%d_%d" % (i, t)) for t in range(4)] for i in range(2)]
        qmT = R(DYN + 81920, [128, 8, 512], BF16); b_qmT = [Buf("qmT%d" % c) for c in range(8)]
        omT = R(DYN + 90112, [128, 8, 512], BF16); b_omT = [Buf("omT%d" % c) for c in range(8)]
        mT = R(DYN + 98304, [128, 8, 256], BF16); b_mT = [Buf("mT0"), Buf("mT1")]
        KmT = R(DYN + 102400, [128, 8, 256], BF16); b_KmT = Buf("KmT")
        Vm = R(DYN + 106496, [128, 2, D], BF16); b_Vm = Buf("Vm")
        Esm = R(DYN + 110592, [128, 4, 256], F32); b_Esm = Buf("Esm")
        probs = [R(DYN + 114688 + i * 2048, [128, 4, 256], BF16) for i in range(2)]; b_probs = [Buf("pr0"), Buf("pr1")]
        pT = R(DYN + 118784, [128, 8, 512], BF16); b_pT = Buf("pT")
        xn3 = [R(DYN + 126976 + i * 2048, [128, D], BF16) for i in range(2)]; b_xn3 = [Buf("xn3_0"), Buf("xn3_1")]
        junk3 = R(DYN + 131072, [128, D], BF16); b_junk3 = Buf("junk3")
        memring = [R(DYN + 118784 + i * 4096, [128, D], F32) for i in range(2)]; b_mr = [Buf("mr0"), Buf("mr1")]

        load_gain(1, norm_mem_kv)
        load_gain(0, norm_mem_q)
        for mt in range(2):
            S.dma("sp", [(memring[mt], memb[mt * 128:(mt + 1) * 128, :], {})], writes=[b_mr[mt]])
            norm_tile(memring[mt], b_mr[mt], 1, xn3[mt], b_xn3[mt], junk3, b_junk3)
            transpose_tile(xn3[mt], b_xn3[mt], mT[:, :, mt * 128:(mt + 1) * 128], b_mT[mt], (4, 5))
        wK = [WS.get(i_wkvK[k]) for k in range(2)]
        for c in range(8):
            bk = 6 + c % 2
            w_, bw_ = wK[c // 4]
            mm_group(bank(bk)[:, 0:256], [(w_[:, kc, (c % 4) * 128:(c % 4 + 1) * 128], mT[:, kc, :]) for kc in range(8)],
                     b_ps[bk], [bw_] + b_mT)
            evac(KmT[:, c, :], bank(bk)[:, 0:256], [b_ps[bk]], [b_KmT])
        WS.release(i_wkvK[0]); WS.release(i_wkvK[1])
        wVv = [WS.get(i_wkvV[k]) for k in range(2)]
        for mt in range(2):
            for hf in range(2):
                bk = 6 + hf
                w_, bw_ = wVv[hf]
                mm_group(bank(bk), [(mT[:, kc, mt * 128:(mt + 1) * 128], w_[:, kc, :]) for kc in range(8)],
                         b_ps[bk], [bw_, b_mT[mt]])
                evac(Vm[:, mt, hf * 512:(hf + 1) * 512], bank(bk), [b_ps[bk]], [b_Vm])
        WS.release(i_wkvV[0]); WS.release(i_wkvV[1])
        S.barrier()

        wMQ = [WS.get(i_wmq[k]) for k in range(2)]
        wMO = [WS.get(i_wmo2[k]) for k in range(2)]
        gb = [0]
        qmT2 = [qmT, R(DYN + 133120, [128, 8, 512], BF16)]
        b_qmT2 = [b_qmT, [Buf("qmTb%d" % c) for c in range(8)]]
        pT2 = [pT, R(DYN + 141312, [128, 8, 512], BF16)]
        b_pT2 = [b_pT, Buf("pTb")]

        def fg_norm(s):
            hb = hTs[s % 2]; bh = b_hTs[s % 2]
            norm_pipeline(4, lambda k, s=s: (X[:, s * 4 + k, :], b_X[s * 4 + k]), 0, xn3, b_xn3, junk3, b_junk3,
                          lambda k, hb=hb, bh=bh: (hb[:, :, k * 128:(k + 1) * 128], bh[k]), (4, 5))

        def q_groups(s):
            hb = hTs[s % 2]; bh = b_hTs[s % 2]
            qm = qmT2[s % 2]; bq = b_qmT2[s % 2]
            out = []
            for c in range(8):
                def f(c=c):
                    bk = 6 + gb[0] % 2; gb[0] += 1
                    w_, bw_ = wMQ[c // 4]
                    mm_group(bank(bk), [(w_[:, kc, (c % 4) * 128:(c % 4 + 1) * 128], hb[:, kc, :]) for kc in range(8)],
                             b_ps[bk], [bw_] + bh)
                    evac(qm[:, c, :], bank(bk), [b_ps[bk]], [bq[c]])
                out.append(f)
            return out

        def pvo_groups(s):
            pT_ = pT2[s % 2]; bpT = b_pT2[s % 2]
            out = []
            for h in range(4):
                for dc in range(2):
                    def f(h=h, dc=dc):
                        c = h * 2 + dc
                        bk = 6 + gb[0] % 2; gb[0] += 1
                        mm_group(bank(bk), [(Vm[:, mt, c * 128:(c + 1) * 128], pT_[:, h * 2 + mt, :]) for mt in range(2)],
                                 b_ps[bk], [b_Vm, bpT])
                        evac(omT[:, c, :], bank(bk), [b_ps[bk]], [b_omT[c]])
                    out.append(f)
            for t in range(4):
                for hf in range(2):
                    def f(t=t, hf=hf):
                        tt = s * 4 + t
                        bk = 6 + gb[0] % 2; gb[0] += 1
                        w_, bw_ = wMO[hf]
                        mm_group(bank(bk), [(omT[:, kc, t * 128:(t + 1) * 128], w_[:, kc, :]) for kc in range(8)],
                                 b_ps[bk], [bw_] + b_omT)
                        xs = X[:, tt, hf * 512:(hf + 1) * 512]
                        S.op("dve", "tensor_tensor", [b_ps[bk], b_X[tt]], [b_X[tt]], out=xs, in0=bank(bk), in1=xs, op=ALU.add)
                    out.append(f)
            return out

        def softmax_steps(s):
            qm = qmT2[s % 2]; bq = b_qmT2[s % 2]
            pT_ = pT2[s % 2]; bpT = b_pT2[s % 2]

            def sA(t):
                sb0 = (t % 2) * 2
                psS = bank(sb0, 2).rearrange("p (a b) -> p a b", a=4)
                for h in range(4):
                    for cc in range(2):
                        c = 2 * h + cc
                        S.op("pe", "matmul", [bq[c], b_KmT], [b_ps[sb0]], psS[:, h, :],
                             lhsT=qm[:, c, t * 128:(t + 1) * 128], rhs=KmT[:, c, :], start=(cc == 0), stop=(cc == 1),
                             _mark=(h == 3 and cc == 1))

            def sB(t):
                sb0 = (t % 2) * 2
                psS = bank(sb0, 2).rearrange("p (a b) -> p a b", a=4)
                b_S = b_ps[sb0]
                pr = probs[t % 2]; bpr = b_probs[t % 2]
                S.op("dve", "tensor_reduce", [b_S], [b_mx], out=mx4, in_=psS, axis=AX.X, op=ALU.max)
                S.op("dve", "tensor_scalar", [b_mx], [b_nb], out=nb4, in0=mx4, scalar1=-1.0 / 16, scalar2=None, op0=ALU.mult)
                for h in range(4):
                    S.op("act", "activation", [b_S, b_nb], [b_Esm, b_sm], out=Esm[:, h, :], in_=psS[:, h, :], func=AF.Exp,
                         scale=1.0 / 16, bias=nb4[:, h:h + 1], accum_out=sm4[:, h:h + 1])
                S.op("dve", "reciprocal", [b_sm], [b_rs], out=rs4, in_=sm4)
                for h in range(4):
                    S.op("dve", "tensor_scalar", [b_Esm, b_rs], [bpr], out=pr[:, h, :], in0=Esm[:, h, :],
                         scalar1=rs4[:, h:h + 1], scalar2=None, op0=ALU.mult)

            def sC(t):
                pr = probs[t % 2]; bpr = b_probs[t % 2]
                tb_ = 4 + (t % 2)
                pst = bank_bf(tb_)
                for h in range(4):
                    for mt in range(2):
                        k8 = h * 2 + mt
                        S.op("pe", "transpose", [bpr, b_ident], [b_ps[tb_]], out=pst[:, k8 * 128:(k8 + 1) * 128],
                             in_=pr[:, h, mt * 128:(mt + 1) * 128], identity=ident, _mark=(k8 == 7))
                S.op("act", "activation", [b_ps[tb_]], [bpT], out=pT_[:, :, t * 128:(t + 1) * 128],
                     in_=pst.rearrange("p (a b) -> p a b", a=8), func=AF.Copy)

            steps = []
            for step in range(6):
                def f(step=step):
                    if step < 4:
                        sA(step)
                    if 0 <= step - 1 < 4:
                        sB(step - 1)
                    if 0 <= step - 2 < 4:
                        sC(step - 2)
                steps.append(f)
            return steps

        fg_norm(0)
        for f in q_groups(0):
            f()
        fg_norm(1)
        for s in range(5):
            fill = []
            if s - 1 >= 0:
                fill += pvo_groups(s - 1)
            if s + 1 < 4:
                fill += q_groups(s + 1)
            if s < 4:
                steps = softmax_steps(s)
                per = (len(fill) + len(steps) - 1) // len(steps) if fill else 0
                for st_ in steps:
                    st_()
                    for _ in range(per):
                        if fill:
                            fill.pop(0)()
            while fill:
                fill.pop(0)()
            if s + 2 < 4:
                fg_norm(s + 2)
        for k in range(2):
            WS.release(i_wmq[k]); WS.release(i_wmo2[k])
        S.barrier()

        hTb = R(DYN + 65536, [128, 8, 1024], BF16); b_hTb = [Buf("hTb%d" % t) for t in range(8)]
        aT = R(DYN + 81920, [128, 22, 1024], BF16); b_aT = [Buf("aT%d" % j) for j in range(22)]
        sgl = [R(DYN + 126976 + i * 2048, [128, 512], F32) for i in range(2)]; b_sgl = [Buf("sgl0"), Buf("sgl1")]
        xn4 = [R(DYN + 131072 + i * 2048, [128, D], BF16) for i in range(2)] + [R(DYN + 147456, [128, D], BF16)]
        b_xn4 = [Buf("xn4_0"), Buf("xn4_1"), Buf("xn4_2")]
        junk4 = R(DYN + 135168, [128, D], BF16); b_junk4 = Buf("junk4")
        load_gain(1, norm_ffn)
        load_gain(0, norm_final)
        otmp = [R(DYN + 137216 + i * 4096, [128, D], F32) for i in range(2)]; b_ot = [Buf("ot0"), Buf("ot1")]
        junk5 = R(DYN + 145408, [128, D], BF16); b_junk5 = Buf("junk5")
        out_evs = []
        fit = 0
        def h_norm(tb, after_fn=None):
            norm_pipeline(8, lambda k, tb=tb: (X[:, tb * 8 + k, :], b_X[tb * 8 + k]), 1, xn4, b_xn4, junk4, b_junk4,
                          lambda k: (hTb[:, :, k * 128:(k + 1) * 128], b_hTb[k]), (4, 6), after_fn, mm=True)

        def ffn_out_groups(tb):
            gu, fo = i_ffn[tb]
            wfo_of = {}
            out = []
            for hf in range(2):
                for t in range(8):
                    def f(hf=hf, t=t):
                        if t == 0:
                            wfo_of[hf] = [WS.get(fo[hf][k3]) for k3 in range(3)]
                        wfo = wfo_of[hf]
                        tt = tb * 8 + t
                        bk = fitb[0] % 4; fitb[0] += 1
                        mm_group(bank(bk), [(aT[:, j, t * 128:(t + 1) * 128], wfo[j // 8][0][:, j % 8, :]) for j in range(22)],
                                 b_ps[bk], [w[1] for w in wfo] + b_aT)
                        xs = X[:, tt, hf * 512:(hf + 1) * 512]
                        S.op("dve", "tensor_tensor", [b_ps[bk], b_X[tt]], [b_X[tt]], out=xs, in0=bank(bk), in1=xs, op=ALU.add)
                        if hf == 1:
                            i = tt % 2
                            norm_tile(X[:, tt, :], b_X[tt], 0, otmp[i], b_ot[i], junk5, b_junk5)
                            out_evs.append(S.dma("sp", [(out_d[tt * 128:(tt + 1) * 128, :], otmp[i], {})], reads=[b_ot[i]]))
                        if t == 7:
                            for k3 in range(3):
                                WS.release(fo[hf][k3])
                    out.append(f)
            return out

        fitb = [0]
        h_norm(0)
        for tb in range(2):
            gu, fo = i_ffn[tb]
            for blk in range(6):
                ig, iu, ncol = gu[blk]
                wg_, b_wg = WS.get(ig)
                wu_, b_wu_ = WS.get(iu)
                for cc in range(ncol // 128):
                    j = blk * 4 + cc
                    ccs = slice(cc * 128, (cc + 1) * 128)
                    for s2 in range(2):
                        hs = slice(s2 * 512, (s2 + 1) * 512)
                        rd = b_hTb[s2 * 4:(s2 + 1) * 4]
                        pb = (fit % 3) * 2
                        sg_ = sgl[fit % 2]; bsg = b_sgl[fit % 2]
                        fit += 1
                        mm_group(bank(pb), [(wg_[:, kc, ccs], hTb[:, kc, hs]) for kc in range(8)], b_ps[pb], [b_wg] + rd)
                        mm_group(bank(pb + 1), [(wu_[:, kc, ccs], hTb[:, kc, hs]) for kc in range(8)], b_ps[pb + 1],
                                 [b_wu_] + rd)
                        S.op("act", "activation", [b_ps[pb]], [bsg], out=sg_, in_=bank(pb), func=AF.Silu)
                        S.op("dve", "tensor_tensor", [b_ps[pb + 1], bsg], [b_aT[j]], out=aT[:, j, hs], in0=bank(pb + 1),
                             in1=sg_, op=ALU.mult)
                WS.release(ig); WS.release(iu)
            grp = ffn_out_groups(tb)
            if tb == 0:
                def aft(k):
                    for _ in range(2):
                        if grp:
                            grp.pop(0)()
                h_norm(1, aft)
            while grp:
                grp.pop(0)()
        for ev in out_evs:
            S._wait("sp", ev)
        print("inst counts", S.n_inst, {e: len(S.prog[e]) for e in S.ENGS})
        S.emit()
    return nc


def _host_inputs(inputs):
    x = np.asarray(inputs["x"], dtype=np.float32)
    mem = np.asarray(inputs["mem"], dtype=np.float32)
    diag = np.zeros((128, 4, 512), np.float32)
    pp = np.arange(128)[:, None]
    kr = np.arange(512)[None, :]
    for i in range(4):
        ql = i * 128 + pp
        diag[:, i, :] = np.where(kr > 511 - ql, 0.0, NEG)
    eye = np.eye(128, dtype=np.float32)
    shared = {
        "diag": diag,
        "norm_mix": np.ascontiguousarray(inputs["norm_mix"][0:1]),
        "w_in": np.ascontiguousarray(inputs["w_in"][0]),
        "conv_w": np.ascontiguousarray(inputs["conv_w"][0]),
        "w_branch_a": np.ascontiguousarray(inputs["w_branch_a"][0]),
        "w_branch_b": np.ascontiguousarray(inputs["w_branch_b"][0]),
        "w_mix_out": np.ascontiguousarray(inputs["w_mix_out"][0]),
        "norm_mem_q": np.ascontiguousarray(inputs["norm_mem_q"][0:1]),
        "norm_mem_kv": np.ascontiguousarray(inputs["norm_mem_kv"][0:1]),
        "w_mem_q": np.ascontiguousarray(inputs["w_mem_q"][0]),
        "w_mem_kv": np.ascontiguousarray(inputs["w_mem_kv"][0]),
        "w_mem_o": np.ascontiguousarray(inputs["w_mem_o"][0]),
        "norm_ffn": np.ascontiguousarray(inputs["norm_ffn"][0:1]),
        "w_ffn_in": np.ascontiguousarray(inputs["w_ffn_in"][0]),
        "w_ffn_out": np.ascontiguousarray(inputs["w_ffn_out"][0]),
        "norm_final": np.ascontiguousarray(np.asarray(inputs["norm_final"]).reshape(1, D)),
    }
    shared = {k: np.asarray(v, dtype=np.float32) for k, v in shared.items()}
    in_maps = []
    for core in range(8):
        b, par = core // 2, core % 2
        tiles = T_OF[par]
        xown = np.zeros((NOWN + 128, D), np.float32)
        seld = np.zeros((128, 4, 2, 128), np.float32)
        seln = np.zeros((1, 4, 2, 128), np.float32)
        for j, T in enumerate(tiles):
            xown[j * 512:(j + 1) * 512] = x[b, T * 512:(T + 1) * 512]
            if T > 0:
                xown[NOWN + 2 * j:NOWN + 2 * j + 2] = x[b, T * 512 - 2:T * 512]
            tmax = NCH[j] - 1
            if T == tmax:
                seld[:, j, 0, :] = eye
            else:
                seln[0, j, 0, :] = 1.0
                seld[:, j, 1, :] = eye
        m = dict(shared)
        m["xall"] = np.ascontiguousarray(x[b, ::-1, :])
        m["xown"] = xown
        m["memb"] = np.ascontiguousarray(mem[b])
        m["seld"] = seld
        m["seln"] = seln
        in_maps.append(m)
    return in_maps


_NC_CACHE = {}


def kernel(**inputs):
    in_maps = _host_inputs(inputs)
    if "nc" not in _NC_CACHE:
        _NC_CACHE["nc"] = build_nc()
    nc = _NC_CACHE["nc"]
    res = run_bass_kernel_spmd(nc, in_maps, core_ids=list(range(8)))
    out = np.zeros((NB, SEQ, D), np.float32)
    for core in range(8):
        b, par = core // 2, core % 2
        o = np.asarray(res.results[core]["out"], dtype=np.float32)
        for j, T in enumerate(T_OF[par]):
            out[b, T * 512:(T + 1) * 512] = o[j * 512:(j + 1) * 512]
    return out
```
